# Optimizing a Trainium2 kernel written in Bass

```python
import math
import jax
import jax.numpy as jnp
from jax import lax
import numpy as np

D_MODEL = 1024
BATCH = 4
SEQ = 8192
DEPTH = 2

GRID_W = 64
CTX_LEN = 256

RW_HEADS = 6
RW_HEAD_DIM = 64
RW_WIDTH = 384
RW_DECAY_LORA = 64
RW_AAA_LORA = 64
RW_GATE_LORA = 128
RW_GN_EPS = 64e-5

S5_GROUPS = 16
S5_GROUP_CH = 16
S5_WIDTH = 256
S5_STATE = 64
S5_DT_MIN = 1e-3
S5_DT_MAX = 1e-1

ML_HEADS = 4
ML_HEAD_DIM = 96
ML_WIDTH = 384
ML_CHUNK = 64
ML_EPS = 1e-5

D_FF = 2816
N_BRANCH = 3
N_MOD = 9
LN_EPS = 1e-5
DEEPNORM_ALPHA = (2.0 * DEPTH) ** 0.25
DEEPNORM_BETA = (8.0 * DEPTH) ** -0.25

CONV_COLS = 3 * RW_WIDTH + 2 * ML_WIDTH
IN_SPLITS = (CONV_COLS, ML_WIDTH, ML_WIDTH, 4 * ML_HEADS, S5_WIDTH, RW_DECAY_LORA, RW_AAA_LORA, RW_GATE_LORA, N_BRANCH * D_MODEL)
IN_WIDTH = CONV_COLS + 2 * ML_WIDTH + 4 * ML_HEADS + S5_WIDTH + RW_DECAY_LORA + RW_AAA_LORA + RW_GATE_LORA + N_BRANCH * D_MODEL

kernel_name = 'hybrid_rwkv7_s5_mlstm_prefix_dit'


def _split(z, sizes):
    parts, start = [], 0
    for s in sizes:
        parts.append(z[..., start:start + s])
        start += s
    return parts


def _standardize(x, eps):
    xf = x.astype(jnp.float32)
    xc = xf - jnp.mean(xf, axis=-1, keepdims=True)
    return xc * lax.rsqrt(jnp.mean(xc * xc, axis=-1, keepdims=True) + eps)


def _post_norm(x, delta, g, b):
    y = _standardize(DEEPNORM_ALPHA * x + delta, LN_EPS) * g.astype(jnp.float32) + b.astype(jnp.float32)
    return y.astype(x.dtype)


def _modulation(cvec, w, b):
    m = jnp.dot(jax.nn.silu(cvec), w) + b
    return jnp.split(m[..., None, :], N_MOD, axis=-1)


def _swiglu(u, wg, wu, wd):
    return jnp.dot(jax.nn.silu(jnp.dot(u, wg)) * jnp.dot(u, wu), wd)


def _ffn_half(x, shift, scale, gate, wg, wu, wd, g, b):
    u = x * (1.0 + scale) + shift
    return _post_norm(x, 0.5 * gate * _swiglu(u, wg, wu, wd), g, b)


def _conv_grid(z, w):
    b, l, ch = z.shape
    rows = l // GRID_W
    y = lax.conv_general_dilated(z.reshape(b, rows, GRID_W, ch), w[:, :, None, :].astype(z.dtype),
                                 window_strides=(1, 1), padding='SAME',
                                 dimension_numbers=('NHWC', 'HWIO', 'NHWC'), feature_group_count=ch)
    return y.reshape(b, l, ch)


def _conv_seq(z, w):
    wr = w[1].astype(z.dtype)
    zp = jnp.pad(z, ((0, 0), (1, 1), (0, 0)))
    return zp[:, :-2] * wr[0] + zp[:, 1:-1] * wr[1] + zp[:, 2:] * wr[2]


def _rwkv7_scan(r, w, k, v, kk, a, s0, reverse):
    def step(s, inp):
        r_t, w_t, k_t, v_t, kk_t, a_t = inp
        sa = jnp.einsum('bhvk,bhk->bhv', s, kk_t)
        s = s * w_t[:, :, None, :] - sa[..., None] * (kk_t * a_t)[:, :, None, :] + v_t[..., None] * k_t[:, :, None, :]
        return s, jnp.einsum('bhvk,bhk->bhv', s, r_t)
    xs = tuple(jnp.moveaxis(t, 1, 0) for t in (r, w, k, v, kk, a))
    s_fin, ys = lax.scan(step, s0, xs, reverse=reverse)
    return jnp.moveaxis(ys, 0, 1), s_fin


def _rwkv7_branch(r, k, v, w_dn, a_dn, g_dn, p, init):
    b, l, _ = r.shape
    f32 = jnp.float32
    hd = lambda t: t.astype(f32).reshape(b, l, RW_HEADS, RW_HEAD_DIM)
    rh, vh = hd(r), hd(v)
    kk = hd(k * p['rw_k_k'])
    kk = kk * lax.rsqrt(jnp.maximum(jnp.sum(kk * kk, axis=-1, keepdims=True), 1e-24))
    g = jnp.dot(jax.nn.sigmoid(g_dn), p['rw_g_up'])
    r_k = p['rw_r_k'].astype(f32)
    y_sum, bonus, finals = None, None, []
    for d, rev in enumerate((False, True)):
        w_log = -jax.nn.softplus(-(p['rw_w0'][d] + jnp.dot(jnp.tanh(w_dn), p['rw_w_up'][d]))) - 0.5
        decay = jnp.exp(-jnp.exp(w_log.astype(f32)))
        a = jax.nn.sigmoid(p['rw_a0'][d] + jnp.dot(a_dn, p['rw_a_up'][d]))
        k_d = hd(k * (1.0 + (a - 1.0) * p['rw_k_a']))
        s0 = jnp.zeros((b, RW_HEADS, RW_HEAD_DIM, RW_HEAD_DIM), f32) if init is None else init[d]
        y_d, s_d = _rwkv7_scan(rh, hd(decay), k_d, vh, kk, hd(a), s0, rev)
        bonus_d = jnp.sum(rh * k_d * r_k, axis=-1, keepdims=True) * vh
        y_sum = y_d if y_sum is None else y_sum + y_d
        bonus = bonus_d if bonus is None else bonus + bonus_d
        finals.append(s_d)
    y = _standardize(y_sum, RW_GN_EPS).reshape(b, l, RW_WIDTH) * p['rw_gn_g'].astype(f32) + p['rw_gn_b'].astype(f32)
    y = (y + bonus.reshape(b, l, RW_WIDTH)) * g
    return y.astype(r.dtype), finals


def _diag_combine(e1, e2):
    a1, b1 = e1
    a2, b2 = e2
    return a1 * a2, a2 * b1 + b2


def _s5_direction(ug, a_re, a_im, log_dt, b_re, b_im, c_re, c_im, x0, reverse):
    f32 = jnp.float32
    lam = lax.complex(jnp.minimum(a_re.astype(f32), -1e-4), a_im.astype(f32))
    a_bar = jnp.exp(lam * jnp.exp(log_dt.astype(f32))[:, None])
    b_bar = ((a_bar - 1.0) / lam)[..., None] * lax.complex(b_re.astype(f32), b_im.astype(f32))
    bu = jnp.einsum('blgh,gnh->blgn', ug.astype(jnp.complex64), b_bar)
    if reverse:
        bu = bu[:, ::-1]
    if x0 is not None:
        bu = bu.at[:, 0].add(a_bar * x0)
    a_seq = jnp.broadcast_to(a_bar, (1,) + bu.shape[1:])
    _, xs = lax.associative_scan(_diag_combine, (a_seq, bu), axis=1)
    y = jnp.einsum('blgn,ghn->blgh', xs, lax.complex(c_re.astype(f32), c_im.astype(f32))).real
    if reverse:
        y = y[:, ::-1]
    return y, xs[:, -1]


def _s5_branch(u, p, init):
    b, l, _ = u.shape
    f32 = jnp.float32
    ug = u.astype(f32).reshape(b, l, S5_GROUPS, S5_GROUP_CH)
    y = ug * p['s5_d'].astype(f32).reshape(S5_GROUPS, S5_GROUP_CH)
    finals = []
    for d, rev in enumerate((False, True)):
        y_d, x_d = _s5_direction(ug, p['s5_a_re'][d], p['s5_a_im'][d], p['s5_log_dt'][d],
                                 p['s5_b_re'][d], p['s5_b_im'][d], p['s5_c_re'][d], p['s5_c_im'][d],
                                 None if init is None else init[d], rev)
        y = y + y_d
        finals.append(x_d)
    y = jax.nn.gelu(y.reshape(b, l, S5_WIDTH))
    y = y * jax.nn.sigmoid(jnp.dot(y, p['s5_glu_w'].astype(f32)) + p['s5_glu_b'].astype(f32))
    return y.astype(u.dtype), finals


def _mlstm_direction(q, k, v, ig, fg, state0, reverse):
    if reverse:
        q, k, v, ig, fg = (t[:, ::-1] for t in (q, k, v, ig, fg))
    b, l, h, dh = q.shape
    nc = l // ML_CHUNK

    def chunks(t):
        t = t.reshape((b, nc, ML_CHUNK) + t.shape[2:])
        return jnp.moveaxis(jnp.moveaxis(t, 1, 0), 3, 2)

    lower_tri = jnp.tril(jnp.ones((ML_CHUNK, ML_CHUNK), dtype=bool))

    def step(carry, inp):
        c_prev, n_prev, m_prev = carry
        qc, kc, vc, ic, fc = inp
        bcum = jnp.cumsum(fc, axis=-1)
        logd = jnp.where(lower_tri, bcum[..., :, None] - bcum[..., None, :] + ic[..., None, :], -jnp.inf)
        log_inter = bcum + m_prev[..., None]
        m = jnp.maximum(log_inter, jnp.max(logd, axis=-1))
        inter = jnp.exp(log_inter - m)
        s = jnp.einsum('bhjd,bhsd->bhjs', qc, kc) * jnp.exp(logd - m[..., None])
        num = inter[..., None] * jnp.einsum('bhvk,bhjk->bhjv', c_prev, qc) + jnp.einsum('bhjs,bhsv->bhjv', s, vc)
        den = inter * jnp.einsum('bhk,bhjk->bhj', n_prev, qc) + jnp.sum(s, axis=-1)
        hc = num / jnp.maximum(jnp.abs(den), jnp.exp(-m))[..., None]
        b_last = bcum[..., -1]
        log_w = b_last[..., None] - bcum + ic
        m_new = jnp.maximum(b_last + m_prev, jnp.max(log_w, axis=-1))
        wgt = jnp.exp(log_w - m_new[..., None])
        dec = jnp.exp(b_last + m_prev - m_new)
        c_new = dec[..., None, None] * c_prev + jnp.einsum('bhs,bhsv,bhsk->bhvk', wgt, vc, kc)
        n_new = dec[..., None] * n_prev + jnp.einsum('bhs,bhsk->bhk', wgt, kc)
        return (c_new, n_new, m_new), hc

    state, hs = lax.scan(step, state0, tuple(chunks(t) for t in (q, k, v, ig, fg)))
    hs = jnp.moveaxis(jnp.moveaxis(hs, 2, 3), 0, 1).reshape(b, l, h, dh)
    if reverse:
        hs = hs[:, ::-1]
    return hs, state


def _mlstm_branch(q, k, v, o, gl, p, init):
    b, l, _ = q.shape
    f32 = jnp.float32
    hd = lambda t: t.astype(f32).reshape(b, l, ML_HEADS, ML_HEAD_DIM)
    qh = hd(jax.nn.silu(q))
    kh = hd(jax.nn.silu(k)) * (ML_HEAD_DIM ** -0.5)
    vh = hd(v)
    gates = gl.astype(f32).reshape(b, l, 2, 2, ML_HEADS) + p['ml_gate_b'].astype(f32)
    h_sum, finals = None, []
    for d, rev in enumerate((False, True)):
        if init is None:
            s0 = (jnp.zeros((b, ML_HEADS, ML_HEAD_DIM, ML_HEAD_DIM), f32),
                  jnp.zeros((b, ML_HEADS, ML_HEAD_DIM), f32),
                  jnp.zeros((b, ML_HEADS), f32))
        else:
            s0 = init[d]
        h_d, s_d = _mlstm_direction(qh, kh, vh, gates[:, :, d, 0], jax.nn.log_sigmoid(gates[:, :, d, 1]), s0, rev)
        h_sum = h_d if h_sum is None else h_sum + h_d
        finals.append(s_d)
    h_gated = jax.nn.sigmoid(hd(o)) * h_sum
    y = _standardize(h_gated, ML_EPS).reshape(b, l, ML_WIDTH) * p['ml_norm_g'].astype(f32)
    return y.astype(q.dtype), finals


def _mixer(u, p, init, grid):
    z = jnp.dot(u, p['w_in'])
    zc, ml_v, ml_o, ml_gl, s5_u, w_dn, a_dn, g_dn, br_gl = _split(z, IN_SPLITS)
    zc = _conv_grid(zc, p['conv_w']) if grid else _conv_seq(zc, p['conv_w'])
    rw_r, rw_k, rw_v, ml_q, ml_k = _split(zc, (RW_WIDTH, RW_WIDTH, RW_WIDTH, ML_WIDTH, ML_WIDTH))
    rw_y, rw_st = _rwkv7_branch(rw_r, rw_k, rw_v, w_dn, a_dn, g_dn, p, None if init is None else init[0])
    s5_y, s5_st = _s5_branch(s5_u, p, None if init is None else init[1])
    ml_y, ml_st = _mlstm_branch(ml_q, ml_k, ml_v, ml_o, ml_gl, p, None if init is None else init[2])
    return (rw_y, s5_y, ml_y, br_gl), (rw_st, s5_st, ml_st)


def _merge(branches, p):
    rw_y, s5_y, ml_y, br_gl = branches
    g_rw, g_s5, g_ml = jnp.split(jax.nn.sigmoid(br_gl + p['br_gate_b']), N_BRANCH, axis=-1)
    y = g_rw * jnp.dot(rw_y, p['up_rw']) + g_s5 * jnp.dot(s5_y, p['up_s5']) + g_ml * jnp.dot(ml_y, p['up_ml'])
    return jnp.dot(y, p['w_out'])


def setup_inputs(seed: int = 0) -> dict:
    key = jax.random.key(seed)
    ks = jax.random.split(key, 48)
    counter = [0]

    def nxt():
        counter[0] += 1
        return ks[counter[0] - 1]

    def nrm(shape, scale):
        return scale * jax.random.normal(nxt(), shape, jnp.float32)

    def uni(shape, lo, hi):
        return jax.random.uniform(nxt(), shape, jnp.float32, lo, hi)

    D = D_MODEL
    x = nrm((BATCH, SEQ, D), 1.0)
    c = nrm((BATCH, D), 1.0)
    ctx = nrm((BATCH, CTX_LEN, D), 1.0)
    c_ctx = nrm((D,), 1.0)
    ada_w = nrm((DEPTH, D, N_MOD * D), 0.5 * D ** -0.5)
    ada_b = nrm((DEPTH, N_MOD * D), 0.02)
    ln_g = 1.0 + nrm((DEPTH, 3, D), 0.02)
    ln_b = nrm((DEPTH, 3, D), 0.02)
    ffn_w_gate = nrm((DEPTH, 2, D, D_FF), D ** -0.5)
    ffn_w_up = nrm((DEPTH, 2, D, D_FF), D ** -0.5)
    ffn_w_down = nrm((DEPTH, 2, D_FF, D), DEEPNORM_BETA * D_FF ** -0.5)
    w_in = nrm((DEPTH, D, IN_WIDTH), D ** -0.5)
    conv_w = nrm((DEPTH, 3, 3, CONV_COLS), 0.15).at[:, 1, 1].add(0.6)
    rw_w0 = uni((DEPTH, 2, RW_WIDTH), -6.0, -1.0)
    rw_w_up = nrm((DEPTH, 2, RW_DECAY_LORA, RW_WIDTH), 0.5 * RW_DECAY_LORA ** -0.5)
    rw_a0 = nrm((DEPTH, 2, RW_WIDTH), 0.1)
    rw_a_up = nrm((DEPTH, 2, RW_AAA_LORA, RW_WIDTH), RW_AAA_LORA ** -0.5)
    rw_g_up = nrm((DEPTH, RW_GATE_LORA, RW_WIDTH), RW_GATE_LORA ** -0.5)
    rw_k_k = 0.85 + nrm((DEPTH, RW_WIDTH), 0.02)
    rw_k_a = 1.0 + nrm((DEPTH, RW_WIDTH), 0.02)
    rw_r_k = nrm((DEPTH, RW_HEADS, RW_HEAD_DIM), 0.1)
    rw_gn_g = 1.0 + nrm((DEPTH, RW_WIDTH), 0.02)
    rw_gn_b = nrm((DEPTH, RW_WIDTH), 0.02)
    s5_a_re = -0.5 + nrm((DEPTH, 2, S5_GROUPS, S5_STATE), 0.01)
    s5_a_im = math.pi * jnp.arange(S5_STATE, dtype=jnp.float32) + nrm((DEPTH, 2, S5_GROUPS, S5_STATE), 0.01)
    s5_log_dt = uni((DEPTH, 2, S5_GROUPS), math.log(S5_DT_MIN), math.log(S5_DT_MAX))
    s5_b_re = nrm((DEPTH, 2, S5_GROUPS, S5_STATE, S5_GROUP_CH), (2.0 * S5_GROUP_CH) ** -0.5)
    s5_b_im = nrm((DEPTH, 2, S5_GROUPS, S5_STATE, S5_GROUP_CH), (2.0 * S5_GROUP_CH) ** -0.5)
    s5_c_re = nrm((DEPTH, 2, S5_GROUPS, S5_GROUP_CH, S5_STATE), (0.5 * S5_STATE) ** -0.5)
    s5_c_im = nrm((DEPTH, 2, S5_GROUPS, S5_GROUP_CH, S5_STATE), (0.5 * S5_STATE) ** -0.5)
    s5_d = nrm((DEPTH, S5_WIDTH), 0.5)
    s5_glu_w = nrm((DEPTH, S5_WIDTH, S5_WIDTH), S5_WIDTH ** -0.5)
    s5_glu_b = nrm((DEPTH, S5_WIDTH), 0.02)
    ig_b = nrm((DEPTH, 2, 1, ML_HEADS), 0.1)
    fg_b = jnp.linspace(3.0, 6.0, ML_HEADS, dtype=jnp.float32) + nrm((DEPTH, 2, 1, ML_HEADS), 0.1)
    ml_gate_b = jnp.concatenate([ig_b, fg_b], axis=2)
    ml_norm_g = 1.0 + nrm((DEPTH, ML_WIDTH), 0.02)
    up_rw = nrm((DEPTH, RW_WIDTH, D), RW_WIDTH ** -0.5)
    up_s5 = nrm((DEPTH, S5_WIDTH, D), S5_WIDTH ** -0.5)
    up_ml = nrm((DEPTH, ML_WIDTH, D), ML_WIDTH ** -0.5)
    br_gate_b = nrm((DEPTH, N_BRANCH * D), 0.1)
    w_out = nrm((DEPTH, D, D), DEEPNORM_BETA * D ** -0.5)
    return {'x': x, 'c': c, 'ctx': ctx, 'c_ctx': c_ctx, 'ada_w': ada_w, 'ada_b': ada_b,
            'ln_g': ln_g, 'ln_b': ln_b, 'ffn_w_gate': ffn_w_gate, 'ffn_w_up': ffn_w_up, 'ffn_w_down': ffn_w_down,
            'w_in': w_in, 'conv_w': conv_w, 'rw_w0': rw_w0, 'rw_w_up': rw_w_up, 'rw_a0': rw_a0, 'rw_a_up': rw_a_up,
            'rw_g_up': rw_g_up, 'rw_k_k': rw_k_k, 'rw_k_a': rw_k_a, 'rw_r_k': rw_r_k, 'rw_gn_g': rw_gn_g,
            'rw_gn_b': rw_gn_b, 's5_a_re': s5_a_re, 's5_a_im': s5_a_im, 's5_log_dt': s5_log_dt,
            's5_b_re': s5_b_re, 's5_b_im': s5_b_im, 's5_c_re': s5_c_re, 's5_c_im': s5_c_im, 's5_d': s5_d,
            's5_glu_w': s5_glu_w, 's5_glu_b': s5_glu_b, 'ml_gate_b': ml_gate_b, 'ml_norm_g': ml_norm_g,
            'up_rw': up_rw, 'up_s5': up_s5, 'up_ml': up_ml, 'br_gate_b': br_gate_b, 'w_out': w_out}


def reference(x, c, ctx, c_ctx, ada_w, ada_b, ln_g, ln_b, ffn_w_gate, ffn_w_up, ffn_w_down, w_in, conv_w,
              rw_w0, rw_w_up, rw_a0, rw_a_up, rw_g_up, rw_k_k, rw_k_a, rw_r_k, rw_gn_g, rw_gn_b,
              s5_a_re, s5_a_im, s5_log_dt, s5_b_re, s5_b_im, s5_c_re, s5_c_im, s5_d, s5_glu_w, s5_glu_b,
              ml_gate_b, ml_norm_g, up_rw, up_s5, up_ml, br_gate_b, w_out):
    h = ctx
    for i in range(DEPTH):
        last = i == DEPTH - 1
        p = dict(w_in=w_in[i], conv_w=conv_w[i], rw_w0=rw_w0[i], rw_w_up=rw_w_up[i], rw_a0=rw_a0[i],
                 rw_a_up=rw_a_up[i], rw_g_up=rw_g_up[i], rw_k_k=rw_k_k[i], rw_k_a=rw_k_a[i], rw_r_k=rw_r_k[i],
                 rw_gn_g=rw_gn_g[i], rw_gn_b=rw_gn_b[i], s5_a_re=s5_a_re[i], s5_a_im=s5_a_im[i],
                 s5_log_dt=s5_log_dt[i], s5_b_re=s5_b_re[i], s5_b_im=s5_b_im[i], s5_c_re=s5_c_re[i],
                 s5_c_im=s5_c_im[i], s5_d=s5_d[i], s5_glu_w=s5_glu_w[i], s5_glu_b=s5_glu_b[i],
                 ml_gate_b=ml_gate_b[i], ml_norm_g=ml_norm_g[i], up_rw=up_rw[i], up_s5=up_s5[i],
                 up_ml=up_ml[i], br_gate_b=br_gate_b[i], w_out=w_out[i])
        mx = _modulation(c, ada_w[i], ada_b[i])
        mc = _modulation(c_ctx, ada_w[i], ada_b[i])
        x = _ffn_half(x, mx[0], mx[1], mx[2], ffn_w_gate[i, 0], ffn_w_up[i, 0], ffn_w_down[i, 0], ln_g[i, 0], ln_b[i, 0])
        h = _ffn_half(h, mc[0], mc[1], mc[2], ffn_w_gate[i, 0], ffn_w_up[i, 0], ffn_w_down[i, 0], ln_g[i, 0], ln_b[i, 0])
        uc = h * (1.0 + mc[4]) + mc[3]
        ctx_branches, ctx_states = _mixer(uc, p, None, False)
        ux = x * (1.0 + mx[4]) + mx[3]
        lat_branches, _ = _mixer(ux, p, ctx_states, True)
        x = _post_norm(x, mx[5] * _merge(lat_branches, p), ln_g[i, 1], ln_b[i, 1])
        x = _ffn_half(x, mx[6], mx[7], mx[8], ffn_w_gate[i, 1], ffn_w_up[i, 1], ffn_w_down[i, 1], ln_g[i, 2], ln_b[i, 2])
        if not last:
            h = _post_norm(h, mc[5] * _merge(ctx_branches, p), ln_g[i, 1], ln_b[i, 1])
            h = _ffn_half(h, mc[6], mc[7], mc[8], ffn_w_gate[i, 1], ffn_w_up[i, 1], ffn_w_down[i, 1], ln_g[i, 2], ln_b[i, 2])
    return x
```

```python
import numpy as np
from contextlib import ExitStack
import concourse.bass as bass
import concourse.mybir as mybir
from concourse.bass_utils import run_bass_kernel_spmd

F32 = mybir.dt.float32
BF16 = mybir.dt.bfloat16
AF = mybir.ActivationFunctionType
ALU = mybir.AluOpType
AX = mybir.AxisListType


class Reg:
    __slots__ = ("name", "w", "r")

    def __init__(self, name):
        self.name = name
        self.w = None
        self.r = []


class Buf:
    def __init__(self, t, name, nreg=1):
        self.t = t
        self.name = name
        self.regs = [Reg(f"{name}.{i}") for i in range(nreg)]

    def ap(self):
        return self.t.ap() if hasattr(self.t, "ap") and not isinstance(self.t, bass.AP) else self.t

    def __getitem__(self, idx):
        return self.ap()[idx]

    def r(self, i):
        return self.regs[i]


NDMA_SEM = 8


class Prog:
    def __init__(self):
        self.nc = bass.Bass("TRN2", target_bir_lowering=False)
        nc = self.nc
        self.stack = ExitStack()
        self.eng = {"pe": nc.tensor, "dve": nc.vector, "act": nc.scalar, "pool": nc.gpsimd, "sp": nc.sync}
        self.sems = {}
        self.semval = {}
        for e in self.eng:
            self.sems[e] = self.stack.enter_context(nc.semaphore(f"s_{e}"))
            self.semval[e] = 0
        self.dq = {}
        for q in ("sp", "act", "pool"):
            lst = []
            for i in range(NDMA_SEM):
                k = f"d_{q}{i}"
                self.sems[k] = self.stack.enter_context(nc.semaphore(k))
                self.semval[k] = 0
                lst.append(k)
            self.dq[q] = [lst, 0]
        self.waited = {e: {} for e in self.eng}
        self.out_waits = []
        self.ninst = {e: 0 for e in self.eng}
        self.pstack = None
        self.pidx = 0

    def barrier(self):
        for e in self.eng:
            for k, v in self.semval.items():
                if v > 0 and k != e:
                    self._wait(e, k, v)

    def phase(self):
        prog = self

        class _Ph:
            def __enter__(self_):
                prog.pidx += 1
                prog.pstack = ExitStack()
                return prog

            def __exit__(self_, *a):
                prog.barrier()
                prog.pstack.close()
                prog.pstack = None
                return False
        return _Ph()

    def dram(self, name, shape, dtype, kind):
        k = {"in": "ExternalInput", "out": "ExternalOutput", "tmp": "Internal"}[kind]
        t = self.nc.dram_tensor(name, list(shape), dtype, kind=k)
        b = Buf(t, name)
        b.kind = kind
        return b

    def tile(self, shape, dtype, name, nreg=1):
        st = self.pstack if self.pstack is not None else self.stack
        name = f"p{self.pidx}_{name}"
        t = st.enter_context(self.nc.sbuf_tensor(name, list(shape), dtype))
        return Buf(t, name, nreg)

    def psum(self, shape, dtype, name, nreg=1):
        st = self.pstack if self.pstack is not None else self.stack
        name = f"p{self.pidx}_{name}"
        t = st.enter_context(self.nc.psum_tensor(name, list(shape), dtype))
        return Buf(t, name, nreg)

    @staticmethod
    def _regs(lst):
        out = []
        for x in lst or []:
            if isinstance(x, Buf):
                out.extend(x.regs)
            elif isinstance(x, Reg):
                out.append(x)
            elif isinstance(x, tuple):
                out.append(x[0].regs[x[1]])
            else:
                raise TypeError(x)
        return out

    def _wait(self, e, key, val):
        if key is None:
            return
        cur = self.waited[e].get(key, 0)
        if cur >= val:
            return
        self.waited[e][key] = val
        self.eng[e].wait_ge(self.sems[key], val)
        self.ninst[e] += 1

    def _deps(self, e, reads, writes):
        need = {}
        for r in reads:
            if r.w is not None:
                need[r.w[0]] = max(need.get(r.w[0], 0), r.w[1])
        for r in writes:
            if r.w is not None:
                need[r.w[0]] = max(need.get(r.w[0], 0), r.w[1])
            for (k, v) in r.r:
                need[k] = max(need.get(k, 0), v)
        for k, v in need.items():
            self._wait(e, k, v)

    def _mark(self, key, val, reads, writes):
        for r in reads:
            r.r.append((key, val))
            if len(r.r) > 12:
                d = {}
                for (k, v) in r.r:
                    d[k] = max(d.get(k, 0), v)
                r.r = list(d.items())
        for r in writes:
            r.w = (key, val)
            r.r = []

    def op(self, e, fn, reads=None, writes=None):
        reads = self._regs(reads)
        writes = self._regs(writes)
        self._deps(e, reads, writes)
        inst = fn(self.eng[e])
        self.semval[e] += 1
        inst.then_inc(self.sems[e], 1)
        self.ninst[e] += 1
        self._mark(e, self.semval[e], reads, writes)
        return inst

    def dma(self, q, out, in_, reads=None, writes=None, **kw):
        rbufs = reads or []
        wbufs = writes or []
        reads = self._regs(reads)
        writes = self._regs(writes)
        self._deps(q, reads, writes)
        lst, i = self.dq[q]
        key = lst[i % NDMA_SEM]
        self.dq[q][1] = i + 1
        self._wait(q, key, self.semval[key])
        inst = self.eng[q].dma_start(out=out, in_=in_, **kw)
        self.semval[key] += 16
        inst.then_inc(self.sems[key], 16)
        self.ninst[q] += 1
        self._mark(key, self.semval[key], reads, writes)
        for b in wbufs:
            if isinstance(b, Buf) and getattr(b, "kind", None) == "out":
                self.out_waits.append((key, self.semval[key]))
        return inst

    def finish(self):
        d = {}
        for k, v in self.out_waits:
            d[k] = max(d.get(k, 0), v)
        for k, v in d.items():
            self._wait("sp", k, v)
        for e in self.eng:
            if e != "sp" and self.semval[e] > 0:
                self._wait("sp", e, self.semval[e])


D = 1024
DFF = 2816
NFC = 22
DEPTH = 2
ALPHA = (2.0 * DEPTH) ** 0.25
LN_EPS = 1e-5
NLAT = 8192
NCTX = 256
NT = NLAT + NCTX
TT = 256
TILES = [(0, NCTX, 1)] + [(NCTX + i * TT, TT, 0) for i in range(NLAT // TT)]
NCORES = 8
INW = 6288


def make_ones(p, n=128, name="ones"):
    t = p.tile([128, n], F32, name)
    p.op("pool", lambda e: e.memset(t.ap(), 1.0), writes=[t])
    return t


def load_weight_bf16(p, dst, dst_view, src_view, stage, i, shape_free):
    st = stage[i % len(stage)]
    sv = st.ap()[:, :shape_free] if isinstance(shape_free, int) else shape_free(st.ap())
    q = ("sp", "act")[i % 2]
    p.dma(q, sv, src_view, writes=[st])
    ce = ("pool", "dve", "act")[i % 3]
    if ce == "act":
        p.op("act", lambda e: e.copy(out=dst_view, in_=sv), reads=[st], writes=[dst])
    else:
        p.op(ce, lambda e: e.tensor_copy(out=dst_view, in_=sv), reads=[st], writes=[dst])


def emit_mod(p, cvec, adaw_ap, adab_ap, out):
    cs = p.tile([128, 8, 2], F32, "cs")
    ab = p.tile([128, 72], F32, "ab")
    mo = p.tile([128, 72, 2], F32, "mo")
    pan = [p.tile([128, 8, 1024], F32, f"pan{i}") for i in range(2)]
    ps = [p.psum([128, 512], F32, f"ps{i}") for i in range(2)]
    p.dma("sp", cs.ap(), cvec.ap(), writes=[cs])
    p.dma("sp", ab.ap(), adab_ap, writes=[ab])
    p.op("act", lambda e: e.activation(out=cs.ap(), in_=cs.ap(), func=AF.Silu), reads=[cs], writes=[cs])
    awv = adaw_ap.rearrange("(kc p) f -> p kc f", p=128)
    for j in range(9):
        pn = pan[j % 2]
        p.dma(("sp", "act")[j % 2], pn.ap(), awv[:, :, j * 1024:(j + 1) * 1024], writes=[pn])
        for dc in range(8):
            ch = j * 8 + dc
            pp = ps[ch % 2]
            for kc in range(8):
                p.op("pe", lambda e, kc=kc, dc=dc, pn=pn, pp=pp: e.matmul(
                    pp.ap()[:, 0:2], lhsT=pn.ap()[:, kc, dc * 128:(dc + 1) * 128], rhs=cs.ap()[:, kc, :],
                    start=(kc == 0), stop=(kc == 7)), reads=[pn, cs], writes=[pp])
            p.op("dve", lambda e, ch=ch, pp=pp: e.tensor_scalar(
                out=mo.ap()[:, ch, :], in0=pp.ap()[:, 0:2], scalar1=ab.ap()[:, ch:ch + 1], scalar2=None,
                op0=ALU.add), reads=[pp, ab], writes=[mo])
    p.dma("sp", out.ap(), mo.ap(), reads=[mo], writes=[out])


def ln_feature_major(p, y, ysq, n, ones, ps1, ps2, mean, rstd, g, b, dst, dst_reads=None):
    p.op("act", lambda e: e.activation(out=ysq.ap()[:, :, :n], in_=y.ap()[:, :, :n], func=AF.Square),
         reads=[y], writes=[ysq])
    for dc in range(8):
        p.op("pe", lambda e, dc=dc: e.matmul(ps1.ap()[:, :n], lhsT=ones.ap(), rhs=y.ap()[:, dc, :n],
                                              start=(dc == 0), stop=(dc == 7)), reads=[ones, y], writes=[ps1])
    for dc in range(8):
        p.op("pe", lambda e, dc=dc: e.matmul(ps2.ap()[:, :n], lhsT=ones.ap(), rhs=ysq.ap()[:, dc, :n],
                                              start=(dc == 0), stop=(dc == 7)), reads=[ones, ysq], writes=[ps2])
    p.op("dve", lambda e: e.tensor_scalar(out=mean.ap()[:, :n], in0=ps1.ap()[:, :n], scalar1=1.0 / D, scalar2=None,
                                          op0=ALU.mult), reads=[ps1], writes=[mean])
    p.op("dve", lambda e: e.tensor_tensor(out=rstd.ap()[:, :n], in0=mean.ap()[:, :n], in1=mean.ap()[:, :n],
                                          op=ALU.mult), reads=[mean], writes=[rstd])
    p.op("dve", lambda e: e.scalar_tensor_tensor(out=rstd.ap()[:, :n], in0=ps2.ap()[:, :n], scalar=1.0 / D,
                                                 in1=rstd.ap()[:, :n], op0=ALU.mult, op1=ALU.subtract),
         reads=[ps2, rstd], writes=[rstd])
    p.op("dve", lambda e: e.tensor_scalar(out=rstd.ap()[:, :n], in0=rstd.ap()[:, :n], scalar1=LN_EPS, scalar2=None,
                                          op0=ALU.add), reads=[rstd], writes=[rstd])
    p.op("act", lambda e: e.activation(out=rstd.ap()[:, :n], in_=rstd.ap()[:, :n], func=AF.Sqrt),
         reads=[rstd], writes=[rstd])
    p.op("dve", lambda e: e.reciprocal(out=rstd.ap()[:, :n], in_=rstd.ap()[:, :n]), reads=[rstd], writes=[rstd])
    for dc in range(8):
        eng = ("dve", "pool")[dc % 2]
        p.op(eng, lambda e, dc=dc: e.tensor_tensor(out=y.ap()[:, dc, :n], in0=y.ap()[:, dc, :n],
                                                   in1=mean.ap()[:, :n], op=ALU.subtract),
             reads=[y, mean], writes=[y])
        p.op(eng, lambda e, dc=dc: e.tensor_tensor(out=y.ap()[:, dc, :n], in0=y.ap()[:, dc, :n],
                                                   in1=rstd.ap()[:, :n], op=ALU.mult),
             reads=[y, rstd], writes=[y])
        p.op(eng, lambda e, dc=dc: e.tensor_scalar(out=dst.ap()[:, dc, :n], in0=y.ap()[:, dc, :n],
                                                   scalar1=g.ap()[:, dc:dc + 1], scalar2=b.ap()[:, dc:dc + 1],
                                                   op0=ALU.mult, op1=ALU.add),
             reads=[y, g, b], writes=[dst])


def emit_ffn(p, j0, xT, oT, modT, lng_ap, lnb_ap, wg_ap, wu_ap, wd_ap):

    wg_sb = p.tile([128, 8, DFF], BF16, "wg_sb")
    wu_sb = p.tile([128, 8, DFF], BF16, "wu_sb")
    wd_sb = p.tile([128, NFC, D], BF16, "wd_sb")
    stage = [p.tile([128, DFF // 2], F32, f"stage{i}") for i in range(2)]
    mod = p.tile([128, 72, 2], F32, "mod")
    g_sb = p.tile([128, 8], F32, "g_sb")
    b_sb = p.tile([128, 8], F32, "b_sb")
    ops_ = p.tile([128, 8, 2], F32, "onepsc")
    hg = p.tile([128, 8, 2], F32, "hg")
    ones = make_ones(p)
    p.dma("sp", mod.ap(), modT.ap(), reads=[modT], writes=[mod])
    p.dma("sp", g_sb.ap(), lng_ap, writes=[g_sb])
    p.dma("sp", b_sb.ap(), lnb_ap, writes=[b_sb])
    p.op("dve", lambda e: e.tensor_scalar(out=ops_.ap(), in0=mod.ap()[:, (j0 + 1) * 8:(j0 + 2) * 8, :], scalar1=1.0,
                                          scalar2=None, op0=ALU.add), reads=[mod], writes=[ops_])
    p.op("dve", lambda e: e.tensor_scalar(out=hg.ap(), in0=mod.ap()[:, (j0 + 2) * 8:(j0 + 3) * 8, :], scalar1=0.5,
                                          scalar2=None, op0=ALU.mult), reads=[mod], writes=[hg])
    k = 0
    wgv = wg_ap.rearrange("(kc p) f -> kc p f", p=128)
    wuv = wu_ap.rearrange("(kc p) f -> kc p f", p=128)
    wdv = wd_ap.rearrange("(fc p) d -> fc p d", p=128)
    HF = DFF // 2
    for kc in range(8):
        for hh in range(2):
            load_weight_bf16(p, wg_sb, wg_sb.ap()[:, kc, hh * HF:(hh + 1) * HF], wgv[kc][:, hh * HF:(hh + 1) * HF], stage, k, HF); k += 1
            load_weight_bf16(p, wu_sb, wu_sb.ap()[:, kc, hh * HF:(hh + 1) * HF], wuv[kc][:, hh * HF:(hh + 1) * HF], stage, k, HF); k += 1
    for fc in range(NFC):
        load_weight_bf16(p, wd_sb, wd_sb.ap()[:, fc, :], wdv[fc], stage, k, D); k += 1

    x_sb = [p.tile([128, 8, TT], F32, f"x_sb{i}") for i in range(2)]
    u_sb = p.tile([128, 8, TT], BF16, "u_sb")
    a_sb = [p.tile([128, NFC, TT], BF16, "a_sb0")]
    sg = [p.tile([128, TT], F32, f"sg{i}") for i in range(2)]
    y_sb = p.tile([128, 8, TT], F32, "y_sb")
    ysq = p.tile([128, 8, TT], F32, "ysq")
    mean = p.tile([128, TT], F32, "mean")
    rstd = p.tile([128, TT], F32, "rstd")
    psg = [p.psum([128, 512], F32, f"psg{i}") for i in range(2)]
    psu = [p.psum([128, 512], F32, f"psu{i}") for i in range(2)]
    psd = [p.psum([128, 512], F32, f"psd{i}") for i in range(2)]
    ps1 = p.psum([128, 512], F32, "ps1")
    ps2 = p.psum([128, 512], F32, "ps2")
    xv = xT.ap().rearrange("c p t -> p c t")
    ov = oT.ap().rearrange("c p t -> p c t")

    for ti, (t0, n, wh) in enumerate(TILES):
        xs = x_sb[ti % 2]
        asb = a_sb[0]
        osb = ysq
        p.dma(("sp", "act")[ti % 2], xs.ap()[:, :, :n], xv[:, :, t0:t0 + n], reads=[xT], writes=[xs])
        for kc in range(8):
            p.op("dve", lambda e, kc=kc: e.tensor_scalar(
                out=u_sb.ap()[:, kc, :n], in0=xs.ap()[:, kc, :n], scalar1=ops_.ap()[:, kc, wh:wh + 1],
                scalar2=mod.ap()[:, j0 * 8 + kc, wh:wh + 1], op0=ALU.mult, op1=ALU.add),
                reads=[xs, ops_, mod], writes=[u_sb])
        for fc in range(NFC):
            pg = psg[fc % 2]
            pu = psu[fc % 2]
            for kc in range(8):
                p.op("pe", lambda e, kc=kc, fc=fc, pg=pg: e.matmul(
                    pg.ap()[:, :n], lhsT=wg_sb.ap()[:, kc, fc * 128:(fc + 1) * 128], rhs=u_sb.ap()[:, kc, :n],
                    start=(kc == 0), stop=(kc == 7)), reads=[wg_sb, u_sb], writes=[pg])
            for kc in range(8):
                p.op("pe", lambda e, kc=kc, fc=fc, pu=pu: e.matmul(
                    pu.ap()[:, :n], lhsT=wu_sb.ap()[:, kc, fc * 128:(fc + 1) * 128], rhs=u_sb.ap()[:, kc, :n],
                    start=(kc == 0), stop=(kc == 7)), reads=[wu_sb, u_sb], writes=[pu])
            s = sg[fc % 2]
            p.op("act", lambda e, pg=pg, s=s: e.activation(out=s.ap()[:, :n], in_=pg.ap()[:, :n], func=AF.Silu),
                 reads=[pg], writes=[s])
            p.op("dve", lambda e, pu=pu, s=s, fc=fc: e.tensor_tensor(
                out=asb.ap()[:, fc, :n], in0=pu.ap()[:, :n], in1=s.ap()[:, :n], op=ALU.mult),
                reads=[pu, s], writes=[asb])
        p.op("act", lambda e: e.mul(out=xs.ap()[:, :, :n], in_=xs.ap()[:, :, :n], mul=ALPHA), reads=[xs], writes=[xs])
        for dc in range(8):
            pd = psd[dc % 2]
            for fc in range(NFC):
                p.op("pe", lambda e, dc=dc, fc=fc, pd=pd: e.matmul(
                    pd.ap()[:, :n], lhsT=wd_sb.ap()[:, fc, dc * 128:(dc + 1) * 128], rhs=asb.ap()[:, fc, :n],
                    start=(fc == 0), stop=(fc == NFC - 1)), reads=[wd_sb, asb], writes=[pd])
            p.op("dve", lambda e, dc=dc, pd=pd: e.scalar_tensor_tensor(
                out=y_sb.ap()[:, dc, :n], in0=pd.ap()[:, :n], scalar=hg.ap()[:, dc, wh:wh + 1],
                in1=xs.ap()[:, dc, :n], op0=ALU.mult, op1=ALU.add), reads=[pd, hg, xs], writes=[y_sb])
        ln_feature_major(p, y_sb, ysq, n, ones, ps1, ps2, mean, rstd, g_sb, b_sb, osb)
        p.dma(("sp", "act")[(ti + 1) % 2], ov[:, :, t0:t0 + n], osb.ap()[:, :, :n], reads=[osb], writes=[oT])


def emit_inproj(p, xT, modT, win_ap, zall, zallb):
    w_sb = p.tile([128, 8, INW], BF16, "w_sb")
    QW = INW // 4
    stage = [p.tile([128, QW], F32, f"stage{i}") for i in range(2)]
    mod = p.tile([128, 72, 2], F32, "mod")
    ops_ = p.tile([128, 8, 2], F32, "onepsc")
    p.dma("sp", mod.ap(), modT.ap(), reads=[modT], writes=[mod])
    p.op("dve", lambda e: e.tensor_scalar(out=ops_.ap(), in0=mod.ap()[:, 32:40, :], scalar1=1.0, scalar2=None,
                                          op0=ALU.add), reads=[mod], writes=[ops_])
    wv = win_ap.rearrange("(kc p) f -> kc p f", p=128)
    k = 0
    for kc in range(8):
        for qq in range(4):
            load_weight_bf16(p, w_sb, w_sb.ap()[:, kc, qq * QW:(qq + 1) * QW], wv[kc][:, qq * QW:(qq + 1) * QW],
                             stage, k, QW); k += 1
    x_sb = [p.tile([128, 8, TT], F32, "x_sb0")]
    u_sb = [p.tile([128, 8, TT], BF16, "u_sb0")]
    zbig = p.tile([128, NZALL, TT], F32, "zbig")
    p.op("pool", lambda e: e.memset(zbig.ap(), 0.0), writes=[zbig])
    ps = [p.psum([128, 512], F32, f"ps{i}") for i in range(4)]
    xv = xT.ap().rearrange("c p t -> p c t")
    zv = zall.ap().rearrange("c p t -> p c t")
    zvb = zallb.ap().rearrange("c p t -> p c t")
    cnt = 0
    for ti, (t0, n, wh) in enumerate(TILES):
        xs = x_sb[0]
        us = u_sb[0]
        p.dma(("sp", "act")[ti % 2], xs.ap()[:, :, :n], xv[:, :, t0:t0 + n], reads=[xT], writes=[xs])
        for kc in range(8):
            p.op(("dve", "pool")[kc % 2], lambda e, kc=kc: e.tensor_scalar(
                out=us.ap()[:, kc, :n], in0=xs.ap()[:, kc, :n], scalar1=ops_.ap()[:, kc, wh:wh + 1],
                scalar2=mod.ap()[:, 24 + kc, wh:wh + 1], op0=ALU.mult, op1=ALU.add),
                reads=[xs, ops_, mod], writes=[us])
        for ci, (nm, c0, nc_) in enumerate(ZALL):
            pp = ps[cnt % 4]
            for kc in range(8):
                p.op("pe", lambda e, kc=kc, pp=pp, c0=c0, nc_=nc_: e.matmul(
                    pp.ap()[:nc_, :n], lhsT=w_sb.ap()[:, kc, c0:c0 + nc_], rhs=us.ap()[:, kc, :n],
                    start=(kc == 0), stop=(kc == 7)), reads=[w_sb, us], writes=[pp])
            if cnt % 2 == 0:
                p.op("act", lambda e, pp=pp, ci=ci, nc_=nc_: e.copy(out=zbig.ap()[:nc_, ci, :n], in_=pp.ap()[:nc_, :n]),
                     reads=[pp], writes=[zbig])
            else:
                p.op("dve", lambda e, pp=pp, ci=ci, nc_=nc_: e.tensor_copy(out=zbig.ap()[:nc_, ci, :n], in_=pp.ap()[:nc_, :n]),
                     reads=[pp], writes=[zbig])
            cnt += 1
        p.dma(("sp", "act")[(ti + 1) % 2], zv[:, :, t0:t0 + n], zbig.ap()[:, 0:NSCH, :n], reads=[zbig], writes=[zall])
        p.dma(("sp", "act")[ti % 2], zvb[:, :, t0:t0 + n], zbig.ap()[:, NSCH:NZALL, :n], reads=[zbig], writes=[zallb])


SCH = []
for _nm, _c0 in (("rw_r", 0), ("rw_k", 384), ("rw_v", 768)):
    for _i in range(6):
        SCH.append((f"{_nm}{_i}", _c0 + 64 * _i, 64))
for _nm, _c0 in (("ml_q", 1152), ("ml_k", 1536)):
    for _i in range(4):
        SCH.append((f"{_nm}{_i}", _c0 + 96 * _i, 96))
NCONV = len(SCH)
for _i in range(4):
    SCH.append((f"ml_v{_i}", 1920 + 96 * _i, 96))
SCH.append(("ml_gl", 2688, 16))
SCH.append(("s5_u0", 2704, 128))
SCH.append(("s5_u1", 2832, 128))
SCH.append(("wa_dn", 2960, 128))
NSCH = len(SCH)
NOTH = NSCH - NCONV
ZALL = list(SCH)
for _i in range(4):
    ZALL.append((f"ml_o{_i}", 2304 + 96 * _i, 96))
ZALL.append(("g_dn", 3088, 128))
for _i in range(24):
    ZALL.append((f"br{_i}", 3216 + 128 * _i, 128))
NZALL = len(ZALL)
TWO_PI = 2.0 * np.pi


def emit_mixer(p, zs, prm, outs, rev, nctx=NCTX, nlat=NLAT):
    NS = nctx + nlat
    NB = NS // 128
    f32 = F32
    o_rw, o_bn, o_s5, o_ml = outs

    def nat(a, b):
        if not rev:
            return slice(a, b)
        if b <= nctx:
            return slice(nctx - b, nctx - a)
        return slice(nctx + NS - b, nctx + NS - a)

    def T(shape, name, dt=f32):
        return p.tile(shape, dt, "t_" + name)

    cw = T([128, NCONV, 9], "cw"); rwp = T([64, 5, 6], "rwp"); waup = T([128, 384], "waup")
    glb = T([16, 1], "glb"); sel = T([16, 8, 96], "sel"); s5p = T([128, 3, 8], "s5p")
    bz = T([128, 16, 128], "bz"); czc = T([128, 16, 128], "czc"); cst = T([128, 5, 128], "cst")
    ri = T([128, 129], "ri")
    for i, (t, d) in enumerate(((cw, "cw"), (rwp, "rwp"), (waup, "wa_up"), (glb, "glb"), (sel, "sel"), (s5p, "s5p"),
                                (bz, "bz"), (czc, "cz"), (cst, "cst"), (ri, "ri"))):
        p.dma(("sp", "act")[i % 2], t.ap(), prm[d], writes=[t])
    ident = cst.ap()[:, 0, :]
    m_su = cst.ap()[:, 1, :]
    m_sl = cst.ap()[:, 2, :]
    m_iu = cst.ap()[:, 3, :]
    m01 = cst.ap()[:, 4, :]
    ones = make_ones(p)
    mask5 = T([64, 5, 64], "mask5")
    for i, m in enumerate((m_su, m_sl, m_su, m_iu, m_iu)):
        p.op("dve", lambda e, i=i, m=m: e.tensor_copy(out=mask5.ap()[:, i, :], in_=m[0:64, 0:64]), reads=[cst], writes=[mask5])

    pss = [p.psum([128, 512], f32, f"pb{i}") for i in range(4)]
    psb = p.psum([128, 2, 1024], f32, "psbig")
    pcnt = [0]

    def PS():
        pcnt[0] += 1
        return pss[pcnt[0] % 4]

    ecnt = [0]

    def EW():
        ecnt[0] += 1
        return ("dve", "pool")[ecnt[0] % 2]

    def rr(x_ap, n, tmpf, tmpi, reads):
        p.op("dve", lambda e: e.tensor_scalar(out=tmpf, in0=x_ap, scalar1=1.0 / TWO_PI, scalar2=0.5, op0=ALU.mult,
                                              op1=ALU.add), reads=reads, writes=reads)
        p.op("dve", lambda e: e.tensor_copy(out=tmpi, in_=tmpf), reads=reads, writes=reads)
        p.op("dve", lambda e: e.tensor_copy(out=tmpf, in_=tmpi), reads=reads, writes=reads)
        p.op("dve", lambda e: e.scalar_tensor_tensor(out=x_ap, in0=tmpf, scalar=-TWO_PI, in1=x_ap, op0=ALU.mult,
                                                     op1=ALU.add), reads=reads, writes=reads)
        p.op("dve", lambda e: e.tensor_scalar(out=tmpf, in0=x_ap, scalar1=-np.pi, scalar2=TWO_PI, op0=ALU.is_lt,
                                              op1=ALU.mult), reads=reads, writes=reads)
        p.op("dve", lambda e: e.tensor_tensor(out=x_ap, in0=x_ap, in1=tmpf, op=ALU.add), reads=reads, writes=reads)
        p.op("dve", lambda e: e.tensor_scalar(out=tmpf, in0=x_ap, scalar1=np.pi, scalar2=-TWO_PI, op0=ALU.is_gt,
                                              op1=ALU.mult), reads=reads, writes=reads)
        p.op("dve", lambda e: e.tensor_tensor(out=x_ap, in0=x_ap, in1=tmpf, op=ALU.add), reads=reads, writes=reads)
        p.op("dve", lambda e: e.tensor_scalar(out=x_ap, in0=x_ap, scalar1=-3.14159, scalar2=3.14159, op0=ALU.max,
                                              op1=ALU.min), reads=reads, writes=reads)

    sp_ = T([128, 16, 8], "s5small")
    SM = lambda i: sp_.ap()[:, i, :]
    ang = T([128, 8, 129], "ang"); ang2 = T([128, 8, 129], "ang2")
    tf = T([128, 8 * 129], "tf"); ti_ = p.tile([128, 8 * 129], mybir.dt.int32, "t_ti")
    Ct = T([128, 8, 129], "Ct"); St = T([128, 8, 129], "St")
    T1re = T([128, 8, 128], "T1re"); T1im = T([128, 8, 128], "T1im"); RHO = T([128, 8, 128], "RHO")
    S5R = [sp_, s5p, ang, ang2, tf, ti_, Ct, St, T1re, T1im, RHO, ri, ones]
    lre, lim, ldt = s5p.ap()[:, 0, :], s5p.ap()[:, 1, :], s5p.ap()[:, 2, :]

    def o5(eng, fn):
        p.op(eng, fn, reads=S5R, writes=S5R)

    o5("dve", lambda e: e.tensor_scalar(out=lre, in0=lre, scalar1=-1e-4, scalar2=None, op0=ALU.min))
    o5("act", lambda e: e.activation(out=SM(0), in_=ldt, func=AF.Exp))
    o5("dve", lambda e: e.tensor_tensor(out=SM(1), in0=lre, in1=SM(0), op=ALU.mult))
    o5("act", lambda e: e.activation(out=SM(1), in_=SM(1), func=AF.Exp))
    o5("dve", lambda e: e.tensor_tensor(out=SM(2), in0=lim, in1=SM(0), op=ALU.mult))
    rr(SM(2), 8, tf.ap()[:, 0:8], ti_.ap()[:, 0:8], S5R)
    for k in range(8):
        o5("dve", lambda e, k=k: e.tensor_scalar(out=ang.ap()[:, k, :], in0=ri.ap(), scalar1=sp_.ap()[:, 2, k:k + 1],
                                                 scalar2=None, op0=ALU.mult))
    angf = ang.ap().rearrange("p k r -> p (k r)")
    ang2f = ang2.ap().rearrange("p k r -> p (k r)")
    o5("dve", lambda e: e.tensor_scalar(out=ang2f, in0=angf, scalar1=np.pi / 2, scalar2=None, op0=ALU.add))
    rr(angf, 8 * 129, tf.ap(), ti_.ap(), S5R)
    rr(ang2f, 8 * 129, tf.ap(), ti_.ap(), S5R)
    o5("act", lambda e: e.activation(out=St.ap().rearrange("p k r -> p (k r)"), in_=angf, func=AF.Sin))
    o5("act", lambda e: e.activation(out=Ct.ap().rearrange("p k r -> p (k r)"), in_=ang2f, func=AF.Sin))
    o5("dve", lambda e: e.tensor_tensor(out=SM(3), in0=SM(1), in1=Ct.ap()[:, :, 1], op=ALU.mult))
    o5("dve", lambda e: e.tensor_tensor(out=SM(4), in0=SM(1), in1=St.ap()[:, :, 1], op=ALU.mult))
    o5("dve", lambda e: e.tensor_scalar(out=SM(3), in0=SM(3), scalar1=-1.0, scalar2=None, op0=ALU.add))
    o5("dve", lambda e: e.tensor_tensor(out=SM(5), in0=lre, in1=lre, op=ALU.mult))
    o5("dve", lambda e: e.tensor_tensor(out=SM(6), in0=lim, in1=lim, op=ALU.mult))
    o5("dve", lambda e: e.tensor_tensor(out=SM(5), in0=SM(5), in1=SM(6), op=ALU.add))
    o5("dve", lambda e: e.reciprocal(out=SM(5), in_=SM(5)))
    o5("dve", lambda e: e.tensor_tensor(out=SM(6), in0=SM(3), in1=lre, op=ALU.mult))
    o5("dve", lambda e: e.tensor_tensor(out=SM(7), in0=SM(4), in1=lim, op=ALU.mult))
    o5("dve", lambda e: e.tensor_tensor(out=SM(6), in0=SM(6), in1=SM(7), op=ALU.add))
    o5("dve", lambda e: e.tensor_tensor(out=SM(8), in0=SM(6), in1=SM(5), op=ALU.mult))
    o5("dve", lambda e: e.tensor_tensor(out=SM(6), in0=SM(4), in1=lre, op=ALU.mult))
    o5("dve", lambda e: e.tensor_tensor(out=SM(7), in0=SM(3), in1=lim, op=ALU.mult))
    o5("dve", lambda e: e.tensor_tensor(out=SM(6), in0=SM(6), in1=SM(7), op=ALU.subtract))
    o5("dve", lambda e: e.tensor_tensor(out=SM(9), in0=SM(6), in1=SM(5), op=ALU.mult))
    o5("dve", lambda e: e.tensor_scalar(out=SM(10), in0=SM(8), scalar1=-1.0, scalar2=None, op0=ALU.mult))
    for k in range(8):
        C_ = Ct.ap()[:, k, 0:128]; S_ = St.ap()[:, k, 0:128]
        gre = sp_.ap()[:, 8, k:k + 1]; gim = sp_.ap()[:, 9, k:k + 1]; ngre = sp_.ap()[:, 10, k:k + 1]
        o5("dve", lambda e, k=k, C_=C_, gre=gre: e.tensor_scalar(out=T1re.ap()[:, k, :], in0=C_, scalar1=gre, scalar2=None, op0=ALU.mult))
        o5("dve", lambda e, k=k, S_=S_, gim=gim: e.scalar_tensor_tensor(out=T1re.ap()[:, k, :], in0=S_, scalar=gim, in1=T1re.ap()[:, k, :], op0=ALU.mult, op1=ALU.add))
        o5("dve", lambda e, k=k, C_=C_, gim=gim: e.tensor_scalar(out=T1im.ap()[:, k, :], in0=C_, scalar1=gim, scalar2=None, op0=ALU.mult))
        o5("dve", lambda e, k=k, S_=S_, ngre=ngre: e.scalar_tensor_tensor(out=T1im.ap()[:, k, :], in0=S_, scalar=ngre, in1=T1im.ap()[:, k, :], op0=ALU.mult, op1=ALU.add))
        o5("dve", lambda e, k=k: e.tensor_scalar(out=RHO.ap()[:, k, :], in0=ones.ap(), scalar1=sp_.ap()[:, 1, k:k + 1], scalar2=None, op0=ALU.mult))
    p.op("dve", lambda e: e.tensor_scalar(out=czc.ap()[:, 8:16, :], in0=czc.ap()[:, 8:16, :], scalar1=-1.0, scalar2=None,
                                          op0=ALU.mult), reads=[czc], writes=[czc])
    s5i = T([128, 2, 8], "s5init")
    p.op("pool", lambda e: e.memset(s5i.ap(), 0.0), writes=[s5i])
    Vr = T([128, 8, 128], "Vr"); Vi = T([128, 8, 128], "Vi"); t5a = T([128, 8, 128], "t5a"); t5b = T([128, 8, 128], "t5b")
    Wr = T([128, 8, 128], "Wr"); Wi = T([128, 8, 128], "Wi")
    ys5 = T([128, 2, 128], "ys5")

    def s5_block(t0, zo):
        bre = psb.ap()[:, 0, :].rearrange("p (k t) -> p k t", k=8)
        bim = psb.ap()[:, 1, :].rearrange("p (k t) -> p k t", k=8)
        for k in range(8):
            u = zo.ap()[:, 5 + k // 4, :]
            p.op("pe", lambda e, k=k, u=u: e.matmul(bre[:, k, :], lhsT=bz.ap()[:, k, :], rhs=u, start=True, stop=True),
                 reads=[bz, zo], writes=[psb])
            p.op("pe", lambda e, k=k, u=u: e.matmul(bim[:, k, :], lhsT=bz.ap()[:, 8 + k, :], rhs=u, start=True, stop=True),
                 reads=[bz, zo], writes=[psb])
        tt = lambda e, o, a, b, op: e.tensor_tensor(out=o, in0=a, in1=b, op=op)
        p.op("dve", lambda e: tt(e, t5a.ap(), bre, T1re.ap(), ALU.mult), reads=[psb, T1re], writes=[t5a])
        p.op("dve", lambda e: tt(e, t5b.ap(), bim, T1im.ap(), ALU.mult), reads=[psb, T1im], writes=[t5b])
        p.op("pool", lambda e: tt(e, Vr.ap(), t5a.ap(), t5b.ap(), ALU.subtract), reads=[t5a, t5b], writes=[Vr])
        p.op("dve", lambda e: tt(e, t5a.ap(), bre, T1im.ap(), ALU.mult), reads=[psb, T1im], writes=[t5a])
        p.op("dve", lambda e: tt(e, t5b.ap(), bim, T1re.ap(), ALU.mult), reads=[psb, T1re], writes=[t5b])
        p.op("pool", lambda e: tt(e, Vi.ap(), t5a.ap(), t5b.ap(), ALU.add), reads=[t5a, t5b], writes=[Vi])
        for k in range(8):
            p.op("dve", lambda e, k=k: e.tensor_tensor_scan(out=Wr.ap()[:, k, :], data0=RHO.ap()[:, k, :], data1=Vr.ap()[:, k, :],
                                                            initial=s5i.ap()[:, 0, k:k + 1], op0=ALU.mult, op1=ALU.add),
                 reads=[RHO, Vr, s5i], writes=[Wr])
            p.op("dve", lambda e, k=k: e.tensor_tensor_scan(out=Wi.ap()[:, k, :], data0=RHO.ap()[:, k, :], data1=Vi.ap()[:, k, :],
                                                            initial=s5i.ap()[:, 1, k:k + 1], op0=ALU.mult, op1=ALU.add),
                 reads=[RHO, Vi, s5i], writes=[Wi])
        wr_l = Wr.ap()[:, :, 127]; wi_l = Wi.ap()[:, :, 127]; c128 = Ct.ap()[:, :, 128]; s128 = St.ap()[:, :, 128]
        p.op("pool", lambda e: tt(e, SM(11), wr_l, c128, ALU.mult), reads=[Wr, Ct], writes=[sp_])
        p.op("pool", lambda e: tt(e, SM(12), wi_l, s128, ALU.mult), reads=[Wi, St, sp_], writes=[sp_])
        p.op("pool", lambda e: tt(e, s5i.ap()[:, 0, :], SM(11), SM(12), ALU.subtract), reads=[sp_], writes=[s5i])
        p.op("pool", lambda e: tt(e, SM(11), wr_l, s128, ALU.mult), reads=[Wr, St, sp_], writes=[sp_])
        p.op("pool", lambda e: tt(e, SM(12), wi_l, c128, ALU.mult), reads=[Wi, Ct, sp_], writes=[sp_])
        p.op("pool", lambda e: tt(e, s5i.ap()[:, 1, :], SM(11), SM(12), ALU.add), reads=[sp_], writes=[s5i])
        C3 = Ct.ap()[:, :, 0:128]; S3 = St.ap()[:, :, 0:128]
        p.op("dve", lambda e: tt(e, t5a.ap(), Wr.ap(), C3, ALU.mult), reads=[Wr, Ct], writes=[t5a])
        p.op("pool", lambda e: tt(e, t5b.ap(), Wi.ap(), S3, ALU.mult), reads=[Wi, St], writes=[t5b])
        p.op("dve", lambda e: tt(e, Vr.ap(), t5a.ap(), t5b.ap(), ALU.subtract), reads=[t5a, t5b], writes=[Vr])
        p.op("dve", lambda e: tt(e, t5a.ap(), Wr.ap(), S3, ALU.mult), reads=[Wr, St], writes=[t5a])
        p.op("pool", lambda e: tt(e, t5b.ap(), Wi.ap(), C3, ALU.mult), reads=[Wi, Ct], writes=[t5b])
        p.op("dve", lambda e: tt(e, Vi.ap(), t5a.ap(), t5b.ap(), ALU.add), reads=[t5a, t5b], writes=[Vi])
        py = PS()
        for mt in range(2):
            for i, k in enumerate(range(4 * mt, 4 * mt + 4)):
                p.op("pe", lambda e, k=k, mt=mt, i=i: e.matmul(py.ap()[:, mt * 128:(mt + 1) * 128], lhsT=czc.ap()[:, k, :],
                                                               rhs=Vr.ap()[:, k, :], start=(i == 0), stop=False),
                     reads=[czc, Vr], writes=[py])
                p.op("pe", lambda e, k=k, mt=mt, i=i: e.matmul(py.ap()[:, mt * 128:(mt + 1) * 128], lhsT=czc.ap()[:, 8 + k, :],
                                                               rhs=Vi.ap()[:, k, :], start=False, stop=(i == 3)),
                     reads=[czc, Vi], writes=[py])
        p.op("act", lambda e: e.copy(out=ys5.ap().rearrange("p m t -> p (m t)"), in_=py.ap()[:, 0:256]), reads=[py], writes=[ys5])
        store(o_s5, "m p t -> p m t", ys5, t0, 128, 2, "sp")

    Wn = [T([128, NCONV, 384], "Wn0")]
    zo_t = [T([128, NOTH, 128], f"zo{i}") for i in range(2)]
    cz = T([128, NCONV, 128], "cz")
    ctmp = T([128, 128], "ctmp")
    zsv = zs.ap().rearrange("c p t -> p c t")
    HC = 7
    CGR = [(g * HC, min((g + 1) * HC, NCONV)) for g in range((NCONV + HC - 1) // HC)]
    if rev:
        wst = T([128, HC, 384], "wst")
        zst = T([128, NOTH, 128], "zst")
        ost = T([128, 6, 128], "ost")

    def store(obuf, pat, ytile, t0, npart, nh, q):
        dst = obuf.ap()[:, :, nat(t0, t0 + 128)].rearrange(pat)
        if not rev:
            p.dma(q, dst, ytile.ap(), reads=[ytile], writes=[obuf])
        else:
            p.op("pool", lambda e: e.tensor_copy(out=ost.ap()[:npart, :nh, :], in_=ytile.ap()[:, :, ::-1]), reads=[ytile], writes=[ost])
            p.dma(q, dst, ost.ap()[:npart, :nh, :], reads=[ost], writes=[obuf])

    def conv_block(j):
        t0 = j * 128
        W = Wn[0]
        zo = zo_t[j % 2]
        lo_r, hi_r = (0, nctx) if t0 < nctx else (nctx, NS)
        a, b = max(t0 - 128, lo_r), min(t0 + 256, hi_r)
        if a > t0 - 128 or b < t0 + 256:
            p.op("pool", lambda e: e.memset(W.ap(), 0.0), writes=[W])
        wa, wb = a - (t0 - 128), b - (t0 - 128)
        if not rev:
            p.dma("sp", W.ap()[:, :, wa:wb], zsv[:, 0:NCONV, a:b], reads=[zs], writes=[W])
            p.dma("act", zo.ap(), zsv[:, NCONV:NSCH, t0:t0 + 128], reads=[zs], writes=[zo])
        else:
            for hh, (c_lo, c_hi) in enumerate(CGR):
                p.dma(("sp", "act")[hh % 2], wst.ap()[:, 0:c_hi - c_lo, 0:b - a], zsv[:, c_lo:c_hi, nat(a, b)], reads=[zs], writes=[wst])
                p.op(("dve", "pool")[hh % 2], lambda e, c_lo=c_lo, c_hi=c_hi: e.tensor_copy(
                    out=W.ap()[:, c_lo:c_hi, wa:wb], in_=wst.ap()[:, 0:c_hi - c_lo, 0:b - a][:, :, ::-1]), reads=[wst], writes=[W])
            p.dma("act", zst.ap(), zsv[:, NCONV:NSCH, nat(t0, t0 + 128)], reads=[zs], writes=[zst])
            p.op("pool", lambda e: e.tensor_copy(out=zo.ap(), in_=zst.ap()[:, :, ::-1]), reads=[zst], writes=[zo])
        grid = t0 >= nctx
        for ci in range(NCONV):
            eng = EW()
            nch = SCH[ci][2]
            o = cz.ap()[:nch, ci, :]
            wv = W.ap()[:nch, ci, :]
            ctr = 4
            p.op(eng, lambda e, o=o, wv=wv, ci=ci, nch=nch: e.tensor_scalar(
                out=o, in0=wv[:, 128:256], scalar1=cw.ap()[:nch, ci, ctr:ctr + 1], scalar2=None, op0=ALU.mult),
                reads=[W, cw], writes=[cz])
            taps = []
            if grid:
                for dy in (-1, 0, 1):
                    for dx in (-1, 0, 1):
                        if dy == 0 and dx == 0:
                            continue
                        taps.append((dy, dx, (dy + 1) * 3 + dx + 1))
            else:
                taps = [(0, -1, 3), (0, 1, 5)]
            for dy, dx, tp in taps:
                s = 128 + 64 * dy
                if grid and dx != 0:
                    src = wv[:, s:s + 128].rearrange("p (r c) -> p r c", c=64)
                    dst = o.rearrange("p (r c) -> p r c", c=64)
                    if dx == -1:
                        src = src[:, :, 0:63]; dst = dst[:, :, 1:64]
                    else:
                        src = src[:, :, 1:64]; dst = dst[:, :, 0:63]
                else:
                    src = wv[:, s + dx:s + dx + 128]; dst = o
                if eng == "dve":
                    p.op(eng, lambda e, src=src, dst=dst, ci=ci, tp=tp, nch=nch: e.scalar_tensor_tensor(
                        out=dst, in0=src, scalar=cw.ap()[:nch, ci, tp:tp + 1], in1=dst, op0=ALU.mult, op1=ALU.add),
                        reads=[W, cw, cz], writes=[cz])
                else:
                    if grid and dx != 0:
                        tmp = ctmp.ap()[:nch, :].rearrange("p (r c) -> p r c", c=64)[:, :, 0:63]
                    else:
                        tmp = ctmp.ap()[:nch, :]
                    p.op(eng, lambda e, src=src, tmp=tmp, ci=ci, tp=tp, nch=nch: e.tensor_scalar(
                        out=tmp, in0=src, scalar1=cw.ap()[:nch, ci, tp:tp + 1], scalar2=None, op0=ALU.mult),
                        reads=[W, cw], writes=[ctmp])
                    p.op(eng, lambda e, dst=dst, tmp=tmp: e.tensor_tensor(out=dst, in0=dst, in1=tmp, op=ALU.add),
                         reads=[ctmp, cz], writes=[cz])
        return zo

    def decay_prep(nk, lw_ap, lw_reads, cs, G_, Ginv, Gend, Gex=None, lwsb=None):
        p.op("dve", lambda e: e.tensor_tensor_scan(out=cs.ap()[:nk, :], data0=m01[:nk, :], data1=lw_ap, initial=0.0,
                                                   op0=ALU.mult, op1=ALU.add), reads=[cst] + lw_reads, writes=[cs])
        p.op("act", lambda e: e.activation(out=G_.ap()[:nk, :], in_=cs.ap()[:nk, :], func=AF.Exp), reads=[cs], writes=[G_])
        p.op("act", lambda e: e.activation(out=Ginv.ap()[:nk, :], in_=cs.ap()[:nk, :], func=AF.Exp, scale=-1.0), reads=[cs], writes=[Ginv])
        for c in range(2):
            p.op("act", lambda e, c=c: e.activation(out=Gend.ap()[:nk, 64 * c:64 * c + 64], in_=cs.ap()[:nk, 64 * c:64 * c + 64],
                                                    func=AF.Exp, scale=-1.0, bias=cs.ap()[:nk, 64 * c + 63:64 * c + 64]),
                 reads=[cs], writes=[Gend])
        if Gex is not None:
            p.op("dve", lambda e: e.tensor_tensor(out=Gex.ap()[:nk, :], in0=cs.ap()[:nk, :], in1=lwsb, op=ALU.subtract),
                 reads=[cs] + lw_reads, writes=[Gex])
            p.op("act", lambda e: e.activation(out=Gex.ap()[:nk, :], in_=Gex.ap()[:nk, :], func=AF.Exp), reads=[Gex], writes=[Gex])

    rS = T([64, 6, 64], "rwS", ); p.op("pool", lambda e: e.memset(rS.ap(), 0.0), writes=[rS])
    tw = T([64, 128], "tw")
    nm = ["sgw", "lw", "a", "kkv", "sq", "rs", "kap", "t1", "kd", "b", "pr", "cs", "G", "Ginv", "Gend", "Gex",
          "rt", "kt", "bt", "kkt", "Bh", "Kh", "QT", "Phi"]
    R_ = {n: T([64, 128], "r_" + n) for n in nm}
    tm4 = T([64, 4, 64], "tm4")
    tt5 = T([64, 5, 64], "tt5")
    Rr = T([64, 128], "Rr"); Xx = T([64, 128], "Xx"); nZ = T([64, 64], "nZ")
    Pb = [T([64, 2, 64], f"Pb{i}") for i in range(2)]
    yrw = T([64, 6, 128], "yrw"); ybn = T([64, 6, 128], "ybn")
    NEG_E = -float(np.exp(-0.5))

    def rwkv_block(t0, zo):
        wadn = zo.ap()[:, 7, :]
        p.op("act", lambda e: e.activation(out=tw.ap(), in_=wadn[0:64, :], func=AF.Tanh), reads=[zo], writes=[tw])
        for h in range(6):
            r = cz.ap()[:64, h, :]; k = cz.ap()[:64, 6 + h, :]; v = cz.ap()[:64, 12 + h, :]
            prm = lambda w: rwp.ap()[:, w, h:h + 1]
            A = lambda n: R_[n].ap()
            ps = PS()
            p.op("pe", lambda e: e.matmul(ps.ap()[:64, 0:128], lhsT=waup.ap()[0:64, h * 64:(h + 1) * 64], rhs=tw.ap(), start=True, stop=True),
                 reads=[waup, tw], writes=[ps])
            p.op("pe", lambda e: e.matmul(ps.ap()[:64, 128:256], lhsT=waup.ap()[64:128, h * 64:(h + 1) * 64], rhs=wadn[64:128, :], start=True, stop=True),
                 reads=[waup, zo], writes=[ps])
            p.op("act", lambda e: e.activation(out=A("sgw"), in_=ps.ap()[:64, 0:128], func=AF.Sigmoid, bias=prm(0)), reads=[ps, rwp], writes=[R_["sgw"]])
            p.op("act", lambda e: e.activation(out=A("a"), in_=ps.ap()[:64, 128:256], func=AF.Sigmoid, bias=prm(1)), reads=[ps, rwp], writes=[R_["a"]])
            p.op("dve", lambda e: e.tensor_scalar(out=A("lw"), in0=A("sgw"), scalar1=NEG_E, scalar2=None, op0=ALU.mult), reads=[R_["sgw"]], writes=[R_["lw"]])
            p.op("dve", lambda e: e.tensor_scalar(out=A("kkv"), in0=k, scalar1=prm(2), scalar2=None, op0=ALU.mult), reads=[cz, rwp], writes=[R_["kkv"]])
            p.op("act", lambda e: e.activation(out=A("sq"), in_=A("kkv"), func=AF.Square), reads=[R_["kkv"]], writes=[R_["sq"]])
            ps2 = PS()
            p.op("pe", lambda e: e.matmul(ps2.ap()[:64, 0:128], lhsT=ones.ap()[0:64, 0:64], rhs=A("sq"), start=True, stop=True), reads=[ones, R_["sq"]], writes=[ps2])
            p.op("dve", lambda e: e.tensor_scalar(out=A("rs"), in0=ps2.ap()[:64, 0:128], scalar1=1e-24, scalar2=None, op0=ALU.max), reads=[ps2], writes=[R_["rs"]])
            p.op("act", lambda e: e.activation(out=A("rs"), in_=A("rs"), func=AF.Sqrt), reads=[R_["rs"]], writes=[R_["rs"]])
            p.op("dve", lambda e: e.reciprocal(out=A("rs"), in_=A("rs")), reads=[R_["rs"]], writes=[R_["rs"]])
            p.op("dve", lambda e: e.tensor_tensor(out=A("kap"), in0=A("kkv"), in1=A("rs"), op=ALU.mult), reads=[R_["kkv"], R_["rs"]], writes=[R_["kap"]])
            p.op("pool", lambda e: e.tensor_scalar(out=A("t1"), in0=A("a"), scalar1=-1.0, scalar2=prm(3), op0=ALU.add, op1=ALU.mult), reads=[R_["a"], rwp], writes=[R_["t1"]])
            p.op("dve", lambda e: e.scalar_tensor_tensor(out=A("kd"), in0=A("t1"), scalar=1.0, in1=k, op0=ALU.add, op1=ALU.mult), reads=[R_["t1"], cz], writes=[R_["kd"]])
            p.op("pool", lambda e: e.tensor_tensor(out=A("b"), in0=A("a"), in1=A("kap"), op=ALU.mult), reads=[R_["a"], R_["kap"]], writes=[R_["b"]])
            p.op("dve", lambda e: e.scalar_tensor_tensor(out=A("pr"), in0=r, scalar=prm(4), in1=A("kd"), op0=ALU.mult, op1=ALU.mult), reads=[cz, rwp, R_["kd"]], writes=[R_["pr"]])
            p.op("pe", lambda e: e.matmul(ps2.ap()[:64, 128:256], lhsT=ones.ap()[0:64, 0:64], rhs=A("pr"), start=True, stop=True), reads=[ones, R_["pr"]], writes=[ps2])
            p.op("dve", lambda e: e.tensor_tensor(out=ybn.ap()[:, h, :], in0=ps2.ap()[:64, 128:256], in1=v, op=ALU.mult), reads=[ps2, cz], writes=[ybn])
            decay_prep(64, A("lw"), [R_["lw"]], R_["cs"], R_["G"], R_["Ginv"], R_["Gend"], R_["Gex"], A("lw"))
            for (o_, x_, g_) in (("rt", r, "G"), ("kt", A("kap"), "Gex"), ("bt", A("b"), "Ginv"), ("kkt", A("kd"), "Ginv"),
                                 ("Bh", A("b"), "Gend"), ("Kh", A("kd"), "Gend")):
                p.op(EW(), lambda e, o_=o_, x_=x_, g_=g_: e.tensor_tensor(out=A(o_), in0=x_, in1=A(g_), op=ALU.mult),
                     reads=[cz, R_["kap"], R_["b"], R_["kd"], R_[g_]], writes=[R_[o_]])
            for c in range(2):
                cs_ = slice(64 * c, 64 * c + 64)
                pt = PS()
                for i, src in enumerate((v[:, cs_], A("kt")[:, cs_], A("Bh")[:, cs_], A("Kh")[:, cs_])):
                    p.op("pe", lambda e, i=i, src=src: e.transpose(pt.ap()[:64, 64 * i:64 * i + 64], src, ident[0:64, 0:64]),
                         reads=[cz, R_["kt"], R_["Bh"], R_["Kh"], cst], writes=[pt])
                p.op("act", lambda e: e.copy(out=tm4.ap().rearrange("p a b -> p (a b)"), in_=pt.ap()[:64, 0:256]), reads=[pt], writes=[tm4])
                Vtm = tm4.ap()[:, 0, :]; Ktm = tm4.ap()[:, 1, :]; Btm = tm4.ap()[:, 2, :]; Khtm = tm4.ap()[:, 3, :]
                p5 = PS()
                for i, (l_, r_) in enumerate((("bt", "kt"), ("kt", "bt"), ("kkt", "kt"), ("bt", "rt"), ("kkt", "rt"))):
                    p.op("pe", lambda e, i=i, l_=l_, r_=r_: e.matmul(p5.ap()[:64, 64 * i:64 * i + 64], lhsT=A(l_)[:, cs_], rhs=A(r_)[:, cs_], start=True, stop=True),
                         reads=[R_[l_], R_[r_]], writes=[p5])
                p.op("dve", lambda e: e.tensor_tensor(out=tt5.ap().rearrange("p a b -> p (a b)"), in0=p5.ap()[:64, 0:320],
                                                      in1=mask5.ap().rearrange("p a b -> p (a b)"), op=ALU.mult), reads=[p5, mask5], writes=[tt5])
                U = tt5.ap()[:, 0, :]; L = tt5.ap()[:, 1, :]; LkT = tt5.ap()[:, 2, :]; AbrT = tt5.ap()[:, 3, :]; AkrT = tt5.ap()[:, 4, :]
                p6 = PS()
                p.op("pe", lambda e: e.matmul(p6.ap()[:64, 0:64], lhsT=LkT, rhs=Vtm, start=True, stop=True), reads=[tt5, tm4], writes=[p6])
                p.op("act", lambda e: e.copy(out=Rr.ap()[:, 64:128], in_=p6.ap()[:64, 0:64]), reads=[p6], writes=[Rr])
                p.op("pool", lambda e: e.tensor_copy(out=Rr.ap()[:, 0:64], in_=Ktm), reads=[tm4], writes=[Rr])
                p7 = PS()
                p.op("pe", lambda e: e.matmul(p7.ap()[:64, 0:128], lhsT=U, rhs=Rr.ap(), start=True, stop=True), reads=[tt5, Rr], writes=[p7])
                p.op("dve", lambda e: e.tensor_tensor(out=Xx.ap(), in0=Rr.ap(), in1=p7.ap()[:64, 0:128], op=ALU.subtract), reads=[Rr, p7], writes=[Xx])
                Pc, PTc, Prd = L, U, [tt5]
                for lvl in range(5):
                    pn = Pb[lvl % 2]
                    pq = PS()
                    last = lvl == 4
                    if not last:
                        p.op("pe", lambda e, Pc=Pc, PTc=PTc: e.matmul(pq.ap()[:64, 0:64], lhsT=PTc, rhs=Pc, start=True, stop=True), reads=Prd, writes=[pq])
                    p.op("pe", lambda e, Pc=Pc, PTc=PTc: e.matmul(pq.ap()[:64, 64:128], lhsT=Pc, rhs=PTc, start=True, stop=True), reads=Prd, writes=[pq])
                    if not last:
                        p.op("act", lambda e, pn=pn: e.copy(out=pn.ap().rearrange("p a b -> p (a b)"), in_=pq.ap()[:64, 0:128]), reads=[pq], writes=[pn])
                    else:
                        p.op("act", lambda e, pn=pn: e.copy(out=pn.ap()[:, 1, :], in_=pq.ap()[:64, 64:128]), reads=[pq], writes=[pn])
                    Pc, PTc, Prd = pn.ap()[:, 0, :], pn.ap()[:, 1, :], [pn]
                    px = PS()
                    p.op("pe", lambda e, PTc=PTc: e.matmul(px.ap()[:64, 0:128], lhsT=PTc, rhs=Xx.ap(), start=True, stop=True), reads=Prd + [Xx], writes=[px])
                    p.op("dve", lambda e: e.tensor_tensor(out=Xx.ap(), in0=Xx.ap(), in1=px.ap()[:64, 0:128], op=ALU.add), reads=[Xx, px], writes=[Xx])
                Gm = Xx.ap()[:, 0:64]
                p.op("act", lambda e: e.mul(out=nZ.ap(), in_=Xx.ap()[:, 64:128], mul=-1.0), reads=[Xx], writes=[nZ])
                p8 = PS()
                p.op("pe", lambda e: e.matmul(p8.ap()[:64, 0:64], lhsT=Gm, rhs=AbrT, start=True, stop=True), reads=[Xx, tt5], writes=[p8])
                p.op("dve", lambda e: e.tensor_tensor(out=A("QT")[:, 0:64], in0=A("rt")[:, cs_], in1=p8.ap()[:64, 0:64], op=ALU.subtract), reads=[R_["rt"], p8], writes=[R_["QT"]])
                p9 = PS()
                p.op("pe", lambda e: e.matmul(p9.ap()[:64, 0:64], lhsT=rS.ap()[:, h, :], rhs=A("QT")[:, 0:64], start=True, stop=False), reads=[rS, R_["QT"]], writes=[p9])
                p.op("pe", lambda e: e.matmul(p9.ap()[:64, 0:64], lhsT=Vtm, rhs=AkrT, start=False, stop=False), reads=[tm4, tt5], writes=[p9])
                p.op("pe", lambda e: e.matmul(p9.ap()[:64, 0:64], lhsT=nZ.ap(), rhs=AbrT, start=False, stop=True), reads=[nZ, tt5], writes=[p9])
                p.op("act", lambda e: e.copy(out=yrw.ap()[:, h, cs_], in_=p9.ap()[:64, 0:64]), reads=[p9], writes=[yrw])
                p.op("pe", lambda e: e.matmul(p8.ap()[:64, 64:128], lhsT=Gm, rhs=Btm, start=True, stop=True), reads=[Xx, tm4], writes=[p8])
                p.op("dve", lambda e: e.scalar_tensor_tensor(out=A("Phi")[:, 0:64], in0=ident[0:64, 0:64], scalar=A("G")[:, 64 * c + 63:64 * c + 64],
                                                             in1=p8.ap()[:64, 64:128], op0=ALU.mult, op1=ALU.subtract), reads=[cst, R_["G"], p8], writes=[R_["Phi"]])
                p.op("pe", lambda e: e.matmul(p9.ap()[:64, 64:128], lhsT=A("Phi")[:, 0:64], rhs=rS.ap()[:, h, :], start=True, stop=False), reads=[R_["Phi"], rS], writes=[p9])
                p.op("pe", lambda e: e.matmul(p9.ap()[:64, 64:128], lhsT=Khtm, rhs=Vtm, start=False, stop=False), reads=[tm4], writes=[p9])
                p.op("pe", lambda e: e.matmul(p9.ap()[:64, 64:128], lhsT=Btm, rhs=nZ.ap(), start=False, stop=True), reads=[tm4, nZ], writes=[p9])
                p.op("dve", lambda e: e.tensor_copy(out=rS.ap()[:, h, :], in_=p9.ap()[:64, 64:128]), reads=[p9], writes=[rS])
        store(o_rw, "h p t -> p h t", yrw, t0, 64, 6, "sp")
        store(o_bn, "h p t -> p h t", ybn, t0, 64, 6, "act")

    mS = T([96, 4, 192], "mlS"); p.op("pool", lambda e: e.memset(mS.ap(), 0.0), writes=[mS])
    gl1 = T([16, 128], "gl1"); gl2 = T([16, 128], "gl2")
    mn = ["q", "ks", "ei", "kp", "cs", "G", "Ginv", "Gend", "rt", "kkt", "Kh", "den"]
    M_ = {n: T([96, 128], "m_" + n) for n in mn}
    mtm = T([64, 2, 96], "mtm"); makr = T([64, 64], "makr")
    yml = T([96, 4, 128], "yml")
    KSC = float(96 ** -0.5)

    def mlstm_block(t0, zo):
        gl = zo.ap()[0:16, 4, :]
        p.op("dve", lambda e: e.tensor_scalar(out=gl1.ap(), in0=gl, scalar1=glb.ap()[:, 0:1], scalar2=None, op0=ALU.add), reads=[zo, glb], writes=[gl1])
        p.op("act", lambda e: e.activation(out=gl2.ap(), in_=gl1.ap(), func=AF.Sigmoid), reads=[gl1], writes=[gl2])
        p.op("act", lambda e: e.activation(out=gl2.ap(), in_=gl2.ap(), func=AF.Ln), reads=[gl2], writes=[gl2])
        for h in range(4):
            B = lambda n: M_[n].ap()
            q = cz.ap()[:96, 18 + h, :]; k = cz.ap()[:96, 22 + h, :]; v = zo.ap()[:96, h, :]
            ps = PS()
            p.op("pe", lambda e: e.matmul(ps.ap()[:96, 0:128], lhsT=sel.ap()[:, 4 + h, :], rhs=gl2.ap(), start=True, stop=True), reads=[sel, gl2], writes=[ps])
            p.op("pe", lambda e: e.matmul(ps.ap()[:96, 128:256], lhsT=sel.ap()[:, h, :], rhs=gl1.ap(), start=True, stop=True), reads=[sel, gl1], writes=[ps])
            p.op("act", lambda e: e.activation(out=B("ei"), in_=ps.ap()[:96, 128:256], func=AF.Exp), reads=[ps], writes=[M_["ei"]])
            p.op("act", lambda e: e.activation(out=B("q"), in_=q, func=AF.Silu), reads=[cz], writes=[M_["q"]])
            p.op("act", lambda e: e.activation(out=B("ks"), in_=k, func=AF.Silu), reads=[cz], writes=[M_["ks"]])
            p.op("dve", lambda e: e.scalar_tensor_tensor(out=B("kp"), in0=B("ks"), scalar=KSC, in1=B("ei"), op0=ALU.mult, op1=ALU.mult), reads=[M_["ks"], M_["ei"]], writes=[M_["kp"]])
            decay_prep(96, ps.ap()[:96, 0:128], [ps], M_["cs"], M_["G"], M_["Ginv"], M_["Gend"])
            for (o_, x_, g_) in (("rt", "q", "G"), ("kkt", "kp", "Ginv"), ("Kh", "kp", "Gend")):
                p.op(EW(), lambda e, o_=o_, x_=x_, g_=g_: e.tensor_tensor(out=B(o_), in0=B(x_), in1=B(g_), op=ALU.mult), reads=[M_[x_], M_[g_]], writes=[M_[o_]])
            for c in range(2):
                cs_ = slice(64 * c, 64 * c + 64)
                pt = PS()
                p.op("pe", lambda e: e.transpose(pt.ap()[:64, 0:96], v[:, cs_], ident[0:96, 0:96]), reads=[zo, cst], writes=[pt])
                p.op("pe", lambda e: e.transpose(pt.ap()[:64, 96:192], B("Kh")[:, cs_], ident[0:96, 0:96]), reads=[M_["Kh"], cst], writes=[pt])
                p.op("act", lambda e: e.copy(out=mtm.ap().rearrange("p a b -> p (a b)"), in_=pt.ap()[:64, 0:192]), reads=[pt], writes=[mtm])
                Vtm = mtm.ap()[:, 0, :]; Khtm = mtm.ap()[:, 1, :]
                pa = PS()
                p.op("pe", lambda e: e.matmul(pa.ap()[:64, 0:64], lhsT=B("kkt")[:, cs_], rhs=B("rt")[:, cs_], start=True, stop=True), reads=[M_["kkt"], M_["rt"]], writes=[pa])
                p.op("dve", lambda e: e.tensor_tensor(out=makr.ap(), in0=pa.ap()[:64, 0:64], in1=m_iu[0:64, 0:64], op=ALU.mult), reads=[pa, cst], writes=[makr])
                py = PS()
                p.op("pe", lambda e: e.matmul(py.ap()[:96, 0:64], lhsT=mS.ap()[:, h, 0:96], rhs=B("rt")[:, cs_], start=True, stop=False), reads=[mS, M_["rt"]], writes=[py])
                p.op("pe", lambda e: e.matmul(py.ap()[:96, 0:64], lhsT=Vtm, rhs=makr.ap(), start=False, stop=True), reads=[mtm, makr], writes=[py])
                p.op("pe", lambda e: e.matmul(py.ap()[:96, 64:128], lhsT=mS.ap()[:, h, 96:192], rhs=B("rt")[:, cs_], start=True, stop=False), reads=[mS, M_["rt"]], writes=[py])
                p.op("pe", lambda e: e.matmul(py.ap()[:96, 64:128], lhsT=ones.ap()[0:64, 0:96], rhs=makr.ap(), start=False, stop=True), reads=[ones, makr], writes=[py])
                p.op("act", lambda e: e.activation(out=B("den")[:, 0:64], in_=py.ap()[:96, 64:128], func=AF.Abs), reads=[py], writes=[M_["den"]])
                p.op("dve", lambda e: e.tensor_scalar(out=B("den")[:, 0:64], in0=B("den")[:, 0:64], scalar1=1.0, scalar2=None, op0=ALU.max), reads=[M_["den"]], writes=[M_["den"]])
                p.op("dve", lambda e: e.reciprocal(out=B("den")[:, 0:64], in_=B("den")[:, 0:64]), reads=[M_["den"]], writes=[M_["den"]])
                p.op("dve", lambda e: e.tensor_tensor(out=yml.ap()[:, h, cs_], in0=py.ap()[:96, 0:64], in1=B("den")[:, 0:64], op=ALU.mult), reads=[py, M_["den"]], writes=[yml])
                pu = PS()
                p.op("pe", lambda e: e.matmul(pu.ap()[:96, 0:96], lhsT=Khtm, rhs=Vtm, start=True, stop=True), reads=[mtm], writes=[pu])
                p.op("pe", lambda e: e.matmul(pu.ap()[:96, 96:192], lhsT=Khtm, rhs=ones.ap()[0:64, 0:96], start=True, stop=True), reads=[mtm, ones], writes=[pu])
                p.op("dve", lambda e: e.scalar_tensor_tensor(out=mS.ap()[:, h, :], in0=mS.ap()[:, h, :], scalar=B("G")[:, 64 * c + 63:64 * c + 64],
                                                             in1=pu.ap()[:96, 0:192], op0=ALU.mult, op1=ALU.add), reads=[mS, M_["G"], pu], writes=[mS])
        store(o_ml, "h p t -> p h t", yml, t0, 96, 4, "act")

    for j in range(NB):
        zo = conv_block(j)
        s5_block(j * 128, zo)
        mlstm_block(j * 128, zo)
        rwkv_block(j * 128, zo)


def mixer_consts():
    a = np.arange(128)
    ident = np.eye(128, dtype=np.float32)
    su = (a[:, None] < a[None, :]).astype(np.float32)
    sl = (a[:, None] > a[None, :]).astype(np.float32)
    iu = (a[:, None] <= a[None, :]).astype(np.float32)
    m01 = np.ones((128, 128), np.float32)
    m01[:, 0] = 0.0
    m01[:, 64] = 0.0
    cst = np.stack([ident, su, sl, iu, m01], 1).copy()
    ri = np.broadcast_to(np.arange(129, dtype=np.float32), (128, 129)).copy()
    return cst, ri


def mixer_params(P, i, d):
    m = {}
    cwf = P["conv_w"][i]
    if d == 1:
        cwf = cwf[::-1, ::-1]
    cw = np.zeros((128, NCONV, 9), np.float32)
    for ci in range(NCONV):
        _, c0, n = SCH[ci]
        cw[:n, ci, :] = cwf[:, :, c0:c0 + n].reshape(9, n).T
    m["cw"] = cw
    rwp = np.stack([P["rw_w0"][i, d].reshape(6, 64).T, P["rw_a0"][i, d].reshape(6, 64).T, P["rw_k_k"][i].reshape(6, 64).T,
                    P["rw_k_a"][i].reshape(6, 64).T, P["rw_r_k"][i].T], 1)
    m["rwp"] = np.ascontiguousarray(rwp, np.float32)
    m["wa_up"] = np.concatenate([P["rw_w_up"][i, d], P["rw_a_up"][i, d]], 0).astype(np.float32)
    m["glb"] = P["ml_gate_b"][i].reshape(16, 1).astype(np.float32)
    sel = np.zeros((16, 8, 96), np.float32)
    for j in range(8):
        sel[d * 8 + j, j, :] = 1.0
    m["sel"] = sel
    s5p = np.zeros((128, 3, 8), np.float32)
    bz = np.zeros((128, 16, 128), np.float32)
    cz = np.zeros((128, 16, 128), np.float32)
    for k in range(8):
        for gl2 in range(2):
            g = 2 * k + gl2
            js = slice(gl2 * 64, gl2 * 64 + 64)
            s5p[js, 0, k] = P["s5_a_re"][i, d, g]
            s5p[js, 1, k] = P["s5_a_im"][i, d, g]
            s5p[js, 2, k] = P["s5_log_dt"][i, d, g]
            r0 = (g % 8) * 16
            bz[r0:r0 + 16, k, js] = P["s5_b_re"][i, d, g].T
            bz[r0:r0 + 16, 8 + k, js] = P["s5_b_im"][i, d, g].T
            cz[js, k, r0:r0 + 16] = P["s5_c_re"][i, d, g].T
            cz[js, 8 + k, r0:r0 + 16] = P["s5_c_im"][i, d, g].T
    m["s5p"] = s5p
    m["bz"] = bz
    m["cz"] = cz
    cst, ri = mixer_consts()
    m["cst"] = cst
    m["ri"] = ri
    return m


def mixer_zs(zseq):
    NS = zseq.shape[0]
    zs = np.zeros((NSCH, 128, NS), np.float32)
    for ci, (_, c0, n) in enumerate(SCH):
        zs[ci, :n, :] = zseq[:, c0:c0 + n].T
    return zs


def emit_merge(p, x1T, oT, modT, lng_ap, lnb_ap, outs2, zall, zallb, mp):
    f32 = F32

    def T(shape, name, dt=f32):
        return p.tile(shape, dt, "g_" + name)

    mod = T([128, 72, 2], "mod"); g_sb = T([128, 8], "lng"); b_sb = T([128, 8], "lnb")
    gn = T([64, 2, 6], "gn"); gup = T([128, 384], "gup"); s5d = T([128, 2, 2], "s5d"); gluw = T([128, 2, 256], "gluw")
    mlg = T([96, 4], "mlg"); brb = T([128, 24], "brb")
    p.dma("sp", mod.ap(), modT.ap(), reads=[modT], writes=[mod])
    for i, (t, d) in enumerate(((g_sb, lng_ap), (b_sb, lnb_ap), (gn, mp["gn"]), (gup, mp["g_up"]), (s5d, mp["s5d"]),
                                (gluw, mp["glu_w"]), (mlg, mp["mlg"]), (brb, mp["brb"]))):
        p.dma(("sp", "act")[i % 2], t.ap(), d, writes=[t])
    ones = make_ones(p)
    uprw = T([64, 6, 1024], "uprw", BF16); ups5 = T([128, 2, 1024], "ups5", BF16); upml = T([96, 4, 1024], "upml", BF16)
    wout = T([128, 8, 1024], "wout", BF16)
    stage = [T([128, 1024], f"stage{i}") for i in range(2)]
    k = 0
    for h in range(6):
        st = stage[k % 2]
        p.dma(("sp", "act")[k % 2], st.ap()[:64, :], mp["up_rw"][:, h, :], writes=[st])
        p.op("dve", lambda e, h=h, st=st: e.tensor_copy(out=uprw.ap()[:, h, :], in_=st.ap()[:64, :]), reads=[st], writes=[uprw]); k += 1
    for h in range(2):
        st = stage[k % 2]
        p.dma(("sp", "act")[k % 2], st.ap(), mp["up_s5"][:, h, :], writes=[st])
        p.op("dve", lambda e, h=h, st=st: e.tensor_copy(out=ups5.ap()[:, h, :], in_=st.ap()), reads=[st], writes=[ups5]); k += 1
    for h in range(4):
        st = stage[k % 2]
        p.dma(("sp", "act")[k % 2], st.ap()[:96, :], mp["up_ml"][:, h, :], writes=[st])
        p.op("dve", lambda e, h=h, st=st: e.tensor_copy(out=upml.ap()[:, h, :], in_=st.ap()[:96, :]), reads=[st], writes=[upml]); k += 1
    for h in range(8):
        st = stage[k % 2]
        p.dma(("sp", "act")[k % 2], st.ap(), mp["w_out"][:, h, :], writes=[st])
        p.op("dve", lambda e, h=h, st=st: e.tensor_copy(out=wout.ap()[:, h, :], in_=st.ap()), reads=[st], writes=[wout]); k += 1

    x_sb = T([128, 8, TT], "x"); yrw = T([64, 2, 6, TT], "yrw"); ybn = T([64, 2, 6, TT], "ybn")
    ys5 = T([128, 2, 2, TT], "ys5"); yml = T([96, 2, 4, TT], "yml"); zg = T([128, 31, TT], "zg")
    rwy = T([64, 6, TT], "rwy", BF16); s5y = T([128, 2, TT], "s5y", BF16); mly = T([96, 4, TT], "mly", BF16)
    ym = T([128, 8, TT], "ym", BF16)
    yg = T([128, 2, TT], "yg")
    ta = T([128, TT], "ta"); tb = T([128, TT], "tb"); tc = T([128, TT], "tc"); td = T([128, TT], "td"); sgd = T([128, TT], "sgd")
    y_sb = T([128, 8, TT], "y"); ysq = T([128, 8, TT], "ysq"); mean = T([128, TT], "mean"); rstd = T([128, TT], "rstd")
    pss = [p.psum([128, 512], f32, f"pg{i}") for i in range(6)]
    ps1 = p.psum([128, 512], f32, "ps1"); ps2 = p.psum([128, 512], f32, "ps2")
    pc = [0]

    def PS():
        pc[0] += 1
        return pss[pc[0] % 6]

    def std_part(x, np_, n, eps, scale_ap, out_ap, out_buf):
        p.op("act", lambda e: e.activation(out=tb.ap()[:np_, :n], in_=x, func=AF.Square), reads=[ta], writes=[tb])
        q = PS()
        p.op("pe", lambda e: e.matmul(q.ap()[:np_, 0:n], lhsT=ones.ap()[0:np_, 0:np_], rhs=x, start=True, stop=True), reads=[ones, ta], writes=[q])
        p.op("pe", lambda e: e.matmul(q.ap()[:np_, 256:256 + n], lhsT=ones.ap()[0:np_, 0:np_], rhs=tb.ap()[:np_, :n], start=True, stop=True), reads=[ones, tb], writes=[q])
        p.op("dve", lambda e: e.tensor_scalar(out=tc.ap()[:np_, :n], in0=q.ap()[:np_, 0:n], scalar1=1.0 / np_, scalar2=None, op0=ALU.mult), reads=[q], writes=[tc])
        p.op("dve", lambda e: e.tensor_tensor(out=td.ap()[:np_, :n], in0=tc.ap()[:np_, :n], in1=tc.ap()[:np_, :n], op=ALU.mult), reads=[tc], writes=[td])
        p.op("dve", lambda e: e.scalar_tensor_tensor(out=td.ap()[:np_, :n], in0=q.ap()[:np_, 256:256 + n], scalar=1.0 / np_, in1=td.ap()[:np_, :n], op0=ALU.mult, op1=ALU.subtract), reads=[q, td], writes=[td])
        p.op("dve", lambda e: e.tensor_scalar(out=td.ap()[:np_, :n], in0=td.ap()[:np_, :n], scalar1=eps, scalar2=None, op0=ALU.add), reads=[td], writes=[td])
        p.op("act", lambda e: e.activation(out=td.ap()[:np_, :n], in_=td.ap()[:np_, :n], func=AF.Sqrt), reads=[td], writes=[td])
        p.op("dve", lambda e: e.reciprocal(out=td.ap()[:np_, :n], in_=td.ap()[:np_, :n]), reads=[td], writes=[td])
        p.op("dve", lambda e: e.tensor_tensor(out=x, in0=x, in1=tc.ap()[:np_, :n], op=ALU.subtract), reads=[ta, tc], writes=[ta])
        p.op("dve", lambda e: e.scalar_tensor_tensor(out=out_ap, in0=x, scalar=scale_ap, in1=td.ap()[:np_, :n], op0=ALU.mult, op1=ALU.mult), reads=[ta, td, gn, mlg], writes=[out_buf])

    xv = x1T.ap().rearrange("c p t -> p c t")
    ov = oT.ap().rearrange("c p t -> p c t")
    GC = 2.0 * float(np.sqrt(2.0 / np.pi))
    for ti, (t0, n, wh) in enumerate(TILES):
        sl = slice(t0, t0 + n)
        p.dma("sp", x_sb.ap()[:, :, :n], xv[:, :, sl], reads=[x1T], writes=[x_sb])
        for d in range(2):
            o_rw, o_bn, o_s5, o_ml = outs2[d]
            p.dma("act", yrw.ap()[:, d, :, :n], o_rw.ap()[:, :, sl].rearrange("h p t -> p h t"), reads=[o_rw], writes=[yrw])
            p.dma("sp", ybn.ap()[:, d, :, :n], o_bn.ap()[:, :, sl].rearrange("h p t -> p h t"), reads=[o_bn], writes=[ybn])
            p.dma("act", ys5.ap()[:, d, :, :n], o_s5.ap()[:, :, sl].rearrange("h p t -> p h t"), reads=[o_s5], writes=[ys5])
            p.dma("sp", yml.ap()[:, d, :, :n], o_ml.ap()[:, :, sl].rearrange("h p t -> p h t"), reads=[o_ml], writes=[yml])
        zav = zall.ap().rearrange("c p t -> p c t")
        zbv = zallb.ap().rearrange("c p t -> p c t")
        p.dma("act", zg.ap()[:, 0:29, :n], zbv[:, :, sl], reads=[zallb], writes=[zg])
        p.dma("sp", zg.ap()[:, 29:31, :n], zav[:, 31:33, sl], reads=[zall], writes=[zg])
        p.op("act", lambda e: e.activation(out=sgd.ap()[:, :n], in_=zg.ap()[:, 4, :n], func=AF.Sigmoid), reads=[zg], writes=[sgd])
        for h in range(6):
            x = ta.ap()[:64, :n]
            p.op("dve", lambda e, h=h: e.tensor_tensor(out=x, in0=yrw.ap()[:, 0, h, :n], in1=yrw.ap()[:, 1, h, :n], op=ALU.add), reads=[yrw], writes=[ta])
            std_part(x, 64, n, 64e-5, gn.ap()[:, 0, h:h + 1], x, ta)
            p.op("dve", lambda e, h=h: e.scalar_tensor_tensor(out=x, in0=x, scalar=gn.ap()[:, 1, h:h + 1], in1=ybn.ap()[:, 0, h, :n], op0=ALU.add, op1=ALU.add), reads=[ta, gn, ybn], writes=[ta])
            p.op("dve", lambda e, h=h: e.tensor_tensor(out=x, in0=x, in1=ybn.ap()[:, 1, h, :n], op=ALU.add), reads=[ta, ybn], writes=[ta])
            q = PS()
            p.op("pe", lambda e, h=h: e.matmul(q.ap()[:64, 0:n], lhsT=gup.ap()[:, h * 64:(h + 1) * 64], rhs=sgd.ap()[:, :n], start=True, stop=True), reads=[gup, sgd], writes=[q])
            p.op("dve", lambda e, h=h: e.tensor_tensor(out=rwy.ap()[:, h, :n], in0=x, in1=q.ap()[:64, 0:n], op=ALU.mult), reads=[ta, q], writes=[rwy])
        for mt in range(2):
            x = yg.ap()[:, mt, :n]
            p.op("dve", lambda e, mt=mt: e.scalar_tensor_tensor(out=x, in0=zg.ap()[:, 29 + mt, :n], scalar=s5d.ap()[:, 0, mt:mt + 1], in1=ys5.ap()[:, 0, mt, :n], op0=ALU.mult, op1=ALU.add), reads=[zg, s5d, ys5], writes=[yg])
            p.op("dve", lambda e, mt=mt: e.tensor_tensor(out=x, in0=x, in1=ys5.ap()[:, 1, mt, :n], op=ALU.add), reads=[yg, ys5], writes=[yg])
            p.op("act", lambda e: e.activation(out=tb.ap()[:, :n], in_=x, func=AF.Square), reads=[yg], writes=[tb])
            p.op("dve", lambda e: e.tensor_scalar(out=tb.ap()[:, :n], in0=tb.ap()[:, :n], scalar1=0.044715, scalar2=1.0, op0=ALU.mult, op1=ALU.add), reads=[tb], writes=[tb])
            p.op("dve", lambda e: e.tensor_tensor(out=tb.ap()[:, :n], in0=tb.ap()[:, :n], in1=x, op=ALU.mult), reads=[tb, yg], writes=[tb])
            p.op("act", lambda e: e.activation(out=tb.ap()[:, :n], in_=tb.ap()[:, :n], func=AF.Sigmoid, scale=GC), reads=[tb], writes=[tb])
            p.op("dve", lambda e: e.tensor_tensor(out=x, in0=x, in1=tb.ap()[:, :n], op=ALU.mult), reads=[yg, tb], writes=[yg])
        for mo in range(2):
            q = PS()
            for kc in range(2):
                p.op("pe", lambda e, mo=mo, kc=kc: e.matmul(q.ap()[:, 0:n], lhsT=gluw.ap()[:, kc, mo * 128:(mo + 1) * 128], rhs=yg.ap()[:, kc, :n], start=(kc == 0), stop=(kc == 1)), reads=[gluw, yg], writes=[q])
            p.op("act", lambda e, mo=mo: e.activation(out=tb.ap()[:, :n], in_=q.ap()[:, 0:n], func=AF.Sigmoid, bias=s5d.ap()[:, 1, mo:mo + 1]), reads=[q, s5d], writes=[tb])
            p.op("dve", lambda e, mo=mo: e.tensor_tensor(out=s5y.ap()[:, mo, :n], in0=yg.ap()[:, mo, :n], in1=tb.ap()[:, :n], op=ALU.mult), reads=[yg, tb], writes=[s5y])
        for h in range(4):
            x = ta.ap()[:96, :n]
            p.op("dve", lambda e, h=h: e.tensor_tensor(out=x, in0=yml.ap()[:, 0, h, :n], in1=yml.ap()[:, 1, h, :n], op=ALU.add), reads=[yml], writes=[ta])
            p.op("act", lambda e, h=h: e.activation(out=tb.ap()[:96, :n], in_=zg.ap()[:96, h, :n], func=AF.Sigmoid), reads=[zg], writes=[tb])
            p.op("dve", lambda e: e.tensor_tensor(out=x, in0=x, in1=tb.ap()[:96, :n], op=ALU.mult), reads=[ta, tb], writes=[ta])
            std_part(x, 96, n, 1e-5, mlg.ap()[:, h:h + 1], mly.ap()[:, h, :n], mly)
        for dc in range(8):
            q = PS()
            for h in range(6):
                p.op("pe", lambda e, h=h, dc=dc: e.matmul(q.ap()[:, 0:n], lhsT=uprw.ap()[:, h, dc * 128:(dc + 1) * 128], rhs=rwy.ap()[:, h, :n], start=(h == 0), stop=(h == 5)), reads=[uprw, rwy], writes=[q])
            p.op("act", lambda e, dc=dc: e.activation(out=tb.ap()[:, :n], in_=zg.ap()[:, 5 + dc, :n], func=AF.Sigmoid, bias=brb.ap()[:, dc:dc + 1]), reads=[zg, brb], writes=[tb])
            p.op("dve", lambda e: e.tensor_tensor(out=ta.ap()[:, :n], in0=q.ap()[:, 0:n], in1=tb.ap()[:, :n], op=ALU.mult), reads=[q, tb], writes=[ta])
            q = PS()
            for h in range(2):
                p.op("pe", lambda e, h=h, dc=dc: e.matmul(q.ap()[:, 0:n], lhsT=ups5.ap()[:, h, dc * 128:(dc + 1) * 128], rhs=s5y.ap()[:, h, :n], start=(h == 0), stop=(h == 1)), reads=[ups5, s5y], writes=[q])
            p.op("act", lambda e, dc=dc: e.activation(out=tb.ap()[:, :n], in_=zg.ap()[:, 13 + dc, :n], func=AF.Sigmoid, bias=brb.ap()[:, 8 + dc:9 + dc]), reads=[zg, brb], writes=[tb])
            p.op("dve", lambda e: e.tensor_tensor(out=tc.ap()[:, :n], in0=q.ap()[:, 0:n], in1=tb.ap()[:, :n], op=ALU.mult), reads=[q, tb], writes=[tc])
            p.op("dve", lambda e: e.tensor_tensor(out=ta.ap()[:, :n], in0=ta.ap()[:, :n], in1=tc.ap()[:, :n], op=ALU.add), reads=[ta, tc], writes=[ta])
            q = PS()
            for h in range(4):
                p.op("pe", lambda e, h=h, dc=dc: e.matmul(q.ap()[:, 0:n], lhsT=upml.ap()[:, h, dc * 128:(dc + 1) * 128], rhs=mly.ap()[:, h, :n], start=(h == 0), stop=(h == 3)), reads=[upml, mly], writes=[q])
            p.op("act", lambda e, dc=dc: e.activation(out=tb.ap()[:, :n], in_=zg.ap()[:, 21 + dc, :n], func=AF.Sigmoid, bias=brb.ap()[:, 16 + dc:17 + dc]), reads=[zg, brb], writes=[tb])
            p.op("dve", lambda e: e.tensor_tensor(out=tc.ap()[:, :n], in0=q.ap()[:, 0:n], in1=tb.ap()[:, :n], op=ALU.mult), reads=[q, tb], writes=[tc])
            p.op("dve", lambda e, dc=dc: e.tensor_tensor(out=ym.ap()[:, dc, :n], in0=ta.ap()[:, :n], in1=tc.ap()[:, :n], op=ALU.add), reads=[ta, tc], writes=[ym])
        p.op("act", lambda e: e.mul(out=x_sb.ap()[:, :, :n], in_=x_sb.ap()[:, :, :n], mul=ALPHA), reads=[x_sb], writes=[x_sb])
        for dc in range(8):
            q = PS()
            for kc in range(8):
                p.op("pe", lambda e, kc=kc, dc=dc: e.matmul(q.ap()[:, 0:n], lhsT=wout.ap()[:, kc, dc * 128:(dc + 1) * 128], rhs=ym.ap()[:, kc, :n], start=(kc == 0), stop=(kc == 7)), reads=[wout, ym], writes=[q])
            p.op("dve", lambda e, dc=dc: e.scalar_tensor_tensor(out=y_sb.ap()[:, dc, :n], in0=q.ap()[:, 0:n], scalar=mod.ap()[:, 40 + dc, wh:wh + 1], in1=x_sb.ap()[:, dc, :n], op0=ALU.mult, op1=ALU.add), reads=[q, mod, x_sb], writes=[y_sb])
        ln_feature_major(p, y_sb, ysq, n, ones, ps1, ps2, mean, rstd, g_sb, b_sb, ysq)
        p.dma("sp", ov[:, :, sl], ysq.ap()[:, :, :n], reads=[ysq], writes=[oT])


MIX_KEYS = (("cw", [128, NCONV, 9]), ("rwp", [64, 5, 6]), ("wa_up", [128, 384]), ("glb", [16, 1]), ("sel", [16, 8, 96]),
            ("s5p", [128, 3, 8]), ("bz", [128, 16, 128]), ("cz", [128, 16, 128]))
MRG_KEYS = (("gn", [64, 2, 6]), ("g_up", [128, 384]), ("s5d", [128, 2, 2]), ("glu_w", [128, 2, 256]), ("mlg", [96, 4]),
            ("up_rw", [64, 6, 1024]), ("up_s5", [128, 2, 1024]), ("up_ml", [96, 4, 1024]), ("brb", [128, 24]),
            ("w_out", [128, 8, 1024]))
NUSED = 4


def set_sizes(nctx, nlat):
    global NLAT, NCTX, NT, TILES
    NLAT, NCTX = nlat, nctx
    NT = NLAT + NCTX
    TILES = [(0, NCTX, 1)] + [(NCTX + i * TT, TT, 0) for i in range(NLAT // TT)]


def build_fused():
    p = Prog()
    NS = NT
    din = lambda n, sh: p.dram(n, sh, F32, "in")
    xT0 = din("xT0", [8, 128, NS])
    cvec = din("cvec", [128, 8, 2])
    ada_w = din("ada_w", [DEPTH, D, 9 * D])
    ada_b = din("ada_b_l", [DEPTH, 128, 72])
    ln_g = din("ln_g_l", [DEPTH, 3, 128, 8])
    ln_b = din("ln_b_l", [DEPTH, 3, 128, 8])
    wg = din("ffn_w_gate", [DEPTH, 2, D, DFF])
    wu = din("ffn_w_up", [DEPTH, 2, D, DFF])
    wd = din("ffn_w_down", [DEPTH, 2, DFF, D])
    w_in = din("w_in", [DEPTH, D, INW])
    cst = din("cst", [128, 5, 128])
    ri = din("ri", [128, 129])
    mixp = {}
    for i in range(DEPTH):
        for d in range(2):
            mixp[(i, d)] = {k: din(f"m{i}{d}_{k}", sh).ap() for k, sh in MIX_KEYS}
            mixp[(i, d)]["cst"] = cst.ap()
            mixp[(i, d)]["ri"] = ri.ap()
    mrgp = {i: {k: din(f"g{i}_{k}", sh).ap() for k, sh in MRG_KEYS} for i in range(DEPTH)}
    oT = p.dram("oT", [8, 128, NS], F32, "out")
    tmp = lambda n, sh: p.dram(n, sh, F32, "tmp")
    S1 = tmp("S1", [8, 128, NS])
    S2 = tmp("S2", [8, 128, NS])
    zall = tmp("zall", [NSCH, 128, NS])
    zallb = tmp("zallb", [NZALL - NSCH, 128, NS])
    modT = [tmp(f"modT{i}", [128, 72, 2]) for i in range(DEPTH)]
    outs = [(tmp(f"o_rw{d}", [6, 64, NS]), tmp(f"o_bn{d}", [6, 64, NS]), tmp(f"o_s5{d}", [2, 128, NS]),
             tmp(f"o_ml{d}", [4, 96, NS])) for d in range(2)]
    chain = [(xT0, S1, S1, S2, S1), (S1, S2, S2, S1, oT)]
    for i in range(DEPTH):
        a_src, a_dst, m_src, m_dst, f_dst = chain[i]
        with p.phase():
            emit_mod(p, cvec, ada_w.ap()[i], ada_b.ap()[i], modT[i])
        with p.phase():
            emit_ffn(p, 0, a_src, a_dst, modT[i], ln_g.ap()[i, 0], ln_b.ap()[i, 0], wg.ap()[i, 0], wu.ap()[i, 0], wd.ap()[i, 0])
        with p.phase():
            emit_inproj(p, a_dst, modT[i], w_in.ap()[i], zall, zallb)
        for d in range(2):
            with p.phase():
                emit_mixer(p, zall, mixp[(i, d)], outs[d], rev=(d == 1), nctx=NCTX, nlat=NLAT)
        with p.phase():
            emit_merge(p, m_src, m_dst, modT[i], ln_g.ap()[i, 1], ln_b.ap()[i, 1], outs, zall, zallb, mrgp[i])
        with p.phase():
            emit_ffn(p, 6, m_dst, f_dst, modT[i], ln_g.ap()[i, 2], ln_b.ap()[i, 2], wg.ap()[i, 1], wu.ap()[i, 1], wd.ap()[i, 1])
    p.finish()
    return p


def merge_params(P, i):
    m = {}
    m["gn"] = np.ascontiguousarray(np.stack([P["rw_gn_g"][i].reshape(6, 64).T, P["rw_gn_b"][i].reshape(6, 64).T], 1))
    m["g_up"] = P["rw_g_up"][i]
    m["s5d"] = np.ascontiguousarray(np.stack([P["s5_d"][i].reshape(2, 128).T, P["s5_glu_b"][i].reshape(2, 128).T], 1))
    m["glu_w"] = np.ascontiguousarray(P["s5_glu_w"][i].reshape(2, 128, 256).transpose(1, 0, 2))
    m["mlg"] = np.ascontiguousarray(P["ml_norm_g"][i].reshape(4, 96).T)
    m["up_rw"] = np.ascontiguousarray(P["up_rw"][i].reshape(6, 64, 1024).transpose(1, 0, 2))
    m["up_s5"] = np.ascontiguousarray(P["up_s5"][i].reshape(2, 128, 1024).transpose(1, 0, 2))
    m["up_ml"] = np.ascontiguousarray(P["up_ml"][i].reshape(4, 96, 1024).transpose(1, 0, 2))
    m["brb"] = np.ascontiguousarray(P["br_gate_b"][i].reshape(24, 128).T)
    m["w_out"] = np.ascontiguousarray(P["w_out"][i].reshape(8, 128, 1024).transpose(1, 0, 2))
    return m


_PROG = []


def kernel(**inputs):
    P = {k: np.asarray(v, dtype=np.float32) for k, v in inputs.items()}
    x, c, ctx, c_ctx = P["x"], P["c"], P["ctx"], P["c_ctx"]
    B, SEQ, _ = x.shape
    if not _PROG:
        set_sizes(ctx.shape[1], SEQ)
        _PROG.append(build_fused())
    prog = _PROG[0]
    NUSED = B
    shared = {
        "ada_w": P["ada_w"],
        "ada_b_l": np.ascontiguousarray(P["ada_b"].reshape(DEPTH, 72, 128).transpose(0, 2, 1)),
        "ln_g_l": np.ascontiguousarray(P["ln_g"].reshape(DEPTH, 3, 8, 128).transpose(0, 1, 3, 2)),
        "ln_b_l": np.ascontiguousarray(P["ln_b"].reshape(DEPTH, 3, 8, 128).transpose(0, 1, 3, 2)),
        "ffn_w_gate": P["ffn_w_gate"], "ffn_w_up": P["ffn_w_up"], "ffn_w_down": P["ffn_w_down"], "w_in": P["w_in"],
    }
    cst, ri = mixer_consts()
    shared["cst"] = cst
    shared["ri"] = ri
    for i in range(DEPTH):
        for d in range(2):
            mp = mixer_params(P, i, d)
            for k, _ in MIX_KEYS:
                shared[f"m{i}{d}_{k}"] = np.ascontiguousarray(mp[k], np.float32)
        gp = merge_params(P, i)
        for k, _ in MRG_KEYS:
            shared[f"g{i}_{k}"] = np.ascontiguousarray(gp[k], np.float32)
    ims = []
    for cid in range(NUSED):
        b = cid % B
        m = dict(shared)
        xx = np.concatenate([ctx[b], x[b]], 0)
        m["xT0"] = np.ascontiguousarray(xx.T).reshape(8, 128, NT)
        m["cvec"] = np.ascontiguousarray(np.stack([c[b], c_ctx], -1).reshape(8, 128, 2).transpose(1, 0, 2))
        ims.append(m)
    res = run_bass_kernel_spmd(prog.nc, ims, core_ids=list(range(NUSED)))
    out = np.empty((B, SEQ, D), np.float32)
    for b in range(B):
        out[b] = res.results[b]["oT"].reshape(D, NT).T[NCTX:]
    return out
```

```python
import numpy as np
from contextlib import ExitStack
import concourse.bass as bass
import concourse.mybir as mybir
from concourse.bass_utils import run_bass_kernel_spmd

F32 = mybir.dt.float32
BF16 = mybir.dt.bfloat16
AF = mybir.ActivationFunctionType
ALU = mybir.AluOpType
AX = mybir.AxisListType


class Reg:
    __slots__ = ("name", "w", "r")

    def __init__(self, name):
        self.name = name
        self.w = None
        self.r = []


class Buf:
    def __init__(self, t, name, nreg=1):
        self.t = t
        self.name = name
        self.regs = [Reg(f"{name}.{i}") for i in range(nreg)]

    def ap(self):
        return self.t.ap() if hasattr(self.t, "ap") and not isinstance(self.t, bass.AP) else self.t

    def __getitem__(self, idx):
        return self.ap()[idx]

    def r(self, i):
        return self.regs[i]


NDMA_SEM = 8
FUSE_WAIT = True


class Prog:
    def __init__(self):
        self.nc = bass.Bass("TRN2", target_bir_lowering=False)
        nc = self.nc
        self.stack = ExitStack()
        self.eng = {"pe": nc.tensor, "dve": nc.vector, "act": nc.scalar, "pool": nc.gpsimd, "sp": nc.sync}
        self.sems = {}
        self.semval = {}
        for e in self.eng:
            self.sems[e] = self.stack.enter_context(nc.semaphore(f"s_{e}"))
            self.semval[e] = 0
        self.dq = {}
        for q in ("sp", "act", "pool"):
            lst = []
            for i in range(NDMA_SEM):
                k = f"d_{q}{i}"
                self.sems[k] = self.stack.enter_context(nc.semaphore(k))
                self.semval[k] = 0
                lst.append(k)
            self.dq[q] = [lst, 0]
        self.waited = {e: {} for e in self.eng}
        self.out_waits = []
        self.ninst = {e: 0 for e in self.eng}
        self.pstack = None
        self.pidx = 0

    def barrier(self):
        for e in self.eng:
            for k, v in self.semval.items():
                if v > 0 and k != e:
                    self._wait(e, k, v)

    def phase(self):
        prog = self

        class _Ph:
            def __enter__(self_):
                prog.pidx += 1
                prog.pstack = ExitStack()
                return prog

            def __exit__(self_, *a):
                prog.barrier()
                prog.pstack.close()
                prog.pstack = None
                return False
        return _Ph()

    def dram(self, name, shape, dtype, kind):
        k = {"in": "ExternalInput", "out": "ExternalOutput", "tmp": "Internal"}[kind]
        t = self.nc.dram_tensor(name, list(shape), dtype, kind=k)
        b = Buf(t, name)
        b.kind = kind
        return b

    def tile(self, shape, dtype, name, nreg=1):
        st = self.pstack if self.pstack is not None else self.stack
        name = f"p{self.pidx}_{name}"
        t = st.enter_context(self.nc.sbuf_tensor(name, list(shape), dtype))
        return Buf(t, name, nreg)

    def psum(self, shape, dtype, name, nreg=1):
        st = self.pstack if self.pstack is not None else self.stack
        name = f"p{self.pidx}_{name}"
        t = st.enter_context(self.nc.psum_tensor(name, list(shape), dtype))
        return Buf(t, name, nreg)

    @staticmethod
    def _regs(lst):
        out = []
        for x in lst or []:
            if isinstance(x, Buf):
                out.extend(x.regs)
            elif isinstance(x, Reg):
                out.append(x)
            elif isinstance(x, tuple):
                out.append(x[0].regs[x[1]])
            else:
                raise TypeError(x)
        return out

    def _wait(self, e, key, val):
        if key is None:
            return
        cur = self.waited[e].get(key, 0)
        if cur >= val:
            return
        self.waited[e][key] = val
        self.eng[e].wait_ge(self.sems[key], val)
        self.ninst[e] += 1

    def _deps(self, e, reads, writes, defer=False):
        need = {}
        for r in reads:
            if r.w is not None:
                need[r.w[0]] = max(need.get(r.w[0], 0), r.w[1])
        for r in writes:
            if r.w is not None:
                need[r.w[0]] = max(need.get(r.w[0], 0), r.w[1])
            for (k, v) in r.r:
                need[k] = max(need.get(k, 0), v)
        todo = [(k, v) for k, v in need.items() if self.waited[e].get(k, 0) < v]
        last = None
        if defer and FUSE_WAIT and todo:
            last = todo.pop()
        for k, v in todo:
            self._wait(e, k, v)
        return last

    def _attach(self, e, inst, last):
        if last is not None:
            k, v = last
            self.waited[e][k] = v
            inst._wait_ge(self.sems[k], v)

    def _mark(self, key, val, reads, writes):
        for r in reads:
            r.r.append((key, val))
            if len(r.r) > 12:
                d = {}
                for (k, v) in r.r:
                    d[k] = max(d.get(k, 0), v)
                r.r = list(d.items())
        for r in writes:
            r.w = (key, val)
            r.r = []

    def op(self, e, fn, reads=None, writes=None):
        reads = self._regs(reads)
        writes = self._regs(writes)
        last = self._deps(e, reads, writes, defer=(e != "pe"))
        inst = fn(self.eng[e])
        self._attach(e, inst, last)
        self.semval[e] += 1
        inst.then_inc(self.sems[e], 1)
        self.ninst[e] += 1
        self._mark(e, self.semval[e], reads, writes)
        return inst

    def dma(self, q, out, in_, reads=None, writes=None, **kw):
        rbufs = reads or []
        wbufs = writes or []
        reads = self._regs(reads)
        writes = self._regs(writes)
        self._deps(q, reads, writes)
        lst, i = self.dq[q]
        key = lst[i % NDMA_SEM]
        self.dq[q][1] = i + 1
        self._wait(q, key, self.semval[key])
        inst = self.eng[q].dma_start(out=out, in_=in_, **kw)
        self.semval[key] += 16
        inst.then_inc(self.sems[key], 16)
        self.ninst[q] += 1
        self._mark(key, self.semval[key], reads, writes)
        for b in wbufs:
            if isinstance(b, Buf) and getattr(b, "kind", None) == "out":
                self.out_waits.append((key, self.semval[key]))
        return inst

    def finish(self):
        d = {}
        for k, v in self.out_waits:
            d[k] = max(d.get(k, 0), v)
        for k, v in d.items():
            self._wait("sp", k, v)
        for e in self.eng:
            if e != "sp" and self.semval[e] > 0:
                self._wait("sp", e, self.semval[e])


D = 1024
DFF = 2816
NFC = 22
DEPTH = 2
ALPHA = (2.0 * DEPTH) ** 0.25
LN_EPS = 1e-5
NLAT = 8192
NCTX = 256
NT = NLAT + NCTX
TT = 256
TILES = [(0, NCTX, 1)] + [(NCTX + i * TT, TT, 0) for i in range(NLAT // TT)]
NCORES = 8
INW = 6288


def make_ones(p, n=128, name="ones"):
    t = p.tile([128, n], F32, name)
    p.op("pool", lambda e: e.memset(t.ap(), 1.0), writes=[t])
    return t


def load_weight_bf16(p, dst, dst_view, src_view, stage, i, shape_free):
    st = stage[i % len(stage)]
    sv = st.ap()[:, :shape_free] if isinstance(shape_free, int) else shape_free(st.ap())
    q = ("sp", "act")[i % 2]
    p.dma(q, sv, src_view, writes=[st])
    ce = ("pool", "dve", "act")[i % 3]
    if ce == "act":
        p.op("act", lambda e: e.copy(out=dst_view, in_=sv), reads=[st], writes=[dst])
    else:
        p.op(ce, lambda e: e.tensor_copy(out=dst_view, in_=sv), reads=[st], writes=[dst])


def emit_mod(p, cvec, adaw_ap, adab_ap, out):
    cs = p.tile([128, 8, 2], F32, "cs")
    ab = p.tile([128, 72], F32, "ab")
    mo = p.tile([128, 72, 2], F32, "mo")
    pan = [p.tile([128, 8, 1024], F32, f"pan{i}") for i in range(2)]
    ps = [p.psum([128, 512], F32, f"ps{i}") for i in range(2)]
    p.dma("sp", cs.ap(), cvec.ap(), writes=[cs])
    p.dma("sp", ab.ap(), adab_ap, writes=[ab])
    p.op("act", lambda e: e.activation(out=cs.ap(), in_=cs.ap(), func=AF.Silu), reads=[cs], writes=[cs])
    awv = adaw_ap.rearrange("(kc p) f -> p kc f", p=128)
    for j in range(9):
        pn = pan[j % 2]
        p.dma(("sp", "act")[j % 2], pn.ap(), awv[:, :, j * 1024:(j + 1) * 1024], writes=[pn])
        for dc in range(8):
            ch = j * 8 + dc
            pp = ps[ch % 2]
            for kc in range(8):
                p.op("pe", lambda e, kc=kc, dc=dc, pn=pn, pp=pp: e.matmul(
                    pp.ap()[:, 0:2], lhsT=pn.ap()[:, kc, dc * 128:(dc + 1) * 128], rhs=cs.ap()[:, kc, :],
                    start=(kc == 0), stop=(kc == 7)), reads=[pn, cs], writes=[pp])
            p.op("dve", lambda e, ch=ch, pp=pp: e.tensor_scalar(
                out=mo.ap()[:, ch, :], in0=pp.ap()[:, 0:2], scalar1=ab.ap()[:, ch:ch + 1], scalar2=None,
                op0=ALU.add), reads=[pp, ab], writes=[mo])
    p.dma("sp", out.ap(), mo.ap(), reads=[mo], writes=[out])


def ln_feature_major(p, y, ysq, n, ones, ps1, ps2, mean, rstd, g, b, dst, dst_reads=None):
    p.op("act", lambda e: e.activation(out=ysq.ap()[:, :, :n], in_=y.ap()[:, :, :n], func=AF.Square),
         reads=[y], writes=[ysq])
    for dc in range(8):
        p.op("pe", lambda e, dc=dc: e.matmul(ps1.ap()[:, :n], lhsT=ones.ap(), rhs=y.ap()[:, dc, :n],
                                              start=(dc == 0), stop=(dc == 7)), reads=[ones, y], writes=[ps1])
    for dc in range(8):
        p.op("pe", lambda e, dc=dc: e.matmul(ps2.ap()[:, :n], lhsT=ones.ap(), rhs=ysq.ap()[:, dc, :n],
                                              start=(dc == 0), stop=(dc == 7)), reads=[ones, ysq], writes=[ps2])
    p.op("dve", lambda e: e.tensor_scalar(out=mean.ap()[:, :n], in0=ps1.ap()[:, :n], scalar1=1.0 / D, scalar2=None,
                                          op0=ALU.mult), reads=[ps1], writes=[mean])
    p.op("dve", lambda e: e.tensor_tensor(out=rstd.ap()[:, :n], in0=mean.ap()[:, :n], in1=mean.ap()[:, :n],
                                          op=ALU.mult), reads=[mean], writes=[rstd])
    p.op("dve", lambda e: e.scalar_tensor_tensor(out=rstd.ap()[:, :n], in0=ps2.ap()[:, :n], scalar=1.0 / D,
                                                 in1=rstd.ap()[:, :n], op0=ALU.mult, op1=ALU.subtract),
         reads=[ps2, rstd], writes=[rstd])
    p.op("dve", lambda e: e.tensor_scalar(out=rstd.ap()[:, :n], in0=rstd.ap()[:, :n], scalar1=LN_EPS, scalar2=None,
                                          op0=ALU.add), reads=[rstd], writes=[rstd])
    p.op("act", lambda e: e.activation(out=rstd.ap()[:, :n], in_=rstd.ap()[:, :n], func=AF.Sqrt),
         reads=[rstd], writes=[rstd])
    p.op("dve", lambda e: e.reciprocal(out=rstd.ap()[:, :n], in_=rstd.ap()[:, :n]), reads=[rstd], writes=[rstd])
    for dc in range(8):
        eng = ("dve", "pool")[dc % 2]
        p.op(eng, lambda e, dc=dc: e.tensor_tensor(out=y.ap()[:, dc, :n], in0=y.ap()[:, dc, :n],
                                                   in1=mean.ap()[:, :n], op=ALU.subtract),
             reads=[y, mean], writes=[y])
        p.op(eng, lambda e, dc=dc: e.tensor_tensor(out=y.ap()[:, dc, :n], in0=y.ap()[:, dc, :n],
                                                   in1=rstd.ap()[:, :n], op=ALU.mult),
             reads=[y, rstd], writes=[y])
        p.op(eng, lambda e, dc=dc: e.tensor_scalar(out=dst.ap()[:, dc, :n], in0=y.ap()[:, dc, :n],
                                                   scalar1=g.ap()[:, dc:dc + 1], scalar2=b.ap()[:, dc:dc + 1],
                                                   op0=ALU.mult, op1=ALU.add),
             reads=[y, g, b], writes=[dst])


def emit_ffn(p, j0, xT, oT, modT, lng_ap, lnb_ap, wg_ap, wu_ap, wd_ap):

    wg_sb = p.tile([128, 8, DFF], BF16, "wg_sb")
    wu_sb = p.tile([128, 8, DFF], BF16, "wu_sb")
    wd_sb = p.tile([128, NFC, D], BF16, "wd_sb")
    stage = [p.tile([128, DFF // 2], F32, f"stage{i}") for i in range(2)]
    mod = p.tile([128, 72, 2], F32, "mod")
    g_sb = p.tile([128, 8], F32, "g_sb")
    b_sb = p.tile([128, 8], F32, "b_sb")
    ops_ = p.tile([128, 8, 2], F32, "onepsc")
    hg = p.tile([128, 8, 2], F32, "hg")
    ones = make_ones(p)
    p.dma("sp", mod.ap(), modT.ap(), reads=[modT], writes=[mod])
    p.dma("sp", g_sb.ap(), lng_ap, writes=[g_sb])
    p.dma("sp", b_sb.ap(), lnb_ap, writes=[b_sb])
    p.op("dve", lambda e: e.tensor_scalar(out=ops_.ap(), in0=mod.ap()[:, (j0 + 1) * 8:(j0 + 2) * 8, :], scalar1=1.0,
                                          scalar2=None, op0=ALU.add), reads=[mod], writes=[ops_])
    p.op("dve", lambda e: e.tensor_scalar(out=hg.ap(), in0=mod.ap()[:, (j0 + 2) * 8:(j0 + 3) * 8, :], scalar1=0.5,
                                          scalar2=None, op0=ALU.mult), reads=[mod], writes=[hg])
    k = 0
    wgv = wg_ap.rearrange("(kc p) f -> kc p f", p=128)
    wuv = wu_ap.rearrange("(kc p) f -> kc p f", p=128)
    wdv = wd_ap.rearrange("(fc p) d -> fc p d", p=128)
    HF = DFF // 2
    for kc in range(8):
        for hh in range(2):
            load_weight_bf16(p, wg_sb, wg_sb.ap()[:, kc, hh * HF:(hh + 1) * HF], wgv[kc][:, hh * HF:(hh + 1) * HF], stage, k, HF); k += 1
            load_weight_bf16(p, wu_sb, wu_sb.ap()[:, kc, hh * HF:(hh + 1) * HF], wuv[kc][:, hh * HF:(hh + 1) * HF], stage, k, HF); k += 1
    for fc in range(NFC):
        load_weight_bf16(p, wd_sb, wd_sb.ap()[:, fc, :], wdv[fc], stage, k, D); k += 1

    x_sb = [p.tile([128, 8, TT], F32, f"x_sb{i}") for i in range(2)]
    u_sb = p.tile([128, 8, TT], BF16, "u_sb")
    a_sb = [p.tile([128, NFC, TT], BF16, "a_sb0")]
    sg = [p.tile([128, TT], F32, f"sg{i}") for i in range(2)]
    y_sb = p.tile([128, 8, TT], F32, "y_sb")
    ysq = p.tile([128, 8, TT], F32, "ysq")
    mean = p.tile([128, TT], F32, "mean")
    rstd = p.tile([128, TT], F32, "rstd")
    psg = [p.psum([128, 512], F32, f"psg{i}") for i in range(2)]
    psu = [p.psum([128, 512], F32, f"psu{i}") for i in range(2)]
    psd = [p.psum([128, 512], F32, f"psd{i}") for i in range(2)]
    ps1 = p.psum([128, 512], F32, "ps1")
    ps2 = p.psum([128, 512], F32, "ps2")
    xv = xT.ap().rearrange("c p t -> p c t")
    ov = oT.ap().rearrange("c p t -> p c t")

    for ti, (t0, n, wh) in enumerate(TILES):
        xs = x_sb[ti % 2]
        asb = a_sb[0]
        osb = ysq
        p.dma(("sp", "act")[ti % 2], xs.ap()[:, :, :n], xv[:, :, t0:t0 + n], reads=[xT], writes=[xs])
        for kc in range(8):
            p.op("dve", lambda e, kc=kc: e.tensor_scalar(
                out=u_sb.ap()[:, kc, :n], in0=xs.ap()[:, kc, :n], scalar1=ops_.ap()[:, kc, wh:wh + 1],
                scalar2=mod.ap()[:, j0 * 8 + kc, wh:wh + 1], op0=ALU.mult, op1=ALU.add),
                reads=[xs, ops_, mod], writes=[u_sb])
        for fc in range(NFC):
            pg = psg[fc % 2]
            pu = psu[fc % 2]
            for kc in range(8):
                p.op("pe", lambda e, kc=kc, fc=fc, pg=pg: e.matmul(
                    pg.ap()[:, :n], lhsT=wg_sb.ap()[:, kc, fc * 128:(fc + 1) * 128], rhs=u_sb.ap()[:, kc, :n],
                    start=(kc == 0), stop=(kc == 7)), reads=[wg_sb, u_sb], writes=[pg])
            for kc in range(8):
                p.op("pe", lambda e, kc=kc, fc=fc, pu=pu: e.matmul(
                    pu.ap()[:, :n], lhsT=wu_sb.ap()[:, kc, fc * 128:(fc + 1) * 128], rhs=u_sb.ap()[:, kc, :n],
                    start=(kc == 0), stop=(kc == 7)), reads=[wu_sb, u_sb], writes=[pu])
            s = sg[fc % 2]
            p.op("act", lambda e, pg=pg, s=s: e.activation(out=s.ap()[:, :n], in_=pg.ap()[:, :n], func=AF.Silu),
                 reads=[pg], writes=[s])
            p.op("dve", lambda e, pu=pu, s=s, fc=fc: e.tensor_tensor(
                out=asb.ap()[:, fc, :n], in0=pu.ap()[:, :n], in1=s.ap()[:, :n], op=ALU.mult),
                reads=[pu, s], writes=[asb])
        p.op("act", lambda e: e.mul(out=xs.ap()[:, :, :n], in_=xs.ap()[:, :, :n], mul=ALPHA), reads=[xs], writes=[xs])
        for dc in range(8):
            pd = psd[dc % 2]
            for fc in range(NFC):
                p.op("pe", lambda e, dc=dc, fc=fc, pd=pd: e.matmul(
                    pd.ap()[:, :n], lhsT=wd_sb.ap()[:, fc, dc * 128:(dc + 1) * 128], rhs=asb.ap()[:, fc, :n],
                    start=(fc == 0), stop=(fc == NFC - 1)), reads=[wd_sb, asb], writes=[pd])
            p.op("dve", lambda e, dc=dc, pd=pd: e.scalar_tensor_tensor(
                out=y_sb.ap()[:, dc, :n], in0=pd.ap()[:, :n], scalar=hg.ap()[:, dc, wh:wh + 1],
                in1=xs.ap()[:, dc, :n], op0=ALU.mult, op1=ALU.add), reads=[pd, hg, xs], writes=[y_sb])
        ln_feature_major(p, y_sb, ysq, n, ones, ps1, ps2, mean, rstd, g_sb, b_sb, osb)
        p.dma(("sp", "act")[(ti + 1) % 2], ov[:, :, t0:t0 + n], osb.ap()[:, :, :n], reads=[osb], writes=[oT])


def emit_inproj(p, xT, modT, win_ap, zall, zallb):
    w_sb = p.tile([128, 8, INW], BF16, "w_sb")
    QW = INW // 4
    stage = [p.tile([128, QW], F32, f"stage{i}") for i in range(2)]
    mod = p.tile([128, 72, 2], F32, "mod")
    ops_ = p.tile([128, 8, 2], F32, "onepsc")
    p.dma("sp", mod.ap(), modT.ap(), reads=[modT], writes=[mod])
    p.op("dve", lambda e: e.tensor_scalar(out=ops_.ap(), in0=mod.ap()[:, 32:40, :], scalar1=1.0, scalar2=None,
                                          op0=ALU.add), reads=[mod], writes=[ops_])
    wv = win_ap.rearrange("(kc p) f -> kc p f", p=128)
    k = 0
    for kc in range(8):
        for qq in range(4):
            load_weight_bf16(p, w_sb, w_sb.ap()[:, kc, qq * QW:(qq + 1) * QW], wv[kc][:, qq * QW:(qq + 1) * QW],
                             stage, k, QW); k += 1
    x_sb = [p.tile([128, 8, TT], F32, "x_sb0")]
    u_sb = [p.tile([128, 8, TT], BF16, "u_sb0")]
    zbig = p.tile([128, NZALL, TT], F32, "zbig")
    p.op("pool", lambda e: e.memset(zbig.ap(), 0.0), writes=[zbig])
    ps = [p.psum([128, 512], F32, f"ps{i}") for i in range(4)]
    xv = xT.ap().rearrange("c p t -> p c t")
    zv = zall.ap().rearrange("c p t -> p c t")
    zvb = zallb.ap().rearrange("c p t -> p c t")
    cnt = 0
    for ti, (t0, n, wh) in enumerate(TILES):
        xs = x_sb[0]
        us = u_sb[0]
        p.dma(("sp", "act")[ti % 2], xs.ap()[:, :, :n], xv[:, :, t0:t0 + n], reads=[xT], writes=[xs])
        for kc in range(8):
            p.op(("dve", "pool")[kc % 2], lambda e, kc=kc: e.tensor_scalar(
                out=us.ap()[:, kc, :n], in0=xs.ap()[:, kc, :n], scalar1=ops_.ap()[:, kc, wh:wh + 1],
                scalar2=mod.ap()[:, 24 + kc, wh:wh + 1], op0=ALU.mult, op1=ALU.add),
                reads=[xs, ops_, mod], writes=[us])
        for ci, (nm, c0, nc_) in enumerate(ZALL):
            pp = ps[cnt % 4]
            for kc in range(8):
                p.op("pe", lambda e, kc=kc, pp=pp, c0=c0, nc_=nc_: e.matmul(
                    pp.ap()[:nc_, :n], lhsT=w_sb.ap()[:, kc, c0:c0 + nc_], rhs=us.ap()[:, kc, :n],
                    start=(kc == 0), stop=(kc == 7)), reads=[w_sb, us], writes=[pp])
            if cnt % 2 == 0:
                p.op("act", lambda e, pp=pp, ci=ci, nc_=nc_: e.copy(out=zbig.ap()[:nc_, ci, :n], in_=pp.ap()[:nc_, :n]),
                     reads=[pp], writes=[zbig])
            else:
                p.op("dve", lambda e, pp=pp, ci=ci, nc_=nc_: e.tensor_copy(out=zbig.ap()[:nc_, ci, :n], in_=pp.ap()[:nc_, :n]),
                     reads=[pp], writes=[zbig])
            cnt += 1
        p.dma(("sp", "act")[(ti + 1) % 2], zv[:, :, t0:t0 + n], zbig.ap()[:, 0:NSCH, :n], reads=[zbig], writes=[zall])
        p.dma(("sp", "act")[ti % 2], zvb[:, :, t0:t0 + n], zbig.ap()[:, NSCH:NZALL, :n], reads=[zbig], writes=[zallb])


SCH = []
for _nm, _c0 in (("rw_r", 0), ("rw_k", 384), ("rw_v", 768)):
    for _i in range(6):
        SCH.append((f"{_nm}{_i}", _c0 + 64 * _i, 64))
for _nm, _c0 in (("ml_q", 1152), ("ml_k", 1536)):
    for _i in range(4):
        SCH.append((f"{_nm}{_i}", _c0 + 96 * _i, 96))
NCONV = len(SCH)
for _i in range(4):
    SCH.append((f"ml_v{_i}", 1920 + 96 * _i, 96))
SCH.append(("ml_gl", 2688, 16))
SCH.append(("s5_u0", 2704, 128))
SCH.append(("s5_u1", 2832, 128))
SCH.append(("wa_dn", 2960, 128))
NSCH = len(SCH)
NOTH = NSCH - NCONV
ZALL = list(SCH)
for _i in range(4):
    ZALL.append((f"ml_o{_i}", 2304 + 96 * _i, 96))
ZALL.append(("g_dn", 3088, 128))
for _i in range(24):
    ZALL.append((f"br{_i}", 3216 + 128 * _i, 128))
NZALL = len(ZALL)
TWO_PI = 2.0 * np.pi


def emit_mixer(p, zs, prm, outs, rev, nctx=NCTX, nlat=NLAT):
    NS = nctx + nlat
    NB = NS // 128
    f32 = F32
    o_rw, o_bn, o_s5, o_ml = outs

    def nat(a, b):
        if not rev:
            return slice(a, b)
        if b <= nctx:
            return slice(nctx - b, nctx - a)
        return slice(nctx + NS - b, nctx + NS - a)

    def T(shape, name, dt=f32):
        return p.tile(shape, dt, "t_" + name)

    cw = T([128, NCONV, 9], "cw"); rwp = T([64, 5, 6], "rwp"); waup = T([128, 384], "waup")
    glb = T([16, 1], "glb"); sel = T([16, 8, 96], "sel"); s5p = T([128, 3, 8], "s5p")
    bz = T([128, 16, 128], "bz"); czc = T([128, 16, 128], "czc"); cst = T([128, 5, 128], "cst")
    ri = T([128, 129], "ri")
    for i, (t, d) in enumerate(((cw, "cw"), (rwp, "rwp"), (waup, "wa_up"), (glb, "glb"), (sel, "sel"), (s5p, "s5p"),
                                (bz, "bz"), (czc, "cz"), (cst, "cst"), (ri, "ri"))):
        p.dma(("sp", "act")[i % 2], t.ap(), prm[d], writes=[t])
    ident = cst.ap()[:, 0, :]
    m_su = cst.ap()[:, 1, :]
    m_sl = cst.ap()[:, 2, :]
    m_iu = cst.ap()[:, 3, :]
    m01 = cst.ap()[:, 4, :]
    ones = make_ones(p)
    mask5 = T([64, 5, 64], "mask5")
    for i, m in enumerate((m_su, m_sl, m_su, m_iu, m_iu)):
        p.op("dve", lambda e, i=i, m=m: e.tensor_copy(out=mask5.ap()[:, i, :], in_=m[0:64, 0:64]), reads=[cst], writes=[mask5])

    pss = [p.psum([128, 512], f32, f"pb{i}") for i in range(4)]
    psb = p.psum([128, 2, 1024], f32, "psbig")
    pcnt = [0]

    def PS():
        pcnt[0] += 1
        return pss[pcnt[0] % 4]

    ecnt = [0]

    def EW():
        ecnt[0] += 1
        return ("dve", "pool")[ecnt[0] % 2]

    def rr(x_ap, n, tmpf, tmpi, reads):
        p.op("dve", lambda e: e.tensor_scalar(out=tmpf, in0=x_ap, scalar1=1.0 / TWO_PI, scalar2=0.5, op0=ALU.mult,
                                              op1=ALU.add), reads=reads, writes=reads)
        p.op("dve", lambda e: e.tensor_copy(out=tmpi, in_=tmpf), reads=reads, writes=reads)
        p.op("dve", lambda e: e.tensor_copy(out=tmpf, in_=tmpi), reads=reads, writes=reads)
        p.op("dve", lambda e: e.scalar_tensor_tensor(out=x_ap, in0=tmpf, scalar=-TWO_PI, in1=x_ap, op0=ALU.mult,
                                                     op1=ALU.add), reads=reads, writes=reads)
        p.op("dve", lambda e: e.tensor_scalar(out=tmpf, in0=x_ap, scalar1=-np.pi, scalar2=TWO_PI, op0=ALU.is_lt,
                                              op1=ALU.mult), reads=reads, writes=reads)
        p.op("dve", lambda e: e.tensor_tensor(out=x_ap, in0=x_ap, in1=tmpf, op=ALU.add), reads=reads, writes=reads)
        p.op("dve", lambda e: e.tensor_scalar(out=tmpf, in0=x_ap, scalar1=np.pi, scalar2=-TWO_PI, op0=ALU.is_gt,
                                              op1=ALU.mult), reads=reads, writes=reads)
        p.op("dve", lambda e: e.tensor_tensor(out=x_ap, in0=x_ap, in1=tmpf, op=ALU.add), reads=reads, writes=reads)
        p.op("dve", lambda e: e.tensor_scalar(out=x_ap, in0=x_ap, scalar1=-3.14159, scalar2=3.14159, op0=ALU.max,
                                              op1=ALU.min), reads=reads, writes=reads)

    sp_ = T([128, 16, 8], "s5small")
    SM = lambda i: sp_.ap()[:, i, :]
    Vr = T([128, 8, 129], "Vr"); Vi = T([128, 8, 129], "Vi"); t5a = T([128, 8, 129], "t5a"); t5b = T([128, 8, 129], "t5b")
    ang, ang2 = Vr, Vi

    class _View:
        def __init__(self, buf, fn):
            self.buf, self.fn = buf, fn

        def ap(self):
            return self.fn(self.buf.ap())
    tf = _View(t5a, lambda a: a.rearrange("p k r -> p (k r)"))
    ti_ = _View(t5b, lambda a: a.rearrange("p k r -> p (k r)").bitcast(mybir.dt.int32))
    Ct = T([128, 8, 129], "Ct"); St = T([128, 8, 129], "St")
    T1re = T([128, 8, 128], "T1re"); T1im = T([128, 8, 128], "T1im"); RHO = T([128, 8, 128], "RHO")
    S5R = [sp_, s5p, Vr, Vi, t5a, t5b, Ct, St, T1re, T1im, RHO, ri, ones]
    lre, lim, ldt = s5p.ap()[:, 0, :], s5p.ap()[:, 1, :], s5p.ap()[:, 2, :]

    def o5(eng, fn):
        p.op(eng, fn, reads=S5R, writes=S5R)

    o5("dve", lambda e: e.tensor_scalar(out=lre, in0=lre, scalar1=-1e-4, scalar2=None, op0=ALU.min))
    o5("act", lambda e: e.activation(out=SM(0), in_=ldt, func=AF.Exp))
    o5("dve", lambda e: e.tensor_tensor(out=SM(1), in0=lre, in1=SM(0), op=ALU.mult))
    o5("act", lambda e: e.activation(out=SM(1), in_=SM(1), func=AF.Exp))
    o5("dve", lambda e: e.tensor_tensor(out=SM(2), in0=lim, in1=SM(0), op=ALU.mult))
    rr(SM(2), 8, tf.ap()[:, 0:8], ti_.ap()[:, 0:8], S5R)
    for k in range(8):
        o5("dve", lambda e, k=k: e.tensor_scalar(out=ang.ap()[:, k, :], in0=ri.ap(), scalar1=sp_.ap()[:, 2, k:k + 1],
                                                 scalar2=None, op0=ALU.mult))
    angf = ang.ap().rearrange("p k r -> p (k r)")
    ang2f = ang2.ap().rearrange("p k r -> p (k r)")
    o5("dve", lambda e: e.tensor_scalar(out=ang2f, in0=angf, scalar1=np.pi / 2, scalar2=None, op0=ALU.add))
    rr(angf, 8 * 129, tf.ap(), ti_.ap(), S5R)
    rr(ang2f, 8 * 129, tf.ap(), ti_.ap(), S5R)
    o5("act", lambda e: e.activation(out=St.ap().rearrange("p k r -> p (k r)"), in_=angf, func=AF.Sin))
    o5("act", lambda e: e.activation(out=Ct.ap().rearrange("p k r -> p (k r)"), in_=ang2f, func=AF.Sin))
    o5("dve", lambda e: e.tensor_tensor(out=SM(3), in0=SM(1), in1=Ct.ap()[:, :, 1], op=ALU.mult))
    o5("dve", lambda e: e.tensor_tensor(out=SM(4), in0=SM(1), in1=St.ap()[:, :, 1], op=ALU.mult))
    o5("dve", lambda e: e.tensor_scalar(out=SM(3), in0=SM(3), scalar1=-1.0, scalar2=None, op0=ALU.add))
    o5("dve", lambda e: e.tensor_tensor(out=SM(5), in0=lre, in1=lre, op=ALU.mult))
    o5("dve", lambda e: e.tensor_tensor(out=SM(6), in0=lim, in1=lim, op=ALU.mult))
    o5("dve", lambda e: e.tensor_tensor(out=SM(5), in0=SM(5), in1=SM(6), op=ALU.add))
    o5("dve", lambda e: e.reciprocal(out=SM(5), in_=SM(5)))
    o5("dve", lambda e: e.tensor_tensor(out=SM(6), in0=SM(3), in1=lre, op=ALU.mult))
    o5("dve", lambda e: e.tensor_tensor(out=SM(7), in0=SM(4), in1=lim, op=ALU.mult))
    o5("dve", lambda e: e.tensor_tensor(out=SM(6), in0=SM(6), in1=SM(7), op=ALU.add))
    o5("dve", lambda e: e.tensor_tensor(out=SM(8), in0=SM(6), in1=SM(5), op=ALU.mult))
    o5("dve", lambda e: e.tensor_tensor(out=SM(6), in0=SM(4), in1=lre, op=ALU.mult))
    o5("dve", lambda e: e.tensor_tensor(out=SM(7), in0=SM(3), in1=lim, op=ALU.mult))
    o5("dve", lambda e: e.tensor_tensor(out=SM(6), in0=SM(6), in1=SM(7), op=ALU.subtract))
    o5("dve", lambda e: e.tensor_tensor(out=SM(9), in0=SM(6), in1=SM(5), op=ALU.mult))
    o5("dve", lambda e: e.tensor_scalar(out=SM(10), in0=SM(8), scalar1=-1.0, scalar2=None, op0=ALU.mult))
    for k in range(8):
        C_ = Ct.ap()[:, k, 0:128]; S_ = St.ap()[:, k, 0:128]
        gre = sp_.ap()[:, 8, k:k + 1]; gim = sp_.ap()[:, 9, k:k + 1]; ngre = sp_.ap()[:, 10, k:k + 1]
        o5("dve", lambda e, k=k, C_=C_, gre=gre: e.tensor_scalar(out=T1re.ap()[:, k, :], in0=C_, scalar1=gre, scalar2=None, op0=ALU.mult))
        o5("dve", lambda e, k=k, S_=S_, gim=gim: e.scalar_tensor_tensor(out=T1re.ap()[:, k, :], in0=S_, scalar=gim, in1=T1re.ap()[:, k, :], op0=ALU.mult, op1=ALU.add))
        o5("dve", lambda e, k=k, C_=C_, gim=gim: e.tensor_scalar(out=T1im.ap()[:, k, :], in0=C_, scalar1=gim, scalar2=None, op0=ALU.mult))
        o5("dve", lambda e, k=k, S_=S_, ngre=ngre: e.scalar_tensor_tensor(out=T1im.ap()[:, k, :], in0=S_, scalar=ngre, in1=T1im.ap()[:, k, :], op0=ALU.mult, op1=ALU.add))
        o5("dve", lambda e, k=k: e.tensor_scalar(out=RHO.ap()[:, k, :], in0=ones.ap(), scalar1=sp_.ap()[:, 1, k:k + 1], scalar2=None, op0=ALU.mult))
    p.op("dve", lambda e: e.tensor_scalar(out=czc.ap()[:, 8:16, :], in0=czc.ap()[:, 8:16, :], scalar1=-1.0, scalar2=None,
                                          op0=ALU.mult), reads=[czc], writes=[czc])
    s5i = T([128, 2, 8], "s5init")
    p.op("pool", lambda e: e.memset(s5i.ap(), 0.0), writes=[s5i])
    Wr = T([128, 8, 128], "Wr"); Wi = T([128, 8, 128], "Wi")
    ys5 = T([128, 2, 128], "ys5")

    def s5_block(t0, zo):
        VrA = Vr.ap()[:, :, 0:128]; ViA = Vi.ap()[:, :, 0:128]; t5aA = t5a.ap()[:, :, 0:128]; t5bA = t5b.ap()[:, :, 0:128]
        bre = psb.ap()[:, 0, :].rearrange("p (k t) -> p k t", k=8)
        bim = psb.ap()[:, 1, :].rearrange("p (k t) -> p k t", k=8)
        for k in range(8):
            u = zo.ap()[:, 5 + k // 4, :]
            p.op("pe", lambda e, k=k, u=u: e.matmul(bre[:, k, :], lhsT=bz.ap()[:, k, :], rhs=u, start=True, stop=True),
                 reads=[bz, zo], writes=[psb])
            p.op("pe", lambda e, k=k, u=u: e.matmul(bim[:, k, :], lhsT=bz.ap()[:, 8 + k, :], rhs=u, start=True, stop=True),
                 reads=[bz, zo], writes=[psb])
        tt = lambda e, o, a, b, op: e.tensor_tensor(out=o, in0=a, in1=b, op=op)
        p.op("dve", lambda e: tt(e, t5aA, bre, T1re.ap(), ALU.mult), reads=[psb, T1re], writes=[t5a])
        p.op("dve", lambda e: tt(e, t5bA, bim, T1im.ap(), ALU.mult), reads=[psb, T1im], writes=[t5b])
        p.op("pool", lambda e: tt(e, VrA, t5aA, t5bA, ALU.subtract), reads=[t5a, t5b], writes=[Vr])
        p.op("dve", lambda e: tt(e, t5aA, bre, T1im.ap(), ALU.mult), reads=[psb, T1im], writes=[t5a])
        p.op("dve", lambda e: tt(e, t5bA, bim, T1re.ap(), ALU.mult), reads=[psb, T1re], writes=[t5b])
        p.op("pool", lambda e: tt(e, ViA, t5aA, t5bA, ALU.add), reads=[t5a, t5b], writes=[Vi])
        for k in range(8):
            p.op("dve", lambda e, k=k: e.tensor_tensor_scan(out=Wr.ap()[:, k, :], data0=RHO.ap()[:, k, :], data1=VrA[:, k, :],
                                                            initial=s5i.ap()[:, 0, k:k + 1], op0=ALU.mult, op1=ALU.add),
                 reads=[RHO, Vr, s5i], writes=[Wr])
            p.op("dve", lambda e, k=k: e.tensor_tensor_scan(out=Wi.ap()[:, k, :], data0=RHO.ap()[:, k, :], data1=ViA[:, k, :],
                                                            initial=s5i.ap()[:, 1, k:k + 1], op0=ALU.mult, op1=ALU.add),
                 reads=[RHO, Vi, s5i], writes=[Wi])
        wr_l = Wr.ap()[:, :, 127]; wi_l = Wi.ap()[:, :, 127]; c128 = Ct.ap()[:, :, 128]; s128 = St.ap()[:, :, 128]
        p.op("pool", lambda e: tt(e, SM(11), wr_l, c128, ALU.mult), reads=[Wr, Ct], writes=[sp_])
        p.op("pool", lambda e: tt(e, SM(12), wi_l, s128, ALU.mult), reads=[Wi, St, sp_], writes=[sp_])
        p.op("pool", lambda e: tt(e, s5i.ap()[:, 0, :], SM(11), SM(12), ALU.subtract), reads=[sp_], writes=[s5i])
        p.op("pool", lambda e: tt(e, SM(11), wr_l, s128, ALU.mult), reads=[Wr, St, sp_], writes=[sp_])
        p.op("pool", lambda e: tt(e, SM(12), wi_l, c128, ALU.mult), reads=[Wi, Ct, sp_], writes=[sp_])
        p.op("pool", lambda e: tt(e, s5i.ap()[:, 1, :], SM(11), SM(12), ALU.add), reads=[sp_], writes=[s5i])
        C3 = Ct.ap()[:, :, 0:128]; S3 = St.ap()[:, :, 0:128]
        p.op("dve", lambda e: tt(e, t5aA, Wr.ap(), C3, ALU.mult), reads=[Wr, Ct], writes=[t5a])
        p.op("pool", lambda e: tt(e, t5bA, Wi.ap(), S3, ALU.mult), reads=[Wi, St], writes=[t5b])
        p.op("dve", lambda e: tt(e, VrA, t5aA, t5bA, ALU.subtract), reads=[t5a, t5b], writes=[Vr])
        p.op("dve", lambda e: tt(e, t5aA, Wr.ap(), S3, ALU.mult), reads=[Wr, St], writes=[t5a])
        p.op("pool", lambda e: tt(e, t5bA, Wi.ap(), C3, ALU.mult), reads=[Wi, Ct], writes=[t5b])
        p.op("dve", lambda e: tt(e, ViA, t5aA, t5bA, ALU.add), reads=[t5a, t5b], writes=[Vi])
        py = PS()
        for mt in range(2):
            for i, k in enumerate(range(4 * mt, 4 * mt + 4)):
                p.op("pe", lambda e, k=k, mt=mt, i=i: e.matmul(py.ap()[:, mt * 128:(mt + 1) * 128], lhsT=czc.ap()[:, k, :],
                                                               rhs=VrA[:, k, :], start=(i == 0), stop=False),
                     reads=[czc, Vr], writes=[py])
                p.op("pe", lambda e, k=k, mt=mt, i=i: e.matmul(py.ap()[:, mt * 128:(mt + 1) * 128], lhsT=czc.ap()[:, 8 + k, :],
                                                               rhs=ViA[:, k, :], start=False, stop=(i == 3)),
                     reads=[czc, Vi], writes=[py])
        p.op("act", lambda e: e.copy(out=ys5.ap().rearrange("p m t -> p (m t)"), in_=py.ap()[:, 0:256]), reads=[py], writes=[ys5])
        store(o_s5, "m p t -> p m t", ys5, t0, 128, 2, "sp")

    Wn = [T([128, NCONV, 384], "Wn0")]
    zo_t = [T([128, NOTH, 128], f"zo{i}") for i in range(2)]
    cz = T([128, NCONV, 128], "cz")
    ctmp = T([128, 128], "ctmp")
    zsv = zs.ap().rearrange("c p t -> p c t")
    HC = 7
    CGR = [(g * HC, min((g + 1) * HC, NCONV)) for g in range((NCONV + HC - 1) // HC)]
    if rev:
        wst = T([128, HC, 384], "wst")
        zst = T([128, NOTH, 128], "zst")
        ost = T([128, 6, 128], "ost")

    def store(obuf, pat, ytile, t0, npart, nh, q):
        dst = obuf.ap()[:, :, nat(t0, t0 + 128)].rearrange(pat)
        if not rev:
            p.dma(q, dst, ytile.ap(), reads=[ytile], writes=[obuf])
        else:
            p.op("pool", lambda e: e.tensor_copy(out=ost.ap()[:npart, :nh, :], in_=ytile.ap()[:, :, ::-1]), reads=[ytile], writes=[ost])
            p.dma(q, dst, ost.ap()[:npart, :nh, :], reads=[ost], writes=[obuf])

    def conv_block(j):
        t0 = j * 128
        W = Wn[0]
        zo = zo_t[j % 2]
        lo_r, hi_r = (0, nctx) if t0 < nctx else (nctx, NS)
        a, b = max(t0 - 128, lo_r), min(t0 + 256, hi_r)
        if a > t0 - 128 or b < t0 + 256:
            p.op("pool", lambda e: e.memset(W.ap(), 0.0), writes=[W])
        wa, wb = a - (t0 - 128), b - (t0 - 128)
        if not rev:
            p.dma("sp", W.ap()[:, :, wa:wb], zsv[:, 0:NCONV, a:b], reads=[zs], writes=[W])
            p.dma("act", zo.ap(), zsv[:, NCONV:NSCH, t0:t0 + 128], reads=[zs], writes=[zo])
        else:
            for hh, (c_lo, c_hi) in enumerate(CGR):
                p.dma(("sp", "act")[hh % 2], wst.ap()[:, 0:c_hi - c_lo, 0:b - a], zsv[:, c_lo:c_hi, nat(a, b)], reads=[zs], writes=[wst])
                p.op(("dve", "pool")[hh % 2], lambda e, c_lo=c_lo, c_hi=c_hi: e.tensor_copy(
                    out=W.ap()[:, c_lo:c_hi, wa:wb], in_=wst.ap()[:, 0:c_hi - c_lo, 0:b - a][:, :, ::-1]), reads=[wst], writes=[W])
            p.dma("act", zst.ap(), zsv[:, NCONV:NSCH, nat(t0, t0 + 128)], reads=[zs], writes=[zst])
            p.op("pool", lambda e: e.tensor_copy(out=zo.ap(), in_=zst.ap()[:, :, ::-1]), reads=[zst], writes=[zo])
        grid = t0 >= nctx
        for ci in range(NCONV):
            eng = EW()
            nch = SCH[ci][2]
            o = cz.ap()[:nch, ci, :]
            wv = W.ap()[:nch, ci, :]
            ctr = 4
            p.op(eng, lambda e, o=o, wv=wv, ci=ci, nch=nch: e.tensor_scalar(
                out=o, in0=wv[:, 128:256], scalar1=cw.ap()[:nch, ci, ctr:ctr + 1], scalar2=None, op0=ALU.mult),
                reads=[W, cw], writes=[cz])
            taps = []
            if grid:
                for dy in (-1, 0, 1):
                    for dx in (-1, 0, 1):
                        if dy == 0 and dx == 0:
                            continue
                        taps.append((dy, dx, (dy + 1) * 3 + dx + 1))
            else:
                taps = [(0, -1, 3), (0, 1, 5)]
            for dy, dx, tp in taps:
                s = 128 + 64 * dy
                if grid and dx != 0:
                    src = wv[:, s:s + 128].rearrange("p (r c) -> p r c", c=64)
                    dst = o.rearrange("p (r c) -> p r c", c=64)
                    if dx == -1:
                        src = src[:, :, 0:63]; dst = dst[:, :, 1:64]
                    else:
                        src = src[:, :, 1:64]; dst = dst[:, :, 0:63]
                else:
                    src = wv[:, s + dx:s + dx + 128]; dst = o
                if eng == "dve":
                    p.op(eng, lambda e, src=src, dst=dst, ci=ci, tp=tp, nch=nch: e.scalar_tensor_tensor(
                        out=dst, in0=src, scalar=cw.ap()[:nch, ci, tp:tp + 1], in1=dst, op0=ALU.mult, op1=ALU.add),
                        reads=[W, cw, cz], writes=[cz])
                else:
                    if grid and dx != 0:
                        tmp = ctmp.ap()[:nch, :].rearrange("p (r c) -> p r c", c=64)[:, :, 0:63]
                    else:
                        tmp = ctmp.ap()[:nch, :]
                    p.op(eng, lambda e, src=src, tmp=tmp, ci=ci, tp=tp, nch=nch: e.tensor_scalar(
                        out=tmp, in0=src, scalar1=cw.ap()[:nch, ci, tp:tp + 1], scalar2=None, op0=ALU.mult),
                        reads=[W, cw], writes=[ctmp])
                    p.op(eng, lambda e, dst=dst, tmp=tmp: e.tensor_tensor(out=dst, in0=dst, in1=tmp, op=ALU.add),
                         reads=[ctmp, cz], writes=[cz])
        return zo

    def decay_prep(nk, lw_ap, lw_reads, cs, G_, Ginv, Gend, Gex=None, lwsb=None):
        p.op("dve", lambda e: e.tensor_tensor_scan(out=cs.ap()[:nk, :], data0=m01[:nk, :], data1=lw_ap, initial=0.0,
                                                   op0=ALU.mult, op1=ALU.add), reads=[cst] + lw_reads, writes=[cs])
        p.op("act", lambda e: e.activation(out=G_.ap()[:nk, :], in_=cs.ap()[:nk, :], func=AF.Exp), reads=[cs], writes=[G_])
        p.op("act", lambda e: e.activation(out=Ginv.ap()[:nk, :], in_=cs.ap()[:nk, :], func=AF.Exp, scale=-1.0), reads=[cs], writes=[Ginv])
        for c in range(2):
            p.op("act", lambda e, c=c: e.activation(out=Gend.ap()[:nk, 64 * c:64 * c + 64], in_=cs.ap()[:nk, 64 * c:64 * c + 64],
                                                    func=AF.Exp, scale=-1.0, bias=cs.ap()[:nk, 64 * c + 63:64 * c + 64]),
                 reads=[cs], writes=[Gend])
        if Gex is not None:
            p.op("dve", lambda e: e.tensor_tensor(out=Gex.ap()[:nk, :], in0=cs.ap()[:nk, :], in1=lwsb, op=ALU.subtract),
                 reads=[cs] + lw_reads, writes=[Gex])
            p.op("act", lambda e: e.activation(out=Gex.ap()[:nk, :], in_=Gex.ap()[:nk, :], func=AF.Exp), reads=[Gex], writes=[Gex])

    rS = T([64, 6, 64], "rwS", ); p.op("pool", lambda e: e.memset(rS.ap(), 0.0), writes=[rS])
    tw = T([64, 128], "tw")
    NG = 3
    nm = ["sgw", "a", "kkv", "sq", "kap", "kd", "b", "cs", "Ginv", "Gend", "Gex"]
    R_ = {n: T([64, 128], "r_" + n) for n in nm}
    RPn = ("rt", "kt", "bt", "kkt", "Bh", "Kh", "G")
    RP = [{n: T([64, 128], f"rp{g}_" + n) for n in RPn} for g in range(NG)]
    tm4s = [T([64, 4, 64], f"tm4_{g}") for g in range(NG)]
    tt5s = [T([64, 5, 64], f"tt5_{g}") for g in range(NG)]
    Rrs = [T([64, 128], f"Rr{g}") for g in range(NG)]
    Xxs = [T([64, 128], f"Xx{g}") for g in range(NG)]
    nZs = [T([64, 64], f"nZ{g}") for g in range(NG)]
    Pbs = [[T([64, 2, 64], f"Pb{g}_{i}") for i in range(2)] for g in range(NG)]
    QPs = [T([64, 2, 64], f"QP{g}") for g in range(NG)]
    yrw = T([64, 6, 128], "yrw"); ybn = T([64, 6, 128], "ybn")
    NEG_E = -float(np.exp(-0.5))

    def rw_prep(h, g, zo):
        wadn = zo.ap()[:, 7, :]
        r = cz.ap()[:64, h, :]; k = cz.ap()[:64, 6 + h, :]; v = cz.ap()[:64, 12 + h, :]
        prm = lambda w: rwp.ap()[:, w, h:h + 1]
        A = lambda n: R_[n].ap()
        Q = lambda n: RP[g][n].ap()
        ps = PS()
        p.op("pe", lambda e: e.matmul(ps.ap()[:64, 0:128], lhsT=waup.ap()[0:64, h * 64:(h + 1) * 64], rhs=tw.ap(), start=True, stop=True),
             reads=[waup, tw], writes=[ps])
        p.op("pe", lambda e: e.matmul(ps.ap()[:64, 128:256], lhsT=waup.ap()[64:128, h * 64:(h + 1) * 64], rhs=wadn[64:128, :], start=True, stop=True),
             reads=[waup, zo], writes=[ps])
        p.op("act", lambda e: e.activation(out=A("sgw"), in_=ps.ap()[:64, 0:128], func=AF.Sigmoid, bias=prm(0)), reads=[ps, rwp], writes=[R_["sgw"]])
        p.op("act", lambda e: e.activation(out=A("a"), in_=ps.ap()[:64, 128:256], func=AF.Sigmoid, bias=prm(1)), reads=[ps, rwp], writes=[R_["a"]])
        p.op("dve", lambda e: e.tensor_scalar(out=A("sgw"), in0=A("sgw"), scalar1=NEG_E, scalar2=None, op0=ALU.mult), reads=[R_["sgw"]], writes=[R_["sgw"]])
        p.op("pool", lambda e: e.tensor_scalar(out=A("kkv"), in0=k, scalar1=prm(2), scalar2=None, op0=ALU.mult), reads=[cz, rwp], writes=[R_["kkv"]])
        p.op("act", lambda e: e.activation(out=A("sq"), in_=A("kkv"), func=AF.Square), reads=[R_["kkv"]], writes=[R_["sq"]])
        ps2 = PS()
        p.op("pe", lambda e: e.matmul(ps2.ap()[:64, 0:128], lhsT=ones.ap()[0:64, 0:64], rhs=A("sq"), start=True, stop=True), reads=[ones, R_["sq"]], writes=[ps2])
        p.op("dve", lambda e: e.tensor_scalar(out=A("sq"), in0=ps2.ap()[:64, 0:128], scalar1=1e-24, scalar2=None, op0=ALU.max), reads=[ps2], writes=[R_["sq"]])
        p.op("act", lambda e: e.activation(out=A("sq"), in_=A("sq"), func=AF.Sqrt), reads=[R_["sq"]], writes=[R_["sq"]])
        p.op("dve", lambda e: e.reciprocal(out=A("sq"), in_=A("sq")), reads=[R_["sq"]], writes=[R_["sq"]])
        p.op("dve", lambda e: e.tensor_tensor(out=A("kap"), in0=A("kkv"), in1=A("sq"), op=ALU.mult), reads=[R_["kkv"], R_["sq"]], writes=[R_["kap"]])
        p.op("pool", lambda e: e.tensor_scalar(out=A("kd"), in0=A("a"), scalar1=-1.0, scalar2=prm(3), op0=ALU.add, op1=ALU.mult), reads=[R_["a"], rwp], writes=[R_["kd"]])
        p.op("dve", lambda e: e.scalar_tensor_tensor(out=A("kd"), in0=A("kd"), scalar=1.0, in1=k, op0=ALU.add, op1=ALU.mult), reads=[R_["kd"], cz], writes=[R_["kd"]])
        p.op("pool", lambda e: e.tensor_tensor(out=A("b"), in0=A("a"), in1=A("kap"), op=ALU.mult), reads=[R_["a"], R_["kap"]], writes=[R_["b"]])
        p.op("dve", lambda e: e.scalar_tensor_tensor(out=A("kkv"), in0=r, scalar=prm(4), in1=A("kd"), op0=ALU.mult, op1=ALU.mult), reads=[cz, rwp, R_["kd"]], writes=[R_["kkv"]])
        p.op("pe", lambda e: e.matmul(ps2.ap()[:64, 128:256], lhsT=ones.ap()[0:64, 0:64], rhs=A("kkv"), start=True, stop=True), reads=[ones, R_["kkv"]], writes=[ps2])
        p.op("dve", lambda e: e.tensor_tensor(out=ybn.ap()[:, h, :], in0=ps2.ap()[:64, 128:256], in1=v, op=ALU.mult), reads=[ps2, cz], writes=[ybn])
        decay_prep(64, A("sgw"), [R_["sgw"]], R_["cs"], RP[g]["G"], R_["Ginv"], R_["Gend"], R_["Gex"], A("sgw"))
        for (o_, x_, xr, g_) in (("rt", r, cz, RP[g]["G"]), ("kt", A("kap"), R_["kap"], R_["Gex"]), ("bt", A("b"), R_["b"], R_["Ginv"]),
                                 ("kkt", A("kd"), R_["kd"], R_["Ginv"]), ("Bh", A("b"), R_["b"], R_["Gend"]), ("Kh", A("kd"), R_["kd"], R_["Gend"])):
            p.op(EW(), lambda e, o_=o_, x_=x_, g_=g_: e.tensor_tensor(out=Q(o_), in0=x_, in1=g_.ap(), op=ALU.mult),
                 reads=[xr, g_], writes=[RP[g][o_]])

    def rw_chunk(h, g, c):
        v = cz.ap()[:64, 12 + h, :]
        Q = lambda n: RP[g][n].ap()
        RQ = lambda n: RP[g][n]
        tm4, tt5, Rr, Xx, nZ, Pb, QP = tm4s[g], tt5s[g], Rrs[g], Xxs[g], nZs[g], Pbs[g], QPs[g]
        cs_ = slice(64 * c, 64 * c + 64)
        pt = PS()
        for i, (src, rd) in enumerate(((v[:, cs_], cz), (Q("kt")[:, cs_], RQ("kt")), (Q("Bh")[:, cs_], RQ("Bh")), (Q("Kh")[:, cs_], RQ("Kh")))):
            p.op("pe", lambda e, i=i, src=src: e.transpose(pt.ap()[:64, 64 * i:64 * i + 64], src, ident[0:64, 0:64]),
                 reads=[rd, cst], writes=[pt])
        p.op("act", lambda e: e.copy(out=tm4.ap().rearrange("p a b -> p (a b)"), in_=pt.ap()[:64, 0:256]), reads=[pt], writes=[tm4])
        Vtm = tm4.ap()[:, 0, :]; Ktm = tm4.ap()[:, 1, :]; Btm = tm4.ap()[:, 2, :]; Khtm = tm4.ap()[:, 3, :]
        p5 = PS()
        for i, (l_, r_) in enumerate((("bt", "kt"), ("kt", "bt"), ("kkt", "kt"), ("bt", "rt"), ("kkt", "rt"))):
            p.op("pe", lambda e, i=i, l_=l_, r_=r_: e.matmul(p5.ap()[:64, 64 * i:64 * i + 64], lhsT=Q(l_)[:, cs_], rhs=Q(r_)[:, cs_], start=True, stop=True),
                 reads=[RQ(l_), RQ(r_)], writes=[p5])
        p.op("dve", lambda e: e.tensor_tensor(out=tt5.ap().rearrange("p a b -> p (a b)"), in0=p5.ap()[:64, 0:320],
                                              in1=mask5.ap().rearrange("p a b -> p (a b)"), op=ALU.mult), reads=[p5, mask5], writes=[tt5])
        U = tt5.ap()[:, 0, :]; L = tt5.ap()[:, 1, :]; LkT = tt5.ap()[:, 2, :]; AbrT = tt5.ap()[:, 3, :]; AkrT = tt5.ap()[:, 4, :]
        yield
        p6 = PS()
        p.op("pe", lambda e: e.matmul(p6.ap()[:64, 0:64], lhsT=LkT, rhs=Vtm, start=True, stop=True), reads=[tt5, tm4], writes=[p6])
        p.op("act", lambda e: e.copy(out=Rr.ap()[:, 64:128], in_=p6.ap()[:64, 0:64]), reads=[p6], writes=[Rr])
        p.op("pool", lambda e: e.tensor_copy(out=Rr.ap()[:, 0:64], in_=Ktm), reads=[tm4], writes=[Rr])
        yield
        p7 = PS()
        p.op("pe", lambda e: e.matmul(p7.ap()[:64, 0:128], lhsT=U, rhs=Rr.ap(), start=True, stop=True), reads=[tt5, Rr], writes=[p7])
        p.op("dve", lambda e: e.tensor_tensor(out=Xx.ap(), in0=Rr.ap(), in1=p7.ap()[:64, 0:128], op=ALU.subtract), reads=[Rr, p7], writes=[Xx])
        Pc, PTc, Prd = L, U, [tt5]
        for lvl in range(5):
            pn = Pb[lvl % 2]
            pq = PS()
            last = lvl == 4
            if not last:
                p.op("pe", lambda e, Pc=Pc, PTc=PTc: e.matmul(pq.ap()[:64, 0:64], lhsT=PTc, rhs=Pc, start=True, stop=True), reads=Prd, writes=[pq])
            p.op("pe", lambda e, Pc=Pc, PTc=PTc: e.matmul(pq.ap()[:64, 64:128], lhsT=Pc, rhs=PTc, start=True, stop=True), reads=Prd, writes=[pq])
            if not last:
                p.op("act", lambda e, pn=pn: e.copy(out=pn.ap().rearrange("p a b -> p (a b)"), in_=pq.ap()[:64, 0:128]), reads=[pq], writes=[pn])
            else:
                p.op("act", lambda e, pn=pn: e.copy(out=pn.ap()[:, 1, :], in_=pq.ap()[:64, 64:128]), reads=[pq], writes=[pn])
            Pc, PTc, Prd = pn.ap()[:, 0, :], pn.ap()[:, 1, :], [pn]
            yield
            px = PS()
            p.op("pe", lambda e, PTc=PTc: e.matmul(px.ap()[:64, 0:128], lhsT=PTc, rhs=Xx.ap(), start=True, stop=True), reads=Prd + [Xx], writes=[px])
            p.op("dve", lambda e: e.tensor_tensor(out=Xx.ap(), in0=Xx.ap(), in1=px.ap()[:64, 0:128], op=ALU.add), reads=[Xx, px], writes=[Xx])
            yield
        Gm = Xx.ap()[:, 0:64]
        p.op("act", lambda e: e.mul(out=nZ.ap(), in_=Xx.ap()[:, 64:128], mul=-1.0), reads=[Xx], writes=[nZ])
        p8 = PS()
        p.op("pe", lambda e: e.matmul(p8.ap()[:64, 0:64], lhsT=Gm, rhs=AbrT, start=True, stop=True), reads=[Xx, tt5], writes=[p8])
        p.op("pe", lambda e: e.matmul(p8.ap()[:64, 64:128], lhsT=Gm, rhs=Btm, start=True, stop=True), reads=[Xx, tm4], writes=[p8])
        p.op("dve", lambda e: e.tensor_tensor(out=QP.ap()[:, 0, :], in0=Q("rt")[:, cs_], in1=p8.ap()[:64, 0:64], op=ALU.subtract), reads=[RQ("rt"), p8], writes=[QP])
        p.op("dve", lambda e: e.scalar_tensor_tensor(out=QP.ap()[:, 1, :], in0=ident[0:64, 0:64], scalar=Q("G")[:, 64 * c + 63:64 * c + 64],
                                                     in1=p8.ap()[:64, 64:128], op0=ALU.mult, op1=ALU.subtract), reads=[cst, RQ("G"), p8], writes=[QP])
        yield
        p9 = PS()
        p.op("pe", lambda e: e.matmul(p9.ap()[:64, 0:64], lhsT=rS.ap()[:, h, :], rhs=QP.ap()[:, 0, :], start=True, stop=False), reads=[rS, QP], writes=[p9])
        p.op("pe", lambda e: e.matmul(p9.ap()[:64, 0:64], lhsT=Vtm, rhs=AkrT, start=False, stop=False), reads=[tm4, tt5], writes=[p9])
        p.op("pe", lambda e: e.matmul(p9.ap()[:64, 0:64], lhsT=nZ.ap(), rhs=AbrT, start=False, stop=True), reads=[nZ, tt5], writes=[p9])
        p.op("act", lambda e: e.copy(out=yrw.ap()[:, h, cs_], in_=p9.ap()[:64, 0:64]), reads=[p9], writes=[yrw])
        p10 = PS()
        p.op("pe", lambda e: e.matmul(p10.ap()[:64, 0:64], lhsT=QP.ap()[:, 1, :], rhs=rS.ap()[:, h, :], start=True, stop=False), reads=[QP, rS], writes=[p10])
        p.op("pe", lambda e: e.matmul(p10.ap()[:64, 0:64], lhsT=Khtm, rhs=Vtm, start=False, stop=False), reads=[tm4], writes=[p10])
        p.op("pe", lambda e: e.matmul(p10.ap()[:64, 0:64], lhsT=Btm, rhs=nZ.ap(), start=False, stop=True), reads=[tm4, nZ], writes=[p10])
        p.op("dve", lambda e: e.tensor_copy(out=rS.ap()[:, h, :], in_=p10.ap()[:64, 0:64]), reads=[p10], writes=[rS])
        yield

    def run_interleaved(gens):
        alive = list(gens)
        while alive:
            nxt = []
            for gen in alive:
                try:
                    next(gen)
                    nxt.append(gen)
                except StopIteration:
                    pass
            alive = nxt

    def rwkv_block(t0, zo):
        wadn = zo.ap()[:, 7, :]
        p.op("act", lambda e: e.activation(out=tw.ap(), in_=wadn[0:64, :], func=AF.Tanh), reads=[zo], writes=[tw])
        for grp in range(6 // NG):
            heads = list(range(grp * NG, (grp + 1) * NG))
            for g, h in enumerate(heads):
                rw_prep(h, g, zo)
            for c in range(2):
                run_interleaved([rw_chunk(h, g, c) for g, h in enumerate(heads)])
        store(o_rw, "h p t -> p h t", yrw, t0, 64, 6, "sp")
        store(o_bn, "h p t -> p h t", ybn, t0, 64, 6, "act")

    mS = T([96, 4, 192], "mlS"); p.op("pool", lambda e: e.memset(mS.ap(), 0.0), writes=[mS])
    gl1 = T([16, 128], "gl1"); gl2 = T([16, 128], "gl2")
    mn = ["q", "ks", "ei", "kp", "cs", "G", "Ginv", "Gend", "rt", "kkt", "Kh", "den"]
    M_ = {n: T([96, 128], "m_" + n) for n in mn}
    mtm = T([64, 2, 96], "mtm"); makr = T([64, 64], "makr")
    yml = T([96, 4, 128], "yml")
    KSC = float(96 ** -0.5)

    def mlstm_block(t0, zo):
        gl = zo.ap()[0:16, 4, :]
        p.op("dve", lambda e: e.tensor_scalar(out=gl1.ap(), in0=gl, scalar1=glb.ap()[:, 0:1], scalar2=None, op0=ALU.add), reads=[zo, glb], writes=[gl1])
        p.op("act", lambda e: e.activation(out=gl2.ap(), in_=gl1.ap(), func=AF.Sigmoid), reads=[gl1], writes=[gl2])
        p.op("act", lambda e: e.activation(out=gl2.ap(), in_=gl2.ap(), func=AF.Ln), reads=[gl2], writes=[gl2])
        for h in range(4):
            B = lambda n: M_[n].ap()
            q = cz.ap()[:96, 18 + h, :]; k = cz.ap()[:96, 22 + h, :]; v = zo.ap()[:96, h, :]
            ps = PS()
            p.op("pe", lambda e: e.matmul(ps.ap()[:96, 0:128], lhsT=sel.ap()[:, 4 + h, :], rhs=gl2.ap(), start=True, stop=True), reads=[sel, gl2], writes=[ps])
            p.op("pe", lambda e: e.matmul(ps.ap()[:96, 128:256], lhsT=sel.ap()[:, h, :], rhs=gl1.ap(), start=True, stop=True), reads=[sel, gl1], writes=[ps])
            p.op("act", lambda e: e.activation(out=B("ei"), in_=ps.ap()[:96, 128:256], func=AF.Exp), reads=[ps], writes=[M_["ei"]])
            p.op("act", lambda e: e.activation(out=B("q"), in_=q, func=AF.Silu), reads=[cz], writes=[M_["q"]])
            p.op("act", lambda e: e.activation(out=B("ks"), in_=k, func=AF.Silu), reads=[cz], writes=[M_["ks"]])
            p.op("dve", lambda e: e.scalar_tensor_tensor(out=B("kp"), in0=B("ks"), scalar=KSC, in1=B("ei"), op0=ALU.mult, op1=ALU.mult), reads=[M_["ks"], M_["ei"]], writes=[M_["kp"]])
            decay_prep(96, ps.ap()[:96, 0:128], [ps], M_["cs"], M_["G"], M_["Ginv"], M_["Gend"])
            for (o_, x_, g_) in (("rt", "q", "G"), ("kkt", "kp", "Ginv"), ("Kh", "kp", "Gend")):
                p.op(EW(), lambda e, o_=o_, x_=x_, g_=g_: e.tensor_tensor(out=B(o_), in0=B(x_), in1=B(g_), op=ALU.mult), reads=[M_[x_], M_[g_]], writes=[M_[o_]])
            for c in range(2):
                cs_ = slice(64 * c, 64 * c + 64)
                pt = PS()
                p.op("pe", lambda e: e.transpose(pt.ap()[:64, 0:96], v[:, cs_], ident[0:96, 0:96]), reads=[zo, cst], writes=[pt])
                p.op("pe", lambda e: e.transpose(pt.ap()[:64, 96:192], B("Kh")[:, cs_], ident[0:96, 0:96]), reads=[M_["Kh"], cst], writes=[pt])
                p.op("act", lambda e: e.copy(out=mtm.ap().rearrange("p a b -> p (a b)"), in_=pt.ap()[:64, 0:192]), reads=[pt], writes=[mtm])
                Vtm = mtm.ap()[:, 0, :]; Khtm = mtm.ap()[:, 1, :]
                pa = PS()
                p.op("pe", lambda e: e.matmul(pa.ap()[:64, 0:64], lhsT=B("kkt")[:, cs_], rhs=B("rt")[:, cs_], start=True, stop=True), reads=[M_["kkt"], M_["rt"]], writes=[pa])
                p.op("dve", lambda e: e.tensor_tensor(out=makr.ap(), in0=pa.ap()[:64, 0:64], in1=m_iu[0:64, 0:64], op=ALU.mult), reads=[pa, cst], writes=[makr])
                py = PS()
                p.op("pe", lambda e: e.matmul(py.ap()[:96, 0:64], lhsT=mS.ap()[:, h, 0:96], rhs=B("rt")[:, cs_], start=True, stop=False), reads=[mS, M_["rt"]], writes=[py])
                p.op("pe", lambda e: e.matmul(py.ap()[:96, 0:64], lhsT=Vtm, rhs=makr.ap(), start=False, stop=True), reads=[mtm, makr], writes=[py])
                p.op("pe", lambda e: e.matmul(py.ap()[:96, 64:128], lhsT=mS.ap()[:, h, 96:192], rhs=B("rt")[:, cs_], start=True, stop=False), reads=[mS, M_["rt"]], writes=[py])
                p.op("pe", lambda e: e.matmul(py.ap()[:96, 64:128], lhsT=ones.ap()[0:64, 0:96], rhs=makr.ap(), start=False, stop=True), reads=[ones, makr], writes=[py])
                p.op("act", lambda e: e.activation(out=B("den")[:, 0:64], in_=py.ap()[:96, 64:128], func=AF.Abs), reads=[py], writes=[M_["den"]])
                p.op("dve", lambda e: e.tensor_scalar(out=B("den")[:, 0:64], in0=B("den")[:, 0:64], scalar1=1.0, scalar2=None, op0=ALU.max), reads=[M_["den"]], writes=[M_["den"]])
                p.op("dve", lambda e: e.reciprocal(out=B("den")[:, 0:64], in_=B("den")[:, 0:64]), reads=[M_["den"]], writes=[M_["den"]])
                p.op("dve", lambda e: e.tensor_tensor(out=yml.ap()[:, h, cs_], in0=py.ap()[:96, 0:64], in1=B("den")[:, 0:64], op=ALU.mult), reads=[py, M_["den"]], writes=[yml])
                pu = PS()
                p.op("pe", lambda e: e.matmul(pu.ap()[:96, 0:96], lhsT=Khtm, rhs=Vtm, start=True, stop=True), reads=[mtm], writes=[pu])
                p.op("pe", lambda e: e.matmul(pu.ap()[:96, 96:192], lhsT=Khtm, rhs=ones.ap()[0:64, 0:96], start=True, stop=True), reads=[mtm, ones], writes=[pu])
                p.op("dve", lambda e: e.scalar_tensor_tensor(out=mS.ap()[:, h, :], in0=mS.ap()[:, h, :], scalar=B("G")[:, 64 * c + 63:64 * c + 64],
                                                             in1=pu.ap()[:96, 0:192], op0=ALU.mult, op1=ALU.add), reads=[mS, M_["G"], pu], writes=[mS])
        store(o_ml, "h p t -> p h t", yml, t0, 96, 4, "act")

    for j in range(NB):
        zo = conv_block(j)
        s5_block(j * 128, zo)
        mlstm_block(j * 128, zo)
        rwkv_block(j * 128, zo)


def mixer_consts():
    a = np.arange(128)
    ident = np.eye(128, dtype=np.float32)
    su = (a[:, None] < a[None, :]).astype(np.float32)
    sl = (a[:, None] > a[None, :]).astype(np.float32)
    iu = (a[:, None] <= a[None, :]).astype(np.float32)
    m01 = np.ones((128, 128), np.float32)
    m01[:, 0] = 0.0
    m01[:, 64] = 0.0
    cst = np.stack([ident, su, sl, iu, m01], 1).copy()
    ri = np.broadcast_to(np.arange(129, dtype=np.float32), (128, 129)).copy()
    return cst, ri


def mixer_params(P, i, d):
    m = {}
    cwf = P["conv_w"][i]
    if d == 1:
        cwf = cwf[::-1, ::-1]
    cw = np.zeros((128, NCONV, 9), np.float32)
    for ci in range(NCONV):
        _, c0, n = SCH[ci]
        cw[:n, ci, :] = cwf[:, :, c0:c0 + n].reshape(9, n).T
    m["cw"] = cw
    rwp = np.stack([P["rw_w0"][i, d].reshape(6, 64).T, P["rw_a0"][i, d].reshape(6, 64).T, P["rw_k_k"][i].reshape(6, 64).T,
                    P["rw_k_a"][i].reshape(6, 64).T, P["rw_r_k"][i].T], 1)
    m["rwp"] = np.ascontiguousarray(rwp, np.float32)
    m["wa_up"] = np.concatenate([P["rw_w_up"][i, d], P["rw_a_up"][i, d]], 0).astype(np.float32)
    m["glb"] = P["ml_gate_b"][i].reshape(16, 1).astype(np.float32)
    sel = np.zeros((16, 8, 96), np.float32)
    for j in range(8):
        sel[d * 8 + j, j, :] = 1.0
    m["sel"] = sel
    s5p = np.zeros((128, 3, 8), np.float32)
    bz = np.zeros((128, 16, 128), np.float32)
    cz = np.zeros((128, 16, 128), np.float32)
    for k in range(8):
        for gl2 in range(2):
            g = 2 * k + gl2
            js = slice(gl2 * 64, gl2 * 64 + 64)
            s5p[js, 0, k] = P["s5_a_re"][i, d, g]
            s5p[js, 1, k] = P["s5_a_im"][i, d, g]
            s5p[js, 2, k] = P["s5_log_dt"][i, d, g]
            r0 = (g % 8) * 16
            bz[r0:r0 + 16, k, js] = P["s5_b_re"][i, d, g].T
            bz[r0:r0 + 16, 8 + k, js] = P["s5_b_im"][i, d, g].T
            cz[js, k, r0:r0 + 16] = P["s5_c_re"][i, d, g].T
            cz[js, 8 + k, r0:r0 + 16] = P["s5_c_im"][i, d, g].T
    m["s5p"] = s5p
    m["bz"] = bz
    m["cz"] = cz
    cst, ri = mixer_consts()
    m["cst"] = cst
    m["ri"] = ri
    return m


def mixer_zs(zseq):
    NS = zseq.shape[0]
    zs = np.zeros((NSCH, 128, NS), np.float32)
    for ci, (_, c0, n) in enumerate(SCH):
        zs[ci, :n, :] = zseq[:, c0:c0 + n].T
    return zs


def emit_merge(p, x1T, oT, modT, lng_ap, lnb_ap, outs2, zall, zallb, mp):
    f32 = F32

    def T(shape, name, dt=f32):
        return p.tile(shape, dt, "g_" + name)

    mod = T([128, 72, 2], "mod"); g_sb = T([128, 8], "lng"); b_sb = T([128, 8], "lnb")
    gn = T([64, 2, 6], "gn"); gup = T([128, 384], "gup"); s5d = T([128, 2, 2], "s5d"); gluw = T([128, 2, 256], "gluw")
    mlg = T([96, 4], "mlg"); brb = T([128, 24], "brb")
    p.dma("sp", mod.ap(), modT.ap(), reads=[modT], writes=[mod])
    for i, (t, d) in enumerate(((g_sb, lng_ap), (b_sb, lnb_ap), (gn, mp["gn"]), (gup, mp["g_up"]), (s5d, mp["s5d"]),
                                (gluw, mp["glu_w"]), (mlg, mp["mlg"]), (brb, mp["brb"]))):
        p.dma(("sp", "act")[i % 2], t.ap(), d, writes=[t])
    ones = make_ones(p)
    uprw = T([64, 6, 1024], "uprw", BF16); ups5 = T([128, 2, 1024], "ups5", BF16); upml = T([96, 4, 1024], "upml", BF16)
    wout = T([128, 8, 1024], "wout", BF16)
    stage = [T([128, 1024], f"stage{i}") for i in range(2)]
    k = 0
    for h in range(6):
        st = stage[k % 2]
        p.dma(("sp", "act")[k % 2], st.ap()[:64, :], mp["up_rw"][:, h, :], writes=[st])
        p.op("dve", lambda e, h=h, st=st: e.tensor_copy(out=uprw.ap()[:, h, :], in_=st.ap()[:64, :]), reads=[st], writes=[uprw]); k += 1
    for h in range(2):
        st = stage[k % 2]
        p.dma(("sp", "act")[k % 2], st.ap(), mp["up_s5"][:, h, :], writes=[st])
        p.op("dve", lambda e, h=h, st=st: e.tensor_copy(out=ups5.ap()[:, h, :], in_=st.ap()), reads=[st], writes=[ups5]); k += 1
    for h in range(4):
        st = stage[k % 2]
        p.dma(("sp", "act")[k % 2], st.ap()[:96, :], mp["up_ml"][:, h, :], writes=[st])
        p.op("dve", lambda e, h=h, st=st: e.tensor_copy(out=upml.ap()[:, h, :], in_=st.ap()[:96, :]), reads=[st], writes=[upml]); k += 1
    for h in range(8):
        st = stage[k % 2]
        p.dma(("sp", "act")[k % 2], st.ap(), mp["w_out"][:, h, :], writes=[st])
        p.op("dve", lambda e, h=h, st=st: e.tensor_copy(out=wout.ap()[:, h, :], in_=st.ap()), reads=[st], writes=[wout]); k += 1

    x_sb = T([128, 8, TT], "x"); yrw = T([64, 2, 6, TT], "yrw"); ybn = T([64, 2, 6, TT], "ybn")
    ys5 = T([128, 2, 2, TT], "ys5"); yml = T([96, 2, 4, TT], "yml"); zg = T([128, 31, TT], "zg")
    rwy = T([64, 6, TT], "rwy", BF16); s5y = T([128, 2, TT], "s5y", BF16); mly = T([96, 4, TT], "mly", BF16)
    ym = T([128, 8, TT], "ym", BF16)
    yg = T([128, 2, TT], "yg")
    ta = T([128, TT], "ta"); tb = T([128, TT], "tb"); tc = T([128, TT], "tc"); td = T([128, TT], "td"); sgd = T([128, TT], "sgd")
    y_sb = T([128, 8, TT], "y"); ysq = T([128, 8, TT], "ysq"); mean = T([128, TT], "mean"); rstd = T([128, TT], "rstd")
    pss = [p.psum([128, 512], f32, f"pg{i}") for i in range(6)]
    ps1 = p.psum([128, 512], f32, "ps1"); ps2 = p.psum([128, 512], f32, "ps2")
    pc = [0]

    def PS():
        pc[0] += 1
        return pss[pc[0] % 6]

    def std_part(x, np_, n, eps, scale_ap, out_ap, out_buf):
        p.op("act", lambda e: e.activation(out=tb.ap()[:np_, :n], in_=x, func=AF.Square), reads=[ta], writes=[tb])
        q = PS()
        p.op("pe", lambda e: e.matmul(q.ap()[:np_, 0:n], lhsT=ones.ap()[0:np_, 0:np_], rhs=x, start=True, stop=True), reads=[ones, ta], writes=[q])
        p.op("pe", lambda e: e.matmul(q.ap()[:np_, 256:256 + n], lhsT=ones.ap()[0:np_, 0:np_], rhs=tb.ap()[:np_, :n], start=True, stop=True), reads=[ones, tb], writes=[q])
        p.op("dve", lambda e: e.tensor_scalar(out=tc.ap()[:np_, :n], in0=q.ap()[:np_, 0:n], scalar1=1.0 / np_, scalar2=None, op0=ALU.mult), reads=[q], writes=[tc])
        p.op("dve", lambda e: e.tensor_tensor(out=td.ap()[:np_, :n], in0=tc.ap()[:np_, :n], in1=tc.ap()[:np_, :n], op=ALU.mult), reads=[tc], writes=[td])
        p.op("dve", lambda e: e.scalar_tensor_tensor(out=td.ap()[:np_, :n], in0=q.ap()[:np_, 256:256 + n], scalar=1.0 / np_, in1=td.ap()[:np_, :n], op0=ALU.mult, op1=ALU.subtract), reads=[q, td], writes=[td])
        p.op("dve", lambda e: e.tensor_scalar(out=td.ap()[:np_, :n], in0=td.ap()[:np_, :n], scalar1=eps, scalar2=None, op0=ALU.add), reads=[td], writes=[td])
        p.op("act", lambda e: e.activation(out=td.ap()[:np_, :n], in_=td.ap()[:np_, :n], func=AF.Sqrt), reads=[td], writes=[td])
        p.op("dve", lambda e: e.reciprocal(out=td.ap()[:np_, :n], in_=td.ap()[:np_, :n]), reads=[td], writes=[td])
        p.op("dve", lambda e: e.tensor_tensor(out=x, in0=x, in1=tc.ap()[:np_, :n], op=ALU.subtract), reads=[ta, tc], writes=[ta])
        p.op("dve", lambda e: e.scalar_tensor_tensor(out=out_ap, in0=x, scalar=scale_ap, in1=td.ap()[:np_, :n], op0=ALU.mult, op1=ALU.mult), reads=[ta, td, gn, mlg], writes=[out_buf])

    xv = x1T.ap().rearrange("c p t -> p c t")
    ov = oT.ap().rearrange("c p t -> p c t")
    GC = 2.0 * float(np.sqrt(2.0 / np.pi))
    for ti, (t0, n, wh) in enumerate(TILES):
        sl = slice(t0, t0 + n)
        p.dma("sp", x_sb.ap()[:, :, :n], xv[:, :, sl], reads=[x1T], writes=[x_sb])
        for d in range(2):
            o_rw, o_bn, o_s5, o_ml = outs2[d]
            p.dma("act", yrw.ap()[:, d, :, :n], o_rw.ap()[:, :, sl].rearrange("h p t -> p h t"), reads=[o_rw], writes=[yrw])
            p.dma("sp", ybn.ap()[:, d, :, :n], o_bn.ap()[:, :, sl].rearrange("h p t -> p h t"), reads=[o_bn], writes=[ybn])
            p.dma("act", ys5.ap()[:, d, :, :n], o_s5.ap()[:, :, sl].rearrange("h p t -> p h t"), reads=[o_s5], writes=[ys5])
            p.dma("sp", yml.ap()[:, d, :, :n], o_ml.ap()[:, :, sl].rearrange("h p t -> p h t"), reads=[o_ml], writes=[yml])
        zav = zall.ap().rearrange("c p t -> p c t")
        zbv = zallb.ap().rearrange("c p t -> p c t")
        p.dma("act", zg.ap()[:, 0:29, :n], zbv[:, :, sl], reads=[zallb], writes=[zg])
        p.dma("sp", zg.ap()[:, 29:31, :n], zav[:, 31:33, sl], reads=[zall], writes=[zg])
        p.op("act", lambda e: e.activation(out=sgd.ap()[:, :n], in_=zg.ap()[:, 4, :n], func=AF.Sigmoid), reads=[zg], writes=[sgd])
        for h in range(6):
            x = ta.ap()[:64, :n]
            p.op("dve", lambda e, h=h: e.tensor_tensor(out=x, in0=yrw.ap()[:, 0, h, :n], in1=yrw.ap()[:, 1, h, :n], op=ALU.add), reads=[yrw], writes=[ta])
            std_part(x, 64, n, 64e-5, gn.ap()[:, 0, h:h + 1], x, ta)
            p.op("dve", lambda e, h=h: e.scalar_tensor_tensor(out=x, in0=x, scalar=gn.ap()[:, 1, h:h + 1], in1=ybn.ap()[:, 0, h, :n], op0=ALU.add, op1=ALU.add), reads=[ta, gn, ybn], writes=[ta])
            p.op("dve", lambda e, h=h: e.tensor_tensor(out=x, in0=x, in1=ybn.ap()[:, 1, h, :n], op=ALU.add), reads=[ta, ybn], writes=[ta])
            q = PS()
            p.op("pe", lambda e, h=h: e.matmul(q.ap()[:64, 0:n], lhsT=gup.ap()[:, h * 64:(h + 1) * 64], rhs=sgd.ap()[:, :n], start=True, stop=True), reads=[gup, sgd], writes=[q])
            p.op("dve", lambda e, h=h: e.tensor_tensor(out=rwy.ap()[:, h, :n], in0=x, in1=q.ap()[:64, 0:n], op=ALU.mult), reads=[ta, q], writes=[rwy])
        for mt in range(2):
            x = yg.ap()[:, mt, :n]
            p.op("dve", lambda e, mt=mt: e.scalar_tensor_tensor(out=x, in0=zg.ap()[:, 29 + mt, :n], scalar=s5d.ap()[:, 0, mt:mt + 1], in1=ys5.ap()[:, 0, mt, :n], op0=ALU.mult, op1=ALU.add), reads=[zg, s5d, ys5], writes=[yg])
            p.op("dve", lambda e, mt=mt: e.tensor_tensor(out=x, in0=x, in1=ys5.ap()[:, 1, mt, :n], op=ALU.add), reads=[yg, ys5], writes=[yg])
            p.op("act", lambda e: e.activation(out=tb.ap()[:, :n], in_=x, func=AF.Square), reads=[yg], writes=[tb])
            p.op("dve", lambda e: e.tensor_scalar(out=tb.ap()[:, :n], in0=tb.ap()[:, :n], scalar1=0.044715, scalar2=1.0, op0=ALU.mult, op1=ALU.add), reads=[tb], writes=[tb])
            p.op("dve", lambda e: e.tensor_tensor(out=tb.ap()[:, :n], in0=tb.ap()[:, :n], in1=x, op=ALU.mult), reads=[tb, yg], writes=[tb])
            p.op("act", lambda e: e.activation(out=tb.ap()[:, :n], in_=tb.ap()[:, :n], func=AF.Sigmoid, scale=GC), reads=[tb], writes=[tb])
            p.op("dve", lambda e: e.tensor_tensor(out=x, in0=x, in1=tb.ap()[:, :n], op=ALU.mult), reads=[yg, tb], writes=[yg])
        for mo in range(2):
            q = PS()
            for kc in range(2):
                p.op("pe", lambda e, mo=mo, kc=kc: e.matmul(q.ap()[:, 0:n], lhsT=gluw.ap()[:, kc, mo * 128:(mo + 1) * 128], rhs=yg.ap()[:, kc, :n], start=(kc == 0), stop=(kc == 1)), reads=[gluw, yg], writes=[q])
            p.op("act", lambda e, mo=mo: e.activation(out=tb.ap()[:, :n], in_=q.ap()[:, 0:n], func=AF.Sigmoid, bias=s5d.ap()[:, 1, mo:mo + 1]), reads=[q, s5d], writes=[tb])
            p.op("dve", lambda e, mo=mo: e.tensor_tensor(out=s5y.ap()[:, mo, :n], in0=yg.ap()[:, mo, :n], in1=tb.ap()[:, :n], op=ALU.mult), reads=[yg, tb], writes=[s5y])
        for h in range(4):
            x = ta.ap()[:96, :n]
            p.op("dve", lambda e, h=h: e.tensor_tensor(out=x, in0=yml.ap()[:, 0, h, :n], in1=yml.ap()[:, 1, h, :n], op=ALU.add), reads=[yml], writes=[ta])
            p.op("act", lambda e, h=h: e.activation(out=tb.ap()[:96, :n], in_=zg.ap()[:96, h, :n], func=AF.Sigmoid), reads=[zg], writes=[tb])
            p.op("dve", lambda e: e.tensor_tensor(out=x, in0=x, in1=tb.ap()[:96, :n], op=ALU.mult), reads=[ta, tb], writes=[ta])
            std_part(x, 96, n, 1e-5, mlg.ap()[:, h:h + 1], mly.ap()[:, h, :n], mly)
        for dc in range(8):
            q = PS()
            for h in range(6):
                p.op("pe", lambda e, h=h, dc=dc: e.matmul(q.ap()[:, 0:n], lhsT=uprw.ap()[:, h, dc * 128:(dc + 1) * 128], rhs=rwy.ap()[:, h, :n], start=(h == 0), stop=(h == 5)), reads=[uprw, rwy], writes=[q])
            p.op("act", lambda e, dc=dc: e.activation(out=tb.ap()[:, :n], in_=zg.ap()[:, 5 + dc, :n], func=AF.Sigmoid, bias=brb.ap()[:, dc:dc + 1]), reads=[zg, brb], writes=[tb])
            p.op("dve", lambda e: e.tensor_tensor(out=ta.ap()[:, :n], in0=q.ap()[:, 0:n], in1=tb.ap()[:, :n], op=ALU.mult), reads=[q, tb], writes=[ta])
            q = PS()
            for h in range(2):
                p.op("pe", lambda e, h=h, dc=dc: e.matmul(q.ap()[:, 0:n], lhsT=ups5.ap()[:, h, dc * 128:(dc + 1) * 128], rhs=s5y.ap()[:, h, :n], start=(h == 0), stop=(h == 1)), reads=[ups5, s5y], writes=[q])
            p.op("act", lambda e, dc=dc: e.activation(out=tb.ap()[:, :n], in_=zg.ap()[:, 13 + dc, :n], func=AF.Sigmoid, bias=brb.ap()[:, 8 + dc:9 + dc]), reads=[zg, brb], writes=[tb])
            p.op("dve", lambda e: e.tensor_tensor(out=tc.ap()[:, :n], in0=q.ap()[:, 0:n], in1=tb.ap()[:, :n], op=ALU.mult), reads=[q, tb], writes=[tc])
            p.op("dve", lambda e: e.tensor_tensor(out=ta.ap()[:, :n], in0=ta.ap()[:, :n], in1=tc.ap()[:, :n], op=ALU.add), reads=[ta, tc], writes=[ta])
            q = PS()
            for h in range(4):
                p.op("pe", lambda e, h=h, dc=dc: e.matmul(q.ap()[:, 0:n], lhsT=upml.ap()[:, h, dc * 128:(dc + 1) * 128], rhs=mly.ap()[:, h, :n], start=(h == 0), stop=(h == 3)), reads=[upml, mly], writes=[q])
            p.op("act", lambda e, dc=dc: e.activation(out=tb.ap()[:, :n], in_=zg.ap()[:, 21 + dc, :n], func=AF.Sigmoid, bias=brb.ap()[:, 16 + dc:17 + dc]), reads=[zg, brb], writes=[tb])
            p.op("dve", lambda e: e.tensor_tensor(out=tc.ap()[:, :n], in0=q.ap()[:, 0:n], in1=tb.ap()[:, :n], op=ALU.mult), reads=[q, tb], writes=[tc])
            p.op("dve", lambda e, dc=dc: e.tensor_tensor(out=ym.ap()[:, dc, :n], in0=ta.ap()[:, :n], in1=tc.ap()[:, :n], op=ALU.add), reads=[ta, tc], writes=[ym])
        p.op("act", lambda e: e.mul(out=x_sb.ap()[:, :, :n], in_=x_sb.ap()[:, :, :n], mul=ALPHA), reads=[x_sb], writes=[x_sb])
        for dc in range(8):
            q = PS()
            for kc in range(8):
                p.op("pe", lambda e, kc=kc, dc=dc: e.matmul(q.ap()[:, 0:n], lhsT=wout.ap()[:, kc, dc * 128:(dc + 1) * 128], rhs=ym.ap()[:, kc, :n], start=(kc == 0), stop=(kc == 7)), reads=[wout, ym], writes=[q])
            p.op("dve", lambda e, dc=dc: e.scalar_tensor_tensor(out=y_sb.ap()[:, dc, :n], in0=q.ap()[:, 0:n], scalar=mod.ap()[:, 40 + dc, wh:wh + 1], in1=x_sb.ap()[:, dc, :n], op0=ALU.mult, op1=ALU.add), reads=[q, mod, x_sb], writes=[y_sb])
        ln_feature_major(p, y_sb, ysq, n, ones, ps1, ps2, mean, rstd, g_sb, b_sb, ysq)
        p.dma("sp", ov[:, :, sl], ysq.ap()[:, :, :n], reads=[ysq], writes=[oT])


MIX_KEYS = (("cw", [128, NCONV, 9]), ("rwp", [64, 5, 6]), ("wa_up", [128, 384]), ("glb", [16, 1]), ("sel", [16, 8, 96]),
            ("s5p", [128, 3, 8]), ("bz", [128, 16, 128]), ("cz", [128, 16, 128]))
MRG_KEYS = (("gn", [64, 2, 6]), ("g_up", [128, 384]), ("s5d", [128, 2, 2]), ("glu_w", [128, 2, 256]), ("mlg", [96, 4]),
            ("up_rw", [64, 6, 1024]), ("up_s5", [128, 2, 1024]), ("up_ml", [96, 4, 1024]), ("brb", [128, 24]),
            ("w_out", [128, 8, 1024]))
NUSED = 4


def set_sizes(nctx, nlat):
    global NLAT, NCTX, NT, TILES
    NLAT, NCTX = nlat, nctx
    NT = NLAT + NCTX
    TILES = [(0, NCTX, 1)] + [(NCTX + i * TT, TT, 0) for i in range(NLAT // TT)]


def build_fused():
    p = Prog()
    NS = NT
    din = lambda n, sh: p.dram(n, sh, F32, "in")
    xT0 = din("xT0", [8, 128, NS])
    cvec = din("cvec", [128, 8, 2])
    ada_w = din("ada_w", [DEPTH, D, 9 * D])
    ada_b = din("ada_b_l", [DEPTH, 128, 72])
    ln_g = din("ln_g_l", [DEPTH, 3, 128, 8])
    ln_b = din("ln_b_l", [DEPTH, 3, 128, 8])
    wg = din("ffn_w_gate", [DEPTH, 2, D, DFF])
    wu = din("ffn_w_up", [DEPTH, 2, D, DFF])
    wd = din("ffn_w_down", [DEPTH, 2, DFF, D])
    w_in = din("w_in", [DEPTH, D, INW])
    cst = din("cst", [128, 5, 128])
    ri = din("ri", [128, 129])
    mixp = {}
    for i in range(DEPTH):
        for d in range(2):
            mixp[(i, d)] = {k: din(f"m{i}{d}_{k}", sh).ap() for k, sh in MIX_KEYS}
            mixp[(i, d)]["cst"] = cst.ap()
            mixp[(i, d)]["ri"] = ri.ap()
    mrgp = {i: {k: din(f"g{i}_{k}", sh).ap() for k, sh in MRG_KEYS} for i in range(DEPTH)}
    oT = p.dram("oT", [8, 128, NS], F32, "out")
    tmp = lambda n, sh: p.dram(n, sh, F32, "tmp")
    S1 = tmp("S1", [8, 128, NS])
    S2 = tmp("S2", [8, 128, NS])
    zall = tmp("zall", [NSCH, 128, NS])
    zallb = tmp("zallb", [NZALL - NSCH, 128, NS])
    modT = [tmp(f"modT{i}", [128, 72, 2]) for i in range(DEPTH)]
    outs = [(tmp(f"o_rw{d}", [6, 64, NS]), tmp(f"o_bn{d}", [6, 64, NS]), tmp(f"o_s5{d}", [2, 128, NS]),
             tmp(f"o_ml{d}", [4, 96, NS])) for d in range(2)]
    chain = [(xT0, S1, S1, S2, S1), (S1, S2, S2, S1, oT)]
    for i in range(DEPTH):
        a_src, a_dst, m_src, m_dst, f_dst = chain[i]
        with p.phase():
            emit_mod(p, cvec, ada_w.ap()[i], ada_b.ap()[i], modT[i])
        with p.phase():
            emit_ffn(p, 0, a_src, a_dst, modT[i], ln_g.ap()[i, 0], ln_b.ap()[i, 0], wg.ap()[i, 0], wu.ap()[i, 0], wd.ap()[i, 0])
        with p.phase():
            emit_inproj(p, a_dst, modT[i], w_in.ap()[i], zall, zallb)
        for d in range(2):
            with p.phase():
                emit_mixer(p, zall, mixp[(i, d)], outs[d], rev=(d == 1), nctx=NCTX, nlat=NLAT)
        with p.phase():
            emit_merge(p, m_src, m_dst, modT[i], ln_g.ap()[i, 1], ln_b.ap()[i, 1], outs, zall, zallb, mrgp[i])
        with p.phase():
            emit_ffn(p, 6, m_dst, f_dst, modT[i], ln_g.ap()[i, 2], ln_b.ap()[i, 2], wg.ap()[i, 1], wu.ap()[i, 1], wd.ap()[i, 1])
    p.finish()
    return p


def merge_params(P, i):
    m = {}
    m["gn"] = np.ascontiguousarray(np.stack([P["rw_gn_g"][i].reshape(6, 64).T, P["rw_gn_b"][i].reshape(6, 64).T], 1))
    m["g_up"] = P["rw_g_up"][i]
    m["s5d"] = np.ascontiguousarray(np.stack([P["s5_d"][i].reshape(2, 128).T, P["s5_glu_b"][i].reshape(2, 128).T], 1))
    m["glu_w"] = np.ascontiguousarray(P["s5_glu_w"][i].reshape(2, 128, 256).transpose(1, 0, 2))
    m["mlg"] = np.ascontiguousarray(P["ml_norm_g"][i].reshape(4, 96).T)
    m["up_rw"] = np.ascontiguousarray(P["up_rw"][i].reshape(6, 64, 1024).transpose(1, 0, 2))
    m["up_s5"] = np.ascontiguousarray(P["up_s5"][i].reshape(2, 128, 1024).transpose(1, 0, 2))
    m["up_ml"] = np.ascontiguousarray(P["up_ml"][i].reshape(4, 96, 1024).transpose(1, 0, 2))
    m["brb"] = np.ascontiguousarray(P["br_gate_b"][i].reshape(24, 128).T)
    m["w_out"] = np.ascontiguousarray(P["w_out"][i].reshape(8, 128, 1024).transpose(1, 0, 2))
    return m


_PROG = []


def kernel(**inputs):
    P = {k: np.asarray(v, dtype=np.float32) for k, v in inputs.items()}
    x, c, ctx, c_ctx = P["x"], P["c"], P["ctx"], P["c_ctx"]
    B, SEQ, _ = x.shape
    if not _PROG:
        set_sizes(ctx.shape[1], SEQ)
        _PROG.append(build_fused())
    prog = _PROG[0]
    NUSED = B
    shared = {
        "ada_w": P["ada_w"],
        "ada_b_l": np.ascontiguousarray(P["ada_b"].reshape(DEPTH, 72, 128).transpose(0, 2, 1)),
        "ln_g_l": np.ascontiguousarray(P["ln_g"].reshape(DEPTH, 3, 8, 128).transpose(0, 1, 3, 2)),
        "ln_b_l": np.ascontiguousarray(P["ln_b"].reshape(DEPTH, 3, 8, 128).transpose(0, 1, 3, 2)),
        "ffn_w_gate": P["ffn_w_gate"], "ffn_w_up": P["ffn_w_up"], "ffn_w_down": P["ffn_w_down"], "w_in": P["w_in"],
    }
    cst, ri = mixer_consts()
    shared["cst"] = cst
    shared["ri"] = ri
    for i in range(DEPTH):
        for d in range(2):
            mp = mixer_params(P, i, d)
            for k, _ in MIX_KEYS:
                shared[f"m{i}{d}_{k}"] = np.ascontiguousarray(mp[k], np.float32)
        gp = merge_params(P, i)
        for k, _ in MRG_KEYS:
            shared[f"g{i}_{k}"] = np.ascontiguousarray(gp[k], np.float32)
    ims = []
    for cid in range(NUSED):
        b = cid % B
        m = dict(shared)
        xx = np.concatenate([ctx[b], x[b]], 0)
        m["xT0"] = np.ascontiguousarray(xx.T).reshape(8, 128, NT)
        m["cvec"] = np.ascontiguousarray(np.stack([c[b], c_ctx], -1).reshape(8, 128, 2).transpose(1, 0, 2))
        ims.append(m)
    res = run_bass_kernel_spmd(prog.nc, ims, core_ids=list(range(NUSED)))
    out = np.empty((B, SEQ, D), np.float32)
    for b in range(B):
        out[b] = res.results[b]["oT"].reshape(D, NT).T[NCTX:]
    return out
```

```python
import numpy as np
from contextlib import ExitStack
import concourse.bass as bass
import concourse.mybir as mybir
from concourse.bass_utils import run_bass_kernel_spmd

F32 = mybir.dt.float32
BF16 = mybir.dt.bfloat16
AF = mybir.ActivationFunctionType
ALU = mybir.AluOpType
AX = mybir.AxisListType


class Reg:
    __slots__ = ("name", "w", "r")

    def __init__(self, name):
        self.name = name
        self.w = None
        self.r = []


class Buf:
    def __init__(self, t, name, nreg=1):
        self.t = t
        self.name = name
        self.regs = [Reg(f"{name}.{i}") for i in range(nreg)]

    def ap(self):
        return self.t.ap() if hasattr(self.t, "ap") and not isinstance(self.t, bass.AP) else self.t

    def __getitem__(self, idx):
        return self.ap()[idx]

    def r(self, i):
        return self.regs[i]


NDMA_SEM = 8
FUSE_WAIT = True


class Prog:
    def __init__(self):
        self.nc = bass.Bass("TRN2", target_bir_lowering=False)
        nc = self.nc
        self.stack = ExitStack()
        self.eng = {"pe": nc.tensor, "dve": nc.vector, "act": nc.scalar, "pool": nc.gpsimd, "sp": nc.sync}
        self.sems = {}
        self.semval = {}
        for e in self.eng:
            self.sems[e] = self.stack.enter_context(nc.semaphore(f"s_{e}"))
            self.semval[e] = 0
        self.dq = {}
        for q in ("sp", "act", "pool"):
            lst = []
            for i in range(NDMA_SEM):
                k = f"d_{q}{i}"
                self.sems[k] = self.stack.enter_context(nc.semaphore(k))
                self.semval[k] = 0
                lst.append(k)
            self.dq[q] = [lst, 0]
        self.waited = {e: {} for e in self.eng}
        self.out_waits = []
        self.ninst = {e: 0 for e in self.eng}
        self.pstack = None
        self.pidx = 0

    def barrier(self):
        for e in self.eng:
            for k, v in self.semval.items():
                if v > 0 and k != e:
                    self._wait(e, k, v)

    def phase(self):
        prog = self

        class _Ph:
            def __enter__(self_):
                prog.pidx += 1
                prog.pstack = ExitStack()
                return prog

            def __exit__(self_, *a):
                prog.barrier()
                prog.pstack.close()
                prog.pstack = None
                return False
        return _Ph()

    def dram(self, name, shape, dtype, kind):
        k = {"in": "ExternalInput", "out": "ExternalOutput", "tmp": "Internal"}[kind]
        t = self.nc.dram_tensor(name, list(shape), dtype, kind=k)
        b = Buf(t, name)
        b.kind = kind
        return b

    def tile(self, shape, dtype, name, nreg=1):
        st = self.pstack if self.pstack is not None else self.stack
        name = f"p{self.pidx}_{name}"
        t = st.enter_context(self.nc.sbuf_tensor(name, list(shape), dtype))
        return Buf(t, name, nreg)

    def psum(self, shape, dtype, name, nreg=1):
        st = self.pstack if self.pstack is not None else self.stack
        name = f"p{self.pidx}_{name}"
        t = st.enter_context(self.nc.psum_tensor(name, list(shape), dtype))
        return Buf(t, name, nreg)

    @staticmethod
    def _regs(lst):
        out = []
        for x in lst or []:
            if isinstance(x, Buf):
                out.extend(x.regs)
            elif isinstance(x, Reg):
                out.append(x)
            elif isinstance(x, tuple):
                out.append(x[0].regs[x[1]])
            else:
                raise TypeError(x)
        return out

    def _wait(self, e, key, val):
        if key is None:
            return
        cur = self.waited[e].get(key, 0)
        if cur >= val:
            return
        self.waited[e][key] = val
        self.eng[e].wait_ge(self.sems[key], val)
        self.ninst[e] += 1

    def _deps(self, e, reads, writes, defer=False):
        need = {}
        for r in reads:
            if r.w is not None:
                need[r.w[0]] = max(need.get(r.w[0], 0), r.w[1])
        for r in writes:
            if r.w is not None:
                need[r.w[0]] = max(need.get(r.w[0], 0), r.w[1])
            for (k, v) in r.r:
                need[k] = max(need.get(k, 0), v)
        todo = [(k, v) for k, v in need.items() if self.waited[e].get(k, 0) < v]
        last = None
        if defer and FUSE_WAIT and todo:
            last = todo.pop()
        for k, v in todo:
            self._wait(e, k, v)
        return last

    def _attach(self, e, inst, last):
        if last is not None:
            k, v = last
            self.waited[e][k] = v
            inst._wait_ge(self.sems[k], v)

    def _mark(self, key, val, reads, writes):
        for r in reads:
            r.r.append((key, val))
            if len(r.r) > 12:
                d = {}
                for (k, v) in r.r:
                    d[k] = max(d.get(k, 0), v)
                r.r = list(d.items())
        for r in writes:
            r.w = (key, val)
            r.r = []

    def op(self, e, fn, reads=None, writes=None):
        reads = self._regs(reads)
        writes = self._regs(writes)
        last = self._deps(e, reads, writes, defer=True)
        inst = fn(self.eng[e])
        self._attach(e, inst, last)
        self.semval[e] += 1
        inst.then_inc(self.sems[e], 1)
        self.ninst[e] += 1
        self._mark(e, self.semval[e], reads, writes)
        return inst

    def dma(self, q, out, in_, reads=None, writes=None, **kw):
        rbufs = reads or []
        wbufs = writes or []
        reads = self._regs(reads)
        writes = self._regs(writes)
        self._deps(q, reads, writes)
        lst, i = self.dq[q]
        key = lst[i % NDMA_SEM]
        self.dq[q][1] = i + 1
        self._wait(q, key, self.semval[key])
        inst = self.eng[q].dma_start(out=out, in_=in_, **kw)
        self.semval[key] += 16
        inst.then_inc(self.sems[key], 16)
        self.ninst[q] += 1
        self._mark(key, self.semval[key], reads, writes)
        for b in wbufs:
            if isinstance(b, Buf) and getattr(b, "kind", None) == "out":
                self.out_waits.append((key, self.semval[key]))
        return inst

    def finish(self):
        d = {}
        for k, v in self.out_waits:
            d[k] = max(d.get(k, 0), v)
        for k, v in d.items():
            self._wait("sp", k, v)
        for e in self.eng:
            if e != "sp" and self.semval[e] > 0:
                self._wait("sp", e, self.semval[e])


D = 1024
DFF = 2816
NFC = 22
DEPTH = 2
ALPHA = (2.0 * DEPTH) ** 0.25
LN_EPS = 1e-5
NLAT = 8192
NCTX = 256
NT = NLAT + NCTX
TT = 256
TILES = [(0, NCTX, 1)] + [(NCTX + i * TT, TT, 0) for i in range(NLAT // TT)]
NCORES = 8
INW = 6288


def make_ones(p, n=128, name="ones"):
    t = p.tile([128, n], F32, name)
    p.op("pool", lambda e: e.memset(t.ap(), 1.0), writes=[t])
    return t


def load_weight_bf16(p, dst, dst_view, src_view, stage, i, shape_free):
    st = stage[i % len(stage)]
    sv = st.ap()[:, :shape_free] if isinstance(shape_free, int) else shape_free(st.ap())
    q = ("sp", "act")[i % 2]
    p.dma(q, sv, src_view, writes=[st])
    ce = ("pool", "dve", "act")[i % 3]
    if ce == "act":
        p.op("act", lambda e: e.copy(out=dst_view, in_=sv), reads=[st], writes=[dst])
    else:
        p.op(ce, lambda e: e.tensor_copy(out=dst_view, in_=sv), reads=[st], writes=[dst])


def emit_mod(p, cvec, adaw_ap, adab_ap, out):
    cs = p.tile([128, 8, 2], F32, "cs")
    ab = p.tile([128, 72], F32, "ab")
    mo = p.tile([128, 72, 2], F32, "mo")
    pan = [p.tile([128, 8, 1024], F32, f"pan{i}") for i in range(2)]
    ps = [p.psum([128, 512], F32, f"ps{i}") for i in range(2)]
    p.dma("sp", cs.ap(), cvec.ap(), writes=[cs])
    p.dma("sp", ab.ap(), adab_ap, writes=[ab])
    p.op("act", lambda e: e.activation(out=cs.ap(), in_=cs.ap(), func=AF.Silu), reads=[cs], writes=[cs])
    awv = adaw_ap.rearrange("(kc p) f -> p kc f", p=128)
    for j in range(9):
        pn = pan[j % 2]
        p.dma(("sp", "act")[j % 2], pn.ap(), awv[:, :, j * 1024:(j + 1) * 1024], writes=[pn])
        for dc in range(8):
            ch = j * 8 + dc
            pp = ps[ch % 2]
            for kc in range(8):
                p.op("pe", lambda e, kc=kc, dc=dc, pn=pn, pp=pp: e.matmul(
                    pp.ap()[:, 0:2], lhsT=pn.ap()[:, kc, dc * 128:(dc + 1) * 128], rhs=cs.ap()[:, kc, :],
                    start=(kc == 0), stop=(kc == 7)), reads=[pn, cs], writes=[pp])
            p.op("dve", lambda e, ch=ch, pp=pp: e.tensor_scalar(
                out=mo.ap()[:, ch, :], in0=pp.ap()[:, 0:2], scalar1=ab.ap()[:, ch:ch + 1], scalar2=None,
                op0=ALU.add), reads=[pp, ab], writes=[mo])
    p.dma("sp", out.ap(), mo.ap(), reads=[mo], writes=[out])


def ln_feature_major(p, y, ysq, n, ones, ps1, ps2, mean, rstd, g, b, dst, dst_reads=None):
    p.op("act", lambda e: e.activation(out=ysq.ap()[:, :, :n], in_=y.ap()[:, :, :n], func=AF.Square),
         reads=[y], writes=[ysq])
    for dc in range(8):
        p.op("pe", lambda e, dc=dc: e.matmul(ps1.ap()[:, :n], lhsT=ones.ap(), rhs=y.ap()[:, dc, :n],
                                              start=(dc == 0), stop=(dc == 7)), reads=[ones, y], writes=[ps1])
    for dc in range(8):
        p.op("pe", lambda e, dc=dc: e.matmul(ps2.ap()[:, :n], lhsT=ones.ap(), rhs=ysq.ap()[:, dc, :n],
                                              start=(dc == 0), stop=(dc == 7)), reads=[ones, ysq], writes=[ps2])
    p.op("dve", lambda e: e.tensor_scalar(out=mean.ap()[:, :n], in0=ps1.ap()[:, :n], scalar1=1.0 / D, scalar2=None,
                                          op0=ALU.mult), reads=[ps1], writes=[mean])
    p.op("dve", lambda e: e.tensor_tensor(out=rstd.ap()[:, :n], in0=mean.ap()[:, :n], in1=mean.ap()[:, :n],
                                          op=ALU.mult), reads=[mean], writes=[rstd])
    p.op("dve", lambda e: e.scalar_tensor_tensor(out=rstd.ap()[:, :n], in0=ps2.ap()[:, :n], scalar=1.0 / D,
                                                 in1=rstd.ap()[:, :n], op0=ALU.mult, op1=ALU.subtract),
         reads=[ps2, rstd], writes=[rstd])
    p.op("dve", lambda e: e.tensor_scalar(out=rstd.ap()[:, :n], in0=rstd.ap()[:, :n], scalar1=LN_EPS, scalar2=None,
                                          op0=ALU.add), reads=[rstd], writes=[rstd])
    p.op("act", lambda e: e.activation(out=rstd.ap()[:, :n], in_=rstd.ap()[:, :n], func=AF.Sqrt),
         reads=[rstd], writes=[rstd])
    p.op("dve", lambda e: e.reciprocal(out=rstd.ap()[:, :n], in_=rstd.ap()[:, :n]), reads=[rstd], writes=[rstd])
    for dc in range(8):
        eng = ("dve", "pool")[dc % 2]
        p.op(eng, lambda e, dc=dc: e.tensor_tensor(out=y.ap()[:, dc, :n], in0=y.ap()[:, dc, :n],
                                                   in1=mean.ap()[:, :n], op=ALU.subtract),
             reads=[y, mean], writes=[y])
        p.op(eng, lambda e, dc=dc: e.tensor_tensor(out=y.ap()[:, dc, :n], in0=y.ap()[:, dc, :n],
                                                   in1=rstd.ap()[:, :n], op=ALU.mult),
             reads=[y, rstd], writes=[y])
        p.op(eng, lambda e, dc=dc: e.tensor_scalar(out=dst.ap()[:, dc, :n], in0=y.ap()[:, dc, :n],
                                                   scalar1=g.ap()[:, dc:dc + 1], scalar2=b.ap()[:, dc:dc + 1],
                                                   op0=ALU.mult, op1=ALU.add),
             reads=[y, g, b], writes=[dst])


def emit_ffn(p, j0, xT, oT, modT, lng_ap, lnb_ap, wg_ap, wu_ap, wd_ap):

    wg_sb = p.tile([128, 8, DFF], BF16, "wg_sb")
    wu_sb = p.tile([128, 8, DFF], BF16, "wu_sb")
    wd_sb = p.tile([128, NFC, D], BF16, "wd_sb")
    stage = [p.tile([128, DFF // 2], F32, f"stage{i}") for i in range(2)]
    mod = p.tile([128, 72, 2], F32, "mod")
    g_sb = p.tile([128, 8], F32, "g_sb")
    b_sb = p.tile([128, 8], F32, "b_sb")
    ops_ = p.tile([128, 8, 2], F32, "onepsc")
    hg = p.tile([128, 8, 2], F32, "hg")
    ones = make_ones(p)
    p.dma("sp", mod.ap(), modT.ap(), reads=[modT], writes=[mod])
    p.dma("sp", g_sb.ap(), lng_ap, writes=[g_sb])
    p.dma("sp", b_sb.ap(), lnb_ap, writes=[b_sb])
    p.op("dve", lambda e: e.tensor_scalar(out=ops_.ap(), in0=mod.ap()[:, (j0 + 1) * 8:(j0 + 2) * 8, :], scalar1=1.0,
                                          scalar2=None, op0=ALU.add), reads=[mod], writes=[ops_])
    p.op("dve", lambda e: e.tensor_scalar(out=hg.ap(), in0=mod.ap()[:, (j0 + 2) * 8:(j0 + 3) * 8, :], scalar1=0.5,
                                          scalar2=None, op0=ALU.mult), reads=[mod], writes=[hg])
    k = 0
    wgv = wg_ap.rearrange("(kc p) f -> kc p f", p=128)
    wuv = wu_ap.rearrange("(kc p) f -> kc p f", p=128)
    wdv = wd_ap.rearrange("(fc p) d -> fc p d", p=128)
    HF = DFF // 2
    for kc in range(8):
        for hh in range(2):
            load_weight_bf16(p, wg_sb, wg_sb.ap()[:, kc, hh * HF:(hh + 1) * HF], wgv[kc][:, hh * HF:(hh + 1) * HF], stage, k, HF); k += 1
            load_weight_bf16(p, wu_sb, wu_sb.ap()[:, kc, hh * HF:(hh + 1) * HF], wuv[kc][:, hh * HF:(hh + 1) * HF], stage, k, HF); k += 1
    for fc in range(NFC):
        load_weight_bf16(p, wd_sb, wd_sb.ap()[:, fc, :], wdv[fc], stage, k, D); k += 1

    x_sb = [p.tile([128, 8, TT], F32, f"x_sb{i}") for i in range(2)]
    u_sb = p.tile([128, 8, TT], BF16, "u_sb")
    a_sb = [p.tile([128, NFC, TT], BF16, "a_sb0")]
    sg = [p.tile([128, TT], F32, f"sg{i}") for i in range(2)]
    y_sb = p.tile([128, 8, TT], F32, "y_sb")
    ysq = p.tile([128, 8, TT], F32, "ysq")
    mean = p.tile([128, TT], F32, "mean")
    rstd = p.tile([128, TT], F32, "rstd")
    psg = [p.psum([128, 512], F32, f"psg{i}") for i in range(2)]
    psu = [p.psum([128, 512], F32, f"psu{i}") for i in range(2)]
    psd = [p.psum([128, 512], F32, f"psd{i}") for i in range(2)]
    ps1 = p.psum([128, 512], F32, "ps1")
    ps2 = p.psum([128, 512], F32, "ps2")
    xv = xT.ap().rearrange("c p t -> p c t")
    ov = oT.ap().rearrange("c p t -> p c t")

    for ti, (t0, n, wh) in enumerate(TILES):
        xs = x_sb[ti % 2]
        asb = a_sb[0]
        osb = ysq
        p.dma(("sp", "act")[ti % 2], xs.ap()[:, :, :n], xv[:, :, t0:t0 + n], reads=[xT], writes=[xs])
        for kc in range(8):
            p.op("dve", lambda e, kc=kc: e.tensor_scalar(
                out=u_sb.ap()[:, kc, :n], in0=xs.ap()[:, kc, :n], scalar1=ops_.ap()[:, kc, wh:wh + 1],
                scalar2=mod.ap()[:, j0 * 8 + kc, wh:wh + 1], op0=ALU.mult, op1=ALU.add),
                reads=[xs, ops_, mod], writes=[u_sb])
        for fc in range(NFC):
            pg = psg[fc % 2]
            pu = psu[fc % 2]
            for kc in range(8):
                p.op("pe", lambda e, kc=kc, fc=fc, pg=pg: e.matmul(
                    pg.ap()[:, :n], lhsT=wg_sb.ap()[:, kc, fc * 128:(fc + 1) * 128], rhs=u_sb.ap()[:, kc, :n],
                    start=(kc == 0), stop=(kc == 7)), reads=[wg_sb, u_sb], writes=[pg])
            for kc in range(8):
                p.op("pe", lambda e, kc=kc, fc=fc, pu=pu: e.matmul(
                    pu.ap()[:, :n], lhsT=wu_sb.ap()[:, kc, fc * 128:(fc + 1) * 128], rhs=u_sb.ap()[:, kc, :n],
                    start=(kc == 0), stop=(kc == 7)), reads=[wu_sb, u_sb], writes=[pu])
            s = sg[fc % 2]
            p.op("act", lambda e, pg=pg, s=s: e.activation(out=s.ap()[:, :n], in_=pg.ap()[:, :n], func=AF.Silu),
                 reads=[pg], writes=[s])
            p.op("dve", lambda e, pu=pu, s=s, fc=fc: e.tensor_tensor(
                out=asb.ap()[:, fc, :n], in0=pu.ap()[:, :n], in1=s.ap()[:, :n], op=ALU.mult),
                reads=[pu, s], writes=[asb])
        p.op("act", lambda e: e.mul(out=xs.ap()[:, :, :n], in_=xs.ap()[:, :, :n], mul=ALPHA), reads=[xs], writes=[xs])
        for dc in range(8):
            pd = psd[dc % 2]
            for fc in range(NFC):
                p.op("pe", lambda e, dc=dc, fc=fc, pd=pd: e.matmul(
                    pd.ap()[:, :n], lhsT=wd_sb.ap()[:, fc, dc * 128:(dc + 1) * 128], rhs=asb.ap()[:, fc, :n],
                    start=(fc == 0), stop=(fc == NFC - 1)), reads=[wd_sb, asb], writes=[pd])
            p.op("dve", lambda e, dc=dc, pd=pd: e.scalar_tensor_tensor(
                out=y_sb.ap()[:, dc, :n], in0=pd.ap()[:, :n], scalar=hg.ap()[:, dc, wh:wh + 1],
                in1=xs.ap()[:, dc, :n], op0=ALU.mult, op1=ALU.add), reads=[pd, hg, xs], writes=[y_sb])
        ln_feature_major(p, y_sb, ysq, n, ones, ps1, ps2, mean, rstd, g_sb, b_sb, osb)
        p.dma(("sp", "act")[(ti + 1) % 2], ov[:, :, t0:t0 + n], osb.ap()[:, :, :n], reads=[osb], writes=[oT])


def emit_inproj(p, xT, modT, win_ap, zall, zallb):
    w_sb = p.tile([128, 8, INW], BF16, "w_sb")
    QW = INW // 4
    stage = [p.tile([128, QW], F32, f"stage{i}") for i in range(2)]
    mod = p.tile([128, 72, 2], F32, "mod")
    ops_ = p.tile([128, 8, 2], F32, "onepsc")
    p.dma("sp", mod.ap(), modT.ap(), reads=[modT], writes=[mod])
    p.op("dve", lambda e: e.tensor_scalar(out=ops_.ap(), in0=mod.ap()[:, 32:40, :], scalar1=1.0, scalar2=None,
                                          op0=ALU.add), reads=[mod], writes=[ops_])
    wv = win_ap.rearrange("(kc p) f -> kc p f", p=128)
    k = 0
    for kc in range(8):
        for qq in range(4):
            load_weight_bf16(p, w_sb, w_sb.ap()[:, kc, qq * QW:(qq + 1) * QW], wv[kc][:, qq * QW:(qq + 1) * QW],
                             stage, k, QW); k += 1
    x_sb = [p.tile([128, 8, TT], F32, "x_sb0")]
    u_sb = [p.tile([128, 8, TT], BF16, "u_sb0")]
    zbig = p.tile([128, NZALL, TT], F32, "zbig")
    p.op("pool", lambda e: e.memset(zbig.ap(), 0.0), writes=[zbig])
    ps = [p.psum([128, 512], F32, f"ps{i}") for i in range(4)]
    xv = xT.ap().rearrange("c p t -> p c t")
    zv = zall.ap().rearrange("c p t -> p c t")
    zvb = zallb.ap().rearrange("c p t -> p c t")
    cnt = 0
    for ti, (t0, n, wh) in enumerate(TILES):
        xs = x_sb[0]
        us = u_sb[0]
        p.dma(("sp", "act")[ti % 2], xs.ap()[:, :, :n], xv[:, :, t0:t0 + n], reads=[xT], writes=[xs])
        for kc in range(8):
            p.op(("dve", "pool")[kc % 2], lambda e, kc=kc: e.tensor_scalar(
                out=us.ap()[:, kc, :n], in0=xs.ap()[:, kc, :n], scalar1=ops_.ap()[:, kc, wh:wh + 1],
                scalar2=mod.ap()[:, 24 + kc, wh:wh + 1], op0=ALU.mult, op1=ALU.add),
                reads=[xs, ops_, mod], writes=[us])
        for ci, (nm, c0, nc_) in enumerate(ZALL):
            pp = ps[cnt % 4]
            for kc in range(8):
                p.op("pe", lambda e, kc=kc, pp=pp, c0=c0, nc_=nc_: e.matmul(
                    pp.ap()[:nc_, :n], lhsT=w_sb.ap()[:, kc, c0:c0 + nc_], rhs=us.ap()[:, kc, :n],
                    start=(kc == 0), stop=(kc == 7)), reads=[w_sb, us], writes=[pp])
            if cnt % 2 == 0:
                p.op("act", lambda e, pp=pp, ci=ci, nc_=nc_: e.copy(out=zbig.ap()[:nc_, ci, :n], in_=pp.ap()[:nc_, :n]),
                     reads=[pp], writes=[zbig])
            else:
                p.op("dve", lambda e, pp=pp, ci=ci, nc_=nc_: e.tensor_copy(out=zbig.ap()[:nc_, ci, :n], in_=pp.ap()[:nc_, :n]),
                     reads=[pp], writes=[zbig])
            cnt += 1
        p.dma(("sp", "act")[(ti + 1) % 2], zv[:, :, t0:t0 + n], zbig.ap()[:, 0:NSCH, :n], reads=[zbig], writes=[zall])
        p.dma(("sp", "act")[ti % 2], zvb[:, :, t0:t0 + n], zbig.ap()[:, NSCH:NZALL, :n], reads=[zbig], writes=[zallb])


SCH = []
for _nm, _c0 in (("rw_r", 0), ("rw_k", 384), ("rw_v", 768)):
    for _i in range(6):
        SCH.append((f"{_nm}{_i}", _c0 + 64 * _i, 64))
for _nm, _c0 in (("ml_q", 1152), ("ml_k", 1536)):
    for _i in range(4):
        SCH.append((f"{_nm}{_i}", _c0 + 96 * _i, 96))
NCONV = len(SCH)
for _i in range(4):
    SCH.append((f"ml_v{_i}", 1920 + 96 * _i, 96))
SCH.append(("ml_gl", 2688, 16))
SCH.append(("s5_u0", 2704, 128))
SCH.append(("s5_u1", 2832, 128))
SCH.append(("wa_dn", 2960, 128))
NSCH = len(SCH)
NOTH = NSCH - NCONV
ZALL = list(SCH)
for _i in range(4):
    ZALL.append((f"ml_o{_i}", 2304 + 96 * _i, 96))
ZALL.append(("g_dn", 3088, 128))
for _i in range(24):
    ZALL.append((f"br{_i}", 3216 + 128 * _i, 128))
NZALL = len(ZALL)
TWO_PI = 2.0 * np.pi


def emit_mixer(p, zs, prm, outs, rev, nctx=NCTX, nlat=NLAT):
    NS = nctx + nlat
    NB = NS // 128
    f32 = F32
    o_rw, o_bn, o_s5, o_ml = outs

    def nat(a, b):
        if not rev:
            return slice(a, b)
        if b <= nctx:
            return slice(nctx - b, nctx - a)
        return slice(nctx + NS - b, nctx + NS - a)

    def T(shape, name, dt=f32):
        return p.tile(shape, dt, "t_" + name)

    cw = T([128, NCONV, 9], "cw"); rwp = T([64, 5, 6], "rwp"); waup = T([128, 384], "waup")
    glb = T([16, 1], "glb"); sel = T([16, 8, 96], "sel"); s5p = T([128, 3, 8], "s5p")
    bz = T([128, 16, 128], "bz"); czc = T([128, 16, 128], "czc"); cst = T([128, 5, 128], "cst")
    ri = T([128, 129], "ri")
    for i, (t, d) in enumerate(((cw, "cw"), (rwp, "rwp"), (waup, "wa_up"), (glb, "glb"), (sel, "sel"), (s5p, "s5p"),
                                (bz, "bz"), (czc, "cz"), (cst, "cst"), (ri, "ri"))):
        p.dma(("sp", "act")[i % 2], t.ap(), prm[d], writes=[t])
    ident = cst.ap()[:, 0, :]
    m_su = cst.ap()[:, 1, :]
    m_sl = cst.ap()[:, 2, :]
    m_iu = cst.ap()[:, 3, :]
    m01 = cst.ap()[:, 4, :]
    ones = make_ones(p)
    mask5 = T([64, 5, 64], "mask5")
    for i, m in enumerate((m_su, m_sl, m_su, m_iu, m_iu)):
        p.op("dve", lambda e, i=i, m=m: e.tensor_copy(out=mask5.ap()[:, i, :], in_=m[0:64, 0:64]), reads=[cst], writes=[mask5])

    pss = [p.psum([128, 512], f32, f"pb{i}") for i in range(4)]
    psb = p.psum([128, 2, 1024], f32, "psbig")
    pcnt = [0]

    def PS():
        pcnt[0] += 1
        return pss[pcnt[0] % 4]

    ecnt = [0]

    def EW():
        ecnt[0] += 1
        return ("dve", "pool")[ecnt[0] % 2]

    def rr(x_ap, n, tmpf, tmpi, reads):
        p.op("dve", lambda e: e.tensor_scalar(out=tmpf, in0=x_ap, scalar1=1.0 / TWO_PI, scalar2=0.5, op0=ALU.mult,
                                              op1=ALU.add), reads=reads, writes=reads)
        p.op("dve", lambda e: e.tensor_copy(out=tmpi, in_=tmpf), reads=reads, writes=reads)
        p.op("dve", lambda e: e.tensor_copy(out=tmpf, in_=tmpi), reads=reads, writes=reads)
        p.op("dve", lambda e: e.scalar_tensor_tensor(out=x_ap, in0=tmpf, scalar=-TWO_PI, in1=x_ap, op0=ALU.mult,
                                                     op1=ALU.add), reads=reads, writes=reads)
        p.op("dve", lambda e: e.tensor_scalar(out=tmpf, in0=x_ap, scalar1=-np.pi, scalar2=TWO_PI, op0=ALU.is_lt,
                                              op1=ALU.mult), reads=reads, writes=reads)
        p.op("dve", lambda e: e.tensor_tensor(out=x_ap, in0=x_ap, in1=tmpf, op=ALU.add), reads=reads, writes=reads)
        p.op("dve", lambda e: e.tensor_scalar(out=tmpf, in0=x_ap, scalar1=np.pi, scalar2=-TWO_PI, op0=ALU.is_gt,
                                              op1=ALU.mult), reads=reads, writes=reads)
        p.op("dve", lambda e: e.tensor_tensor(out=x_ap, in0=x_ap, in1=tmpf, op=ALU.add), reads=reads, writes=reads)
        p.op("dve", lambda e: e.tensor_scalar(out=x_ap, in0=x_ap, scalar1=-3.14159, scalar2=3.14159, op0=ALU.max,
                                              op1=ALU.min), reads=reads, writes=reads)

    sp_ = T([128, 16, 8], "s5small")
    SM = lambda i: sp_.ap()[:, i, :]
    Vr = T([128, 8, 129], "Vr"); Vi = T([128, 8, 129], "Vi"); t5a = T([128, 8, 129], "t5a"); t5b = T([128, 8, 129], "t5b")
    ang, ang2 = Vr, Vi

    class _View:
        def __init__(self, buf, fn):
            self.buf, self.fn = buf, fn

        def ap(self):
            return self.fn(self.buf.ap())
    tf = _View(t5a, lambda a: a.rearrange("p k r -> p (k r)"))
    ti_ = _View(t5b, lambda a: a.rearrange("p k r -> p (k r)").bitcast(mybir.dt.int32))
    Ct = T([128, 8, 129], "Ct"); St = T([128, 8, 129], "St")
    T1re = T([128, 8, 128], "T1re"); T1im = T([128, 8, 128], "T1im"); RHO = T([128, 8, 128], "RHO")
    S5R = [sp_, s5p, Vr, Vi, t5a, t5b, Ct, St, T1re, T1im, RHO, ri, ones]
    lre, lim, ldt = s5p.ap()[:, 0, :], s5p.ap()[:, 1, :], s5p.ap()[:, 2, :]

    def o5(eng, fn):
        p.op(eng, fn, reads=S5R, writes=S5R)

    o5("dve", lambda e: e.tensor_scalar(out=lre, in0=lre, scalar1=-1e-4, scalar2=None, op0=ALU.min))
    o5("act", lambda e: e.activation(out=SM(0), in_=ldt, func=AF.Exp))
    o5("dve", lambda e: e.tensor_tensor(out=SM(1), in0=lre, in1=SM(0), op=ALU.mult))
    o5("act", lambda e: e.activation(out=SM(1), in_=SM(1), func=AF.Exp))
    o5("dve", lambda e: e.tensor_tensor(out=SM(2), in0=lim, in1=SM(0), op=ALU.mult))
    rr(SM(2), 8, tf.ap()[:, 0:8], ti_.ap()[:, 0:8], S5R)
    for k in range(8):
        o5("dve", lambda e, k=k: e.tensor_scalar(out=ang.ap()[:, k, :], in0=ri.ap(), scalar1=sp_.ap()[:, 2, k:k + 1],
                                                 scalar2=None, op0=ALU.mult))
    angf = ang.ap().rearrange("p k r -> p (k r)")
    ang2f = ang2.ap().rearrange("p k r -> p (k r)")
    o5("dve", lambda e: e.tensor_scalar(out=ang2f, in0=angf, scalar1=np.pi / 2, scalar2=None, op0=ALU.add))
    rr(angf, 8 * 129, tf.ap(), ti_.ap(), S5R)
    rr(ang2f, 8 * 129, tf.ap(), ti_.ap(), S5R)
    o5("act", lambda e: e.activation(out=St.ap().rearrange("p k r -> p (k r)"), in_=angf, func=AF.Sin))
    o5("act", lambda e: e.activation(out=Ct.ap().rearrange("p k r -> p (k r)"), in_=ang2f, func=AF.Sin))
    o5("dve", lambda e: e.tensor_tensor(out=SM(3), in0=SM(1), in1=Ct.ap()[:, :, 1], op=ALU.mult))
    o5("dve", lambda e: e.tensor_tensor(out=SM(4), in0=SM(1), in1=St.ap()[:, :, 1], op=ALU.mult))
    o5("dve", lambda e: e.tensor_scalar(out=SM(3), in0=SM(3), scalar1=-1.0, scalar2=None, op0=ALU.add))
    o5("dve", lambda e: e.tensor_tensor(out=SM(5), in0=lre, in1=lre, op=ALU.mult))
    o5("dve", lambda e: e.tensor_tensor(out=SM(6), in0=lim, in1=lim, op=ALU.mult))
    o5("dve", lambda e: e.tensor_tensor(out=SM(5), in0=SM(5), in1=SM(6), op=ALU.add))
    o5("dve", lambda e: e.reciprocal(out=SM(5), in_=SM(5)))
    o5("dve", lambda e: e.tensor_tensor(out=SM(6), in0=SM(3), in1=lre, op=ALU.mult))
    o5("dve", lambda e: e.tensor_tensor(out=SM(7), in0=SM(4), in1=lim, op=ALU.mult))
    o5("dve", lambda e: e.tensor_tensor(out=SM(6), in0=SM(6), in1=SM(7), op=ALU.add))
    o5("dve", lambda e: e.tensor_tensor(out=SM(8), in0=SM(6), in1=SM(5), op=ALU.mult))
    o5("dve", lambda e: e.tensor_tensor(out=SM(6), in0=SM(4), in1=lre, op=ALU.mult))
    o5("dve", lambda e: e.tensor_tensor(out=SM(7), in0=SM(3), in1=lim, op=ALU.mult))
    o5("dve", lambda e: e.tensor_tensor(out=SM(6), in0=SM(6), in1=SM(7), op=ALU.subtract))
    o5("dve", lambda e: e.tensor_tensor(out=SM(9), in0=SM(6), in1=SM(5), op=ALU.mult))
    o5("dve", lambda e: e.tensor_scalar(out=SM(10), in0=SM(8), scalar1=-1.0, scalar2=None, op0=ALU.mult))
    for k in range(8):
        C_ = Ct.ap()[:, k, 0:128]; S_ = St.ap()[:, k, 0:128]
        gre = sp_.ap()[:, 8, k:k + 1]; gim = sp_.ap()[:, 9, k:k + 1]; ngre = sp_.ap()[:, 10, k:k + 1]
        o5("dve", lambda e, k=k, C_=C_, gre=gre: e.tensor_scalar(out=T1re.ap()[:, k, :], in0=C_, scalar1=gre, scalar2=None, op0=ALU.mult))
        o5("dve", lambda e, k=k, S_=S_, gim=gim: e.scalar_tensor_tensor(out=T1re.ap()[:, k, :], in0=S_, scalar=gim, in1=T1re.ap()[:, k, :], op0=ALU.mult, op1=ALU.add))
        o5("dve", lambda e, k=k, C_=C_, gim=gim: e.tensor_scalar(out=T1im.ap()[:, k, :], in0=C_, scalar1=gim, scalar2=None, op0=ALU.mult))
        o5("dve", lambda e, k=k, S_=S_, ngre=ngre: e.scalar_tensor_tensor(out=T1im.ap()[:, k, :], in0=S_, scalar=ngre, in1=T1im.ap()[:, k, :], op0=ALU.mult, op1=ALU.add))
        o5("dve", lambda e, k=k: e.tensor_scalar(out=RHO.ap()[:, k, :], in0=ones.ap(), scalar1=sp_.ap()[:, 1, k:k + 1], scalar2=None, op0=ALU.mult))
    p.op("dve", lambda e: e.tensor_scalar(out=czc.ap()[:, 8:16, :], in0=czc.ap()[:, 8:16, :], scalar1=-1.0, scalar2=None,
                                          op0=ALU.mult), reads=[czc], writes=[czc])
    s5i = T([128, 2, 8], "s5init")
    p.op("pool", lambda e: e.memset(s5i.ap(), 0.0), writes=[s5i])
    Wr = T([128, 8, 128], "Wr"); Wi = T([128, 8, 128], "Wi")
    ys5 = T([128, 2, 128], "ys5")

    def s5_block(t0, zo):
        VrA = Vr.ap()[:, :, 0:128]; ViA = Vi.ap()[:, :, 0:128]; t5aA = t5a.ap()[:, :, 0:128]; t5bA = t5b.ap()[:, :, 0:128]
        bre = psb.ap()[:, 0, :].rearrange("p (k t) -> p k t", k=8)
        bim = psb.ap()[:, 1, :].rearrange("p (k t) -> p k t", k=8)
        for k in range(8):
            u = zo.ap()[:, 5 + k // 4, :]
            p.op("pe", lambda e, k=k, u=u: e.matmul(bre[:, k, :], lhsT=bz.ap()[:, k, :], rhs=u, start=True, stop=True),
                 reads=[bz, zo], writes=[psb])
            p.op("pe", lambda e, k=k, u=u: e.matmul(bim[:, k, :], lhsT=bz.ap()[:, 8 + k, :], rhs=u, start=True, stop=True),
                 reads=[bz, zo], writes=[psb])
        tt = lambda e, o, a, b, op: e.tensor_tensor(out=o, in0=a, in1=b, op=op)
        yield
        p.op("dve", lambda e: tt(e, t5aA, bre, T1re.ap(), ALU.mult), reads=[psb, T1re], writes=[t5a])
        p.op("dve", lambda e: tt(e, t5bA, bim, T1im.ap(), ALU.mult), reads=[psb, T1im], writes=[t5b])
        p.op("pool", lambda e: tt(e, VrA, t5aA, t5bA, ALU.subtract), reads=[t5a, t5b], writes=[Vr])
        p.op("dve", lambda e: tt(e, t5aA, bre, T1im.ap(), ALU.mult), reads=[psb, T1im], writes=[t5a])
        p.op("dve", lambda e: tt(e, t5bA, bim, T1re.ap(), ALU.mult), reads=[psb, T1re], writes=[t5b])
        p.op("pool", lambda e: tt(e, ViA, t5aA, t5bA, ALU.add), reads=[t5a, t5b], writes=[Vi])
        yield
        for k in range(8):
            p.op("dve", lambda e, k=k: e.tensor_tensor_scan(out=Wr.ap()[:, k, :], data0=RHO.ap()[:, k, :], data1=VrA[:, k, :],
                                                            initial=s5i.ap()[:, 0, k:k + 1], op0=ALU.mult, op1=ALU.add),
                 reads=[RHO, Vr, s5i], writes=[Wr])
            p.op("dve", lambda e, k=k: e.tensor_tensor_scan(out=Wi.ap()[:, k, :], data0=RHO.ap()[:, k, :], data1=ViA[:, k, :],
                                                            initial=s5i.ap()[:, 1, k:k + 1], op0=ALU.mult, op1=ALU.add),
                 reads=[RHO, Vi, s5i], writes=[Wi])
        yield
        wr_l = Wr.ap()[:, :, 127]; wi_l = Wi.ap()[:, :, 127]; c128 = Ct.ap()[:, :, 128]; s128 = St.ap()[:, :, 128]
        p.op("pool", lambda e: tt(e, SM(11), wr_l, c128, ALU.mult), reads=[Wr, Ct], writes=[sp_])
        p.op("pool", lambda e: tt(e, SM(12), wi_l, s128, ALU.mult), reads=[Wi, St, sp_], writes=[sp_])
        p.op("pool", lambda e: tt(e, s5i.ap()[:, 0, :], SM(11), SM(12), ALU.subtract), reads=[sp_], writes=[s5i])
        p.op("pool", lambda e: tt(e, SM(11), wr_l, s128, ALU.mult), reads=[Wr, St, sp_], writes=[sp_])
        p.op("pool", lambda e: tt(e, SM(12), wi_l, c128, ALU.mult), reads=[Wi, Ct, sp_], writes=[sp_])
        p.op("pool", lambda e: tt(e, s5i.ap()[:, 1, :], SM(11), SM(12), ALU.add), reads=[sp_], writes=[s5i])
        yield
        C3 = Ct.ap()[:, :, 0:128]; S3 = St.ap()[:, :, 0:128]
        p.op("dve", lambda e: tt(e, t5aA, Wr.ap(), C3, ALU.mult), reads=[Wr, Ct], writes=[t5a])
        p.op("pool", lambda e: tt(e, t5bA, Wi.ap(), S3, ALU.mult), reads=[Wi, St], writes=[t5b])
        p.op("dve", lambda e: tt(e, VrA, t5aA, t5bA, ALU.subtract), reads=[t5a, t5b], writes=[Vr])
        p.op("dve", lambda e: tt(e, t5aA, Wr.ap(), S3, ALU.mult), reads=[Wr, St], writes=[t5a])
        p.op("pool", lambda e: tt(e, t5bA, Wi.ap(), C3, ALU.mult), reads=[Wi, Ct], writes=[t5b])
        p.op("dve", lambda e: tt(e, ViA, t5aA, t5bA, ALU.add), reads=[t5a, t5b], writes=[Vi])
        yield
        py = PS()
        for mt in range(2):
            for i, k in enumerate(range(4 * mt, 4 * mt + 4)):
                p.op("pe", lambda e, k=k, mt=mt, i=i: e.matmul(py.ap()[:, mt * 128:(mt + 1) * 128], lhsT=czc.ap()[:, k, :],
                                                               rhs=VrA[:, k, :], start=(i == 0), stop=False),
                     reads=[czc, Vr], writes=[py])
                p.op("pe", lambda e, k=k, mt=mt, i=i: e.matmul(py.ap()[:, mt * 128:(mt + 1) * 128], lhsT=czc.ap()[:, 8 + k, :],
                                                               rhs=ViA[:, k, :], start=False, stop=(i == 3)),
                     reads=[czc, Vi], writes=[py])
        p.op("act", lambda e: e.copy(out=ys5.ap().rearrange("p m t -> p (m t)"), in_=py.ap()[:, 0:256]), reads=[py], writes=[ys5])
        store(o_s5, "m p t -> p m t", ys5, t0, 128, 2, "sp")
        yield

    Wn = [T([128, NCONV, 384], "Wn0")]
    zo_t = [T([128, NOTH, 128], f"zo{i}") for i in range(2)]
    cz = T([128, NCONV, 128], "cz")
    ctmp = T([128, 128], "ctmp")
    zsv = zs.ap().rearrange("c p t -> p c t")
    HC = 7
    CGR = [(g * HC, min((g + 1) * HC, NCONV)) for g in range((NCONV + HC - 1) // HC)]
    if rev:
        wst = T([128, HC, 384], "wst")
        zst = T([128, NOTH, 128], "zst")
        ost = T([128, 6, 128], "ost")

    def store(obuf, pat, ytile, t0, npart, nh, q):
        dst = obuf.ap()[:, :, nat(t0, t0 + 128)].rearrange(pat)
        if not rev:
            p.dma(q, dst, ytile.ap(), reads=[ytile], writes=[obuf])
        else:
            p.op("pool", lambda e: e.tensor_copy(out=ost.ap()[:npart, :nh, :], in_=ytile.ap()[:, :, ::-1]), reads=[ytile], writes=[ost])
            p.dma(q, dst, ost.ap()[:npart, :nh, :], reads=[ost], writes=[obuf])

    def conv_block(j):
        t0 = j * 128
        W = Wn[0]
        zo = zo_t[j % 2]
        lo_r, hi_r = (0, nctx) if t0 < nctx else (nctx, NS)
        a, b = max(t0 - 128, lo_r), min(t0 + 256, hi_r)
        if a > t0 - 128 or b < t0 + 256:
            p.op("pool", lambda e: e.memset(W.ap(), 0.0), writes=[W])
        wa, wb = a - (t0 - 128), b - (t0 - 128)
        if not rev:
            p.dma("sp", W.ap()[:, :, wa:wb], zsv[:, 0:NCONV, a:b], reads=[zs], writes=[W])
            p.dma("act", zo.ap(), zsv[:, NCONV:NSCH, t0:t0 + 128], reads=[zs], writes=[zo])
        else:
            for hh, (c_lo, c_hi) in enumerate(CGR):
                p.dma(("sp", "act")[hh % 2], wst.ap()[:, 0:c_hi - c_lo, 0:b - a], zsv[:, c_lo:c_hi, nat(a, b)], reads=[zs], writes=[wst])
                p.op(("dve", "pool")[hh % 2], lambda e, c_lo=c_lo, c_hi=c_hi: e.tensor_copy(
                    out=W.ap()[:, c_lo:c_hi, wa:wb], in_=wst.ap()[:, 0:c_hi - c_lo, 0:b - a][:, :, ::-1]), reads=[wst], writes=[W])
            p.dma("act", zst.ap(), zsv[:, NCONV:NSCH, nat(t0, t0 + 128)], reads=[zs], writes=[zst])
            p.op("pool", lambda e: e.tensor_copy(out=zo.ap(), in_=zst.ap()[:, :, ::-1]), reads=[zst], writes=[zo])
        grid = t0 >= nctx
        for ci in range(NCONV):
            eng = EW()
            nch = SCH[ci][2]
            o = cz.ap()[:nch, ci, :]
            wv = W.ap()[:nch, ci, :]
            ctr = 4
            p.op(eng, lambda e, o=o, wv=wv, ci=ci, nch=nch: e.tensor_scalar(
                out=o, in0=wv[:, 128:256], scalar1=cw.ap()[:nch, ci, ctr:ctr + 1], scalar2=None, op0=ALU.mult),
                reads=[W, cw], writes=[cz])
            taps = []
            if grid:
                for dy in (-1, 0, 1):
                    for dx in (-1, 0, 1):
                        if dy == 0 and dx == 0:
                            continue
                        taps.append((dy, dx, (dy + 1) * 3 + dx + 1))
            else:
                taps = [(0, -1, 3), (0, 1, 5)]
            for dy, dx, tp in taps:
                s = 128 + 64 * dy
                if grid and dx != 0:
                    src = wv[:, s:s + 128].rearrange("p (r c) -> p r c", c=64)
                    dst = o.rearrange("p (r c) -> p r c", c=64)
                    if dx == -1:
                        src = src[:, :, 0:63]; dst = dst[:, :, 1:64]
                    else:
                        src = src[:, :, 1:64]; dst = dst[:, :, 0:63]
                else:
                    src = wv[:, s + dx:s + dx + 128]; dst = o
                if eng == "dve":
                    p.op(eng, lambda e, src=src, dst=dst, ci=ci, tp=tp, nch=nch: e.scalar_tensor_tensor(
                        out=dst, in0=src, scalar=cw.ap()[:nch, ci, tp:tp + 1], in1=dst, op0=ALU.mult, op1=ALU.add),
                        reads=[W, cw, cz], writes=[cz])
                else:
                    if grid and dx != 0:
                        tmp = ctmp.ap()[:nch, :].rearrange("p (r c) -> p r c", c=64)[:, :, 0:63]
                    else:
                        tmp = ctmp.ap()[:nch, :]
                    p.op(eng, lambda e, src=src, tmp=tmp, ci=ci, tp=tp, nch=nch: e.tensor_scalar(
                        out=tmp, in0=src, scalar1=cw.ap()[:nch, ci, tp:tp + 1], scalar2=None, op0=ALU.mult),
                        reads=[W, cw], writes=[ctmp])
                    p.op(eng, lambda e, dst=dst, tmp=tmp: e.tensor_tensor(out=dst, in0=dst, in1=tmp, op=ALU.add),
                         reads=[ctmp, cz], writes=[cz])
        return zo

    def decay_prep(nk, lw_ap, lw_reads, cs, G_, Ginv, Gend, Gex=None, lwsb=None):
        p.op("dve", lambda e: e.tensor_tensor_scan(out=cs.ap()[:nk, :], data0=m01[:nk, :], data1=lw_ap, initial=0.0,
                                                   op0=ALU.mult, op1=ALU.add), reads=[cst] + lw_reads, writes=[cs])
        p.op("act", lambda e: e.activation(out=G_.ap()[:nk, :], in_=cs.ap()[:nk, :], func=AF.Exp), reads=[cs], writes=[G_])
        p.op("act", lambda e: e.activation(out=Ginv.ap()[:nk, :], in_=cs.ap()[:nk, :], func=AF.Exp, scale=-1.0), reads=[cs], writes=[Ginv])
        for c in range(2):
            p.op("act", lambda e, c=c: e.activation(out=Gend.ap()[:nk, 64 * c:64 * c + 64], in_=cs.ap()[:nk, 64 * c:64 * c + 64],
                                                    func=AF.Exp, scale=-1.0, bias=cs.ap()[:nk, 64 * c + 63:64 * c + 64]),
                 reads=[cs], writes=[Gend])
        if Gex is not None:
            p.op("dve", lambda e: e.tensor_tensor(out=Gex.ap()[:nk, :], in0=cs.ap()[:nk, :], in1=lwsb, op=ALU.subtract),
                 reads=[cs] + lw_reads, writes=[Gex])
            p.op("act", lambda e: e.activation(out=Gex.ap()[:nk, :], in_=Gex.ap()[:nk, :], func=AF.Exp), reads=[Gex], writes=[Gex])

    rS = T([64, 6, 64], "rwS", ); p.op("pool", lambda e: e.memset(rS.ap(), 0.0), writes=[rS])
    tw = T([64, 128], "tw")
    NG = 3
    nm = ["sgw", "a", "kkv", "sq", "kap", "kd", "b", "cs", "Ginv", "Gend", "Gex"]
    R_ = {n: T([64, 128], "r_" + n) for n in nm}
    RPn = ("rt", "kt", "bt", "kkt", "Bh", "Kh", "G")
    RP = [{n: T([64, 128], f"rp{g}_" + n) for n in RPn} for g in range(NG)]
    tm4s = [T([64, 4, 64], f"tm4_{g}") for g in range(NG)]
    tt5s = [T([64, 5, 64], f"tt5_{g}") for g in range(NG)]
    Rrs = [T([64, 128], f"Rr{g}") for g in range(NG)]
    Xxs = [T([64, 128], f"Xx{g}") for g in range(NG)]
    nZs = [T([64, 64], f"nZ{g}") for g in range(NG)]
    Pbs = [[T([64, 2, 64], f"Pb{g}_{i}") for i in range(2)] for g in range(NG)]
    QPs = [T([64, 2, 64], f"QP{g}") for g in range(NG)]
    yrw = T([64, 6, 128], "yrw"); ybn = T([64, 6, 128], "ybn")
    NEG_E = -float(np.exp(-0.5))

    def rw_prep(h, g, zo):
        wadn = zo.ap()[:, 7, :]
        r = cz.ap()[:64, h, :]; k = cz.ap()[:64, 6 + h, :]; v = cz.ap()[:64, 12 + h, :]
        prm = lambda w: rwp.ap()[:, w, h:h + 1]
        A = lambda n: R_[n].ap()
        Q = lambda n: RP[g][n].ap()
        ps = PS()
        p.op("pe", lambda e: e.matmul(ps.ap()[:64, 0:128], lhsT=waup.ap()[0:64, h * 64:(h + 1) * 64], rhs=tw.ap(), start=True, stop=True),
             reads=[waup, tw], writes=[ps])
        p.op("pe", lambda e: e.matmul(ps.ap()[:64, 128:256], lhsT=waup.ap()[64:128, h * 64:(h + 1) * 64], rhs=wadn[64:128, :], start=True, stop=True),
             reads=[waup, zo], writes=[ps])
        p.op("act", lambda e: e.activation(out=A("sgw"), in_=ps.ap()[:64, 0:128], func=AF.Sigmoid, bias=prm(0)), reads=[ps, rwp], writes=[R_["sgw"]])
        p.op("act", lambda e: e.activation(out=A("a"), in_=ps.ap()[:64, 128:256], func=AF.Sigmoid, bias=prm(1)), reads=[ps, rwp], writes=[R_["a"]])
        p.op("dve", lambda e: e.tensor_scalar(out=A("sgw"), in0=A("sgw"), scalar1=NEG_E, scalar2=None, op0=ALU.mult), reads=[R_["sgw"]], writes=[R_["sgw"]])
        p.op("pool", lambda e: e.tensor_scalar(out=A("kkv"), in0=k, scalar1=prm(2), scalar2=None, op0=ALU.mult), reads=[cz, rwp], writes=[R_["kkv"]])
        p.op("act", lambda e: e.activation(out=A("sq"), in_=A("kkv"), func=AF.Square), reads=[R_["kkv"]], writes=[R_["sq"]])
        ps2 = PS()
        p.op("pe", lambda e: e.matmul(ps2.ap()[:64, 0:128], lhsT=ones.ap()[0:64, 0:64], rhs=A("sq"), start=True, stop=True), reads=[ones, R_["sq"]], writes=[ps2])
        p.op("dve", lambda e: e.tensor_scalar(out=A("sq"), in0=ps2.ap()[:64, 0:128], scalar1=1e-24, scalar2=None, op0=ALU.max), reads=[ps2], writes=[R_["sq"]])
        p.op("act", lambda e: e.activation(out=A("sq"), in_=A("sq"), func=AF.Sqrt), reads=[R_["sq"]], writes=[R_["sq"]])
        p.op("dve", lambda e: e.reciprocal(out=A("sq"), in_=A("sq")), reads=[R_["sq"]], writes=[R_["sq"]])
        p.op("dve", lambda e: e.tensor_tensor(out=A("kap"), in0=A("kkv"), in1=A("sq"), op=ALU.mult), reads=[R_["kkv"], R_["sq"]], writes=[R_["kap"]])
        p.op("pool", lambda e: e.tensor_scalar(out=A("kd"), in0=A("a"), scalar1=-1.0, scalar2=prm(3), op0=ALU.add, op1=ALU.mult), reads=[R_["a"], rwp], writes=[R_["kd"]])
        p.op("dve", lambda e: e.scalar_tensor_tensor(out=A("kd"), in0=A("kd"), scalar=1.0, in1=k, op0=ALU.add, op1=ALU.mult), reads=[R_["kd"], cz], writes=[R_["kd"]])
        p.op("pool", lambda e: e.tensor_tensor(out=A("b"), in0=A("a"), in1=A("kap"), op=ALU.mult), reads=[R_["a"], R_["kap"]], writes=[R_["b"]])
        p.op("dve", lambda e: e.scalar_tensor_tensor(out=A("kkv"), in0=r, scalar=prm(4), in1=A("kd"), op0=ALU.mult, op1=ALU.mult), reads=[cz, rwp, R_["kd"]], writes=[R_["kkv"]])
        p.op("pe", lambda e: e.matmul(ps2.ap()[:64, 128:256], lhsT=ones.ap()[0:64, 0:64], rhs=A("kkv"), start=True, stop=True), reads=[ones, R_["kkv"]], writes=[ps2])
        p.op("dve", lambda e: e.tensor_tensor(out=ybn.ap()[:, h, :], in0=ps2.ap()[:64, 128:256], in1=v, op=ALU.mult), reads=[ps2, cz], writes=[ybn])
        decay_prep(64, A("sgw"), [R_["sgw"]], R_["cs"], RP[g]["G"], R_["Ginv"], R_["Gend"], R_["Gex"], A("sgw"))
        for (o_, x_, xr, g_) in (("rt", r, cz, RP[g]["G"]), ("kt", A("kap"), R_["kap"], R_["Gex"]), ("bt", A("b"), R_["b"], R_["Ginv"]),
                                 ("kkt", A("kd"), R_["kd"], R_["Ginv"]), ("Bh", A("b"), R_["b"], R_["Gend"]), ("Kh", A("kd"), R_["kd"], R_["Gend"])):
            p.op(EW(), lambda e, o_=o_, x_=x_, g_=g_: e.tensor_tensor(out=Q(o_), in0=x_, in1=g_.ap(), op=ALU.mult),
                 reads=[xr, g_], writes=[RP[g][o_]])

    def rw_chunk(h, g, c):
        v = cz.ap()[:64, 12 + h, :]
        Q = lambda n: RP[g][n].ap()
        RQ = lambda n: RP[g][n]
        tm4, tt5, Rr, Xx, nZ, Pb, QP = tm4s[g], tt5s[g], Rrs[g], Xxs[g], nZs[g], Pbs[g], QPs[g]
        cs_ = slice(64 * c, 64 * c + 64)
        pt = PS()
        for i, (src, rd) in enumerate(((v[:, cs_], cz), (Q("kt")[:, cs_], RQ("kt")), (Q("Bh")[:, cs_], RQ("Bh")), (Q("Kh")[:, cs_], RQ("Kh")))):
            p.op("pe", lambda e, i=i, src=src: e.transpose(pt.ap()[:64, 64 * i:64 * i + 64], src, ident[0:64, 0:64]),
                 reads=[rd, cst], writes=[pt])
        p.op("act", lambda e: e.copy(out=tm4.ap().rearrange("p a b -> p (a b)"), in_=pt.ap()[:64, 0:256]), reads=[pt], writes=[tm4])
        Vtm = tm4.ap()[:, 0, :]; Ktm = tm4.ap()[:, 1, :]; Btm = tm4.ap()[:, 2, :]; Khtm = tm4.ap()[:, 3, :]
        p5 = PS()
        for i, (l_, r_) in enumerate((("bt", "kt"), ("kt", "bt"), ("kkt", "kt"), ("bt", "rt"), ("kkt", "rt"))):
            p.op("pe", lambda e, i=i, l_=l_, r_=r_: e.matmul(p5.ap()[:64, 64 * i:64 * i + 64], lhsT=Q(l_)[:, cs_], rhs=Q(r_)[:, cs_], start=True, stop=True),
                 reads=[RQ(l_), RQ(r_)], writes=[p5])
        p.op("dve", lambda e: e.tensor_tensor(out=tt5.ap().rearrange("p a b -> p (a b)"), in0=p5.ap()[:64, 0:320],
                                              in1=mask5.ap().rearrange("p a b -> p (a b)"), op=ALU.mult), reads=[p5, mask5], writes=[tt5])
        U = tt5.ap()[:, 0, :]; L = tt5.ap()[:, 1, :]; LkT = tt5.ap()[:, 2, :]; AbrT = tt5.ap()[:, 3, :]; AkrT = tt5.ap()[:, 4, :]
        yield
        p6 = PS()
        p.op("pe", lambda e: e.matmul(p6.ap()[:64, 0:64], lhsT=LkT, rhs=Vtm, start=True, stop=True), reads=[tt5, tm4], writes=[p6])
        p.op("act", lambda e: e.copy(out=Rr.ap()[:, 64:128], in_=p6.ap()[:64, 0:64]), reads=[p6], writes=[Rr])
        p.op("pool", lambda e: e.tensor_copy(out=Rr.ap()[:, 0:64], in_=Ktm), reads=[tm4], writes=[Rr])
        yield
        p7 = PS()
        p.op("pe", lambda e: e.matmul(p7.ap()[:64, 0:128], lhsT=U, rhs=Rr.ap(), start=True, stop=True), reads=[tt5, Rr], writes=[p7])
        p.op("dve", lambda e: e.tensor_tensor(out=Xx.ap(), in0=Rr.ap(), in1=p7.ap()[:64, 0:128], op=ALU.subtract), reads=[Rr, p7], writes=[Xx])
        Pc, PTc, Prd = L, U, [tt5]
        for lvl in range(5):
            pn = Pb[lvl % 2]
            pq = PS()
            last = lvl == 4
            if not last:
                p.op("pe", lambda e, Pc=Pc, PTc=PTc: e.matmul(pq.ap()[:64, 0:64], lhsT=PTc, rhs=Pc, start=True, stop=True), reads=Prd, writes=[pq])
            p.op("pe", lambda e, Pc=Pc, PTc=PTc: e.matmul(pq.ap()[:64, 64:128], lhsT=Pc, rhs=PTc, start=True, stop=True), reads=Prd, writes=[pq])
            if not last:
                p.op("act", lambda e, pn=pn: e.copy(out=pn.ap().rearrange("p a b -> p (a b)"), in_=pq.ap()[:64, 0:128]), reads=[pq], writes=[pn])
            else:
                p.op("act", lambda e, pn=pn: e.copy(out=pn.ap()[:, 1, :], in_=pq.ap()[:64, 64:128]), reads=[pq], writes=[pn])
            Pc, PTc, Prd = pn.ap()[:, 0, :], pn.ap()[:, 1, :], [pn]
            yield
            px = PS()
            p.op("pe", lambda e, PTc=PTc: e.matmul(px.ap()[:64, 0:128], lhsT=PTc, rhs=Xx.ap(), start=True, stop=True), reads=Prd + [Xx], writes=[px])
            p.op("dve", lambda e: e.tensor_tensor(out=Xx.ap(), in0=Xx.ap(), in1=px.ap()[:64, 0:128], op=ALU.add), reads=[Xx, px], writes=[Xx])
            yield
        Gm = Xx.ap()[:, 0:64]
        p.op("act", lambda e: e.mul(out=nZ.ap(), in_=Xx.ap()[:, 64:128], mul=-1.0), reads=[Xx], writes=[nZ])
        p8 = PS()
        p.op("pe", lambda e: e.matmul(p8.ap()[:64, 0:64], lhsT=Gm, rhs=AbrT, start=True, stop=True), reads=[Xx, tt5], writes=[p8])
        p.op("pe", lambda e: e.matmul(p8.ap()[:64, 64:128], lhsT=Gm, rhs=Btm, start=True, stop=True), reads=[Xx, tm4], writes=[p8])
        p.op("dve", lambda e: e.tensor_tensor(out=QP.ap()[:, 0, :], in0=Q("rt")[:, cs_], in1=p8.ap()[:64, 0:64], op=ALU.subtract), reads=[RQ("rt"), p8], writes=[QP])
        p.op("dve", lambda e: e.scalar_tensor_tensor(out=QP.ap()[:, 1, :], in0=ident[0:64, 0:64], scalar=Q("G")[:, 64 * c + 63:64 * c + 64],
                                                     in1=p8.ap()[:64, 64:128], op0=ALU.mult, op1=ALU.subtract), reads=[cst, RQ("G"), p8], writes=[QP])
        yield
        p9 = PS()
        p.op("pe", lambda e: e.matmul(p9.ap()[:64, 0:64], lhsT=rS.ap()[:, h, :], rhs=QP.ap()[:, 0, :], start=True, stop=False), reads=[rS, QP], writes=[p9])
        p.op("pe", lambda e: e.matmul(p9.ap()[:64, 0:64], lhsT=Vtm, rhs=AkrT, start=False, stop=False), reads=[tm4, tt5], writes=[p9])
        p.op("pe", lambda e: e.matmul(p9.ap()[:64, 0:64], lhsT=nZ.ap(), rhs=AbrT, start=False, stop=True), reads=[nZ, tt5], writes=[p9])
        p.op("act", lambda e: e.copy(out=yrw.ap()[:, h, cs_], in_=p9.ap()[:64, 0:64]), reads=[p9], writes=[yrw])
        p10 = PS()
        p.op("pe", lambda e: e.matmul(p10.ap()[:64, 0:64], lhsT=QP.ap()[:, 1, :], rhs=rS.ap()[:, h, :], start=True, stop=False), reads=[QP, rS], writes=[p10])
        p.op("pe", lambda e: e.matmul(p10.ap()[:64, 0:64], lhsT=Khtm, rhs=Vtm, start=False, stop=False), reads=[tm4], writes=[p10])
        p.op("pe", lambda e: e.matmul(p10.ap()[:64, 0:64], lhsT=Btm, rhs=nZ.ap(), start=False, stop=True), reads=[tm4, nZ], writes=[p10])
        p.op("dve", lambda e: e.tensor_copy(out=rS.ap()[:, h, :], in_=p10.ap()[:64, 0:64]), reads=[p10], writes=[rS])
        yield

    def run_interleaved(gens):
        alive = list(gens)
        while alive:
            nxt = []
            for gen in alive:
                try:
                    next(gen)
                    nxt.append(gen)
                except StopIteration:
                    pass
            alive = nxt

    def rwkv_block(t0, zo):
        wadn = zo.ap()[:, 7, :]
        p.op("act", lambda e: e.activation(out=tw.ap(), in_=wadn[0:64, :], func=AF.Tanh), reads=[zo], writes=[tw])
        for grp in range(6 // NG):
            heads = list(range(grp * NG, (grp + 1) * NG))
            for g, h in enumerate(heads):
                rw_prep(h, g, zo)
                yield
            for c in range(2):
                alive = [rw_chunk(h, g, c) for g, h in enumerate(heads)]
                while alive:
                    nxt = []
                    for gen in alive:
                        try:
                            next(gen)
                            nxt.append(gen)
                        except StopIteration:
                            pass
                    alive = nxt
                    yield
        store(o_rw, "h p t -> p h t", yrw, t0, 64, 6, "sp")
        store(o_bn, "h p t -> p h t", ybn, t0, 64, 6, "act")
        yield

    mS = T([96, 4, 192], "mlS"); p.op("pool", lambda e: e.memset(mS.ap(), 0.0), writes=[mS])
    gl1 = T([16, 128], "gl1"); gl2 = T([16, 128], "gl2")
    mn = ["q", "ks", "ei", "kp", "cs", "G", "Ginv", "Gend", "rt", "kkt", "Kh", "den"]
    M_ = {n: T([96, 128], "m_" + n) for n in mn}
    mtm = T([64, 2, 96], "mtm"); makr = T([64, 64], "makr")
    yml = T([96, 4, 128], "yml")
    KSC = float(96 ** -0.5)

    def mlstm_block(t0, zo):
        gl = zo.ap()[0:16, 4, :]
        p.op("dve", lambda e: e.tensor_scalar(out=gl1.ap(), in0=gl, scalar1=glb.ap()[:, 0:1], scalar2=None, op0=ALU.add), reads=[zo, glb], writes=[gl1])
        p.op("act", lambda e: e.activation(out=gl2.ap(), in_=gl1.ap(), func=AF.Sigmoid), reads=[gl1], writes=[gl2])
        p.op("act", lambda e: e.activation(out=gl2.ap(), in_=gl2.ap(), func=AF.Ln), reads=[gl2], writes=[gl2])
        for h in range(4):
            B = lambda n: M_[n].ap()
            q = cz.ap()[:96, 18 + h, :]; k = cz.ap()[:96, 22 + h, :]; v = zo.ap()[:96, h, :]
            ps = PS()
            p.op("pe", lambda e: e.matmul(ps.ap()[:96, 0:128], lhsT=sel.ap()[:, 4 + h, :], rhs=gl2.ap(), start=True, stop=True), reads=[sel, gl2], writes=[ps])
            p.op("pe", lambda e: e.matmul(ps.ap()[:96, 128:256], lhsT=sel.ap()[:, h, :], rhs=gl1.ap(), start=True, stop=True), reads=[sel, gl1], writes=[ps])
            p.op("act", lambda e: e.activation(out=B("ei"), in_=ps.ap()[:96, 128:256], func=AF.Exp), reads=[ps], writes=[M_["ei"]])
            p.op("act", lambda e: e.activation(out=B("q"), in_=q, func=AF.Silu), reads=[cz], writes=[M_["q"]])
            p.op("act", lambda e: e.activation(out=B("ks"), in_=k, func=AF.Silu), reads=[cz], writes=[M_["ks"]])
            p.op("dve", lambda e: e.scalar_tensor_tensor(out=B("kp"), in0=B("ks"), scalar=KSC, in1=B("ei"), op0=ALU.mult, op1=ALU.mult), reads=[M_["ks"], M_["ei"]], writes=[M_["kp"]])
            decay_prep(96, ps.ap()[:96, 0:128], [ps], M_["cs"], M_["G"], M_["Ginv"], M_["Gend"])
            for (o_, x_, g_) in (("rt", "q", "G"), ("kkt", "kp", "Ginv"), ("Kh", "kp", "Gend")):
                p.op(EW(), lambda e, o_=o_, x_=x_, g_=g_: e.tensor_tensor(out=B(o_), in0=B(x_), in1=B(g_), op=ALU.mult), reads=[M_[x_], M_[g_]], writes=[M_[o_]])
            yield
            for c in range(2):
                cs_ = slice(64 * c, 64 * c + 64)
                pt = PS()
                p.op("pe", lambda e: e.transpose(pt.ap()[:64, 0:96], v[:, cs_], ident[0:96, 0:96]), reads=[zo, cst], writes=[pt])
                p.op("pe", lambda e: e.transpose(pt.ap()[:64, 96:192], B("Kh")[:, cs_], ident[0:96, 0:96]), reads=[M_["Kh"], cst], writes=[pt])
                p.op("act", lambda e: e.copy(out=mtm.ap().rearrange("p a b -> p (a b)"), in_=pt.ap()[:64, 0:192]), reads=[pt], writes=[mtm])
                Vtm = mtm.ap()[:, 0, :]; Khtm = mtm.ap()[:, 1, :]
                yield
                pa = PS()
                p.op("pe", lambda e: e.matmul(pa.ap()[:64, 0:64], lhsT=B("kkt")[:, cs_], rhs=B("rt")[:, cs_], start=True, stop=True), reads=[M_["kkt"], M_["rt"]], writes=[pa])
                p.op("dve", lambda e: e.tensor_tensor(out=makr.ap(), in0=pa.ap()[:64, 0:64], in1=m_iu[0:64, 0:64], op=ALU.mult), reads=[pa, cst], writes=[makr])
                yield
                py = PS()
                p.op("pe", lambda e: e.matmul(py.ap()[:96, 0:64], lhsT=mS.ap()[:, h, 0:96], rhs=B("rt")[:, cs_], start=True, stop=False), reads=[mS, M_["rt"]], writes=[py])
                p.op("pe", lambda e: e.matmul(py.ap()[:96, 0:64], lhsT=Vtm, rhs=makr.ap(), start=False, stop=True), reads=[mtm, makr], writes=[py])
                p.op("pe", lambda e: e.matmul(py.ap()[:96, 64:128], lhsT=mS.ap()[:, h, 96:192], rhs=B("rt")[:, cs_], start=True, stop=False), reads=[mS, M_["rt"]], writes=[py])
                p.op("pe", lambda e: e.matmul(py.ap()[:96, 64:128], lhsT=ones.ap()[0:64, 0:96], rhs=makr.ap(), start=False, stop=True), reads=[ones, makr], writes=[py])
                p.op("act", lambda e: e.activation(out=B("den")[:, 0:64], in_=py.ap()[:96, 64:128], func=AF.Abs), reads=[py], writes=[M_["den"]])
                p.op("dve", lambda e: e.tensor_scalar(out=B("den")[:, 0:64], in0=B("den")[:, 0:64], scalar1=1.0, scalar2=None, op0=ALU.max), reads=[M_["den"]], writes=[M_["den"]])
                p.op("dve", lambda e: e.reciprocal(out=B("den")[:, 0:64], in_=B("den")[:, 0:64]), reads=[M_["den"]], writes=[M_["den"]])
                p.op("dve", lambda e: e.tensor_tensor(out=yml.ap()[:, h, cs_], in0=py.ap()[:96, 0:64], in1=B("den")[:, 0:64], op=ALU.mult), reads=[py, M_["den"]], writes=[yml])
                yield
                pu = PS()
                p.op("pe", lambda e: e.matmul(pu.ap()[:96, 0:96], lhsT=Khtm, rhs=Vtm, start=True, stop=True), reads=[mtm], writes=[pu])
                p.op("pe", lambda e: e.matmul(pu.ap()[:96, 96:192], lhsT=Khtm, rhs=ones.ap()[0:64, 0:96], start=True, stop=True), reads=[mtm, ones], writes=[pu])
                p.op("dve", lambda e: e.scalar_tensor_tensor(out=mS.ap()[:, h, :], in0=mS.ap()[:, h, :], scalar=B("G")[:, 64 * c + 63:64 * c + 64],
                                                             in1=pu.ap()[:96, 0:192], op0=ALU.mult, op1=ALU.add), reads=[mS, M_["G"], pu], writes=[mS])
        store(o_ml, "h p t -> p h t", yml, t0, 96, 4, "act")
        yield

    for j in range(NB):
        zo = conv_block(j)
        run_interleaved([rwkv_block(j * 128, zo), mlstm_block(j * 128, zo), s5_block(j * 128, zo)])


def mixer_consts():
    a = np.arange(128)
    ident = np.eye(128, dtype=np.float32)
    su = (a[:, None] < a[None, :]).astype(np.float32)
    sl = (a[:, None] > a[None, :]).astype(np.float32)
    iu = (a[:, None] <= a[None, :]).astype(np.float32)
    m01 = np.ones((128, 128), np.float32)
    m01[:, 0] = 0.0
    m01[:, 64] = 0.0
    cst = np.stack([ident, su, sl, iu, m01], 1).copy()
    ri = np.broadcast_to(np.arange(129, dtype=np.float32), (128, 129)).copy()
    return cst, ri


def mixer_params(P, i, d):
    m = {}
    cwf = P["conv_w"][i]
    if d == 1:
        cwf = cwf[::-1, ::-1]
    cw = np.zeros((128, NCONV, 9), np.float32)
    for ci in range(NCONV):
        _, c0, n = SCH[ci]
        cw[:n, ci, :] = cwf[:, :, c0:c0 + n].reshape(9, n).T
    m["cw"] = cw
    rwp = np.stack([P["rw_w0"][i, d].reshape(6, 64).T, P["rw_a0"][i, d].reshape(6, 64).T, P["rw_k_k"][i].reshape(6, 64).T,
                    P["rw_k_a"][i].reshape(6, 64).T, P["rw_r_k"][i].T], 1)
    m["rwp"] = np.ascontiguousarray(rwp, np.float32)
    m["wa_up"] = np.concatenate([P["rw_w_up"][i, d], P["rw_a_up"][i, d]], 0).astype(np.float32)
    m["glb"] = P["ml_gate_b"][i].reshape(16, 1).astype(np.float32)
    sel = np.zeros((16, 8, 96), np.float32)
    for j in range(8):
        sel[d * 8 + j, j, :] = 1.0
    m["sel"] = sel
    s5p = np.zeros((128, 3, 8), np.float32)
    bz = np.zeros((128, 16, 128), np.float32)
    cz = np.zeros((128, 16, 128), np.float32)
    for k in range(8):
        for gl2 in range(2):
            g = 2 * k + gl2
            js = slice(gl2 * 64, gl2 * 64 + 64)
            s5p[js, 0, k] = P["s5_a_re"][i, d, g]
            s5p[js, 1, k] = P["s5_a_im"][i, d, g]
            s5p[js, 2, k] = P["s5_log_dt"][i, d, g]
            r0 = (g % 8) * 16
            bz[r0:r0 + 16, k, js] = P["s5_b_re"][i, d, g].T
            bz[r0:r0 + 16, 8 + k, js] = P["s5_b_im"][i, d, g].T
            cz[js, k, r0:r0 + 16] = P["s5_c_re"][i, d, g].T
            cz[js, 8 + k, r0:r0 + 16] = P["s5_c_im"][i, d, g].T
    m["s5p"] = s5p
    m["bz"] = bz
    m["cz"] = cz
    cst, ri = mixer_consts()
    m["cst"] = cst
    m["ri"] = ri
    return m


def mixer_zs(zseq):
    NS = zseq.shape[0]
    zs = np.zeros((NSCH, 128, NS), np.float32)
    for ci, (_, c0, n) in enumerate(SCH):
        zs[ci, :n, :] = zseq[:, c0:c0 + n].T
    return zs


def emit_merge(p, x1T, oT, modT, lng_ap, lnb_ap, outs2, zall, zallb, mp):
    f32 = F32

    def T(shape, name, dt=f32):
        return p.tile(shape, dt, "g_" + name)

    mod = T([128, 72, 2], "mod"); g_sb = T([128, 8], "lng"); b_sb = T([128, 8], "lnb")
    gn = T([64, 2, 6], "gn"); gup = T([128, 384], "gup"); s5d = T([128, 2, 2], "s5d"); gluw = T([128, 2, 256], "gluw")
    mlg = T([96, 4], "mlg"); brb = T([128, 24], "brb")
    p.dma("sp", mod.ap(), modT.ap(), reads=[modT], writes=[mod])
    for i, (t, d) in enumerate(((g_sb, lng_ap), (b_sb, lnb_ap), (gn, mp["gn"]), (gup, mp["g_up"]), (s5d, mp["s5d"]),
                                (gluw, mp["glu_w"]), (mlg, mp["mlg"]), (brb, mp["brb"]))):
        p.dma(("sp", "act")[i % 2], t.ap(), d, writes=[t])
    ones = make_ones(p)
    uprw = T([64, 6, 1024], "uprw", BF16); ups5 = T([128, 2, 1024], "ups5", BF16); upml = T([96, 4, 1024], "upml", BF16)
    wout = T([128, 8, 1024], "wout", BF16)
    stage = [T([128, 1024], f"stage{i}") for i in range(2)]
    k = 0
    for h in range(6):
        st = stage[k % 2]
        p.dma(("sp", "act")[k % 2], st.ap()[:64, :], mp["up_rw"][:, h, :], writes=[st])
        p.op("dve", lambda e, h=h, st=st: e.tensor_copy(out=uprw.ap()[:, h, :], in_=st.ap()[:64, :]), reads=[st], writes=[uprw]); k += 1
    for h in range(2):
        st = stage[k % 2]
        p.dma(("sp", "act")[k % 2], st.ap(), mp["up_s5"][:, h, :], writes=[st])
        p.op("dve", lambda e, h=h, st=st: e.tensor_copy(out=ups5.ap()[:, h, :], in_=st.ap()), reads=[st], writes=[ups5]); k += 1
    for h in range(4):
        st = stage[k % 2]
        p.dma(("sp", "act")[k % 2], st.ap()[:96, :], mp["up_ml"][:, h, :], writes=[st])
        p.op("dve", lambda e, h=h, st=st: e.tensor_copy(out=upml.ap()[:, h, :], in_=st.ap()[:96, :]), reads=[st], writes=[upml]); k += 1
    for h in range(8):
        st = stage[k % 2]
        p.dma(("sp", "act")[k % 2], st.ap(), mp["w_out"][:, h, :], writes=[st])
        p.op("dve", lambda e, h=h, st=st: e.tensor_copy(out=wout.ap()[:, h, :], in_=st.ap()), reads=[st], writes=[wout]); k += 1

    x_sb = T([128, 8, TT], "x"); yrw = T([64, 2, 6, TT], "yrw"); ybn = T([64, 2, 6, TT], "ybn")
    ys5 = T([128, 2, 2, TT], "ys5"); yml = T([96, 2, 4, TT], "yml"); zg = T([128, 31, TT], "zg")
    rwy = T([64, 6, TT], "rwy", BF16); s5y = T([128, 2, TT], "s5y", BF16); mly = T([96, 4, TT], "mly", BF16)
    ym = T([128, 8, TT], "ym", BF16)
    yg = T([128, 2, TT], "yg")
    ta = T([128, TT], "ta"); tb = T([128, TT], "tb"); tc = T([128, TT], "tc"); td = T([128, TT], "td"); sgd = T([128, TT], "sgd")
    y_sb = T([128, 8, TT], "y"); ysq = T([128, 8, TT], "ysq"); mean = T([128, TT], "mean"); rstd = T([128, TT], "rstd")
    pss = [p.psum([128, 512], f32, f"pg{i}") for i in range(6)]
    ps1 = p.psum([128, 512], f32, "ps1"); ps2 = p.psum([128, 512], f32, "ps2")
    pc = [0]

    def PS():
        pc[0] += 1
        return pss[pc[0] % 6]

    def std_part(x, np_, n, eps, scale_ap, out_ap, out_buf):
        p.op("act", lambda e: e.activation(out=tb.ap()[:np_, :n], in_=x, func=AF.Square), reads=[ta], writes=[tb])
        q = PS()
        p.op("pe", lambda e: e.matmul(q.ap()[:np_, 0:n], lhsT=ones.ap()[0:np_, 0:np_], rhs=x, start=True, stop=True), reads=[ones, ta], writes=[q])
        p.op("pe", lambda e: e.matmul(q.ap()[:np_, 256:256 + n], lhsT=ones.ap()[0:np_, 0:np_], rhs=tb.ap()[:np_, :n], start=True, stop=True), reads=[ones, tb], writes=[q])
        p.op("dve", lambda e: e.tensor_scalar(out=tc.ap()[:np_, :n], in0=q.ap()[:np_, 0:n], scalar1=1.0 / np_, scalar2=None, op0=ALU.mult), reads=[q], writes=[tc])
        p.op("dve", lambda e: e.tensor_tensor(out=td.ap()[:np_, :n], in0=tc.ap()[:np_, :n], in1=tc.ap()[:np_, :n], op=ALU.mult), reads=[tc], writes=[td])
        p.op("dve", lambda e: e.scalar_tensor_tensor(out=td.ap()[:np_, :n], in0=q.ap()[:np_, 256:256 + n], scalar=1.0 / np_, in1=td.ap()[:np_, :n], op0=ALU.mult, op1=ALU.subtract), reads=[q, td], writes=[td])
        p.op("dve", lambda e: e.tensor_scalar(out=td.ap()[:np_, :n], in0=td.ap()[:np_, :n], scalar1=eps, scalar2=None, op0=ALU.add), reads=[td], writes=[td])
        p.op("act", lambda e: e.activation(out=td.ap()[:np_, :n], in_=td.ap()[:np_, :n], func=AF.Sqrt), reads=[td], writes=[td])
        p.op("dve", lambda e: e.reciprocal(out=td.ap()[:np_, :n], in_=td.ap()[:np_, :n]), reads=[td], writes=[td])
        p.op("dve", lambda e: e.tensor_tensor(out=x, in0=x, in1=tc.ap()[:np_, :n], op=ALU.subtract), reads=[ta, tc], writes=[ta])
        p.op("dve", lambda e: e.scalar_tensor_tensor(out=out_ap, in0=x, scalar=scale_ap, in1=td.ap()[:np_, :n], op0=ALU.mult, op1=ALU.mult), reads=[ta, td, gn, mlg], writes=[out_buf])

    xv = x1T.ap().rearrange("c p t -> p c t")
    ov = oT.ap().rearrange("c p t -> p c t")
    GC = 2.0 * float(np.sqrt(2.0 / np.pi))
    for ti, (t0, n, wh) in enumerate(TILES):
        sl = slice(t0, t0 + n)
        p.dma("sp", x_sb.ap()[:, :, :n], xv[:, :, sl], reads=[x1T], writes=[x_sb])
        for d in range(2):
            o_rw, o_bn, o_s5, o_ml = outs2[d]
            p.dma("act", yrw.ap()[:, d, :, :n], o_rw.ap()[:, :, sl].rearrange("h p t -> p h t"), reads=[o_rw], writes=[yrw])
            p.dma("sp", ybn.ap()[:, d, :, :n], o_bn.ap()[:, :, sl].rearrange("h p t -> p h t"), reads=[o_bn], writes=[ybn])
            p.dma("act", ys5.ap()[:, d, :, :n], o_s5.ap()[:, :, sl].rearrange("h p t -> p h t"), reads=[o_s5], writes=[ys5])
            p.dma("sp", yml.ap()[:, d, :, :n], o_ml.ap()[:, :, sl].rearrange("h p t -> p h t"), reads=[o_ml], writes=[yml])
        zav = zall.ap().rearrange("c p t -> p c t")
        zbv = zallb.ap().rearrange("c p t -> p c t")
        p.dma("act", zg.ap()[:, 0:29, :n], zbv[:, :, sl], reads=[zallb], writes=[zg])
        p.dma("sp", zg.ap()[:, 29:31, :n], zav[:, 31:33, sl], reads=[zall], writes=[zg])
        p.op("act", lambda e: e.activation(out=sgd.ap()[:, :n], in_=zg.ap()[:, 4, :n], func=AF.Sigmoid), reads=[zg], writes=[sgd])
        for h in range(6):
            x = ta.ap()[:64, :n]
            p.op("dve", lambda e, h=h: e.tensor_tensor(out=x, in0=yrw.ap()[:, 0, h, :n], in1=yrw.ap()[:, 1, h, :n], op=ALU.add), reads=[yrw], writes=[ta])
            std_part(x, 64, n, 64e-5, gn.ap()[:, 0, h:h + 1], x, ta)
            p.op("dve", lambda e, h=h: e.scalar_tensor_tensor(out=x, in0=x, scalar=gn.ap()[:, 1, h:h + 1], in1=ybn.ap()[:, 0, h, :n], op0=ALU.add, op1=ALU.add), reads=[ta, gn, ybn], writes=[ta])
            p.op("dve", lambda e, h=h: e.tensor_tensor(out=x, in0=x, in1=ybn.ap()[:, 1, h, :n], op=ALU.add), reads=[ta, ybn], writes=[ta])
            q = PS()
            p.op("pe", lambda e, h=h: e.matmul(q.ap()[:64, 0:n], lhsT=gup.ap()[:, h * 64:(h + 1) * 64], rhs=sgd.ap()[:, :n], start=True, stop=True), reads=[gup, sgd], writes=[q])
            p.op("dve", lambda e, h=h: e.tensor_tensor(out=rwy.ap()[:, h, :n], in0=x, in1=q.ap()[:64, 0:n], op=ALU.mult), reads=[ta, q], writes=[rwy])
        for mt in range(2):
            x = yg.ap()[:, mt, :n]
            p.op("dve", lambda e, mt=mt: e.scalar_tensor_tensor(out=x, in0=zg.ap()[:, 29 + mt, :n], scalar=s5d.ap()[:, 0, mt:mt + 1], in1=ys5.ap()[:, 0, mt, :n], op0=ALU.mult, op1=ALU.add), reads=[zg, s5d, ys5], writes=[yg])
            p.op("dve", lambda e, mt=mt: e.tensor_tensor(out=x, in0=x, in1=ys5.ap()[:, 1, mt, :n], op=ALU.add), reads=[yg, ys5], writes=[yg])
            p.op("act", lambda e: e.activation(out=tb.ap()[:, :n], in_=x, func=AF.Square), reads=[yg], writes=[tb])
            p.op("dve", lambda e: e.tensor_scalar(out=tb.ap()[:, :n], in0=tb.ap()[:, :n], scalar1=0.044715, scalar2=1.0, op0=ALU.mult, op1=ALU.add), reads=[tb], writes=[tb])
            p.op("dve", lambda e: e.tensor_tensor(out=tb.ap()[:, :n], in0=tb.ap()[:, :n], in1=x, op=ALU.mult), reads=[tb, yg], writes=[tb])
            p.op("act", lambda e: e.activation(out=tb.ap()[:, :n], in_=tb.ap()[:, :n], func=AF.Sigmoid, scale=GC), reads=[tb], writes=[tb])
            p.op("dve", lambda e: e.tensor_tensor(out=x, in0=x, in1=tb.ap()[:, :n], op=ALU.mult), reads=[yg, tb], writes=[yg])
        for mo in range(2):
            q = PS()
            for kc in range(2):
                p.op("pe", lambda e, mo=mo, kc=kc: e.matmul(q.ap()[:, 0:n], lhsT=gluw.ap()[:, kc, mo * 128:(mo + 1) * 128], rhs=yg.ap()[:, kc, :n], start=(kc == 0), stop=(kc == 1)), reads=[gluw, yg], writes=[q])
            p.op("act", lambda e, mo=mo: e.activation(out=tb.ap()[:, :n], in_=q.ap()[:, 0:n], func=AF.Sigmoid, bias=s5d.ap()[:, 1, mo:mo + 1]), reads=[q, s5d], writes=[tb])
            p.op("dve", lambda e, mo=mo: e.tensor_tensor(out=s5y.ap()[:, mo, :n], in0=yg.ap()[:, mo, :n], in1=tb.ap()[:, :n], op=ALU.mult), reads=[yg, tb], writes=[s5y])
        for h in range(4):
            x = ta.ap()[:96, :n]
            p.op("dve", lambda e, h=h: e.tensor_tensor(out=x, in0=yml.ap()[:, 0, h, :n], in1=yml.ap()[:, 1, h, :n], op=ALU.add), reads=[yml], writes=[ta])
            p.op("act", lambda e, h=h: e.activation(out=tb.ap()[:96, :n], in_=zg.ap()[:96, h, :n], func=AF.Sigmoid), reads=[zg], writes=[tb])
            p.op("dve", lambda e: e.tensor_tensor(out=x, in0=x, in1=tb.ap()[:96, :n], op=ALU.mult), reads=[ta, tb], writes=[ta])
            std_part(x, 96, n, 1e-5, mlg.ap()[:, h:h + 1], mly.ap()[:, h, :n], mly)
        for dc in range(8):
            q = PS()
            for h in range(6):
                p.op("pe", lambda e, h=h, dc=dc: e.matmul(q.ap()[:, 0:n], lhsT=uprw.ap()[:, h, dc * 128:(dc + 1) * 128], rhs=rwy.ap()[:, h, :n], start=(h == 0), stop=(h == 5)), reads=[uprw, rwy], writes=[q])
            p.op("act", lambda e, dc=dc: e.activation(out=tb.ap()[:, :n], in_=zg.ap()[:, 5 + dc, :n], func=AF.Sigmoid, bias=brb.ap()[:, dc:dc + 1]), reads=[zg, brb], writes=[tb])
            p.op("dve", lambda e: e.tensor_tensor(out=ta.ap()[:, :n], in0=q.ap()[:, 0:n], in1=tb.ap()[:, :n], op=ALU.mult), reads=[q, tb], writes=[ta])
            q = PS()
            for h in range(2):
                p.op("pe", lambda e, h=h, dc=dc: e.matmul(q.ap()[:, 0:n], lhsT=ups5.ap()[:, h, dc * 128:(dc + 1) * 128], rhs=s5y.ap()[:, h, :n], start=(h == 0), stop=(h == 1)), reads=[ups5, s5y], writes=[q])
            p.op("act", lambda e, dc=dc: e.activation(out=tb.ap()[:, :n], in_=zg.ap()[:, 13 + dc, :n], func=AF.Sigmoid, bias=brb.ap()[:, 8 + dc:9 + dc]), reads=[zg, brb], writes=[tb])
            p.op("dve", lambda e: e.tensor_tensor(out=tc.ap()[:, :n], in0=q.ap()[:, 0:n], in1=tb.ap()[:, :n], op=ALU.mult), reads=[q, tb], writes=[tc])
            p.op("dve", lambda e: e.tensor_tensor(out=ta.ap()[:, :n], in0=ta.ap()[:, :n], in1=tc.ap()[:, :n], op=ALU.add), reads=[ta, tc], writes=[ta])
            q = PS()
            for h in range(4):
                p.op("pe", lambda e, h=h, dc=dc: e.matmul(q.ap()[:, 0:n], lhsT=upml.ap()[:, h, dc * 128:(dc + 1) * 128], rhs=mly.ap()[:, h, :n], start=(h == 0), stop=(h == 3)), reads=[upml, mly], writes=[q])
            p.op("act", lambda e, dc=dc: e.activation(out=tb.ap()[:, :n], in_=zg.ap()[:, 21 + dc, :n], func=AF.Sigmoid, bias=brb.ap()[:, 16 + dc:17 + dc]), reads=[zg, brb], writes=[tb])
            p.op("dve", lambda e: e.tensor_tensor(out=tc.ap()[:, :n], in0=q.ap()[:, 0:n], in1=tb.ap()[:, :n], op=ALU.mult), reads=[q, tb], writes=[tc])
            p.op("dve", lambda e, dc=dc: e.tensor_tensor(out=ym.ap()[:, dc, :n], in0=ta.ap()[:, :n], in1=tc.ap()[:, :n], op=ALU.add), reads=[ta, tc], writes=[ym])
        p.op("act", lambda e: e.mul(out=x_sb.ap()[:, :, :n], in_=x_sb.ap()[:, :, :n], mul=ALPHA), reads=[x_sb], writes=[x_sb])
        for dc in range(8):
            q = PS()
            for kc in range(8):
                p.op("pe", lambda e, kc=kc, dc=dc: e.matmul(q.ap()[:, 0:n], lhsT=wout.ap()[:, kc, dc * 128:(dc + 1) * 128], rhs=ym.ap()[:, kc, :n], start=(kc == 0), stop=(kc == 7)), reads=[wout, ym], writes=[q])
            p.op("dve", lambda e, dc=dc: e.scalar_tensor_tensor(out=y_sb.ap()[:, dc, :n], in0=q.ap()[:, 0:n], scalar=mod.ap()[:, 40 + dc, wh:wh + 1], in1=x_sb.ap()[:, dc, :n], op0=ALU.mult, op1=ALU.add), reads=[q, mod, x_sb], writes=[y_sb])
        ln_feature_major(p, y_sb, ysq, n, ones, ps1, ps2, mean, rstd, g_sb, b_sb, ysq)
        p.dma("sp", ov[:, :, sl], ysq.ap()[:, :, :n], reads=[ysq], writes=[oT])


MIX_KEYS = (("cw", [128, NCONV, 9]), ("rwp", [64, 5, 6]), ("wa_up", [128, 384]), ("glb", [16, 1]), ("sel", [16, 8, 96]),
            ("s5p", [128, 3, 8]), ("bz", [128, 16, 128]), ("cz", [128, 16, 128]))
MRG_KEYS = (("gn", [64, 2, 6]), ("g_up", [128, 384]), ("s5d", [128, 2, 2]), ("glu_w", [128, 2, 256]), ("mlg", [96, 4]),
            ("up_rw", [64, 6, 1024]), ("up_s5", [128, 2, 1024]), ("up_ml", [96, 4, 1024]), ("brb", [128, 24]),
            ("w_out", [128, 8, 1024]))
NUSED = 4


def set_sizes(nctx, nlat):
    global NLAT, NCTX, NT, TILES
    NLAT, NCTX = nlat, nctx
    NT = NLAT + NCTX
    TILES = [(0, NCTX, 1)] + [(NCTX + i * TT, TT, 0) for i in range(NLAT // TT)]


def build_fused():
    p = Prog()
    NS = NT
    din = lambda n, sh: p.dram(n, sh, F32, "in")
    xT0 = din("xT0", [8, 128, NS])
    cvec = din("cvec", [128, 8, 2])
    ada_w = din("ada_w", [DEPTH, D, 9 * D])
    ada_b = din("ada_b_l", [DEPTH, 128, 72])
    ln_g = din("ln_g_l", [DEPTH, 3, 128, 8])
    ln_b = din("ln_b_l", [DEPTH, 3, 128, 8])
    wg = din("ffn_w_gate", [DEPTH, 2, D, DFF])
    wu = din("ffn_w_up", [DEPTH, 2, D, DFF])
    wd = din("ffn_w_down", [DEPTH, 2, DFF, D])
    w_in = din("w_in", [DEPTH, D, INW])
    cst = din("cst", [128, 5, 128])
    ri = din("ri", [128, 129])
    mixp = {}
    for i in range(DEPTH):
        for d in range(2):
            mixp[(i, d)] = {k: din(f"m{i}{d}_{k}", sh).ap() for k, sh in MIX_KEYS}
            mixp[(i, d)]["cst"] = cst.ap()
            mixp[(i, d)]["ri"] = ri.ap()
    mrgp = {i: {k: din(f"g{i}_{k}", sh).ap() for k, sh in MRG_KEYS} for i in range(DEPTH)}
    oT = p.dram("oT", [8, 128, NS], F32, "out")
    tmp = lambda n, sh: p.dram(n, sh, F32, "tmp")
    S1 = tmp("S1", [8, 128, NS])
    S2 = tmp("S2", [8, 128, NS])
    zall = tmp("zall", [NSCH, 128, NS])
    zallb = tmp("zallb", [NZALL - NSCH, 128, NS])
    modT = [tmp(f"modT{i}", [128, 72, 2]) for i in range(DEPTH)]
    outs = [(tmp(f"o_rw{d}", [6, 64, NS]), tmp(f"o_bn{d}", [6, 64, NS]), tmp(f"o_s5{d}", [2, 128, NS]),
             tmp(f"o_ml{d}", [4, 96, NS])) for d in range(2)]
    chain = [(xT0, S1, S1, S2, S1), (S1, S2, S2, S1, oT)]
    for i in range(DEPTH):
        a_src, a_dst, m_src, m_dst, f_dst = chain[i]
        with p.phase():
            emit_mod(p, cvec, ada_w.ap()[i], ada_b.ap()[i], modT[i])
        with p.phase():
            emit_ffn(p, 0, a_src, a_dst, modT[i], ln_g.ap()[i, 0], ln_b.ap()[i, 0], wg.ap()[i, 0], wu.ap()[i, 0], wd.ap()[i, 0])
        with p.phase():
            emit_inproj(p, a_dst, modT[i], w_in.ap()[i], zall, zallb)
        for d in range(2):
            with p.phase():
                emit_mixer(p, zall, mixp[(i, d)], outs[d], rev=(d == 1), nctx=NCTX, nlat=NLAT)
        with p.phase():
            emit_merge(p, m_src, m_dst, modT[i], ln_g.ap()[i, 1], ln_b.ap()[i, 1], outs, zall, zallb, mrgp[i])
        with p.phase():
            emit_ffn(p, 6, m_dst, f_dst, modT[i], ln_g.ap()[i, 2], ln_b.ap()[i, 2], wg.ap()[i, 1], wu.ap()[i, 1], wd.ap()[i, 1])
    p.finish()
    return p


def merge_params(P, i):
    m = {}
    m["gn"] = np.ascontiguousarray(np.stack([P["rw_gn_g"][i].reshape(6, 64).T, P["rw_gn_b"][i].reshape(6, 64).T], 1))
    m["g_up"] = P["rw_g_up"][i]
    m["s5d"] = np.ascontiguousarray(np.stack([P["s5_d"][i].reshape(2, 128).T, P["s5_glu_b"][i].reshape(2, 128).T], 1))
    m["glu_w"] = np.ascontiguousarray(P["s5_glu_w"][i].reshape(2, 128, 256).transpose(1, 0, 2))
    m["mlg"] = np.ascontiguousarray(P["ml_norm_g"][i].reshape(4, 96).T)
    m["up_rw"] = np.ascontiguousarray(P["up_rw"][i].reshape(6, 64, 1024).transpose(1, 0, 2))
    m["up_s5"] = np.ascontiguousarray(P["up_s5"][i].reshape(2, 128, 1024).transpose(1, 0, 2))
    m["up_ml"] = np.ascontiguousarray(P["up_ml"][i].reshape(4, 96, 1024).transpose(1, 0, 2))
    m["brb"] = np.ascontiguousarray(P["br_gate_b"][i].reshape(24, 128).T)
    m["w_out"] = np.ascontiguousarray(P["w_out"][i].reshape(8, 128, 1024).transpose(1, 0, 2))
    return m


_PROG = []


def kernel(**inputs):
    P = {k: np.asarray(v, dtype=np.float32) for k, v in inputs.items()}
    x, c, ctx, c_ctx = P["x"], P["c"], P["ctx"], P["c_ctx"]
    B, SEQ, _ = x.shape
    if not _PROG:
        set_sizes(ctx.shape[1], SEQ)
        _PROG.append(build_fused())
    prog = _PROG[0]
    NUSED = B
    shared = {
        "ada_w": P["ada_w"],
        "ada_b_l": np.ascontiguousarray(P["ada_b"].reshape(DEPTH, 72, 128).transpose(0, 2, 1)),
        "ln_g_l": np.ascontiguousarray(P["ln_g"].reshape(DEPTH, 3, 8, 128).transpose(0, 1, 3, 2)),
        "ln_b_l": np.ascontiguousarray(P["ln_b"].reshape(DEPTH, 3, 8, 128).transpose(0, 1, 3, 2)),
        "ffn_w_gate": P["ffn_w_gate"], "ffn_w_up": P["ffn_w_up"], "ffn_w_down": P["ffn_w_down"], "w_in": P["w_in"],
    }
    cst, ri = mixer_consts()
    shared["cst"] = cst
    shared["ri"] = ri
    for i in range(DEPTH):
        for d in range(2):
            mp = mixer_params(P, i, d)
            for k, _ in MIX_KEYS:
                shared[f"m{i}{d}_{k}"] = np.ascontiguousarray(mp[k], np.float32)
        gp = merge_params(P, i)
        for k, _ in MRG_KEYS:
            shared[f"g{i}_{k}"] = np.ascontiguousarray(gp[k], np.float32)
    ims = []
    for cid in range(NUSED):
        b = cid % B
        m = dict(shared)
        xx = np.concatenate([ctx[b], x[b]], 0)
        m["xT0"] = np.ascontiguousarray(xx.T).reshape(8, 128, NT)
        m["cvec"] = np.ascontiguousarray(np.stack([c[b], c_ctx], -1).reshape(8, 128, 2).transpose(1, 0, 2))
        ims.append(m)
    res = run_bass_kernel_spmd(prog.nc, ims, core_ids=list(range(NUSED)))
    out = np.empty((B, SEQ, D), np.float32)
    for b in range(B):
        out[b] = res.results[b]["oT"].reshape(D, NT).T[NCTX:]
    return out
```

```python
import numpy as np
from contextlib import ExitStack
import concourse.bass as bass
import concourse.mybir as mybir
from concourse.bass_utils import run_bass_kernel_spmd

F32 = mybir.dt.float32
BF16 = mybir.dt.bfloat16
AF = mybir.ActivationFunctionType
ALU = mybir.AluOpType
AX = mybir.AxisListType


class Reg:
    __slots__ = ("name", "w", "r")

    def __init__(self, name):
        self.name = name
        self.w = None
        self.r = []


class Buf:
    def __init__(self, t, name, nreg=1):
        self.t = t
        self.name = name
        self.regs = [Reg(f"{name}.{i}") for i in range(nreg)]

    def ap(self):
        return self.t.ap() if hasattr(self.t, "ap") and not isinstance(self.t, bass.AP) else self.t

    def __getitem__(self, idx):
        return self.ap()[idx]

    def r(self, i):
        return self.regs[i]


NDMA_SEM = 8
FUSE_WAIT = True


class Prog:
    def __init__(self):
        self.nc = bass.Bass("TRN2", target_bir_lowering=False)
        nc = self.nc
        self.stack = ExitStack()
        self.eng = {"pe": nc.tensor, "dve": nc.vector, "act": nc.scalar, "pool": nc.gpsimd, "sp": nc.sync}
        self.sems = {}
        self.semval = {}
        for e in self.eng:
            self.sems[e] = self.stack.enter_context(nc.semaphore(f"s_{e}"))
            self.semval[e] = 0
        self.dq = {}
        for q in ("sp", "act", "pool"):
            lst = []
            for i in range(NDMA_SEM):
                k = f"d_{q}{i}"
                self.sems[k] = self.stack.enter_context(nc.semaphore(k))
                self.semval[k] = 0
                lst.append(k)
            self.dq[q] = [lst, 0]
        self.waited = {e: {} for e in self.eng}
        self.out_waits = []
        self.ninst = {e: 0 for e in self.eng}
        self.pstack = None
        self.pidx = 0

    def barrier(self):
        for e in self.eng:
            for k, v in self.semval.items():
                if v > 0 and k != e:
                    self._wait(e, k, v)

    def phase(self):
        prog = self

        class _Ph:
            def __enter__(self_):
                prog.pidx += 1
                prog.pstack = ExitStack()
                return prog

            def __exit__(self_, *a):
                prog.barrier()
                prog.pstack.close()
                prog.pstack = None
                return False
        return _Ph()

    def dram(self, name, shape, dtype, kind):
        k = {"in": "ExternalInput", "out": "ExternalOutput", "tmp": "Internal"}[kind]
        t = self.nc.dram_tensor(name, list(shape), dtype, kind=k)
        b = Buf(t, name)
        b.kind = kind
        return b

    def tile(self, shape, dtype, name, nreg=1):
        st = self.pstack if self.pstack is not None else self.stack
        name = f"p{self.pidx}_{name}"
        t = st.enter_context(self.nc.sbuf_tensor(name, list(shape), dtype))
        return Buf(t, name, nreg)

    def psum(self, shape, dtype, name, nreg=1):
        st = self.pstack if self.pstack is not None else self.stack
        name = f"p{self.pidx}_{name}"
        t = st.enter_context(self.nc.psum_tensor(name, list(shape), dtype))
        return Buf(t, name, nreg)

    @staticmethod
    def _regs(lst):
        out = []
        for x in lst or []:
            if isinstance(x, Buf):
                out.extend(x.regs)
            elif isinstance(x, Reg):
                out.append(x)
            elif isinstance(x, tuple):
                out.append(x[0].regs[x[1]])
            else:
                raise TypeError(x)
        return out

    def _wait(self, e, key, val):
        if key is None:
            return
        cur = self.waited[e].get(key, 0)
        if cur >= val:
            return
        self.waited[e][key] = val
        self.eng[e].wait_ge(self.sems[key], val)
        self.ninst[e] += 1

    def _deps(self, e, reads, writes, defer=False):
        need = {}
        for r in reads:
            if r.w is not None:
                need[r.w[0]] = max(need.get(r.w[0], 0), r.w[1])
        for r in writes:
            if r.w is not None:
                need[r.w[0]] = max(need.get(r.w[0], 0), r.w[1])
            for (k, v) in r.r:
                need[k] = max(need.get(k, 0), v)
        todo = [(k, v) for k, v in need.items() if self.waited[e].get(k, 0) < v]
        last = None
        if defer and FUSE_WAIT and todo:
            last = todo.pop()
        for k, v in todo:
            self._wait(e, k, v)
        return last

    def _attach(self, e, inst, last):
        if last is not None:
            k, v = last
            self.waited[e][k] = v
            inst._wait_ge(self.sems[k], v)

    def _mark(self, key, val, reads, writes):
        for r in reads:
            r.r.append((key, val))
            if len(r.r) > 12:
                d = {}
                for (k, v) in r.r:
                    d[k] = max(d.get(k, 0), v)
                r.r = list(d.items())
        for r in writes:
            r.w = (key, val)
            r.r = []

    def op(self, e, fn, reads=None, writes=None):
        reads = self._regs(reads)
        writes = self._regs(writes)
        last = self._deps(e, reads, writes, defer=True)
        inst = fn(self.eng[e])
        self._attach(e, inst, last)
        self.semval[e] += 1
        inst.then_inc(self.sems[e], 1)
        self.ninst[e] += 1
        self._mark(e, self.semval[e], reads, writes)
        return inst

    def dma(self, q, out, in_, reads=None, writes=None, **kw):
        rbufs = reads or []
        wbufs = writes or []
        reads = self._regs(reads)
        writes = self._regs(writes)
        self._deps(q, reads, writes)
        lst, i = self.dq[q]
        key = lst[i % NDMA_SEM]
        self.dq[q][1] = i + 1
        self._wait(q, key, self.semval[key])
        inst = self.eng[q].dma_start(out=out, in_=in_, **kw)
        self.semval[key] += 16
        inst.then_inc(self.sems[key], 16)
        self.ninst[q] += 1
        self._mark(key, self.semval[key], reads, writes)
        for b in wbufs:
            if isinstance(b, Buf) and getattr(b, "kind", None) == "out":
                self.out_waits.append((key, self.semval[key]))
        return inst

    def finish(self):
        d = {}
        for k, v in self.out_waits:
            d[k] = max(d.get(k, 0), v)
        for k, v in d.items():
            self._wait("sp", k, v)
        for e in self.eng:
            if e != "sp" and self.semval[e] > 0:
                self._wait("sp", e, self.semval[e])


D = 1024
DFF = 2816
NFC = 22
DEPTH = 2
ALPHA = (2.0 * DEPTH) ** 0.25
LN_EPS = 1e-5
NLAT = 8192
NCTX = 256
NT = NLAT + NCTX
TT = 256
TILES = [(0, NCTX, 1)] + [(NCTX + i * TT, TT, 0) for i in range(NLAT // TT)]
NCORES = 8
INW = 6288


def make_ones(p, n=128, name="ones"):
    t = p.tile([128, n], F32, name)
    p.op("pool", lambda e: e.memset(t.ap(), 1.0), writes=[t])
    return t


def load_weight_bf16(p, dst, dst_view, src_view, stage, i, shape_free):
    st = stage[i % len(stage)]
    sv = st.ap()[:, :shape_free] if isinstance(shape_free, int) else shape_free(st.ap())
    q = ("sp", "act")[i % 2]
    p.dma(q, sv, src_view, writes=[st])
    ce = ("pool", "dve", "act")[i % 3]
    if ce == "act":
        p.op("act", lambda e: e.copy(out=dst_view, in_=sv), reads=[st], writes=[dst])
    else:
        p.op(ce, lambda e: e.tensor_copy(out=dst_view, in_=sv), reads=[st], writes=[dst])


def emit_mod(p, cvec, adaw_ap, adab_ap, out):
    cs = p.tile([128, 8, 2], F32, "cs")
    ab = p.tile([128, 72], F32, "ab")
    mo = p.tile([128, 72, 2], F32, "mo")
    pan = [p.tile([128, 8, 1024], F32, f"pan{i}") for i in range(2)]
    ps = [p.psum([128, 512], F32, f"ps{i}") for i in range(2)]
    p.dma("sp", cs.ap(), cvec.ap(), writes=[cs])
    p.dma("sp", ab.ap(), adab_ap, writes=[ab])
    p.op("act", lambda e: e.activation(out=cs.ap(), in_=cs.ap(), func=AF.Silu), reads=[cs], writes=[cs])
    awv = adaw_ap.rearrange("(kc p) f -> p kc f", p=128)
    for j in range(9):
        pn = pan[j % 2]
        p.dma(("sp", "act")[j % 2], pn.ap(), awv[:, :, j * 1024:(j + 1) * 1024], writes=[pn])
        for dc in range(8):
            ch = j * 8 + dc
            pp = ps[ch % 2]
            for kc in range(8):
                p.op("pe", lambda e, kc=kc, dc=dc, pn=pn, pp=pp: e.matmul(
                    pp.ap()[:, 0:2], lhsT=pn.ap()[:, kc, dc * 128:(dc + 1) * 128], rhs=cs.ap()[:, kc, :],
                    start=(kc == 0), stop=(kc == 7)), reads=[pn, cs], writes=[pp])
            p.op("dve", lambda e, ch=ch, pp=pp: e.tensor_scalar(
                out=mo.ap()[:, ch, :], in0=pp.ap()[:, 0:2], scalar1=ab.ap()[:, ch:ch + 1], scalar2=None,
                op0=ALU.add), reads=[pp, ab], writes=[mo])
    p.dma("sp", out.ap(), mo.ap(), reads=[mo], writes=[out])


def ln_feature_major(p, y, ysq, n, ones, ps1, ps2, mean, rstd, g, b, dst, dst_reads=None):
    p.op("act", lambda e: e.activation(out=ysq.ap()[:, :, :n], in_=y.ap()[:, :, :n], func=AF.Square),
         reads=[y], writes=[ysq])
    for dc in range(8):
        p.op("pe", lambda e, dc=dc: e.matmul(ps1.ap()[:, :n], lhsT=ones.ap(), rhs=y.ap()[:, dc, :n],
                                              start=(dc == 0), stop=(dc == 7)), reads=[ones, y], writes=[ps1])
    for dc in range(8):
        p.op("pe", lambda e, dc=dc: e.matmul(ps2.ap()[:, :n], lhsT=ones.ap(), rhs=ysq.ap()[:, dc, :n],
                                              start=(dc == 0), stop=(dc == 7)), reads=[ones, ysq], writes=[ps2])
    p.op("dve", lambda e: e.tensor_scalar(out=mean.ap()[:, :n], in0=ps1.ap()[:, :n], scalar1=1.0 / D, scalar2=None,
                                          op0=ALU.mult), reads=[ps1], writes=[mean])
    p.op("dve", lambda e: e.tensor_tensor(out=rstd.ap()[:, :n], in0=mean.ap()[:, :n], in1=mean.ap()[:, :n],
                                          op=ALU.mult), reads=[mean], writes=[rstd])
    p.op("dve", lambda e: e.scalar_tensor_tensor(out=rstd.ap()[:, :n], in0=ps2.ap()[:, :n], scalar=1.0 / D,
                                                 in1=rstd.ap()[:, :n], op0=ALU.mult, op1=ALU.subtract),
         reads=[ps2, rstd], writes=[rstd])
    p.op("dve", lambda e: e.tensor_scalar(out=rstd.ap()[:, :n], in0=rstd.ap()[:, :n], scalar1=LN_EPS, scalar2=None,
                                          op0=ALU.add), reads=[rstd], writes=[rstd])
    p.op("act", lambda e: e.activation(out=rstd.ap()[:, :n], in_=rstd.ap()[:, :n], func=AF.Sqrt),
         reads=[rstd], writes=[rstd])
    p.op("dve", lambda e: e.reciprocal(out=rstd.ap()[:, :n], in_=rstd.ap()[:, :n]), reads=[rstd], writes=[rstd])
    for dc in range(8):
        eng = ("dve", "pool")[dc % 2]
        p.op(eng, lambda e, dc=dc: e.tensor_tensor(out=y.ap()[:, dc, :n], in0=y.ap()[:, dc, :n],
                                                   in1=mean.ap()[:, :n], op=ALU.subtract),
             reads=[y, mean], writes=[y])
        p.op(eng, lambda e, dc=dc: e.tensor_tensor(out=y.ap()[:, dc, :n], in0=y.ap()[:, dc, :n],
                                                   in1=rstd.ap()[:, :n], op=ALU.mult),
             reads=[y, rstd], writes=[y])
        p.op(eng, lambda e, dc=dc: e.tensor_scalar(out=dst.ap()[:, dc, :n], in0=y.ap()[:, dc, :n],
                                                   scalar1=g.ap()[:, dc:dc + 1], scalar2=b.ap()[:, dc:dc + 1],
                                                   op0=ALU.mult, op1=ALU.add),
             reads=[y, g, b], writes=[dst])


def emit_ffn(p, j0, xT, oT, modT, lng_ap, lnb_ap, wg_ap, wu_ap, wd_ap):

    wg_sb = p.tile([128, 8, DFF], BF16, "wg_sb")
    wu_sb = p.tile([128, 8, DFF], BF16, "wu_sb")
    wd_sb = p.tile([128, NFC, D], BF16, "wd_sb")
    stage = [p.tile([128, DFF // 2], F32, f"stage{i}") for i in range(2)]
    mod = p.tile([128, 72, 2], F32, "mod")
    g_sb = p.tile([128, 8], F32, "g_sb")
    b_sb = p.tile([128, 8], F32, "b_sb")
    ops_ = p.tile([128, 8, 2], F32, "onepsc")
    hg = p.tile([128, 8, 2], F32, "hg")
    ones = make_ones(p)
    p.dma("sp", mod.ap(), modT.ap(), reads=[modT], writes=[mod])
    p.dma("sp", g_sb.ap(), lng_ap, writes=[g_sb])
    p.dma("sp", b_sb.ap(), lnb_ap, writes=[b_sb])
    p.op("dve", lambda e: e.tensor_scalar(out=ops_.ap(), in0=mod.ap()[:, (j0 + 1) * 8:(j0 + 2) * 8, :], scalar1=1.0,
                                          scalar2=None, op0=ALU.add), reads=[mod], writes=[ops_])
    p.op("dve", lambda e: e.tensor_scalar(out=hg.ap(), in0=mod.ap()[:, (j0 + 2) * 8:(j0 + 3) * 8, :], scalar1=0.5,
                                          scalar2=None, op0=ALU.mult), reads=[mod], writes=[hg])
    k = 0
    wgv = wg_ap.rearrange("(kc p) f -> kc p f", p=128)
    wuv = wu_ap.rearrange("(kc p) f -> kc p f", p=128)
    wdv = wd_ap.rearrange("(fc p) d -> fc p d", p=128)
    HF = DFF // 2
    for kc in range(8):
        for hh in range(2):
            load_weight_bf16(p, wg_sb, wg_sb.ap()[:, kc, hh * HF:(hh + 1) * HF], wgv[kc][:, hh * HF:(hh + 1) * HF], stage, k, HF); k += 1
            load_weight_bf16(p, wu_sb, wu_sb.ap()[:, kc, hh * HF:(hh + 1) * HF], wuv[kc][:, hh * HF:(hh + 1) * HF], stage, k, HF); k += 1
    for fc in range(NFC):
        load_weight_bf16(p, wd_sb, wd_sb.ap()[:, fc, :], wdv[fc], stage, k, D); k += 1

    x_sb = [p.tile([128, 8, TT], F32, f"x_sb{i}") for i in range(2)]
    u_sb = p.tile([128, 8, TT], BF16, "u_sb")
    a_sb = [p.tile([128, NFC, TT], BF16, "a_sb0")]
    sg = [p.tile([128, TT], F32, f"sg{i}") for i in range(2)]
    y_sb = p.tile([128, 8, TT], F32, "y_sb")
    ysq = p.tile([128, 8, TT], F32, "ysq")
    mean = p.tile([128, TT], F32, "mean")
    rstd = p.tile([128, TT], F32, "rstd")
    psg = [p.psum([128, 512], F32, f"psg{i}") for i in range(2)]
    psu = [p.psum([128, 512], F32, f"psu{i}") for i in range(2)]
    psd = [p.psum([128, 512], F32, f"psd{i}") for i in range(2)]
    ps1 = p.psum([128, 512], F32, "ps1")
    ps2 = p.psum([128, 512], F32, "ps2")
    xv = xT.ap().rearrange("c p t -> p c t")
    ov = oT.ap().rearrange("c p t -> p c t")

    for ti, (t0, n, wh) in enumerate(TILES):
        xs = x_sb[ti % 2]
        asb = a_sb[0]
        osb = ysq
        p.dma(("sp", "act")[ti % 2], xs.ap()[:, :, :n], xv[:, :, t0:t0 + n], reads=[xT], writes=[xs])
        for kc in range(8):
            p.op("dve", lambda e, kc=kc: e.tensor_scalar(
                out=u_sb.ap()[:, kc, :n], in0=xs.ap()[:, kc, :n], scalar1=ops_.ap()[:, kc, wh:wh + 1],
                scalar2=mod.ap()[:, j0 * 8 + kc, wh:wh + 1], op0=ALU.mult, op1=ALU.add),
                reads=[xs, ops_, mod], writes=[u_sb])
        for fc in range(NFC):
            pg = psg[fc % 2]
            pu = psu[fc % 2]
            for kc in range(8):
                p.op("pe", lambda e, kc=kc, fc=fc, pg=pg: e.matmul(
                    pg.ap()[:, :n], lhsT=wg_sb.ap()[:, kc, fc * 128:(fc + 1) * 128], rhs=u_sb.ap()[:, kc, :n],
                    start=(kc == 0), stop=(kc == 7)), reads=[wg_sb, u_sb], writes=[pg])
            for kc in range(8):
                p.op("pe", lambda e, kc=kc, fc=fc, pu=pu: e.matmul(
                    pu.ap()[:, :n], lhsT=wu_sb.ap()[:, kc, fc * 128:(fc + 1) * 128], rhs=u_sb.ap()[:, kc, :n],
                    start=(kc == 0), stop=(kc == 7)), reads=[wu_sb, u_sb], writes=[pu])
            s = sg[fc % 2]
            p.op("act", lambda e, pg=pg, s=s: e.activation(out=s.ap()[:, :n], in_=pg.ap()[:, :n], func=AF.Silu),
                 reads=[pg], writes=[s])
            p.op("dve", lambda e, pu=pu, s=s, fc=fc: e.tensor_tensor(
                out=asb.ap()[:, fc, :n], in0=pu.ap()[:, :n], in1=s.ap()[:, :n], op=ALU.mult),
                reads=[pu, s], writes=[asb])
        p.op("act", lambda e: e.mul(out=xs.ap()[:, :, :n], in_=xs.ap()[:, :, :n], mul=ALPHA), reads=[xs], writes=[xs])
        for dc in range(8):
            pd = psd[dc % 2]
            for fc in range(NFC):
                p.op("pe", lambda e, dc=dc, fc=fc, pd=pd: e.matmul(
                    pd.ap()[:, :n], lhsT=wd_sb.ap()[:, fc, dc * 128:(dc + 1) * 128], rhs=asb.ap()[:, fc, :n],
                    start=(fc == 0), stop=(fc == NFC - 1)), reads=[wd_sb, asb], writes=[pd])
            p.op("dve", lambda e, dc=dc, pd=pd: e.scalar_tensor_tensor(
                out=y_sb.ap()[:, dc, :n], in0=pd.ap()[:, :n], scalar=hg.ap()[:, dc, wh:wh + 1],
                in1=xs.ap()[:, dc, :n], op0=ALU.mult, op1=ALU.add), reads=[pd, hg, xs], writes=[y_sb])
        ln_feature_major(p, y_sb, ysq, n, ones, ps1, ps2, mean, rstd, g_sb, b_sb, osb)
        p.dma(("sp", "act")[(ti + 1) % 2], ov[:, :, t0:t0 + n], osb.ap()[:, :, :n], reads=[osb], writes=[oT])


def emit_inproj(p, xT, modT, win_ap, zall, zallb):
    w_sb = p.tile([128, 8, INW], BF16, "w_sb")
    QW = INW // 4
    stage = [p.tile([128, QW], F32, f"stage{i}") for i in range(2)]
    mod = p.tile([128, 72, 2], F32, "mod")
    ops_ = p.tile([128, 8, 2], F32, "onepsc")
    p.dma("sp", mod.ap(), modT.ap(), reads=[modT], writes=[mod])
    p.op("dve", lambda e: e.tensor_scalar(out=ops_.ap(), in0=mod.ap()[:, 32:40, :], scalar1=1.0, scalar2=None,
                                          op0=ALU.add), reads=[mod], writes=[ops_])
    wv = win_ap.rearrange("(kc p) f -> kc p f", p=128)
    k = 0
    for kc in range(8):
        for qq in range(4):
            load_weight_bf16(p, w_sb, w_sb.ap()[:, kc, qq * QW:(qq + 1) * QW], wv[kc][:, qq * QW:(qq + 1) * QW],
                             stage, k, QW); k += 1
    x_sb = [p.tile([128, 8, TT], F32, "x_sb0")]
    u_sb = [p.tile([128, 8, TT], BF16, "u_sb0")]
    zbig = p.tile([128, NZALL, TT], F32, "zbig")
    p.op("pool", lambda e: e.memset(zbig.ap(), 0.0), writes=[zbig])
    ps = [p.psum([128, 512], F32, f"ps{i}") for i in range(4)]
    xv = xT.ap().rearrange("c p t -> p c t")
    zv = zall.ap().rearrange("c p t -> p c t")
    zvb = zallb.ap().rearrange("c p t -> p c t")
    cnt = 0
    for ti, (t0, n, wh) in enumerate(TILES):
        xs = x_sb[0]
        us = u_sb[0]
        p.dma(("sp", "act")[ti % 2], xs.ap()[:, :, :n], xv[:, :, t0:t0 + n], reads=[xT], writes=[xs])
        for kc in range(8):
            p.op(("dve", "pool")[kc % 2], lambda e, kc=kc: e.tensor_scalar(
                out=us.ap()[:, kc, :n], in0=xs.ap()[:, kc, :n], scalar1=ops_.ap()[:, kc, wh:wh + 1],
                scalar2=mod.ap()[:, 24 + kc, wh:wh + 1], op0=ALU.mult, op1=ALU.add),
                reads=[xs, ops_, mod], writes=[us])
        for ci, (nm, c0, nc_) in enumerate(ZALL):
            pp = ps[cnt % 4]
            for kc in range(8):
                p.op("pe", lambda e, kc=kc, pp=pp, c0=c0, nc_=nc_: e.matmul(
                    pp.ap()[:nc_, :n], lhsT=w_sb.ap()[:, kc, c0:c0 + nc_], rhs=us.ap()[:, kc, :n],
                    start=(kc == 0), stop=(kc == 7)), reads=[w_sb, us], writes=[pp])
            if cnt % 2 == 0:
                p.op("act", lambda e, pp=pp, ci=ci, nc_=nc_: e.copy(out=zbig.ap()[:nc_, ci, :n], in_=pp.ap()[:nc_, :n]),
                     reads=[pp], writes=[zbig])
            else:
                p.op("dve", lambda e, pp=pp, ci=ci, nc_=nc_: e.tensor_copy(out=zbig.ap()[:nc_, ci, :n], in_=pp.ap()[:nc_, :n]),
                     reads=[pp], writes=[zbig])
            cnt += 1
        p.dma(("sp", "act")[(ti + 1) % 2], zv[:, :, t0:t0 + n], zbig.ap()[:, 0:NSCH, :n], reads=[zbig], writes=[zall])
        p.dma(("sp", "act")[ti % 2], zvb[:, :, t0:t0 + n], zbig.ap()[:, NSCH:NZALL, :n], reads=[zbig], writes=[zallb])


SCH = []
for _nm, _c0 in (("rw_r", 0), ("rw_k", 384), ("rw_v", 768)):
    for _i in range(6):
        SCH.append((f"{_nm}{_i}", _c0 + 64 * _i, 64))
for _nm, _c0 in (("ml_q", 1152), ("ml_k", 1536)):
    for _i in range(4):
        SCH.append((f"{_nm}{_i}", _c0 + 96 * _i, 96))
NCONV = len(SCH)
for _i in range(4):
    SCH.append((f"ml_v{_i}", 1920 + 96 * _i, 96))
SCH.append(("ml_gl", 2688, 16))
SCH.append(("s5_u0", 2704, 128))
SCH.append(("s5_u1", 2832, 128))
SCH.append(("wa_dn", 2960, 128))
NSCH = len(SCH)
NOTH = NSCH - NCONV
ZALL = list(SCH)
for _i in range(4):
    ZALL.append((f"ml_o{_i}", 2304 + 96 * _i, 96))
ZALL.append(("g_dn", 3088, 128))
for _i in range(24):
    ZALL.append((f"br{_i}", 3216 + 128 * _i, 128))
NZALL = len(ZALL)
TWO_PI = 2.0 * np.pi


def emit_mixer(p, zs, prm, outs, rev, nctx=NCTX, nlat=NLAT):
    NS = nctx + nlat
    NB = NS // 128
    f32 = F32
    o_rw, o_bn, o_s5, o_ml = outs

    def nat(a, b):
        if not rev:
            return slice(a, b)
        if b <= nctx:
            return slice(nctx - b, nctx - a)
        return slice(nctx + NS - b, nctx + NS - a)

    def T(shape, name, dt=f32):
        return p.tile(shape, dt, "t_" + name)

    cw = T([128, NCONV, 9], "cw"); rwp = T([64, 5, 6], "rwp"); waup = T([128, 384], "waup")
    glb = T([16, 1], "glb"); sel = T([16, 8, 96], "sel"); s5p = T([128, 3, 8], "s5p")
    bz = T([128, 16, 128], "bz"); czc = T([128, 16, 128], "czc"); cst = T([128, 5, 128], "cst")
    ri = T([128, 129], "ri")
    for i, (t, d) in enumerate(((cw, "cw"), (rwp, "rwp"), (waup, "wa_up"), (glb, "glb"), (sel, "sel"), (s5p, "s5p"),
                                (bz, "bz"), (czc, "cz"), (cst, "cst"), (ri, "ri"))):
        p.dma(("sp", "act")[i % 2], t.ap(), prm[d], writes=[t])
    ident = cst.ap()[:, 0, :]
    m_su = cst.ap()[:, 1, :]
    m_sl = cst.ap()[:, 2, :]
    m_iu = cst.ap()[:, 3, :]
    m01 = cst.ap()[:, 4, :]
    ones = make_ones(p)
    mask5 = T([64, 5, 64], "mask5")
    for i, m in enumerate((m_su, m_sl, m_su, m_iu, m_iu)):
        p.op("dve", lambda e, i=i, m=m: e.tensor_copy(out=mask5.ap()[:, i, :], in_=m[0:64, 0:64]), reads=[cst], writes=[mask5])

    pss = [p.psum([128, 512], f32, f"pb{i}") for i in range(4)]
    psb = p.psum([128, 2, 1024], f32, "psbig")
    pcnt = [0]

    def PS():
        pcnt[0] += 1
        return pss[pcnt[0] % 4]

    ecnt = [0]

    def EW():
        ecnt[0] += 1
        return ("dve", "pool")[ecnt[0] % 2]

    def rr(x_ap, n, tmpf, tmpi, reads):
        p.op("dve", lambda e: e.tensor_scalar(out=tmpf, in0=x_ap, scalar1=1.0 / TWO_PI, scalar2=0.5, op0=ALU.mult,
                                              op1=ALU.add), reads=reads, writes=reads)
        p.op("dve", lambda e: e.tensor_copy(out=tmpi, in_=tmpf), reads=reads, writes=reads)
        p.op("dve", lambda e: e.tensor_copy(out=tmpf, in_=tmpi), reads=reads, writes=reads)
        p.op("dve", lambda e: e.scalar_tensor_tensor(out=x_ap, in0=tmpf, scalar=-TWO_PI, in1=x_ap, op0=ALU.mult,
                                                     op1=ALU.add), reads=reads, writes=reads)
        p.op("dve", lambda e: e.tensor_scalar(out=tmpf, in0=x_ap, scalar1=-np.pi, scalar2=TWO_PI, op0=ALU.is_lt,
                                              op1=ALU.mult), reads=reads, writes=reads)
        p.op("dve", lambda e: e.tensor_tensor(out=x_ap, in0=x_ap, in1=tmpf, op=ALU.add), reads=reads, writes=reads)
        p.op("dve", lambda e: e.tensor_scalar(out=tmpf, in0=x_ap, scalar1=np.pi, scalar2=-TWO_PI, op0=ALU.is_gt,
                                              op1=ALU.mult), reads=reads, writes=reads)
        p.op("dve", lambda e: e.tensor_tensor(out=x_ap, in0=x_ap, in1=tmpf, op=ALU.add), reads=reads, writes=reads)
        p.op("dve", lambda e: e.tensor_scalar(out=x_ap, in0=x_ap, scalar1=-3.14159, scalar2=3.14159, op0=ALU.max,
                                              op1=ALU.min), reads=reads, writes=reads)

    sp_ = T([128, 16, 8], "s5small")
    SM = lambda i: sp_.ap()[:, i, :]
    Vr = T([128, 8, 129], "Vr"); Vi = T([128, 8, 129], "Vi"); t5a = T([128, 8, 129], "t5a"); t5b = T([128, 8, 129], "t5b")
    ang, ang2 = Vr, Vi

    class _View:
        def __init__(self, buf, fn):
            self.buf, self.fn = buf, fn

        def ap(self):
            return self.fn(self.buf.ap())
    tf = _View(t5a, lambda a: a.rearrange("p k r -> p (k r)"))
    ti_ = _View(t5b, lambda a: a.rearrange("p k r -> p (k r)").bitcast(mybir.dt.int32))
    Ct = T([128, 8, 129], "Ct"); St = T([128, 8, 129], "St")
    T1re = T([128, 8, 128], "T1re"); T1im = T([128, 8, 128], "T1im"); RHO = T([128, 8, 128], "RHO")
    S5R = [sp_, s5p, Vr, Vi, t5a, t5b, Ct, St, T1re, T1im, RHO, ri, ones]
    lre, lim, ldt = s5p.ap()[:, 0, :], s5p.ap()[:, 1, :], s5p.ap()[:, 2, :]

    def o5(eng, fn):
        p.op(eng, fn, reads=S5R, writes=S5R)

    o5("dve", lambda e: e.tensor_scalar(out=lre, in0=lre, scalar1=-1e-4, scalar2=None, op0=ALU.min))
    o5("act", lambda e: e.activation(out=SM(0), in_=ldt, func=AF.Exp))
    o5("dve", lambda e: e.tensor_tensor(out=SM(1), in0=lre, in1=SM(0), op=ALU.mult))
    o5("act", lambda e: e.activation(out=SM(1), in_=SM(1), func=AF.Exp))
    o5("dve", lambda e: e.tensor_tensor(out=SM(2), in0=lim, in1=SM(0), op=ALU.mult))
    rr(SM(2), 8, tf.ap()[:, 0:8], ti_.ap()[:, 0:8], S5R)
    for k in range(8):
        o5("dve", lambda e, k=k: e.tensor_scalar(out=ang.ap()[:, k, :], in0=ri.ap(), scalar1=sp_.ap()[:, 2, k:k + 1],
                                                 scalar2=None, op0=ALU.mult))
    angf = ang.ap().rearrange("p k r -> p (k r)")
    ang2f = ang2.ap().rearrange("p k r -> p (k r)")
    o5("dve", lambda e: e.tensor_scalar(out=ang2f, in0=angf, scalar1=np.pi / 2, scalar2=None, op0=ALU.add))
    rr(angf, 8 * 129, tf.ap(), ti_.ap(), S5R)
    rr(ang2f, 8 * 129, tf.ap(), ti_.ap(), S5R)
    o5("act", lambda e: e.activation(out=St.ap().rearrange("p k r -> p (k r)"), in_=angf, func=AF.Sin))
    o5("act", lambda e: e.activation(out=Ct.ap().rearrange("p k r -> p (k r)"), in_=ang2f, func=AF.Sin))
    o5("dve", lambda e: e.tensor_tensor(out=SM(3), in0=SM(1), in1=Ct.ap()[:, :, 1], op=ALU.mult))
    o5("dve", lambda e: e.tensor_tensor(out=SM(4), in0=SM(1), in1=St.ap()[:, :, 1], op=ALU.mult))
    o5("dve", lambda e: e.tensor_scalar(out=SM(3), in0=SM(3), scalar1=-1.0, scalar2=None, op0=ALU.add))
    o5("dve", lambda e: e.tensor_tensor(out=SM(5), in0=lre, in1=lre, op=ALU.mult))
    o5("dve", lambda e: e.tensor_tensor(out=SM(6), in0=lim, in1=lim, op=ALU.mult))
    o5("dve", lambda e: e.tensor_tensor(out=SM(5), in0=SM(5), in1=SM(6), op=ALU.add))
    o5("dve", lambda e: e.reciprocal(out=SM(5), in_=SM(5)))
    o5("dve", lambda e: e.tensor_tensor(out=SM(6), in0=SM(3), in1=lre, op=ALU.mult))
    o5("dve", lambda e: e.tensor_tensor(out=SM(7), in0=SM(4), in1=lim, op=ALU.mult))
    o5("dve", lambda e: e.tensor_tensor(out=SM(6), in0=SM(6), in1=SM(7), op=ALU.add))
    o5("dve", lambda e: e.tensor_tensor(out=SM(8), in0=SM(6), in1=SM(5), op=ALU.mult))
    o5("dve", lambda e: e.tensor_tensor(out=SM(6), in0=SM(4), in1=lre, op=ALU.mult))
    o5("dve", lambda e: e.tensor_tensor(out=SM(7), in0=SM(3), in1=lim, op=ALU.mult))
    o5("dve", lambda e: e.tensor_tensor(out=SM(6), in0=SM(6), in1=SM(7), op=ALU.subtract))
    o5("dve", lambda e: e.tensor_tensor(out=SM(9), in0=SM(6), in1=SM(5), op=ALU.mult))
    o5("dve", lambda e: e.tensor_scalar(out=SM(10), in0=SM(8), scalar1=-1.0, scalar2=None, op0=ALU.mult))
    for k in range(8):
        C_ = Ct.ap()[:, k, 0:128]; S_ = St.ap()[:, k, 0:128]
        gre = sp_.ap()[:, 8, k:k + 1]; gim = sp_.ap()[:, 9, k:k + 1]; ngre = sp_.ap()[:, 10, k:k + 1]
        o5("dve", lambda e, k=k, C_=C_, gre=gre: e.tensor_scalar(out=T1re.ap()[:, k, :], in0=C_, scalar1=gre, scalar2=None, op0=ALU.mult))
        o5("dve", lambda e, k=k, S_=S_, gim=gim: e.scalar_tensor_tensor(out=T1re.ap()[:, k, :], in0=S_, scalar=gim, in1=T1re.ap()[:, k, :], op0=ALU.mult, op1=ALU.add))
        o5("dve", lambda e, k=k, C_=C_, gim=gim: e.tensor_scalar(out=T1im.ap()[:, k, :], in0=C_, scalar1=gim, scalar2=None, op0=ALU.mult))
        o5("dve", lambda e, k=k, S_=S_, ngre=ngre: e.scalar_tensor_tensor(out=T1im.ap()[:, k, :], in0=S_, scalar=ngre, in1=T1im.ap()[:, k, :], op0=ALU.mult, op1=ALU.add))
        o5("dve", lambda e, k=k: e.tensor_scalar(out=RHO.ap()[:, k, :], in0=ones.ap(), scalar1=sp_.ap()[:, 1, k:k + 1], scalar2=None, op0=ALU.mult))
    p.op("dve", lambda e: e.tensor_scalar(out=czc.ap()[:, 8:16, :], in0=czc.ap()[:, 8:16, :], scalar1=-1.0, scalar2=None,
                                          op0=ALU.mult), reads=[czc], writes=[czc])
    s5i = T([128, 2, 8], "s5init")
    p.op("pool", lambda e: e.memset(s5i.ap(), 0.0), writes=[s5i])
    Wr = T([128, 8, 128], "Wr"); Wi = T([128, 8, 128], "Wi")
    ys5 = T([128, 2, 128], "ys5")

    def s5_block(t0, zo):
        VrA = Vr.ap()[:, :, 0:128]; ViA = Vi.ap()[:, :, 0:128]; t5aA = t5a.ap()[:, :, 0:128]; t5bA = t5b.ap()[:, :, 0:128]
        bre = psb.ap()[:, 0, :].rearrange("p (k t) -> p k t", k=8)
        bim = psb.ap()[:, 1, :].rearrange("p (k t) -> p k t", k=8)
        for k in range(8):
            u = zo.ap()[:, 5 + k // 4, :]
            p.op("pe", lambda e, k=k, u=u: e.matmul(bre[:, k, :], lhsT=bz.ap()[:, k, :], rhs=u, start=True, stop=True),
                 reads=[bz, zo], writes=[psb])
            p.op("pe", lambda e, k=k, u=u: e.matmul(bim[:, k, :], lhsT=bz.ap()[:, 8 + k, :], rhs=u, start=True, stop=True),
                 reads=[bz, zo], writes=[psb])
        tt = lambda e, o, a, b, op: e.tensor_tensor(out=o, in0=a, in1=b, op=op)
        yield
        p.op("dve", lambda e: tt(e, t5aA, bre, T1re.ap(), ALU.mult), reads=[psb, T1re], writes=[t5a])
        p.op("dve", lambda e: tt(e, t5bA, bim, T1im.ap(), ALU.mult), reads=[psb, T1im], writes=[t5b])
        p.op("pool", lambda e: tt(e, VrA, t5aA, t5bA, ALU.subtract), reads=[t5a, t5b], writes=[Vr])
        p.op("dve", lambda e: tt(e, t5aA, bre, T1im.ap(), ALU.mult), reads=[psb, T1im], writes=[t5a])
        p.op("dve", lambda e: tt(e, t5bA, bim, T1re.ap(), ALU.mult), reads=[psb, T1re], writes=[t5b])
        p.op("pool", lambda e: tt(e, ViA, t5aA, t5bA, ALU.add), reads=[t5a, t5b], writes=[Vi])
        yield
        for k in range(8):
            p.op("dve", lambda e, k=k: e.tensor_tensor_scan(out=Wr.ap()[:, k, :], data0=RHO.ap()[:, k, :], data1=VrA[:, k, :],
                                                            initial=s5i.ap()[:, 0, k:k + 1], op0=ALU.mult, op1=ALU.add),
                 reads=[RHO, Vr, s5i], writes=[Wr])
            p.op("dve", lambda e, k=k: e.tensor_tensor_scan(out=Wi.ap()[:, k, :], data0=RHO.ap()[:, k, :], data1=ViA[:, k, :],
                                                            initial=s5i.ap()[:, 1, k:k + 1], op0=ALU.mult, op1=ALU.add),
                 reads=[RHO, Vi, s5i], writes=[Wi])
        yield
        wr_l = Wr.ap()[:, :, 127]; wi_l = Wi.ap()[:, :, 127]; c128 = Ct.ap()[:, :, 128]; s128 = St.ap()[:, :, 128]
        p.op("pool", lambda e: tt(e, SM(11), wr_l, c128, ALU.mult), reads=[Wr, Ct], writes=[sp_])
        p.op("pool", lambda e: tt(e, SM(12), wi_l, s128, ALU.mult), reads=[Wi, St, sp_], writes=[sp_])
        p.op("pool", lambda e: tt(e, s5i.ap()[:, 0, :], SM(11), SM(12), ALU.subtract), reads=[sp_], writes=[s5i])
        p.op("pool", lambda e: tt(e, SM(11), wr_l, s128, ALU.mult), reads=[Wr, St, sp_], writes=[sp_])
        p.op("pool", lambda e: tt(e, SM(12), wi_l, c128, ALU.mult), reads=[Wi, Ct, sp_], writes=[sp_])
        p.op("pool", lambda e: tt(e, s5i.ap()[:, 1, :], SM(11), SM(12), ALU.add), reads=[sp_], writes=[s5i])
        yield
        C3 = Ct.ap()[:, :, 0:128]; S3 = St.ap()[:, :, 0:128]
        p.op("dve", lambda e: tt(e, t5aA, Wr.ap(), C3, ALU.mult), reads=[Wr, Ct], writes=[t5a])
        p.op("pool", lambda e: tt(e, t5bA, Wi.ap(), S3, ALU.mult), reads=[Wi, St], writes=[t5b])
        p.op("dve", lambda e: tt(e, VrA, t5aA, t5bA, ALU.subtract), reads=[t5a, t5b], writes=[Vr])
        p.op("dve", lambda e: tt(e, t5aA, Wr.ap(), S3, ALU.mult), reads=[Wr, St], writes=[t5a])
        p.op("pool", lambda e: tt(e, t5bA, Wi.ap(), C3, ALU.mult), reads=[Wi, Ct], writes=[t5b])
        p.op("dve", lambda e: tt(e, ViA, t5aA, t5bA, ALU.add), reads=[t5a, t5b], writes=[Vi])
        yield
        py = PS()
        for mt in range(2):
            for i, k in enumerate(range(4 * mt, 4 * mt + 4)):
                p.op("pe", lambda e, k=k, mt=mt, i=i: e.matmul(py.ap()[:, mt * 128:(mt + 1) * 128], lhsT=czc.ap()[:, k, :],
                                                               rhs=VrA[:, k, :], start=(i == 0), stop=False),
                     reads=[czc, Vr], writes=[py])
                p.op("pe", lambda e, k=k, mt=mt, i=i: e.matmul(py.ap()[:, mt * 128:(mt + 1) * 128], lhsT=czc.ap()[:, 8 + k, :],
                                                               rhs=ViA[:, k, :], start=False, stop=(i == 3)),
                     reads=[czc, Vi], writes=[py])
        p.op("act", lambda e: e.copy(out=ys5.ap().rearrange("p m t -> p (m t)"), in_=py.ap()[:, 0:256]), reads=[py], writes=[ys5])
        store(o_s5, "m p t -> p m t", ys5, t0, 128, 2, "sp")
        yield

    WW = 322
    Wn = [T([128, NCONV, WW], "Wn0")]
    zo_t = [T([128, NOTH, 128], f"zo{i}") for i in range(2)]
    cz2 = [T([128, NCONV, 128], f"cz{i}") for i in range(2)]
    czb = [cz2[0]]
    ctmp = T([128, 128], "ctmp")
    zsv = zs.ap().rearrange("c p t -> p c t")
    HC = 7
    CGR = [(g * HC, min((g + 1) * HC, NCONV)) for g in range((NCONV + HC - 1) // HC)]
    if rev:
        wst = T([128, HC, WW], "wst")
        zst = T([128, NOTH, 128], "zst")
        ost = T([128, 6, 128], "ost")

    def store(obuf, pat, ytile, t0, npart, nh, q):
        dst = obuf.ap()[:, :, nat(t0, t0 + 128)].rearrange(pat)
        if not rev:
            p.dma(q, dst, ytile.ap(), reads=[ytile], writes=[obuf])
        else:
            p.op("pool", lambda e: e.tensor_copy(out=ost.ap()[:npart, :nh, :], in_=ytile.ap()[:, :, ::-1]), reads=[ytile], writes=[ost])
            p.dma(q, dst, ost.ap()[:npart, :nh, :], reads=[ost], writes=[obuf])

    def conv_block(j):
        t0 = j * 128
        W = Wn[0]
        cz = cz2[j % 2]
        zo = zo_t[j % 2]
        lo_r, hi_r = (0, nctx) if t0 < nctx else (nctx, NS)
        a, b = max(t0 - 128, lo_r), min(t0 - 128 + WW, hi_r)
        if a > t0 - 128 or b < t0 - 128 + WW:
            p.op("pool", lambda e: e.memset(W.ap(), 0.0), writes=[W])
        wa, wb = a - (t0 - 128), b - (t0 - 128)
        if not rev:
            p.dma("sp", W.ap()[:, :, wa:wb], zsv[:, 0:NCONV, a:b], reads=[zs], writes=[W])
            p.dma("act", zo.ap(), zsv[:, NCONV:NSCH, t0:t0 + 128], reads=[zs], writes=[zo])
        else:
            for hh, (c_lo, c_hi) in enumerate(CGR):
                p.dma(("sp", "act")[hh % 2], wst.ap()[:, 0:c_hi - c_lo, 0:b - a], zsv[:, c_lo:c_hi, nat(a, b)], reads=[zs], writes=[wst])
                p.op(("dve", "pool")[hh % 2], lambda e, c_lo=c_lo, c_hi=c_hi: e.tensor_copy(
                    out=W.ap()[:, c_lo:c_hi, wa:wb], in_=wst.ap()[:, 0:c_hi - c_lo, 0:b - a][:, :, ::-1]), reads=[wst], writes=[W])
            p.dma("act", zst.ap(), zsv[:, NCONV:NSCH, nat(t0, t0 + 128)], reads=[zs], writes=[zst])
            p.op("pool", lambda e: e.tensor_copy(out=zo.ap(), in_=zst.ap()[:, :, ::-1]), reads=[zst], writes=[zo])
        grid = t0 >= nctx
        for ci in range(NCONV):
            if ci % 2 == 0:
                yield
            eng = EW()
            nch = SCH[ci][2]
            o = cz.ap()[:nch, ci, :]
            wv = W.ap()[:nch, ci, :]
            ctr = 4
            p.op(eng, lambda e, o=o, wv=wv, ci=ci, nch=nch: e.tensor_scalar(
                out=o, in0=wv[:, 128:256], scalar1=cw.ap()[:nch, ci, ctr:ctr + 1], scalar2=None, op0=ALU.mult),
                reads=[W, cw], writes=[cz])
            taps = []
            if grid:
                for dy in (-1, 0, 1):
                    for dx in (-1, 0, 1):
                        if dy == 0 and dx == 0:
                            continue
                        taps.append((dy, dx, (dy + 1) * 3 + dx + 1))
            else:
                taps = [(0, -1, 3), (0, 1, 5)]
            for dy, dx, tp in taps:
                s = 128 + 64 * dy
                if grid and dx != 0:
                    src = wv[:, s:s + 128].rearrange("p (r c) -> p r c", c=64)
                    dst = o.rearrange("p (r c) -> p r c", c=64)
                    if dx == -1:
                        src = src[:, :, 0:63]; dst = dst[:, :, 1:64]
                    else:
                        src = src[:, :, 1:64]; dst = dst[:, :, 0:63]
                else:
                    src = wv[:, s + dx:s + dx + 128]; dst = o
                if eng == "dve":
                    p.op(eng, lambda e, src=src, dst=dst, ci=ci, tp=tp, nch=nch: e.scalar_tensor_tensor(
                        out=dst, in0=src, scalar=cw.ap()[:nch, ci, tp:tp + 1], in1=dst, op0=ALU.mult, op1=ALU.add),
                        reads=[W, cw, cz], writes=[cz])
                else:
                    if grid and dx != 0:
                        tmp = ctmp.ap()[:nch, :].rearrange("p (r c) -> p r c", c=64)[:, :, 0:63]
                    else:
                        tmp = ctmp.ap()[:nch, :]
                    p.op(eng, lambda e, src=src, tmp=tmp, ci=ci, tp=tp, nch=nch: e.tensor_scalar(
                        out=tmp, in0=src, scalar1=cw.ap()[:nch, ci, tp:tp + 1], scalar2=None, op0=ALU.mult),
                        reads=[W, cw], writes=[ctmp])
                    p.op(eng, lambda e, dst=dst, tmp=tmp: e.tensor_tensor(out=dst, in0=dst, in1=tmp, op=ALU.add),
                         reads=[ctmp, cz], writes=[cz])
        yield

    def decay_prep(nk, lw_ap, lw_reads, cs, G_, Ginv, Gend, Gex=None, lwsb=None):
        p.op("dve", lambda e: e.tensor_tensor_scan(out=cs.ap()[:nk, :], data0=m01[:nk, :], data1=lw_ap, initial=0.0,
                                                   op0=ALU.mult, op1=ALU.add), reads=[cst] + lw_reads, writes=[cs])
        p.op("act", lambda e: e.activation(out=G_.ap()[:nk, :], in_=cs.ap()[:nk, :], func=AF.Exp), reads=[cs], writes=[G_])
        p.op("act", lambda e: e.activation(out=Ginv.ap()[:nk, :], in_=cs.ap()[:nk, :], func=AF.Exp, scale=-1.0), reads=[cs], writes=[Ginv])
        for c in range(2):
            p.op("act", lambda e, c=c: e.activation(out=Gend.ap()[:nk, 64 * c:64 * c + 64], in_=cs.ap()[:nk, 64 * c:64 * c + 64],
                                                    func=AF.Exp, scale=-1.0, bias=cs.ap()[:nk, 64 * c + 63:64 * c + 64]),
                 reads=[cs], writes=[Gend])
        if Gex is not None:
            p.op("dve", lambda e: e.tensor_tensor(out=Gex.ap()[:nk, :], in0=cs.ap()[:nk, :], in1=lwsb, op=ALU.subtract),
                 reads=[cs] + lw_reads, writes=[Gex])
            p.op("act", lambda e: e.activation(out=Gex.ap()[:nk, :], in_=Gex.ap()[:nk, :], func=AF.Exp), reads=[Gex], writes=[Gex])

    rS = T([64, 6, 64], "rwS", ); p.op("pool", lambda e: e.memset(rS.ap(), 0.0), writes=[rS])
    tw = T([64, 128], "tw")
    NG = 3
    nm = ["sgw", "a", "kkv", "sq", "kap", "kd", "b", "cs", "Ginv", "Gend", "Gex"]
    R_ = {n: T([64, 128], "r_" + n) for n in nm}
    RPn = ("rt", "kt", "bt", "kkt", "Bh", "Kh", "G")
    RP = [{n: T([64, 128], f"rp{g}_" + n) for n in RPn} for g in range(NG)]
    tm4s = [T([64, 4, 64], f"tm4_{g}") for g in range(NG)]
    tt5s = [T([64, 5, 64], f"tt5_{g}") for g in range(NG)]
    Rrs = [T([64, 128], f"Rr{g}") for g in range(NG)]
    Xxs = [T([64, 128], f"Xx{g}") for g in range(NG)]
    nZs = [T([64, 64], f"nZ{g}") for g in range(NG)]
    Pbs = [[T([64, 2, 64], f"Pb{g}_{i}") for i in range(2)] for g in range(NG)]
    QPs = [T([64, 2, 64], f"QP{g}") for g in range(NG)]
    yrw = T([64, 6, 128], "yrw"); ybn = T([64, 6, 128], "ybn")
    NEG_E = -float(np.exp(-0.5))

    def rw_prep(h, g, zo):
        wadn = zo.ap()[:, 7, :]
        r = czb[0].ap()[:64, h, :]; k = czb[0].ap()[:64, 6 + h, :]; v = czb[0].ap()[:64, 12 + h, :]
        prm = lambda w: rwp.ap()[:, w, h:h + 1]
        A = lambda n: R_[n].ap()
        Q = lambda n: RP[g][n].ap()
        ps = PS()
        p.op("pe", lambda e: e.matmul(ps.ap()[:64, 0:128], lhsT=waup.ap()[0:64, h * 64:(h + 1) * 64], rhs=tw.ap(), start=True, stop=True),
             reads=[waup, tw], writes=[ps])
        p.op("pe", lambda e: e.matmul(ps.ap()[:64, 128:256], lhsT=waup.ap()[64:128, h * 64:(h + 1) * 64], rhs=wadn[64:128, :], start=True, stop=True),
             reads=[waup, zo], writes=[ps])
        p.op("act", lambda e: e.activation(out=A("sgw"), in_=ps.ap()[:64, 0:128], func=AF.Sigmoid, bias=prm(0)), reads=[ps, rwp], writes=[R_["sgw"]])
        p.op("act", lambda e: e.activation(out=A("a"), in_=ps.ap()[:64, 128:256], func=AF.Sigmoid, bias=prm(1)), reads=[ps, rwp], writes=[R_["a"]])
        p.op("dve", lambda e: e.tensor_scalar(out=A("sgw"), in0=A("sgw"), scalar1=NEG_E, scalar2=None, op0=ALU.mult), reads=[R_["sgw"]], writes=[R_["sgw"]])
        p.op("pool", lambda e: e.tensor_scalar(out=A("kkv"), in0=k, scalar1=prm(2), scalar2=None, op0=ALU.mult), reads=[czb[0], rwp], writes=[R_["kkv"]])
        p.op("act", lambda e: e.activation(out=A("sq"), in_=A("kkv"), func=AF.Square), reads=[R_["kkv"]], writes=[R_["sq"]])
        ps2 = PS()
        p.op("pe", lambda e: e.matmul(ps2.ap()[:64, 0:128], lhsT=ones.ap()[0:64, 0:64], rhs=A("sq"), start=True, stop=True), reads=[ones, R_["sq"]], writes=[ps2])
        p.op("dve", lambda e: e.tensor_scalar(out=A("sq"), in0=ps2.ap()[:64, 0:128], scalar1=1e-24, scalar2=None, op0=ALU.max), reads=[ps2], writes=[R_["sq"]])
        p.op("act", lambda e: e.activation(out=A("sq"), in_=A("sq"), func=AF.Sqrt), reads=[R_["sq"]], writes=[R_["sq"]])
        p.op("dve", lambda e: e.reciprocal(out=A("sq"), in_=A("sq")), reads=[R_["sq"]], writes=[R_["sq"]])
        p.op("dve", lambda e: e.tensor_tensor(out=A("kap"), in0=A("kkv"), in1=A("sq"), op=ALU.mult), reads=[R_["kkv"], R_["sq"]], writes=[R_["kap"]])
        p.op("pool", lambda e: e.tensor_scalar(out=A("kd"), in0=A("a"), scalar1=-1.0, scalar2=prm(3), op0=ALU.add, op1=ALU.mult), reads=[R_["a"], rwp], writes=[R_["kd"]])
        p.op("dve", lambda e: e.scalar_tensor_tensor(out=A("kd"), in0=A("kd"), scalar=1.0, in1=k, op0=ALU.add, op1=ALU.mult), reads=[R_["kd"], czb[0]], writes=[R_["kd"]])
        p.op("pool", lambda e: e.tensor_tensor(out=A("b"), in0=A("a"), in1=A("kap"), op=ALU.mult), reads=[R_["a"], R_["kap"]], writes=[R_["b"]])
        p.op("dve", lambda e: e.scalar_tensor_tensor(out=A("kkv"), in0=r, scalar=prm(4), in1=A("kd"), op0=ALU.mult, op1=ALU.mult), reads=[czb[0], rwp, R_["kd"]], writes=[R_["kkv"]])
        p.op("pe", lambda e: e.matmul(ps2.ap()[:64, 128:256], lhsT=ones.ap()[0:64, 0:64], rhs=A("kkv"), start=True, stop=True), reads=[ones, R_["kkv"]], writes=[ps2])
        p.op("dve", lambda e: e.tensor_tensor(out=ybn.ap()[:, h, :], in0=ps2.ap()[:64, 128:256], in1=v, op=ALU.mult), reads=[ps2, czb[0]], writes=[ybn])
        decay_prep(64, A("sgw"), [R_["sgw"]], R_["cs"], RP[g]["G"], R_["Ginv"], R_["Gend"], R_["Gex"], A("sgw"))
        for (o_, x_, xr, g_) in (("rt", r, czb[0], RP[g]["G"]), ("kt", A("kap"), R_["kap"], R_["Gex"]), ("bt", A("b"), R_["b"], R_["Ginv"]),
                                 ("kkt", A("kd"), R_["kd"], R_["Ginv"]), ("Bh", A("b"), R_["b"], R_["Gend"]), ("Kh", A("kd"), R_["kd"], R_["Gend"])):
            p.op(EW(), lambda e, o_=o_, x_=x_, g_=g_: e.tensor_tensor(out=Q(o_), in0=x_, in1=g_.ap(), op=ALU.mult),
                 reads=[xr, g_], writes=[RP[g][o_]])

    def rw_chunk(h, g, c):
        v = czb[0].ap()[:64, 12 + h, :]
        Q = lambda n: RP[g][n].ap()
        RQ = lambda n: RP[g][n]
        tm4, tt5, Rr, Xx, nZ, Pb, QP = tm4s[g], tt5s[g], Rrs[g], Xxs[g], nZs[g], Pbs[g], QPs[g]
        cs_ = slice(64 * c, 64 * c + 64)
        pt = PS()
        for i, (src, rd) in enumerate(((v[:, cs_], czb[0]), (Q("kt")[:, cs_], RQ("kt")), (Q("Bh")[:, cs_], RQ("Bh")), (Q("Kh")[:, cs_], RQ("Kh")))):
            p.op("pe", lambda e, i=i, src=src: e.transpose(pt.ap()[:64, 64 * i:64 * i + 64], src, ident[0:64, 0:64]),
                 reads=[rd, cst], writes=[pt])
        p.op("act", lambda e: e.copy(out=tm4.ap().rearrange("p a b -> p (a b)"), in_=pt.ap()[:64, 0:256]), reads=[pt], writes=[tm4])
        Vtm = tm4.ap()[:, 0, :]; Ktm = tm4.ap()[:, 1, :]; Btm = tm4.ap()[:, 2, :]; Khtm = tm4.ap()[:, 3, :]
        p5 = PS()
        for i, (l_, r_) in enumerate((("bt", "kt"), ("kt", "bt"), ("kkt", "kt"), ("bt", "rt"), ("kkt", "rt"))):
            p.op("pe", lambda e, i=i, l_=l_, r_=r_: e.matmul(p5.ap()[:64, 64 * i:64 * i + 64], lhsT=Q(l_)[:, cs_], rhs=Q(r_)[:, cs_], start=True, stop=True),
                 reads=[RQ(l_), RQ(r_)], writes=[p5])
        p.op("dve", lambda e: e.tensor_tensor(out=tt5.ap().rearrange("p a b -> p (a b)"), in0=p5.ap()[:64, 0:320],
                                              in1=mask5.ap().rearrange("p a b -> p (a b)"), op=ALU.mult), reads=[p5, mask5], writes=[tt5])
        U = tt5.ap()[:, 0, :]; L = tt5.ap()[:, 1, :]; LkT = tt5.ap()[:, 2, :]; AbrT = tt5.ap()[:, 3, :]; AkrT = tt5.ap()[:, 4, :]
        yield
        p6 = PS()
        p.op("pe", lambda e: e.matmul(p6.ap()[:64, 0:64], lhsT=LkT, rhs=Vtm, start=True, stop=True), reads=[tt5, tm4], writes=[p6])
        p.op("act", lambda e: e.copy(out=Rr.ap()[:, 64:128], in_=p6.ap()[:64, 0:64]), reads=[p6], writes=[Rr])
        p.op("pool", lambda e: e.tensor_copy(out=Rr.ap()[:, 0:64], in_=Ktm), reads=[tm4], writes=[Rr])
        yield
        p7 = PS()
        p.op("pe", lambda e: e.matmul(p7.ap()[:64, 0:128], lhsT=U, rhs=Rr.ap(), start=True, stop=True), reads=[tt5, Rr], writes=[p7])
        p.op("dve", lambda e: e.tensor_tensor(out=Xx.ap(), in0=Rr.ap(), in1=p7.ap()[:64, 0:128], op=ALU.subtract), reads=[Rr, p7], writes=[Xx])
        Pc, PTc, Prd = L, U, [tt5]
        for lvl in range(5):
            pn = Pb[lvl % 2]
            pq = PS()
            last = lvl == 4
            if not last:
                p.op("pe", lambda e, Pc=Pc, PTc=PTc: e.matmul(pq.ap()[:64, 0:64], lhsT=PTc, rhs=Pc, start=True, stop=True), reads=Prd, writes=[pq])
            p.op("pe", lambda e, Pc=Pc, PTc=PTc: e.matmul(pq.ap()[:64, 64:128], lhsT=Pc, rhs=PTc, start=True, stop=True), reads=Prd, writes=[pq])
            if not last:
                p.op("act", lambda e, pn=pn: e.copy(out=pn.ap().rearrange("p a b -> p (a b)"), in_=pq.ap()[:64, 0:128]), reads=[pq], writes=[pn])
            else:
                p.op("act", lambda e, pn=pn: e.copy(out=pn.ap()[:, 1, :], in_=pq.ap()[:64, 64:128]), reads=[pq], writes=[pn])
            Pc, PTc, Prd = pn.ap()[:, 0, :], pn.ap()[:, 1, :], [pn]
            yield
            px = PS()
            p.op("pe", lambda e, PTc=PTc: e.matmul(px.ap()[:64, 0:128], lhsT=PTc, rhs=Xx.ap(), start=True, stop=True), reads=Prd + [Xx], writes=[px])
            p.op("dve", lambda e: e.tensor_tensor(out=Xx.ap(), in0=Xx.ap(), in1=px.ap()[:64, 0:128], op=ALU.add), reads=[Xx, px], writes=[Xx])
            yield
        Gm = Xx.ap()[:, 0:64]
        p.op("act", lambda e: e.mul(out=nZ.ap(), in_=Xx.ap()[:, 64:128], mul=-1.0), reads=[Xx], writes=[nZ])
        p8 = PS()
        p.op("pe", lambda e: e.matmul(p8.ap()[:64, 0:64], lhsT=Gm, rhs=AbrT, start=True, stop=True), reads=[Xx, tt5], writes=[p8])
        p.op("pe", lambda e: e.matmul(p8.ap()[:64, 64:128], lhsT=Gm, rhs=Btm, start=True, stop=True), reads=[Xx, tm4], writes=[p8])
        p.op("dve", lambda e: e.tensor_tensor(out=QP.ap()[:, 0, :], in0=Q("rt")[:, cs_], in1=p8.ap()[:64, 0:64], op=ALU.subtract), reads=[RQ("rt"), p8], writes=[QP])
        p.op("dve", lambda e: e.scalar_tensor_tensor(out=QP.ap()[:, 1, :], in0=ident[0:64, 0:64], scalar=Q("G")[:, 64 * c + 63:64 * c + 64],
                                                     in1=p8.ap()[:64, 64:128], op0=ALU.mult, op1=ALU.subtract), reads=[cst, RQ("G"), p8], writes=[QP])
        yield
        p9 = PS()
        p.op("pe", lambda e: e.matmul(p9.ap()[:64, 0:64], lhsT=rS.ap()[:, h, :], rhs=QP.ap()[:, 0, :], start=True, stop=False), reads=[rS, QP], writes=[p9])
        p.op("pe", lambda e: e.matmul(p9.ap()[:64, 0:64], lhsT=Vtm, rhs=AkrT, start=False, stop=False), reads=[tm4, tt5], writes=[p9])
        p.op("pe", lambda e: e.matmul(p9.ap()[:64, 0:64], lhsT=nZ.ap(), rhs=AbrT, start=False, stop=True), reads=[nZ, tt5], writes=[p9])
        p.op("act", lambda e: e.copy(out=yrw.ap()[:, h, cs_], in_=p9.ap()[:64, 0:64]), reads=[p9], writes=[yrw])
        p10 = PS()
        p.op("pe", lambda e: e.matmul(p10.ap()[:64, 0:64], lhsT=QP.ap()[:, 1, :], rhs=rS.ap()[:, h, :], start=True, stop=False), reads=[QP, rS], writes=[p10])
        p.op("pe", lambda e: e.matmul(p10.ap()[:64, 0:64], lhsT=Khtm, rhs=Vtm, start=False, stop=False), reads=[tm4], writes=[p10])
        p.op("pe", lambda e: e.matmul(p10.ap()[:64, 0:64], lhsT=Btm, rhs=nZ.ap(), start=False, stop=True), reads=[tm4, nZ], writes=[p10])
        p.op("dve", lambda e: e.tensor_copy(out=rS.ap()[:, h, :], in_=p10.ap()[:64, 0:64]), reads=[p10], writes=[rS])
        yield

    def run_interleaved(gens):
        alive = list(gens)
        while alive:
            nxt = []
            for gen in alive:
                try:
                    next(gen)
                    nxt.append(gen)
                except StopIteration:
                    pass
            alive = nxt

    def rwkv_block(t0, zo):
        wadn = zo.ap()[:, 7, :]
        p.op("act", lambda e: e.activation(out=tw.ap(), in_=wadn[0:64, :], func=AF.Tanh), reads=[zo], writes=[tw])
        for grp in range(6 // NG):
            heads = list(range(grp * NG, (grp + 1) * NG))
            for g, h in enumerate(heads):
                rw_prep(h, g, zo)
                yield
            for c in range(2):
                alive = [rw_chunk(h, g, c) for g, h in enumerate(heads)]
                while alive:
                    nxt = []
                    for gen in alive:
                        try:
                            next(gen)
                            nxt.append(gen)
                        except StopIteration:
                            pass
                    alive = nxt
                    yield
        store(o_rw, "h p t -> p h t", yrw, t0, 64, 6, "sp")
        store(o_bn, "h p t -> p h t", ybn, t0, 64, 6, "act")
        yield

    mS = T([96, 4, 192], "mlS"); p.op("pool", lambda e: e.memset(mS.ap(), 0.0), writes=[mS])
    gl1 = T([16, 128], "gl1"); gl2 = T([16, 128], "gl2")
    mn = ["q", "ks", "ei", "kp", "cs", "G", "Ginv", "Gend", "rt", "kkt", "Kh", "den"]
    M_ = {n: T([96, 128], "m_" + n) for n in mn}
    mtm = T([64, 2, 96], "mtm"); makr = T([64, 64], "makr")
    yml = T([96, 4, 128], "yml")
    KSC = float(96 ** -0.5)

    def mlstm_block(t0, zo):
        gl = zo.ap()[0:16, 4, :]
        p.op("dve", lambda e: e.tensor_scalar(out=gl1.ap(), in0=gl, scalar1=glb.ap()[:, 0:1], scalar2=None, op0=ALU.add), reads=[zo, glb], writes=[gl1])
        p.op("act", lambda e: e.activation(out=gl2.ap(), in_=gl1.ap(), func=AF.Sigmoid), reads=[gl1], writes=[gl2])
        p.op("act", lambda e: e.activation(out=gl2.ap(), in_=gl2.ap(), func=AF.Ln), reads=[gl2], writes=[gl2])
        for h in range(4):
            B = lambda n: M_[n].ap()
            q = czb[0].ap()[:96, 18 + h, :]; k = czb[0].ap()[:96, 22 + h, :]; v = zo.ap()[:96, h, :]
            ps = PS()
            p.op("pe", lambda e: e.matmul(ps.ap()[:96, 0:128], lhsT=sel.ap()[:, 4 + h, :], rhs=gl2.ap(), start=True, stop=True), reads=[sel, gl2], writes=[ps])
            p.op("pe", lambda e: e.matmul(ps.ap()[:96, 128:256], lhsT=sel.ap()[:, h, :], rhs=gl1.ap(), start=True, stop=True), reads=[sel, gl1], writes=[ps])
            p.op("act", lambda e: e.activation(out=B("ei"), in_=ps.ap()[:96, 128:256], func=AF.Exp), reads=[ps], writes=[M_["ei"]])
            p.op("act", lambda e: e.activation(out=B("q"), in_=q, func=AF.Silu), reads=[czb[0]], writes=[M_["q"]])
            p.op("act", lambda e: e.activation(out=B("ks"), in_=k, func=AF.Silu), reads=[czb[0]], writes=[M_["ks"]])
            p.op("dve", lambda e: e.scalar_tensor_tensor(out=B("kp"), in0=B("ks"), scalar=KSC, in1=B("ei"), op0=ALU.mult, op1=ALU.mult), reads=[M_["ks"], M_["ei"]], writes=[M_["kp"]])
            decay_prep(96, ps.ap()[:96, 0:128], [ps], M_["cs"], M_["G"], M_["Ginv"], M_["Gend"])
            for (o_, x_, g_) in (("rt", "q", "G"), ("kkt", "kp", "Ginv"), ("Kh", "kp", "Gend")):
                p.op(EW(), lambda e, o_=o_, x_=x_, g_=g_: e.tensor_tensor(out=B(o_), in0=B(x_), in1=B(g_), op=ALU.mult), reads=[M_[x_], M_[g_]], writes=[M_[o_]])
            yield
            for c in range(2):
                cs_ = slice(64 * c, 64 * c + 64)
                pt = PS()
                p.op("pe", lambda e: e.transpose(pt.ap()[:64, 0:96], v[:, cs_], ident[0:96, 0:96]), reads=[zo, cst], writes=[pt])
                p.op("pe", lambda e: e.transpose(pt.ap()[:64, 96:192], B("Kh")[:, cs_], ident[0:96, 0:96]), reads=[M_["Kh"], cst], writes=[pt])
                p.op("act", lambda e: e.copy(out=mtm.ap().rearrange("p a b -> p (a b)"), in_=pt.ap()[:64, 0:192]), reads=[pt], writes=[mtm])
                Vtm = mtm.ap()[:, 0, :]; Khtm = mtm.ap()[:, 1, :]
                yield
                pa = PS()
                p.op("pe", lambda e: e.matmul(pa.ap()[:64, 0:64], lhsT=B("kkt")[:, cs_], rhs=B("rt")[:, cs_], start=True, stop=True), reads=[M_["kkt"], M_["rt"]], writes=[pa])
                p.op("dve", lambda e: e.tensor_tensor(out=makr.ap(), in0=pa.ap()[:64, 0:64], in1=m_iu[0:64, 0:64], op=ALU.mult), reads=[pa, cst], writes=[makr])
                yield
                py = PS()
                p.op("pe", lambda e: e.matmul(py.ap()[:96, 0:64], lhsT=mS.ap()[:, h, 0:96], rhs=B("rt")[:, cs_], start=True, stop=False), reads=[mS, M_["rt"]], writes=[py])
                p.op("pe", lambda e: e.matmul(py.ap()[:96, 0:64], lhsT=Vtm, rhs=makr.ap(), start=False, stop=True), reads=[mtm, makr], writes=[py])
                p.op("pe", lambda e: e.matmul(py.ap()[:96, 64:128], lhsT=mS.ap()[:, h, 96:192], rhs=B("rt")[:, cs_], start=True, stop=False), reads=[mS, M_["rt"]], writes=[py])
                p.op("pe", lambda e: e.matmul(py.ap()[:96, 64:128], lhsT=ones.ap()[0:64, 0:96], rhs=makr.ap(), start=False, stop=True), reads=[ones, makr], writes=[py])
                p.op("act", lambda e: e.activation(out=B("den")[:, 0:64], in_=py.ap()[:96, 64:128], func=AF.Abs), reads=[py], writes=[M_["den"]])
                p.op("dve", lambda e: e.tensor_scalar(out=B("den")[:, 0:64], in0=B("den")[:, 0:64], scalar1=1.0, scalar2=None, op0=ALU.max), reads=[M_["den"]], writes=[M_["den"]])
                p.op("dve", lambda e: e.reciprocal(out=B("den")[:, 0:64], in_=B("den")[:, 0:64]), reads=[M_["den"]], writes=[M_["den"]])
                p.op("dve", lambda e: e.tensor_tensor(out=yml.ap()[:, h, cs_], in0=py.ap()[:96, 0:64], in1=B("den")[:, 0:64], op=ALU.mult), reads=[py, M_["den"]], writes=[yml])
                yield
                pu = PS()
                p.op("pe", lambda e: e.matmul(pu.ap()[:96, 0:96], lhsT=Khtm, rhs=Vtm, start=True, stop=True), reads=[mtm], writes=[pu])
                p.op("pe", lambda e: e.matmul(pu.ap()[:96, 96:192], lhsT=Khtm, rhs=ones.ap()[0:64, 0:96], start=True, stop=True), reads=[mtm, ones], writes=[pu])
                p.op("dve", lambda e: e.scalar_tensor_tensor(out=mS.ap()[:, h, :], in0=mS.ap()[:, h, :], scalar=B("G")[:, 64 * c + 63:64 * c + 64],
                                                             in1=pu.ap()[:96, 0:192], op0=ALU.mult, op1=ALU.add), reads=[mS, M_["G"], pu], writes=[mS])
        store(o_ml, "h p t -> p h t", yml, t0, 96, 4, "act")
        yield

    for _ in conv_block(0):
        pass
    for j in range(NB):
        czb[0] = cz2[j % 2]
        zo = zo_t[j % 2]
        th = [rwkv_block(j * 128, zo), mlstm_block(j * 128, zo), s5_block(j * 128, zo)]
        if j + 1 < NB:
            th.append(conv_block(j + 1))
        run_interleaved(th)


def mixer_consts():
    a = np.arange(128)
    ident = np.eye(128, dtype=np.float32)
    su = (a[:, None] < a[None, :]).astype(np.float32)
    sl = (a[:, None] > a[None, :]).astype(np.float32)
    iu = (a[:, None] <= a[None, :]).astype(np.float32)
    m01 = np.ones((128, 128), np.float32)
    m01[:, 0] = 0.0
    m01[:, 64] = 0.0
    cst = np.stack([ident, su, sl, iu, m01], 1).copy()
    ri = np.broadcast_to(np.arange(129, dtype=np.float32), (128, 129)).copy()
    return cst, ri


def mixer_params(P, i, d):
    m = {}
    cwf = P["conv_w"][i]
    if d == 1:
        cwf = cwf[::-1, ::-1]
    cw = np.zeros((128, NCONV, 9), np.float32)
    for ci in range(NCONV):
        _, c0, n = SCH[ci]
        cw[:n, ci, :] = cwf[:, :, c0:c0 + n].reshape(9, n).T
    m["cw"] = cw
    rwp = np.stack([P["rw_w0"][i, d].reshape(6, 64).T, P["rw_a0"][i, d].reshape(6, 64).T, P["rw_k_k"][i].reshape(6, 64).T,
                    P["rw_k_a"][i].reshape(6, 64).T, P["rw_r_k"][i].T], 1)
    m["rwp"] = np.ascontiguousarray(rwp, np.float32)
    m["wa_up"] = np.concatenate([P["rw_w_up"][i, d], P["rw_a_up"][i, d]], 0).astype(np.float32)
    m["glb"] = P["ml_gate_b"][i].reshape(16, 1).astype(np.float32)
    sel = np.zeros((16, 8, 96), np.float32)
    for j in range(8):
        sel[d * 8 + j, j, :] = 1.0
    m["sel"] = sel
    s5p = np.zeros((128, 3, 8), np.float32)
    bz = np.zeros((128, 16, 128), np.float32)
    cz = np.zeros((128, 16, 128), np.float32)
    for k in range(8):
        for gl2 in range(2):
            g = 2 * k + gl2
            js = slice(gl2 * 64, gl2 * 64 + 64)
            s5p[js, 0, k] = P["s5_a_re"][i, d, g]
            s5p[js, 1, k] = P["s5_a_im"][i, d, g]
            s5p[js, 2, k] = P["s5_log_dt"][i, d, g]
            r0 = (g % 8) * 16
            bz[r0:r0 + 16, k, js] = P["s5_b_re"][i, d, g].T
            bz[r0:r0 + 16, 8 + k, js] = P["s5_b_im"][i, d, g].T
            cz[js, k, r0:r0 + 16] = P["s5_c_re"][i, d, g].T
            cz[js, 8 + k, r0:r0 + 16] = P["s5_c_im"][i, d, g].T
    m["s5p"] = s5p
    m["bz"] = bz
    m["cz"] = cz
    cst, ri = mixer_consts()
    m["cst"] = cst
    m["ri"] = ri
    return m


def mixer_zs(zseq):
    NS = zseq.shape[0]
    zs = np.zeros((NSCH, 128, NS), np.float32)
    for ci, (_, c0, n) in enumerate(SCH):
        zs[ci, :n, :] = zseq[:, c0:c0 + n].T
    return zs


def emit_merge(p, x1T, oT, modT, lng_ap, lnb_ap, outs2, zall, zallb, mp):
    f32 = F32

    def T(shape, name, dt=f32):
        return p.tile(shape, dt, "g_" + name)

    mod = T([128, 72, 2], "mod"); g_sb = T([128, 8], "lng"); b_sb = T([128, 8], "lnb")
    gn = T([64, 2, 6], "gn"); gup = T([128, 384], "gup"); s5d = T([128, 2, 2], "s5d"); gluw = T([128, 2, 256], "gluw")
    mlg = T([96, 4], "mlg"); brb = T([128, 24], "brb")
    p.dma("sp", mod.ap(), modT.ap(), reads=[modT], writes=[mod])
    for i, (t, d) in enumerate(((g_sb, lng_ap), (b_sb, lnb_ap), (gn, mp["gn"]), (gup, mp["g_up"]), (s5d, mp["s5d"]),
                                (gluw, mp["glu_w"]), (mlg, mp["mlg"]), (brb, mp["brb"]))):
        p.dma(("sp", "act")[i % 2], t.ap(), d, writes=[t])
    ones = make_ones(p)
    uprw = T([64, 6, 1024], "uprw", BF16); ups5 = T([128, 2, 1024], "ups5", BF16); upml = T([96, 4, 1024], "upml", BF16)
    wout = T([128, 8, 1024], "wout", BF16)
    stage = [T([128, 1024], f"stage{i}") for i in range(2)]
    k = 0
    for h in range(6):
        st = stage[k % 2]
        p.dma(("sp", "act")[k % 2], st.ap()[:64, :], mp["up_rw"][:, h, :], writes=[st])
        p.op("dve", lambda e, h=h, st=st: e.tensor_copy(out=uprw.ap()[:, h, :], in_=st.ap()[:64, :]), reads=[st], writes=[uprw]); k += 1
    for h in range(2):
        st = stage[k % 2]
        p.dma(("sp", "act")[k % 2], st.ap(), mp["up_s5"][:, h, :], writes=[st])
        p.op("dve", lambda e, h=h, st=st: e.tensor_copy(out=ups5.ap()[:, h, :], in_=st.ap()), reads=[st], writes=[ups5]); k += 1
    for h in range(4):
        st = stage[k % 2]
        p.dma(("sp", "act")[k % 2], st.ap()[:96, :], mp["up_ml"][:, h, :], writes=[st])
        p.op("dve", lambda e, h=h, st=st: e.tensor_copy(out=upml.ap()[:, h, :], in_=st.ap()[:96, :]), reads=[st], writes=[upml]); k += 1
    for h in range(8):
        st = stage[k % 2]
        p.dma(("sp", "act")[k % 2], st.ap(), mp["w_out"][:, h, :], writes=[st])
        p.op("dve", lambda e, h=h, st=st: e.tensor_copy(out=wout.ap()[:, h, :], in_=st.ap()), reads=[st], writes=[wout]); k += 1

    x_sb = T([128, 8, TT], "x"); yrw = T([64, 2, 6, TT], "yrw"); ybn = T([64, 2, 6, TT], "ybn")
    ys5 = T([128, 2, 2, TT], "ys5"); yml = T([96, 2, 4, TT], "yml"); zg = T([128, 31, TT], "zg")
    rwy = T([64, 6, TT], "rwy", BF16); s5y = T([128, 2, TT], "s5y", BF16); mly = T([96, 4, TT], "mly", BF16)
    ym = T([128, 8, TT], "ym", BF16)
    yg = T([128, 2, TT], "yg")
    ta = T([128, TT], "ta"); tb = T([128, TT], "tb"); tc = T([128, TT], "tc"); td = T([128, TT], "td"); sgd = T([128, TT], "sgd")
    y_sb = T([128, 8, TT], "y"); ysq = T([128, 8, TT], "ysq"); mean = T([128, TT], "mean"); rstd = T([128, TT], "rstd")
    pss = [p.psum([128, 512], f32, f"pg{i}") for i in range(6)]
    ps1 = p.psum([128, 512], f32, "ps1"); ps2 = p.psum([128, 512], f32, "ps2")
    pc = [0]

    def PS():
        pc[0] += 1
        return pss[pc[0] % 6]

    def std_part(x, np_, n, eps, scale_ap, out_ap, out_buf):
        p.op("act", lambda e: e.activation(out=tb.ap()[:np_, :n], in_=x, func=AF.Square), reads=[ta], writes=[tb])
        q = PS()
        p.op("pe", lambda e: e.matmul(q.ap()[:np_, 0:n], lhsT=ones.ap()[0:np_, 0:np_], rhs=x, start=True, stop=True), reads=[ones, ta], writes=[q])
        p.op("pe", lambda e: e.matmul(q.ap()[:np_, 256:256 + n], lhsT=ones.ap()[0:np_, 0:np_], rhs=tb.ap()[:np_, :n], start=True, stop=True), reads=[ones, tb], writes=[q])
        p.op("dve", lambda e: e.tensor_scalar(out=tc.ap()[:np_, :n], in0=q.ap()[:np_, 0:n], scalar1=1.0 / np_, scalar2=None, op0=ALU.mult), reads=[q], writes=[tc])
        p.op("dve", lambda e: e.tensor_tensor(out=td.ap()[:np_, :n], in0=tc.ap()[:np_, :n], in1=tc.ap()[:np_, :n], op=ALU.mult), reads=[tc], writes=[td])
        p.op("dve", lambda e: e.scalar_tensor_tensor(out=td.ap()[:np_, :n], in0=q.ap()[:np_, 256:256 + n], scalar=1.0 / np_, in1=td.ap()[:np_, :n], op0=ALU.mult, op1=ALU.subtract), reads=[q, td], writes=[td])
        p.op("dve", lambda e: e.tensor_scalar(out=td.ap()[:np_, :n], in0=td.ap()[:np_, :n], scalar1=eps, scalar2=None, op0=ALU.add), reads=[td], writes=[td])
        p.op("act", lambda e: e.activation(out=td.ap()[:np_, :n], in_=td.ap()[:np_, :n], func=AF.Sqrt), reads=[td], writes=[td])
        p.op("dve", lambda e: e.reciprocal(out=td.ap()[:np_, :n], in_=td.ap()[:np_, :n]), reads=[td], writes=[td])
        p.op("dve", lambda e: e.tensor_tensor(out=x, in0=x, in1=tc.ap()[:np_, :n], op=ALU.subtract), reads=[ta, tc], writes=[ta])
        p.op("dve", lambda e: e.scalar_tensor_tensor(out=out_ap, in0=x, scalar=scale_ap, in1=td.ap()[:np_, :n], op0=ALU.mult, op1=ALU.mult), reads=[ta, td, gn, mlg], writes=[out_buf])

    xv = x1T.ap().rearrange("c p t -> p c t")
    ov = oT.ap().rearrange("c p t -> p c t")
    GC = 2.0 * float(np.sqrt(2.0 / np.pi))
    for ti, (t0, n, wh) in enumerate(TILES):
        sl = slice(t0, t0 + n)
        p.dma("sp", x_sb.ap()[:, :, :n], xv[:, :, sl], reads=[x1T], writes=[x_sb])
        for d in range(2):
            o_rw, o_bn, o_s5, o_ml = outs2[d]
            p.dma("act", yrw.ap()[:, d, :, :n], o_rw.ap()[:, :, sl].rearrange("h p t -> p h t"), reads=[o_rw], writes=[yrw])
            p.dma("sp", ybn.ap()[:, d, :, :n], o_bn.ap()[:, :, sl].rearrange("h p t -> p h t"), reads=[o_bn], writes=[ybn])
            p.dma("act", ys5.ap()[:, d, :, :n], o_s5.ap()[:, :, sl].rearrange("h p t -> p h t"), reads=[o_s5], writes=[ys5])
            p.dma("sp", yml.ap()[:, d, :, :n], o_ml.ap()[:, :, sl].rearrange("h p t -> p h t"), reads=[o_ml], writes=[yml])
        zav = zall.ap().rearrange("c p t -> p c t")
        zbv = zallb.ap().rearrange("c p t -> p c t")
        p.dma("act", zg.ap()[:, 0:29, :n], zbv[:, :, sl], reads=[zallb], writes=[zg])
        p.dma("sp", zg.ap()[:, 29:31, :n], zav[:, 31:33, sl], reads=[zall], writes=[zg])
        p.op("act", lambda e: e.activation(out=sgd.ap()[:, :n], in_=zg.ap()[:, 4, :n], func=AF.Sigmoid), reads=[zg], writes=[sgd])
        for h in range(6):
            x = ta.ap()[:64, :n]
            p.op("dve", lambda e, h=h: e.tensor_tensor(out=x, in0=yrw.ap()[:, 0, h, :n], in1=yrw.ap()[:, 1, h, :n], op=ALU.add), reads=[yrw], writes=[ta])
            std_part(x, 64, n, 64e-5, gn.ap()[:, 0, h:h + 1], x, ta)
            p.op("dve", lambda e, h=h: e.scalar_tensor_tensor(out=x, in0=x, scalar=gn.ap()[:, 1, h:h + 1], in1=ybn.ap()[:, 0, h, :n], op0=ALU.add, op1=ALU.add), reads=[ta, gn, ybn], writes=[ta])
            p.op("dve", lambda e, h=h: e.tensor_tensor(out=x, in0=x, in1=ybn.ap()[:, 1, h, :n], op=ALU.add), reads=[ta, ybn], writes=[ta])
            q = PS()
            p.op("pe", lambda e, h=h: e.matmul(q.ap()[:64, 0:n], lhsT=gup.ap()[:, h * 64:(h + 1) * 64], rhs=sgd.ap()[:, :n], start=True, stop=True), reads=[gup, sgd], writes=[q])
            p.op("dve", lambda e, h=h: e.tensor_tensor(out=rwy.ap()[:, h, :n], in0=x, in1=q.ap()[:64, 0:n], op=ALU.mult), reads=[ta, q], writes=[rwy])
        for mt in range(2):
            x = yg.ap()[:, mt, :n]
            p.op("dve", lambda e, mt=mt: e.scalar_tensor_tensor(out=x, in0=zg.ap()[:, 29 + mt, :n], scalar=s5d.ap()[:, 0, mt:mt + 1], in1=ys5.ap()[:, 0, mt, :n], op0=ALU.mult, op1=ALU.add), reads=[zg, s5d, ys5], writes=[yg])
            p.op("dve", lambda e, mt=mt: e.tensor_tensor(out=x, in0=x, in1=ys5.ap()[:, 1, mt, :n], op=ALU.add), reads=[yg, ys5], writes=[yg])
            p.op("act", lambda e: e.activation(out=tb.ap()[:, :n], in_=x, func=AF.Square), reads=[yg], writes=[tb])
            p.op("dve", lambda e: e.tensor_scalar(out=tb.ap()[:, :n], in0=tb.ap()[:, :n], scalar1=0.044715, scalar2=1.0, op0=ALU.mult, op1=ALU.add), reads=[tb], writes=[tb])
            p.op("dve", lambda e: e.tensor_tensor(out=tb.ap()[:, :n], in0=tb.ap()[:, :n], in1=x, op=ALU.mult), reads=[tb, yg], writes=[tb])
            p.op("act", lambda e: e.activation(out=tb.ap()[:, :n], in_=tb.ap()[:, :n], func=AF.Sigmoid, scale=GC), reads=[tb], writes=[tb])
            p.op("dve", lambda e: e.tensor_tensor(out=x, in0=x, in1=tb.ap()[:, :n], op=ALU.mult), reads=[yg, tb], writes=[yg])
        for mo in range(2):
            q = PS()
            for kc in range(2):
                p.op("pe", lambda e, mo=mo, kc=kc: e.matmul(q.ap()[:, 0:n], lhsT=gluw.ap()[:, kc, mo * 128:(mo + 1) * 128], rhs=yg.ap()[:, kc, :n], start=(kc == 0), stop=(kc == 1)), reads=[gluw, yg], writes=[q])
            p.op("act", lambda e, mo=mo: e.activation(out=tb.ap()[:, :n], in_=q.ap()[:, 0:n], func=AF.Sigmoid, bias=s5d.ap()[:, 1, mo:mo + 1]), reads=[q, s5d], writes=[tb])
            p.op("dve", lambda e, mo=mo: e.tensor_tensor(out=s5y.ap()[:, mo, :n], in0=yg.ap()[:, mo, :n], in1=tb.ap()[:, :n], op=ALU.mult), reads=[yg, tb], writes=[s5y])
        for h in range(4):
            x = ta.ap()[:96, :n]
            p.op("dve", lambda e, h=h: e.tensor_tensor(out=x, in0=yml.ap()[:, 0, h, :n], in1=yml.ap()[:, 1, h, :n], op=ALU.add), reads=[yml], writes=[ta])
            p.op("act", lambda e, h=h: e.activation(out=tb.ap()[:96, :n], in_=zg.ap()[:96, h, :n], func=AF.Sigmoid), reads=[zg], writes=[tb])
            p.op("dve", lambda e: e.tensor_tensor(out=x, in0=x, in1=tb.ap()[:96, :n], op=ALU.mult), reads=[ta, tb], writes=[ta])
            std_part(x, 96, n, 1e-5, mlg.ap()[:, h:h + 1], mly.ap()[:, h, :n], mly)
        for dc in range(8):
            q = PS()
            for h in range(6):
                p.op("pe", lambda e, h=h, dc=dc: e.matmul(q.ap()[:, 0:n], lhsT=uprw.ap()[:, h, dc * 128:(dc + 1) * 128], rhs=rwy.ap()[:, h, :n], start=(h == 0), stop=(h == 5)), reads=[uprw, rwy], writes=[q])
            p.op("act", lambda e, dc=dc: e.activation(out=tb.ap()[:, :n], in_=zg.ap()[:, 5 + dc, :n], func=AF.Sigmoid, bias=brb.ap()[:, dc:dc + 1]), reads=[zg, brb], writes=[tb])
            p.op("dve", lambda e: e.tensor_tensor(out=ta.ap()[:, :n], in0=q.ap()[:, 0:n], in1=tb.ap()[:, :n], op=ALU.mult), reads=[q, tb], writes=[ta])
            q = PS()
            for h in range(2):
                p.op("pe", lambda e, h=h, dc=dc: e.matmul(q.ap()[:, 0:n], lhsT=ups5.ap()[:, h, dc * 128:(dc + 1) * 128], rhs=s5y.ap()[:, h, :n], start=(h == 0), stop=(h == 1)), reads=[ups5, s5y], writes=[q])
            p.op("act", lambda e, dc=dc: e.activation(out=tb.ap()[:, :n], in_=zg.ap()[:, 13 + dc, :n], func=AF.Sigmoid, bias=brb.ap()[:, 8 + dc:9 + dc]), reads=[zg, brb], writes=[tb])
            p.op("dve", lambda e: e.tensor_tensor(out=tc.ap()[:, :n], in0=q.ap()[:, 0:n], in1=tb.ap()[:, :n], op=ALU.mult), reads=[q, tb], writes=[tc])
            p.op("dve", lambda e: e.tensor_tensor(out=ta.ap()[:, :n], in0=ta.ap()[:, :n], in1=tc.ap()[:, :n], op=ALU.add), reads=[ta, tc], writes=[ta])
            q = PS()
            for h in range(4):
                p.op("pe", lambda e, h=h, dc=dc: e.matmul(q.ap()[:, 0:n], lhsT=upml.ap()[:, h, dc * 128:(dc + 1) * 128], rhs=mly.ap()[:, h, :n], start=(h == 0), stop=(h == 3)), reads=[upml, mly], writes=[q])
            p.op("act", lambda e, dc=dc: e.activation(out=tb.ap()[:, :n], in_=zg.ap()[:, 21 + dc, :n], func=AF.Sigmoid, bias=brb.ap()[:, 16 + dc:17 + dc]), reads=[zg, brb], writes=[tb])
            p.op("dve", lambda e: e.tensor_tensor(out=tc.ap()[:, :n], in0=q.ap()[:, 0:n], in1=tb.ap()[:, :n], op=ALU.mult), reads=[q, tb], writes=[tc])
            p.op("dve", lambda e, dc=dc: e.tensor_tensor(out=ym.ap()[:, dc, :n], in0=ta.ap()[:, :n], in1=tc.ap()[:, :n], op=ALU.add), reads=[ta, tc], writes=[ym])
        p.op("act", lambda e: e.mul(out=x_sb.ap()[:, :, :n], in_=x_sb.ap()[:, :, :n], mul=ALPHA), reads=[x_sb], writes=[x_sb])
        for dc in range(8):
            q = PS()
            for kc in range(8):
                p.op("pe", lambda e, kc=kc, dc=dc: e.matmul(q.ap()[:, 0:n], lhsT=wout.ap()[:, kc, dc * 128:(dc + 1) * 128], rhs=ym.ap()[:, kc, :n], start=(kc == 0), stop=(kc == 7)), reads=[wout, ym], writes=[q])
            p.op("dve", lambda e, dc=dc: e.scalar_tensor_tensor(out=y_sb.ap()[:, dc, :n], in0=q.ap()[:, 0:n], scalar=mod.ap()[:, 40 + dc, wh:wh + 1], in1=x_sb.ap()[:, dc, :n], op0=ALU.mult, op1=ALU.add), reads=[q, mod, x_sb], writes=[y_sb])
        ln_feature_major(p, y_sb, ysq, n, ones, ps1, ps2, mean, rstd, g_sb, b_sb, ysq)
        p.dma("sp", ov[:, :, sl], ysq.ap()[:, :, :n], reads=[ysq], writes=[oT])


MIX_KEYS = (("cw", [128, NCONV, 9]), ("rwp", [64, 5, 6]), ("wa_up", [128, 384]), ("glb", [16, 1]), ("sel", [16, 8, 96]),
            ("s5p", [128, 3, 8]), ("bz", [128, 16, 128]), ("cz", [128, 16, 128]))
MRG_KEYS = (("gn", [64, 2, 6]), ("g_up", [128, 384]), ("s5d", [128, 2, 2]), ("glu_w", [128, 2, 256]), ("mlg", [96, 4]),
            ("up_rw", [64, 6, 1024]), ("up_s5", [128, 2, 1024]), ("up_ml", [96, 4, 1024]), ("brb", [128, 24]),
            ("w_out", [128, 8, 1024]))
NUSED = 4


def set_sizes(nctx, nlat):
    global NLAT, NCTX, NT, TILES
    NLAT, NCTX = nlat, nctx
    NT = NLAT + NCTX
    TILES = [(0, NCTX, 1)] + [(NCTX + i * TT, TT, 0) for i in range(NLAT // TT)]


def build_fused():
    p = Prog()
    NS = NT
    din = lambda n, sh: p.dram(n, sh, F32, "in")
    xT0 = din("xT0", [8, 128, NS])
    cvec = din("cvec", [128, 8, 2])
    ada_w = din("ada_w", [DEPTH, D, 9 * D])
    ada_b = din("ada_b_l", [DEPTH, 128, 72])
    ln_g = din("ln_g_l", [DEPTH, 3, 128, 8])
    ln_b = din("ln_b_l", [DEPTH, 3, 128, 8])
    wg = din("ffn_w_gate", [DEPTH, 2, D, DFF])
    wu = din("ffn_w_up", [DEPTH, 2, D, DFF])
    wd = din("ffn_w_down", [DEPTH, 2, DFF, D])
    w_in = din("w_in", [DEPTH, D, INW])
    cst = din("cst", [128, 5, 128])
    ri = din("ri", [128, 129])
    mixp = {}
    for i in range(DEPTH):
        for d in range(2):
            mixp[(i, d)] = {k: din(f"m{i}{d}_{k}", sh).ap() for k, sh in MIX_KEYS}
            mixp[(i, d)]["cst"] = cst.ap()
            mixp[(i, d)]["ri"] = ri.ap()
    mrgp = {i: {k: din(f"g{i}_{k}", sh).ap() for k, sh in MRG_KEYS} for i in range(DEPTH)}
    oT = p.dram("oT", [8, 128, NS], F32, "out")
    tmp = lambda n, sh: p.dram(n, sh, F32, "tmp")
    S1 = tmp("S1", [8, 128, NS])
    S2 = tmp("S2", [8, 128, NS])
    zall = tmp("zall", [NSCH, 128, NS])
    zallb = tmp("zallb", [NZALL - NSCH, 128, NS])
    modT = [tmp(f"modT{i}", [128, 72, 2]) for i in range(DEPTH)]
    outs = [(tmp(f"o_rw{d}", [6, 64, NS]), tmp(f"o_bn{d}", [6, 64, NS]), tmp(f"o_s5{d}", [2, 128, NS]),
             tmp(f"o_ml{d}", [4, 96, NS])) for d in range(2)]
    chain = [(xT0, S1, S1, S2, S1), (S1, S2, S2, S1, oT)]
    for i in range(DEPTH):
        a_src, a_dst, m_src, m_dst, f_dst = chain[i]
        with p.phase():
            emit_mod(p, cvec, ada_w.ap()[i], ada_b.ap()[i], modT[i])
        with p.phase():
            emit_ffn(p, 0, a_src, a_dst, modT[i], ln_g.ap()[i, 0], ln_b.ap()[i, 0], wg.ap()[i, 0], wu.ap()[i, 0], wd.ap()[i, 0])
        with p.phase():
            emit_inproj(p, a_dst, modT[i], w_in.ap()[i], zall, zallb)
        for d in range(2):
            with p.phase():
                emit_mixer(p, zall, mixp[(i, d)], outs[d], rev=(d == 1), nctx=NCTX, nlat=NLAT)
        with p.phase():
            emit_merge(p, m_src, m_dst, modT[i], ln_g.ap()[i, 1], ln_b.ap()[i, 1], outs, zall, zallb, mrgp[i])
        with p.phase():
            emit_ffn(p, 6, m_dst, f_dst, modT[i], ln_g.ap()[i, 2], ln_b.ap()[i, 2], wg.ap()[i, 1], wu.ap()[i, 1], wd.ap()[i, 1])
    p.finish()
    return p


def merge_params(P, i):
    m = {}
    m["gn"] = np.ascontiguousarray(np.stack([P["rw_gn_g"][i].reshape(6, 64).T, P["rw_gn_b"][i].reshape(6, 64).T], 1))
    m["g_up"] = P["rw_g_up"][i]
    m["s5d"] = np.ascontiguousarray(np.stack([P["s5_d"][i].reshape(2, 128).T, P["s5_glu_b"][i].reshape(2, 128).T], 1))
    m["glu_w"] = np.ascontiguousarray(P["s5_glu_w"][i].reshape(2, 128, 256).transpose(1, 0, 2))
    m["mlg"] = np.ascontiguousarray(P["ml_norm_g"][i].reshape(4, 96).T)
    m["up_rw"] = np.ascontiguousarray(P["up_rw"][i].reshape(6, 64, 1024).transpose(1, 0, 2))
    m["up_s5"] = np.ascontiguousarray(P["up_s5"][i].reshape(2, 128, 1024).transpose(1, 0, 2))
    m["up_ml"] = np.ascontiguousarray(P["up_ml"][i].reshape(4, 96, 1024).transpose(1, 0, 2))
    m["brb"] = np.ascontiguousarray(P["br_gate_b"][i].reshape(24, 128).T)
    m["w_out"] = np.ascontiguousarray(P["w_out"][i].reshape(8, 128, 1024).transpose(1, 0, 2))
    return m


_PROG = []


def kernel(**inputs):
    P = {k: np.asarray(v, dtype=np.float32) for k, v in inputs.items()}
    x, c, ctx, c_ctx = P["x"], P["c"], P["ctx"], P["c_ctx"]
    B, SEQ, _ = x.shape
    if not _PROG:
        set_sizes(ctx.shape[1], SEQ)
        _PROG.append(build_fused())
    prog = _PROG[0]
    NUSED = B
    shared = {
        "ada_w": P["ada_w"],
        "ada_b_l": np.ascontiguousarray(P["ada_b"].reshape(DEPTH, 72, 128).transpose(0, 2, 1)),
        "ln_g_l": np.ascontiguousarray(P["ln_g"].reshape(DEPTH, 3, 8, 128).transpose(0, 1, 3, 2)),
        "ln_b_l": np.ascontiguousarray(P["ln_b"].reshape(DEPTH, 3, 8, 128).transpose(0, 1, 3, 2)),
        "ffn_w_gate": P["ffn_w_gate"], "ffn_w_up": P["ffn_w_up"], "ffn_w_down": P["ffn_w_down"], "w_in": P["w_in"],
    }
    cst, ri = mixer_consts()
    shared["cst"] = cst
    shared["ri"] = ri
    for i in range(DEPTH):
        for d in range(2):
            mp = mixer_params(P, i, d)
            for k, _ in MIX_KEYS:
                shared[f"m{i}{d}_{k}"] = np.ascontiguousarray(mp[k], np.float32)
        gp = merge_params(P, i)
        for k, _ in MRG_KEYS:
            shared[f"g{i}_{k}"] = np.ascontiguousarray(gp[k], np.float32)
    ims = []
    for cid in range(NUSED):
        b = cid % B
        m = dict(shared)
        xx = np.concatenate([ctx[b], x[b]], 0)
        m["xT0"] = np.ascontiguousarray(xx.T).reshape(8, 128, NT)
        m["cvec"] = np.ascontiguousarray(np.stack([c[b], c_ctx], -1).reshape(8, 128, 2).transpose(1, 0, 2))
        ims.append(m)
    res = run_bass_kernel_spmd(prog.nc, ims, core_ids=list(range(NUSED)))
    out = np.empty((B, SEQ, D), np.float32)
    for b in range(B):
        out[b] = res.results[b]["oT"].reshape(D, NT).T[NCTX:]
    return out
```

```python
import numpy as np
from contextlib import ExitStack
import concourse.bass as bass
import concourse.mybir as mybir
from concourse.bass_utils import run_bass_kernel_spmd

F32 = mybir.dt.float32
BF16 = mybir.dt.bfloat16
AF = mybir.ActivationFunctionType
ALU = mybir.AluOpType
AX = mybir.AxisListType


class Reg:
    __slots__ = ("name", "w", "r")

    def __init__(self, name):
        self.name = name
        self.w = None
        self.r = []


class Buf:
    def __init__(self, t, name, nreg=1):
        self.t = t
        self.name = name
        self.regs = [Reg(f"{name}.{i}") for i in range(nreg)]

    def ap(self):
        return self.t.ap() if hasattr(self.t, "ap") and not isinstance(self.t, bass.AP) else self.t

    def __getitem__(self, idx):
        return self.ap()[idx]

    def r(self, i):
        return self.regs[i]


NDMA_SEM = 8
FUSE_WAIT = True


class Prog:
    def __init__(self):
        self.nc = bass.Bass("TRN2", target_bir_lowering=False)
        nc = self.nc
        self.stack = ExitStack()
        self.eng = {"pe": nc.tensor, "dve": nc.vector, "act": nc.scalar, "pool": nc.gpsimd, "sp": nc.sync}
        self.sems = {}
        self.semval = {}
        for e in self.eng:
            self.sems[e] = self.stack.enter_context(nc.semaphore(f"s_{e}"))
            self.semval[e] = 0
        self.dq = {}
        for q in ("sp", "act", "pool"):
            lst = []
            for i in range(NDMA_SEM):
                k = f"d_{q}{i}"
                self.sems[k] = self.stack.enter_context(nc.semaphore(k))
                self.semval[k] = 0
                lst.append(k)
            self.dq[q] = [lst, 0]
        self.waited = {e: {} for e in self.eng}
        self.out_waits = []
        self.ninst = {e: 0 for e in self.eng}
        self.pstack = None
        self.pidx = 0

    def barrier(self):
        for e in self.eng:
            for k, v in self.semval.items():
                if v > 0 and k != e:
                    self._wait(e, k, v)

    def phase(self):
        prog = self

        class _Ph:
            def __enter__(self_):
                prog.pidx += 1
                prog.pstack = ExitStack()
                return prog

            def __exit__(self_, *a):
                prog.barrier()
                prog.pstack.close()
                prog.pstack = None
                return False
        return _Ph()

    def dram(self, name, shape, dtype, kind):
        k = {"in": "ExternalInput", "out": "ExternalOutput", "tmp": "Internal"}[kind]
        t = self.nc.dram_tensor(name, list(shape), dtype, kind=k)
        b = Buf(t, name)
        b.kind = kind
        return b

    def tile(self, shape, dtype, name, nreg=1):
        st = self.pstack if self.pstack is not None else self.stack
        name = f"p{self.pidx}_{name}"
        t = st.enter_context(self.nc.sbuf_tensor(name, list(shape), dtype))
        return Buf(t, name, nreg)

    def psum(self, shape, dtype, name, nreg=1):
        st = self.pstack if self.pstack is not None else self.stack
        name = f"p{self.pidx}_{name}"
        t = st.enter_context(self.nc.psum_tensor(name, list(shape), dtype))
        return Buf(t, name, nreg)

    @staticmethod
    def _regs(lst):
        out = []
        for x in lst or []:
            if isinstance(x, Buf):
                out.extend(x.regs)
            elif isinstance(x, Reg):
                out.append(x)
            elif isinstance(x, tuple):
                out.append(x[0].regs[x[1]])
            else:
                raise TypeError(x)
        return out

    def _wait(self, e, key, val):
        if key is None:
            return
        cur = self.waited[e].get(key, 0)
        if cur >= val:
            return
        self.waited[e][key] = val
        self.eng[e].wait_ge(self.sems[key], val)
        self.ninst[e] += 1

    def _deps(self, e, reads, writes, defer=False):
        need = {}
        for r in reads:
            if r.w is not None:
                need[r.w[0]] = max(need.get(r.w[0], 0), r.w[1])
        for r in writes:
            if r.w is not None:
                need[r.w[0]] = max(need.get(r.w[0], 0), r.w[1])
            for (k, v) in r.r:
                need[k] = max(need.get(k, 0), v)
        todo = [(k, v) for k, v in need.items() if self.waited[e].get(k, 0) < v]
        last = None
        if defer and FUSE_WAIT and todo:
            last = todo.pop()
        for k, v in todo:
            self._wait(e, k, v)
        return last

    def _attach(self, e, inst, last):
        if last is not None:
            k, v = last
            self.waited[e][k] = v
            inst._wait_ge(self.sems[k], v)

    def _mark(self, key, val, reads, writes):
        for r in reads:
            r.r.append((key, val))
            if len(r.r) > 12:
                d = {}
                for (k, v) in r.r:
                    d[k] = max(d.get(k, 0), v)
                r.r = list(d.items())
        for r in writes:
            r.w = (key, val)
            r.r = []

    def op(self, e, fn, reads=None, writes=None):
        reads = self._regs(reads)
        writes = self._regs(writes)
        last = self._deps(e, reads, writes, defer=True)
        inst = fn(self.eng[e])
        self._attach(e, inst, last)
        self.semval[e] += 1
        inst.then_inc(self.sems[e], 1)
        self.ninst[e] += 1
        self._mark(e, self.semval[e], reads, writes)
        return inst

    def dma(self, q, out, in_, reads=None, writes=None, **kw):
        rbufs = reads or []
        wbufs = writes or []
        reads = self._regs(reads)
        writes = self._regs(writes)
        self._deps(q, reads, writes)
        lst, i = self.dq[q]
        key = lst[i % NDMA_SEM]
        self.dq[q][1] = i + 1
        self._wait(q, key, self.semval[key])
        inst = self.eng[q].dma_start(out=out, in_=in_, **kw)
        self.semval[key] += 16
        inst.then_inc(self.sems[key], 16)
        self.ninst[q] += 1
        self._mark(key, self.semval[key], reads, writes)
        for b in wbufs:
            if isinstance(b, Buf) and getattr(b, "kind", None) == "out":
                self.out_waits.append((key, self.semval[key]))
        return inst

    def finish(self):
        d = {}
        for k, v in self.out_waits:
            d[k] = max(d.get(k, 0), v)
        for k, v in d.items():
            self._wait("sp", k, v)
        for e in self.eng:
            if e != "sp" and self.semval[e] > 0:
                self._wait("sp", e, self.semval[e])


D = 1024
DFF = 2816
NFC = 22
DEPTH = 2
ALPHA = (2.0 * DEPTH) ** 0.25
LN_EPS = 1e-5
NLAT = 8192
NCTX = 256
NT = NLAT + NCTX
TT = 256
TILES = [(0, NCTX, 1)] + [(NCTX + i * TT, TT, 0) for i in range(NLAT // TT)]
NCORES = 8
INW = 6288


def make_ones(p, n=128, name="ones"):
    t = p.tile([128, n], F32, name)
    p.op("pool", lambda e: e.memset(t.ap(), 1.0), writes=[t])
    return t


def load_weight_bf16(p, dst, dst_view, src_view, stage, i, shape_free):
    st = stage[i % len(stage)]
    sv = st.ap()[:, :shape_free] if isinstance(shape_free, int) else shape_free(st.ap())
    q = ("sp", "act")[i % 2]
    p.dma(q, sv, src_view, writes=[st])
    ce = ("pool", "dve", "act")[i % 3]
    if ce == "act":
        p.op("act", lambda e: e.copy(out=dst_view, in_=sv), reads=[st], writes=[dst])
    else:
        p.op(ce, lambda e: e.tensor_copy(out=dst_view, in_=sv), reads=[st], writes=[dst])


def emit_mod(p, cvec, adaw_ap, adab_ap, out):
    cs = p.tile([128, 8, 2], F32, "cs")
    ab = p.tile([128, 72], F32, "ab")
    mo = p.tile([128, 72, 2], F32, "mo")
    pan = [p.tile([128, 8, 1024], F32, f"pan{i}") for i in range(2)]
    ps = [p.psum([128, 512], F32, f"ps{i}") for i in range(2)]
    p.dma("sp", cs.ap(), cvec.ap(), writes=[cs])
    p.dma("sp", ab.ap(), adab_ap, writes=[ab])
    p.op("act", lambda e: e.activation(out=cs.ap(), in_=cs.ap(), func=AF.Silu), reads=[cs], writes=[cs])
    awv = adaw_ap.rearrange("(kc p) f -> p kc f", p=128)
    for j in range(9):
        pn = pan[j % 2]
        p.dma(("sp", "act")[j % 2], pn.ap(), awv[:, :, j * 1024:(j + 1) * 1024], writes=[pn])
        for dc in range(8):
            ch = j * 8 + dc
            pp = ps[ch % 2]
            for kc in range(8):
                p.op("pe", lambda e, kc=kc, dc=dc, pn=pn, pp=pp: e.matmul(
                    pp.ap()[:, 0:2], lhsT=pn.ap()[:, kc, dc * 128:(dc + 1) * 128], rhs=cs.ap()[:, kc, :],
                    start=(kc == 0), stop=(kc == 7)), reads=[pn, cs], writes=[pp])
            p.op("dve", lambda e, ch=ch, pp=pp: e.tensor_scalar(
                out=mo.ap()[:, ch, :], in0=pp.ap()[:, 0:2], scalar1=ab.ap()[:, ch:ch + 1], scalar2=None,
                op0=ALU.add), reads=[pp, ab], writes=[mo])
    p.dma("sp", out.ap(), mo.ap(), reads=[mo], writes=[out])


def ln_feature_major(p, y, ysq, n, ones, ps1, ps2, mean, rstd, g, b, dst, dst_reads=None):
    p.op("act", lambda e: e.activation(out=ysq.ap()[:, :, :n], in_=y.ap()[:, :, :n], func=AF.Square),
         reads=[y], writes=[ysq])
    for dc in range(8):
        p.op("pe", lambda e, dc=dc: e.matmul(ps1.ap()[:, :n], lhsT=ones.ap(), rhs=y.ap()[:, dc, :n],
                                              start=(dc == 0), stop=(dc == 7)), reads=[ones, y], writes=[ps1])
    for dc in range(8):
        p.op("pe", lambda e, dc=dc: e.matmul(ps2.ap()[:, :n], lhsT=ones.ap(), rhs=ysq.ap()[:, dc, :n],
                                              start=(dc == 0), stop=(dc == 7)), reads=[ones, ysq], writes=[ps2])
    p.op("dve", lambda e: e.tensor_scalar(out=mean.ap()[:, :n], in0=ps1.ap()[:, :n], scalar1=1.0 / D, scalar2=None,
                                          op0=ALU.mult), reads=[ps1], writes=[mean])
    p.op("dve", lambda e: e.tensor_tensor(out=rstd.ap()[:, :n], in0=mean.ap()[:, :n], in1=mean.ap()[:, :n],
                                          op=ALU.mult), reads=[mean], writes=[rstd])
    p.op("dve", lambda e: e.scalar_tensor_tensor(out=rstd.ap()[:, :n], in0=ps2.ap()[:, :n], scalar=1.0 / D,
                                                 in1=rstd.ap()[:, :n], op0=ALU.mult, op1=ALU.subtract),
         reads=[ps2, rstd], writes=[rstd])
    p.op("dve", lambda e: e.tensor_scalar(out=rstd.ap()[:, :n], in0=rstd.ap()[:, :n], scalar1=LN_EPS, scalar2=None,
                                          op0=ALU.add), reads=[rstd], writes=[rstd])
    p.op("act", lambda e: e.activation(out=rstd.ap()[:, :n], in_=rstd.ap()[:, :n], func=AF.Sqrt),
         reads=[rstd], writes=[rstd])
    p.op("dve", lambda e: e.reciprocal(out=rstd.ap()[:, :n], in_=rstd.ap()[:, :n]), reads=[rstd], writes=[rstd])
    for dc in range(8):
        eng = ("dve", "pool")[dc % 2]
        p.op(eng, lambda e, dc=dc: e.tensor_tensor(out=y.ap()[:, dc, :n], in0=y.ap()[:, dc, :n],
                                                   in1=mean.ap()[:, :n], op=ALU.subtract),
             reads=[y, mean], writes=[y])
        p.op(eng, lambda e, dc=dc: e.tensor_tensor(out=y.ap()[:, dc, :n], in0=y.ap()[:, dc, :n],
                                                   in1=rstd.ap()[:, :n], op=ALU.mult),
             reads=[y, rstd], writes=[y])
        p.op(eng, lambda e, dc=dc: e.tensor_scalar(out=dst.ap()[:, dc, :n], in0=y.ap()[:, dc, :n],
                                                   scalar1=g.ap()[:, dc:dc + 1], scalar2=b.ap()[:, dc:dc + 1],
                                                   op0=ALU.mult, op1=ALU.add),
             reads=[y, g, b], writes=[dst])


def emit_ffn(p, j0, xT, oT, modT, lng_ap, lnb_ap, wg_ap, wu_ap, wd_ap):

    wg_sb = p.tile([128, 8, DFF], BF16, "wg_sb")
    wu_sb = p.tile([128, 8, DFF], BF16, "wu_sb")
    wd_sb = p.tile([128, NFC, D], BF16, "wd_sb")
    stage = [p.tile([128, DFF // 2], F32, f"stage{i}") for i in range(2)]
    mod = p.tile([128, 72, 2], F32, "mod")
    g_sb = p.tile([128, 8], F32, "g_sb")
    b_sb = p.tile([128, 8], F32, "b_sb")
    ops_ = p.tile([128, 8, 2], F32, "onepsc")
    hg = p.tile([128, 8, 2], F32, "hg")
    ones = make_ones(p)
    p.dma("sp", mod.ap(), modT.ap(), reads=[modT], writes=[mod])
    p.dma("sp", g_sb.ap(), lng_ap, writes=[g_sb])
    p.dma("sp", b_sb.ap(), lnb_ap, writes=[b_sb])
    p.op("dve", lambda e: e.tensor_scalar(out=ops_.ap(), in0=mod.ap()[:, (j0 + 1) * 8:(j0 + 2) * 8, :], scalar1=1.0,
                                          scalar2=None, op0=ALU.add), reads=[mod], writes=[ops_])
    p.op("dve", lambda e: e.tensor_scalar(out=hg.ap(), in0=mod.ap()[:, (j0 + 2) * 8:(j0 + 3) * 8, :], scalar1=0.5,
                                          scalar2=None, op0=ALU.mult), reads=[mod], writes=[hg])
    k = 0
    wgv = wg_ap.rearrange("(kc p) f -> kc p f", p=128)
    wuv = wu_ap.rearrange("(kc p) f -> kc p f", p=128)
    wdv = wd_ap.rearrange("(fc p) d -> fc p d", p=128)
    HF = DFF // 2
    for kc in range(8):
        for hh in range(2):
            load_weight_bf16(p, wg_sb, wg_sb.ap()[:, kc, hh * HF:(hh + 1) * HF], wgv[kc][:, hh * HF:(hh + 1) * HF], stage, k, HF); k += 1
            load_weight_bf16(p, wu_sb, wu_sb.ap()[:, kc, hh * HF:(hh + 1) * HF], wuv[kc][:, hh * HF:(hh + 1) * HF], stage, k, HF); k += 1
    for fc in range(NFC):
        load_weight_bf16(p, wd_sb, wd_sb.ap()[:, fc, :], wdv[fc], stage, k, D); k += 1

    x_sb = [p.tile([128, 8, TT], F32, f"x_sb{i}") for i in range(2)]
    u_sb = p.tile([128, 8, TT], BF16, "u_sb")
    a_sb = [p.tile([128, NFC, TT], BF16, "a_sb0")]
    sg = [p.tile([128, TT], F32, f"sg{i}") for i in range(2)]
    y_sb = p.tile([128, 8, TT], F32, "y_sb")
    ysq = p.tile([128, 8, TT], F32, "ysq")
    mean = p.tile([128, TT], F32, "mean")
    rstd = p.tile([128, TT], F32, "rstd")
    psg = [p.psum([128, 512], F32, f"psg{i}") for i in range(2)]
    psu = [p.psum([128, 512], F32, f"psu{i}") for i in range(2)]
    psd = [p.psum([128, 512], F32, f"psd{i}") for i in range(2)]
    ps1 = p.psum([128, 512], F32, "ps1")
    ps2 = p.psum([128, 512], F32, "ps2")
    xv = xT.ap().rearrange("c p t -> p c t")
    ov = oT.ap().rearrange("c p t -> p c t")

    for ti, (t0, n, wh) in enumerate(TILES):
        xs = x_sb[ti % 2]
        asb = a_sb[0]
        osb = ysq
        p.dma(("sp", "act")[ti % 2], xs.ap()[:, :, :n], xv[:, :, t0:t0 + n], reads=[xT], writes=[xs])
        for kc in range(8):
            p.op("dve", lambda e, kc=kc: e.tensor_scalar(
                out=u_sb.ap()[:, kc, :n], in0=xs.ap()[:, kc, :n], scalar1=ops_.ap()[:, kc, wh:wh + 1],
                scalar2=mod.ap()[:, j0 * 8 + kc, wh:wh + 1], op0=ALU.mult, op1=ALU.add),
                reads=[xs, ops_, mod], writes=[u_sb])
        for fc in range(NFC):
            pg = psg[fc % 2]
            pu = psu[fc % 2]
            for kc in range(8):
                p.op("pe", lambda e, kc=kc, fc=fc, pg=pg: e.matmul(
                    pg.ap()[:, :n], lhsT=wg_sb.ap()[:, kc, fc * 128:(fc + 1) * 128], rhs=u_sb.ap()[:, kc, :n],
                    start=(kc == 0), stop=(kc == 7)), reads=[wg_sb, u_sb], writes=[pg])
            for kc in range(8):
                p.op("pe", lambda e, kc=kc, fc=fc, pu=pu: e.matmul(
                    pu.ap()[:, :n], lhsT=wu_sb.ap()[:, kc, fc * 128:(fc + 1) * 128], rhs=u_sb.ap()[:, kc, :n],
                    start=(kc == 0), stop=(kc == 7)), reads=[wu_sb, u_sb], writes=[pu])
            s = sg[fc % 2]
            p.op("act", lambda e, pg=pg, s=s: e.activation(out=s.ap()[:, :n], in_=pg.ap()[:, :n], func=AF.Silu),
                 reads=[pg], writes=[s])
            p.op("dve", lambda e, pu=pu, s=s, fc=fc: e.tensor_tensor(
                out=asb.ap()[:, fc, :n], in0=pu.ap()[:, :n], in1=s.ap()[:, :n], op=ALU.mult),
                reads=[pu, s], writes=[asb])
        p.op("act", lambda e: e.mul(out=xs.ap()[:, :, :n], in_=xs.ap()[:, :, :n], mul=ALPHA), reads=[xs], writes=[xs])
        for dc in range(8):
            pd = psd[dc % 2]
            for fc in range(NFC):
                p.op("pe", lambda e, dc=dc, fc=fc, pd=pd: e.matmul(
                    pd.ap()[:, :n], lhsT=wd_sb.ap()[:, fc, dc * 128:(dc + 1) * 128], rhs=asb.ap()[:, fc, :n],
                    start=(fc == 0), stop=(fc == NFC - 1)), reads=[wd_sb, asb], writes=[pd])
            p.op("dve", lambda e, dc=dc, pd=pd: e.scalar_tensor_tensor(
                out=y_sb.ap()[:, dc, :n], in0=pd.ap()[:, :n], scalar=hg.ap()[:, dc, wh:wh + 1],
                in1=xs.ap()[:, dc, :n], op0=ALU.mult, op1=ALU.add), reads=[pd, hg, xs], writes=[y_sb])
        ln_feature_major(p, y_sb, ysq, n, ones, ps1, ps2, mean, rstd, g_sb, b_sb, osb)
        p.dma(("sp", "act")[(ti + 1) % 2], ov[:, :, t0:t0 + n], osb.ap()[:, :, :n], reads=[osb], writes=[oT])


def emit_inproj(p, xT, modT, win_ap, zall, zallb):
    w_sb = p.tile([128, 8, INW], BF16, "w_sb")
    QW = INW // 4
    stage = [p.tile([128, QW], F32, f"stage{i}") for i in range(2)]
    mod = p.tile([128, 72, 2], F32, "mod")
    ops_ = p.tile([128, 8, 2], F32, "onepsc")
    p.dma("sp", mod.ap(), modT.ap(), reads=[modT], writes=[mod])
    p.op("dve", lambda e: e.tensor_scalar(out=ops_.ap(), in0=mod.ap()[:, 32:40, :], scalar1=1.0, scalar2=None,
                                          op0=ALU.add), reads=[mod], writes=[ops_])
    wv = win_ap.rearrange("(kc p) f -> kc p f", p=128)
    k = 0
    for kc in range(8):
        for qq in range(4):
            load_weight_bf16(p, w_sb, w_sb.ap()[:, kc, qq * QW:(qq + 1) * QW], wv[kc][:, qq * QW:(qq + 1) * QW],
                             stage, k, QW); k += 1
    x_sb = [p.tile([128, 8, TT], F32, "x_sb0")]
    u_sb = [p.tile([128, 8, TT], BF16, "u_sb0")]
    zbig = p.tile([128, NZALL, TT], F32, "zbig")
    p.op("pool", lambda e: e.memset(zbig.ap(), 0.0), writes=[zbig])
    ps = [p.psum([128, 512], F32, f"ps{i}") for i in range(4)]
    xv = xT.ap().rearrange("c p t -> p c t")
    zv = zall.ap().rearrange("c p t -> p c t")
    zvb = zallb.ap().rearrange("c p t -> p c t")
    cnt = 0
    for ti, (t0, n, wh) in enumerate(TILES):
        xs = x_sb[0]
        us = u_sb[0]
        p.dma(("sp", "act")[ti % 2], xs.ap()[:, :, :n], xv[:, :, t0:t0 + n], reads=[xT], writes=[xs])
        for kc in range(8):
            p.op(("dve", "pool")[kc % 2], lambda e, kc=kc: e.tensor_scalar(
                out=us.ap()[:, kc, :n], in0=xs.ap()[:, kc, :n], scalar1=ops_.ap()[:, kc, wh:wh + 1],
                scalar2=mod.ap()[:, 24 + kc, wh:wh + 1], op0=ALU.mult, op1=ALU.add),
                reads=[xs, ops_, mod], writes=[us])
        for ci, (nm, c0, nc_) in enumerate(ZALL):
            pp = ps[cnt % 4]
            for kc in range(8):
                p.op("pe", lambda e, kc=kc, pp=pp, c0=c0, nc_=nc_: e.matmul(
                    pp.ap()[:nc_, :n], lhsT=w_sb.ap()[:, kc, c0:c0 + nc_], rhs=us.ap()[:, kc, :n],
                    start=(kc == 0), stop=(kc == 7)), reads=[w_sb, us], writes=[pp])
            if cnt % 2 == 0:
                p.op("act", lambda e, pp=pp, ci=ci, nc_=nc_: e.copy(out=zbig.ap()[:nc_, ci, :n], in_=pp.ap()[:nc_, :n]),
                     reads=[pp], writes=[zbig])
            else:
                p.op("dve", lambda e, pp=pp, ci=ci, nc_=nc_: e.tensor_copy(out=zbig.ap()[:nc_, ci, :n], in_=pp.ap()[:nc_, :n]),
                     reads=[pp], writes=[zbig])
            cnt += 1
        p.dma(("sp", "act")[(ti + 1) % 2], zv[:, :, t0:t0 + n], zbig.ap()[:, 0:NSCH, :n], reads=[zbig], writes=[zall])
        p.dma(("sp", "act")[ti % 2], zvb[:, :, t0:t0 + n], zbig.ap()[:, NSCH:NZALL, :n], reads=[zbig], writes=[zallb])


SCH = []
for _nm, _c0 in (("rw_r", 0), ("rw_k", 384), ("rw_v", 768)):
    for _i in range(6):
        SCH.append((f"{_nm}{_i}", _c0 + 64 * _i, 64))
for _nm, _c0 in (("ml_q", 1152), ("ml_k", 1536)):
    for _i in range(4):
        SCH.append((f"{_nm}{_i}", _c0 + 96 * _i, 96))
NCONV = len(SCH)
for _i in range(4):
    SCH.append((f"ml_v{_i}", 1920 + 96 * _i, 96))
SCH.append(("ml_gl", 2688, 16))
SCH.append(("s5_u0", 2704, 128))
SCH.append(("s5_u1", 2832, 128))
SCH.append(("wa_dn", 2960, 128))
NSCH = len(SCH)
NOTH = NSCH - NCONV
ZALL = list(SCH)
for _i in range(4):
    ZALL.append((f"ml_o{_i}", 2304 + 96 * _i, 96))
ZALL.append(("g_dn", 3088, 128))
for _i in range(24):
    ZALL.append((f"br{_i}", 3216 + 128 * _i, 128))
NZALL = len(ZALL)
TWO_PI = 2.0 * np.pi


def emit_mixer(p, zs, prm, outs, rev, nctx=NCTX, nlat=NLAT):
    NS = nctx + nlat
    NB = NS // 128
    f32 = F32
    o_rw, o_bn, o_s5, o_ml = outs

    def nat(a, b):
        if not rev:
            return slice(a, b)
        if b <= nctx:
            return slice(nctx - b, nctx - a)
        return slice(nctx + NS - b, nctx + NS - a)

    def T(shape, name, dt=f32):
        return p.tile(shape, dt, "t_" + name)

    cw = T([128, NCONV, 9], "cw"); rwp = T([64, 5, 6], "rwp"); waup = T([128, 384], "waup")
    glb = T([16, 1], "glb"); sel = T([16, 8, 96], "sel"); s5p = T([128, 3, 8], "s5p")
    bz = T([128, 16, 128], "bz"); czc = T([128, 16, 128], "czc"); cst = T([128, 5, 128], "cst")
    ri = T([128, 129], "ri")
    for i, (t, d) in enumerate(((cw, "cw"), (rwp, "rwp"), (waup, "wa_up"), (glb, "glb"), (sel, "sel"), (s5p, "s5p"),
                                (bz, "bz"), (czc, "cz"), (cst, "cst"), (ri, "ri"))):
        p.dma(("sp", "act")[i % 2], t.ap(), prm[d], writes=[t])
    ident = cst.ap()[:, 0, :]
    m_su = cst.ap()[:, 1, :]
    m_sl = cst.ap()[:, 2, :]
    m_iu = cst.ap()[:, 3, :]
    m01 = cst.ap()[:, 4, :]
    ones = make_ones(p)
    mask5 = T([64, 5, 64], "mask5")
    for i, m in enumerate((m_su, m_sl, m_su, m_iu, m_iu)):
        p.op("dve", lambda e, i=i, m=m: e.tensor_copy(out=mask5.ap()[:, i, :], in_=m[0:64, 0:64]), reads=[cst], writes=[mask5])

    pss = [p.psum([128, 512], f32, f"pb{i}") for i in range(4)]
    psb = p.psum([128, 2, 1024], f32, "psbig")
    pcnt = [0]

    def PS():
        pcnt[0] += 1
        return pss[pcnt[0] % 4]

    ecnt = [0]

    def EWc():
        ecnt[0] += 1
        return ("dve", "pool")[ecnt[0] % 2]

    def EW():
        return "dve"

    def rr(x_ap, n, tmpf, tmpi, reads):
        p.op("dve", lambda e: e.tensor_scalar(out=tmpf, in0=x_ap, scalar1=1.0 / TWO_PI, scalar2=0.5, op0=ALU.mult,
                                              op1=ALU.add), reads=reads, writes=reads)
        p.op("dve", lambda e: e.tensor_copy(out=tmpi, in_=tmpf), reads=reads, writes=reads)
        p.op("dve", lambda e: e.tensor_copy(out=tmpf, in_=tmpi), reads=reads, writes=reads)
        p.op("dve", lambda e: e.scalar_tensor_tensor(out=x_ap, in0=tmpf, scalar=-TWO_PI, in1=x_ap, op0=ALU.mult,
                                                     op1=ALU.add), reads=reads, writes=reads)
        p.op("dve", lambda e: e.tensor_scalar(out=tmpf, in0=x_ap, scalar1=-np.pi, scalar2=TWO_PI, op0=ALU.is_lt,
                                              op1=ALU.mult), reads=reads, writes=reads)
        p.op("dve", lambda e: e.tensor_tensor(out=x_ap, in0=x_ap, in1=tmpf, op=ALU.add), reads=reads, writes=reads)
        p.op("dve", lambda e: e.tensor_scalar(out=tmpf, in0=x_ap, scalar1=np.pi, scalar2=-TWO_PI, op0=ALU.is_gt,
                                              op1=ALU.mult), reads=reads, writes=reads)
        p.op("dve", lambda e: e.tensor_tensor(out=x_ap, in0=x_ap, in1=tmpf, op=ALU.add), reads=reads, writes=reads)
        p.op("dve", lambda e: e.tensor_scalar(out=x_ap, in0=x_ap, scalar1=-3.14159, scalar2=3.14159, op0=ALU.max,
                                              op1=ALU.min), reads=reads, writes=reads)

    sp_ = T([128, 16, 8], "s5small")
    SM = lambda i: sp_.ap()[:, i, :]
    Vr = T([128, 8, 129], "Vr"); Vi = T([128, 8, 129], "Vi"); t5a = T([128, 8, 129], "t5a"); t5b = T([128, 8, 129], "t5b")
    ang, ang2 = Vr, Vi

    class _View:
        def __init__(self, buf, fn):
            self.buf, self.fn = buf, fn

        def ap(self):
            return self.fn(self.buf.ap())
    tf = _View(t5a, lambda a: a.rearrange("p k r -> p (k r)"))
    ti_ = _View(t5b, lambda a: a.rearrange("p k r -> p (k r)").bitcast(mybir.dt.int32))
    Ct = T([128, 8, 129], "Ct"); St = T([128, 8, 129], "St")
    T1re = T([128, 8, 128], "T1re"); T1im = T([128, 8, 128], "T1im"); RHO = T([128, 8, 128], "RHO")
    S5R = [sp_, s5p, Vr, Vi, t5a, t5b, Ct, St, T1re, T1im, RHO, ri, ones]
    lre, lim, ldt = s5p.ap()[:, 0, :], s5p.ap()[:, 1, :], s5p.ap()[:, 2, :]

    def o5(eng, fn):
        p.op(eng, fn, reads=S5R, writes=S5R)

    o5("dve", lambda e: e.tensor_scalar(out=lre, in0=lre, scalar1=-1e-4, scalar2=None, op0=ALU.min))
    o5("act", lambda e: e.activation(out=SM(0), in_=ldt, func=AF.Exp))
    o5("dve", lambda e: e.tensor_tensor(out=SM(1), in0=lre, in1=SM(0), op=ALU.mult))
    o5("act", lambda e: e.activation(out=SM(1), in_=SM(1), func=AF.Exp))
    o5("dve", lambda e: e.tensor_tensor(out=SM(2), in0=lim, in1=SM(0), op=ALU.mult))
    rr(SM(2), 8, tf.ap()[:, 0:8], ti_.ap()[:, 0:8], S5R)
    for k in range(8):
        o5("dve", lambda e, k=k: e.tensor_scalar(out=ang.ap()[:, k, :], in0=ri.ap(), scalar1=sp_.ap()[:, 2, k:k + 1],
                                                 scalar2=None, op0=ALU.mult))
    angf = ang.ap().rearrange("p k r -> p (k r)")
    ang2f = ang2.ap().rearrange("p k r -> p (k r)")
    o5("dve", lambda e: e.tensor_scalar(out=ang2f, in0=angf, scalar1=np.pi / 2, scalar2=None, op0=ALU.add))
    rr(angf, 8 * 129, tf.ap(), ti_.ap(), S5R)
    rr(ang2f, 8 * 129, tf.ap(), ti_.ap(), S5R)
    o5("act", lambda e: e.activation(out=St.ap().rearrange("p k r -> p (k r)"), in_=angf, func=AF.Sin))
    o5("act", lambda e: e.activation(out=Ct.ap().rearrange("p k r -> p (k r)"), in_=ang2f, func=AF.Sin))
    o5("dve", lambda e: e.tensor_tensor(out=SM(3), in0=SM(1), in1=Ct.ap()[:, :, 1], op=ALU.mult))
    o5("dve", lambda e: e.tensor_tensor(out=SM(4), in0=SM(1), in1=St.ap()[:, :, 1], op=ALU.mult))
    o5("dve", lambda e: e.tensor_scalar(out=SM(3), in0=SM(3), scalar1=-1.0, scalar2=None, op0=ALU.add))
    o5("dve", lambda e: e.tensor_tensor(out=SM(5), in0=lre, in1=lre, op=ALU.mult))
    o5("dve", lambda e: e.tensor_tensor(out=SM(6), in0=lim, in1=lim, op=ALU.mult))
    o5("dve", lambda e: e.tensor_tensor(out=SM(5), in0=SM(5), in1=SM(6), op=ALU.add))
    o5("dve", lambda e: e.reciprocal(out=SM(5), in_=SM(5)))
    o5("dve", lambda e: e.tensor_tensor(out=SM(6), in0=SM(3), in1=lre, op=ALU.mult))
    o5("dve", lambda e: e.tensor_tensor(out=SM(7), in0=SM(4), in1=lim, op=ALU.mult))
    o5("dve", lambda e: e.tensor_tensor(out=SM(6), in0=SM(6), in1=SM(7), op=ALU.add))
    o5("dve", lambda e: e.tensor_tensor(out=SM(8), in0=SM(6), in1=SM(5), op=ALU.mult))
    o5("dve", lambda e: e.tensor_tensor(out=SM(6), in0=SM(4), in1=lre, op=ALU.mult))
    o5("dve", lambda e: e.tensor_tensor(out=SM(7), in0=SM(3), in1=lim, op=ALU.mult))
    o5("dve", lambda e: e.tensor_tensor(out=SM(6), in0=SM(6), in1=SM(7), op=ALU.subtract))
    o5("dve", lambda e: e.tensor_tensor(out=SM(9), in0=SM(6), in1=SM(5), op=ALU.mult))
    o5("dve", lambda e: e.tensor_scalar(out=SM(10), in0=SM(8), scalar1=-1.0, scalar2=None, op0=ALU.mult))
    for k in range(8):
        C_ = Ct.ap()[:, k, 0:128]; S_ = St.ap()[:, k, 0:128]
        gre = sp_.ap()[:, 8, k:k + 1]; gim = sp_.ap()[:, 9, k:k + 1]; ngre = sp_.ap()[:, 10, k:k + 1]
        o5("dve", lambda e, k=k, C_=C_, gre=gre: e.tensor_scalar(out=T1re.ap()[:, k, :], in0=C_, scalar1=gre, scalar2=None, op0=ALU.mult))
        o5("dve", lambda e, k=k, S_=S_, gim=gim: e.scalar_tensor_tensor(out=T1re.ap()[:, k, :], in0=S_, scalar=gim, in1=T1re.ap()[:, k, :], op0=ALU.mult, op1=ALU.add))
        o5("dve", lambda e, k=k, C_=C_, gim=gim: e.tensor_scalar(out=T1im.ap()[:, k, :], in0=C_, scalar1=gim, scalar2=None, op0=ALU.mult))
        o5("dve", lambda e, k=k, S_=S_, ngre=ngre: e.scalar_tensor_tensor(out=T1im.ap()[:, k, :], in0=S_, scalar=ngre, in1=T1im.ap()[:, k, :], op0=ALU.mult, op1=ALU.add))
        o5("dve", lambda e, k=k: e.tensor_scalar(out=RHO.ap()[:, k, :], in0=ones.ap(), scalar1=sp_.ap()[:, 1, k:k + 1], scalar2=None, op0=ALU.mult))
    p.op("dve", lambda e: e.tensor_scalar(out=czc.ap()[:, 8:16, :], in0=czc.ap()[:, 8:16, :], scalar1=-1.0, scalar2=None,
                                          op0=ALU.mult), reads=[czc], writes=[czc])
    s5i = T([128, 2, 8], "s5init")
    p.op("pool", lambda e: e.memset(s5i.ap(), 0.0), writes=[s5i])
    Wr = T([128, 8, 128], "Wr"); Wi = T([128, 8, 128], "Wi")
    ys5 = T([128, 2, 128], "ys5")

    def s5_block(t0, zo):
        VrA = Vr.ap()[:, :, 0:128]; ViA = Vi.ap()[:, :, 0:128]; t5aA = t5a.ap()[:, :, 0:128]; t5bA = t5b.ap()[:, :, 0:128]
        bre = psb.ap()[:, 0, :].rearrange("p (k t) -> p k t", k=8)
        bim = psb.ap()[:, 1, :].rearrange("p (k t) -> p k t", k=8)
        for k in range(8):
            u = zo.ap()[:, 5 + k // 4, :]
            p.op("pe", lambda e, k=k, u=u: e.matmul(bre[:, k, :], lhsT=bz.ap()[:, k, :], rhs=u, start=True, stop=True),
                 reads=[bz, zo], writes=[psb])
            p.op("pe", lambda e, k=k, u=u: e.matmul(bim[:, k, :], lhsT=bz.ap()[:, 8 + k, :], rhs=u, start=True, stop=True),
                 reads=[bz, zo], writes=[psb])
        tt = lambda e, o, a, b, op: e.tensor_tensor(out=o, in0=a, in1=b, op=op)
        yield
        p.op("dve", lambda e: tt(e, t5aA, bre, T1re.ap(), ALU.mult), reads=[psb, T1re], writes=[t5a])
        p.op("dve", lambda e: tt(e, t5bA, bim, T1im.ap(), ALU.mult), reads=[psb, T1im], writes=[t5b])
        p.op("pool", lambda e: tt(e, VrA, t5aA, t5bA, ALU.subtract), reads=[t5a, t5b], writes=[Vr])
        p.op("dve", lambda e: tt(e, t5aA, bre, T1im.ap(), ALU.mult), reads=[psb, T1im], writes=[t5a])
        p.op("dve", lambda e: tt(e, t5bA, bim, T1re.ap(), ALU.mult), reads=[psb, T1re], writes=[t5b])
        p.op("pool", lambda e: tt(e, ViA, t5aA, t5bA, ALU.add), reads=[t5a, t5b], writes=[Vi])
        yield
        for k in range(8):
            p.op("dve", lambda e, k=k: e.tensor_tensor_scan(out=Wr.ap()[:, k, :], data0=RHO.ap()[:, k, :], data1=VrA[:, k, :],
                                                            initial=s5i.ap()[:, 0, k:k + 1], op0=ALU.mult, op1=ALU.add),
                 reads=[RHO, Vr, s5i], writes=[Wr])
            p.op("dve", lambda e, k=k: e.tensor_tensor_scan(out=Wi.ap()[:, k, :], data0=RHO.ap()[:, k, :], data1=ViA[:, k, :],
                                                            initial=s5i.ap()[:, 1, k:k + 1], op0=ALU.mult, op1=ALU.add),
                 reads=[RHO, Vi, s5i], writes=[Wi])
        yield
        wr_l = Wr.ap()[:, :, 127]; wi_l = Wi.ap()[:, :, 127]; c128 = Ct.ap()[:, :, 128]; s128 = St.ap()[:, :, 128]
        p.op("pool", lambda e: tt(e, SM(11), wr_l, c128, ALU.mult), reads=[Wr, Ct], writes=[sp_])
        p.op("pool", lambda e: tt(e, SM(12), wi_l, s128, ALU.mult), reads=[Wi, St, sp_], writes=[sp_])
        p.op("pool", lambda e: tt(e, s5i.ap()[:, 0, :], SM(11), SM(12), ALU.subtract), reads=[sp_], writes=[s5i])
        p.op("pool", lambda e: tt(e, SM(11), wr_l, s128, ALU.mult), reads=[Wr, St, sp_], writes=[sp_])
        p.op("pool", lambda e: tt(e, SM(12), wi_l, c128, ALU.mult), reads=[Wi, Ct, sp_], writes=[sp_])
        p.op("pool", lambda e: tt(e, s5i.ap()[:, 1, :], SM(11), SM(12), ALU.add), reads=[sp_], writes=[s5i])
        yield
        C3 = Ct.ap()[:, :, 0:128]; S3 = St.ap()[:, :, 0:128]
        p.op("dve", lambda e: tt(e, t5aA, Wr.ap(), C3, ALU.mult), reads=[Wr, Ct], writes=[t5a])
        p.op("pool", lambda e: tt(e, t5bA, Wi.ap(), S3, ALU.mult), reads=[Wi, St], writes=[t5b])
        p.op("dve", lambda e: tt(e, VrA, t5aA, t5bA, ALU.subtract), reads=[t5a, t5b], writes=[Vr])
        p.op("dve", lambda e: tt(e, t5aA, Wr.ap(), S3, ALU.mult), reads=[Wr, St], writes=[t5a])
        p.op("pool", lambda e: tt(e, t5bA, Wi.ap(), C3, ALU.mult), reads=[Wi, Ct], writes=[t5b])
        p.op("dve", lambda e: tt(e, ViA, t5aA, t5bA, ALU.add), reads=[t5a, t5b], writes=[Vi])
        yield
        py = PS()
        for mt in range(2):
            for i, k in enumerate(range(4 * mt, 4 * mt + 4)):
                p.op("pe", lambda e, k=k, mt=mt, i=i: e.matmul(py.ap()[:, mt * 128:(mt + 1) * 128], lhsT=czc.ap()[:, k, :],
                                                               rhs=VrA[:, k, :], start=(i == 0), stop=False),
                     reads=[czc, Vr], writes=[py])
                p.op("pe", lambda e, k=k, mt=mt, i=i: e.matmul(py.ap()[:, mt * 128:(mt + 1) * 128], lhsT=czc.ap()[:, 8 + k, :],
                                                               rhs=ViA[:, k, :], start=False, stop=(i == 3)),
                     reads=[czc, Vi], writes=[py])
        p.op("act", lambda e: e.copy(out=ys5.ap().rearrange("p m t -> p (m t)"), in_=py.ap()[:, 0:256]), reads=[py], writes=[ys5])
        store(o_s5, "m p t -> p m t", ys5, t0, 128, 2, "sp")
        yield

    WW = 322
    Wn = [T([128, NCONV, WW], "Wn0")]
    zo_t = [T([128, NOTH, 128], f"zo{i}") for i in range(2)]
    cz2 = [T([128, NCONV, 128], f"cz{i}") for i in range(2)]
    czb = [cz2[0]]
    ctmp = T([128, 128], "ctmp")
    zsv = zs.ap().rearrange("c p t -> p c t")
    HC = 7
    CGR = [(g * HC, min((g + 1) * HC, NCONV)) for g in range((NCONV + HC - 1) // HC)]
    if rev:
        wst = T([128, HC, WW], "wst")
        zst = T([128, NOTH, 128], "zst")
        ost = T([128, 6, 128], "ost")

    def store(obuf, pat, ytile, t0, npart, nh, q):
        dst = obuf.ap()[:, :, nat(t0, t0 + 128)].rearrange(pat)
        if not rev:
            p.dma(q, dst, ytile.ap(), reads=[ytile], writes=[obuf])
        else:
            p.op("pool", lambda e: e.tensor_copy(out=ost.ap()[:npart, :nh, :], in_=ytile.ap()[:, :, ::-1]), reads=[ytile], writes=[ost])
            p.dma(q, dst, ost.ap()[:npart, :nh, :], reads=[ost], writes=[obuf])

    def conv_block(j):
        t0 = j * 128
        W = Wn[0]
        cz = cz2[j % 2]
        zo = zo_t[j % 2]
        lo_r, hi_r = (0, nctx) if t0 < nctx else (nctx, NS)
        a, b = max(t0 - 128, lo_r), min(t0 - 128 + WW, hi_r)
        if a > t0 - 128 or b < t0 - 128 + WW:
            p.op("pool", lambda e: e.memset(W.ap(), 0.0), writes=[W])
        wa, wb = a - (t0 - 128), b - (t0 - 128)
        if not rev:
            p.dma("sp", W.ap()[:, :, wa:wb], zsv[:, 0:NCONV, a:b], reads=[zs], writes=[W])
            p.dma("act", zo.ap(), zsv[:, NCONV:NSCH, t0:t0 + 128], reads=[zs], writes=[zo])
        else:
            for hh, (c_lo, c_hi) in enumerate(CGR):
                p.dma(("sp", "act")[hh % 2], wst.ap()[:, 0:c_hi - c_lo, 0:b - a], zsv[:, c_lo:c_hi, nat(a, b)], reads=[zs], writes=[wst])
                p.op(("dve", "pool")[hh % 2], lambda e, c_lo=c_lo, c_hi=c_hi: e.tensor_copy(
                    out=W.ap()[:, c_lo:c_hi, wa:wb], in_=wst.ap()[:, 0:c_hi - c_lo, 0:b - a][:, :, ::-1]), reads=[wst], writes=[W])
            p.dma("act", zst.ap(), zsv[:, NCONV:NSCH, nat(t0, t0 + 128)], reads=[zs], writes=[zst])
            p.op("pool", lambda e: e.tensor_copy(out=zo.ap(), in_=zst.ap()[:, :, ::-1]), reads=[zst], writes=[zo])
        grid = t0 >= nctx
        for ci in range(NCONV):
            if ci % 2 == 0:
                yield
            eng = EWc()
            nch = SCH[ci][2]
            o = cz.ap()[:nch, ci, :]
            wv = W.ap()[:nch, ci, :]
            ctr = 4
            p.op(eng, lambda e, o=o, wv=wv, ci=ci, nch=nch: e.tensor_scalar(
                out=o, in0=wv[:, 128:256], scalar1=cw.ap()[:nch, ci, ctr:ctr + 1], scalar2=None, op0=ALU.mult),
                reads=[W, cw], writes=[cz])
            taps = []
            if grid:
                for dy in (-1, 0, 1):
                    for dx in (-1, 0, 1):
                        if dy == 0 and dx == 0:
                            continue
                        taps.append((dy, dx, (dy + 1) * 3 + dx + 1))
            else:
                taps = [(0, -1, 3), (0, 1, 5)]
            for dy, dx, tp in taps:
                s = 128 + 64 * dy
                if grid and dx != 0:
                    src = wv[:, s:s + 128].rearrange("p (r c) -> p r c", c=64)
                    dst = o.rearrange("p (r c) -> p r c", c=64)
                    if dx == -1:
                        src = src[:, :, 0:63]; dst = dst[:, :, 1:64]
                    else:
                        src = src[:, :, 1:64]; dst = dst[:, :, 0:63]
                else:
                    src = wv[:, s + dx:s + dx + 128]; dst = o
                if eng == "dve":
                    p.op(eng, lambda e, src=src, dst=dst, ci=ci, tp=tp, nch=nch: e.scalar_tensor_tensor(
                        out=dst, in0=src, scalar=cw.ap()[:nch, ci, tp:tp + 1], in1=dst, op0=ALU.mult, op1=ALU.add),
                        reads=[W, cw, cz], writes=[cz])
                else:
                    if grid and dx != 0:
                        tmp = ctmp.ap()[:nch, :].rearrange("p (r c) -> p r c", c=64)[:, :, 0:63]
                    else:
                        tmp = ctmp.ap()[:nch, :]
                    p.op(eng, lambda e, src=src, tmp=tmp, ci=ci, tp=tp, nch=nch: e.tensor_scalar(
                        out=tmp, in0=src, scalar1=cw.ap()[:nch, ci, tp:tp + 1], scalar2=None, op0=ALU.mult),
                        reads=[W, cw], writes=[ctmp])
                    p.op(eng, lambda e, dst=dst, tmp=tmp: e.tensor_tensor(out=dst, in0=dst, in1=tmp, op=ALU.add),
                         reads=[ctmp, cz], writes=[cz])
        yield

    def decay_prep(nk, lw_ap, lw_reads, cs, G_, Ginv, Gend, Gex=None, lwsb=None):
        p.op("dve", lambda e: e.tensor_tensor_scan(out=cs.ap()[:nk, :], data0=m01[:nk, :], data1=lw_ap, initial=0.0,
                                                   op0=ALU.mult, op1=ALU.add), reads=[cst] + lw_reads, writes=[cs])
        p.op("act", lambda e: e.activation(out=G_.ap()[:nk, :], in_=cs.ap()[:nk, :], func=AF.Exp), reads=[cs], writes=[G_])
        p.op("act", lambda e: e.activation(out=Ginv.ap()[:nk, :], in_=cs.ap()[:nk, :], func=AF.Exp, scale=-1.0), reads=[cs], writes=[Ginv])
        for c in range(2):
            p.op("act", lambda e, c=c: e.activation(out=Gend.ap()[:nk, 64 * c:64 * c + 64], in_=cs.ap()[:nk, 64 * c:64 * c + 64],
                                                    func=AF.Exp, scale=-1.0, bias=cs.ap()[:nk, 64 * c + 63:64 * c + 64]),
                 reads=[cs], writes=[Gend])
        if Gex is not None:
            p.op("dve", lambda e: e.tensor_tensor(out=Gex.ap()[:nk, :], in0=cs.ap()[:nk, :], in1=lwsb, op=ALU.subtract),
                 reads=[cs] + lw_reads, writes=[Gex])
            p.op("act", lambda e: e.activation(out=Gex.ap()[:nk, :], in_=Gex.ap()[:nk, :], func=AF.Exp), reads=[Gex], writes=[Gex])

    rS = T([64, 6, 64], "rwS", ); p.op("pool", lambda e: e.memset(rS.ap(), 0.0), writes=[rS])
    tw = T([64, 128], "tw")
    NG = 3
    nm = ["sgw", "a", "kkv", "sq", "kap", "kd", "b", "cs", "Ginv", "Gend", "Gex"]
    R_ = {n: T([64, 128], "r_" + n) for n in nm}
    RPn = ("rt", "kt", "bt", "kkt", "Bh", "Kh", "G")
    RP = [{n: T([64, 128], f"rp{g}_" + n) for n in RPn} for g in range(NG)]
    tm4s = [T([64, 4, 64], f"tm4_{g}") for g in range(NG)]
    tt5s = [T([64, 5, 64], f"tt5_{g}") for g in range(NG)]
    Rrs = [T([64, 128], f"Rr{g}") for g in range(NG)]
    Xxs = [T([64, 128], f"Xx{g}") for g in range(NG)]
    nZs = [T([64, 64], f"nZ{g}") for g in range(NG)]
    Pbs = [[T([64, 2, 64], f"Pb{g}_{i}") for i in range(2)] for g in range(NG)]
    QPs = [T([64, 2, 64], f"QP{g}") for g in range(NG)]
    yrw = T([64, 6, 128], "yrw"); ybn = T([64, 6, 128], "ybn")
    NEG_E = -float(np.exp(-0.5))

    def rw_prep(h, g, zo):
        wadn = zo.ap()[:, 7, :]
        r = czb[0].ap()[:64, h, :]; k = czb[0].ap()[:64, 6 + h, :]; v = czb[0].ap()[:64, 12 + h, :]
        prm = lambda w: rwp.ap()[:, w, h:h + 1]
        A = lambda n: R_[n].ap()
        Q = lambda n: RP[g][n].ap()
        ps = PS()
        p.op("pe", lambda e: e.matmul(ps.ap()[:64, 0:128], lhsT=waup.ap()[0:64, h * 64:(h + 1) * 64], rhs=tw.ap(), start=True, stop=True),
             reads=[waup, tw], writes=[ps])
        p.op("pe", lambda e: e.matmul(ps.ap()[:64, 128:256], lhsT=waup.ap()[64:128, h * 64:(h + 1) * 64], rhs=wadn[64:128, :], start=True, stop=True),
             reads=[waup, zo], writes=[ps])
        p.op("act", lambda e: e.activation(out=A("sgw"), in_=ps.ap()[:64, 0:128], func=AF.Sigmoid, bias=prm(0)), reads=[ps, rwp], writes=[R_["sgw"]])
        p.op("act", lambda e: e.activation(out=A("a"), in_=ps.ap()[:64, 128:256], func=AF.Sigmoid, bias=prm(1)), reads=[ps, rwp], writes=[R_["a"]])
        p.op("dve", lambda e: e.tensor_scalar(out=A("sgw"), in0=A("sgw"), scalar1=NEG_E, scalar2=None, op0=ALU.mult), reads=[R_["sgw"]], writes=[R_["sgw"]])
        p.op("dve", lambda e: e.tensor_scalar(out=A("kkv"), in0=k, scalar1=prm(2), scalar2=None, op0=ALU.mult), reads=[czb[0], rwp], writes=[R_["kkv"]])
        p.op("act", lambda e: e.activation(out=A("sq"), in_=A("kkv"), func=AF.Square), reads=[R_["kkv"]], writes=[R_["sq"]])
        ps2 = PS()
        p.op("pe", lambda e: e.matmul(ps2.ap()[:64, 0:128], lhsT=ones.ap()[0:64, 0:64], rhs=A("sq"), start=True, stop=True), reads=[ones, R_["sq"]], writes=[ps2])
        p.op("dve", lambda e: e.tensor_scalar(out=A("sq"), in0=ps2.ap()[:64, 0:128], scalar1=1e-24, scalar2=None, op0=ALU.max), reads=[ps2], writes=[R_["sq"]])
        p.op("act", lambda e: e.activation(out=A("sq"), in_=A("sq"), func=AF.Sqrt), reads=[R_["sq"]], writes=[R_["sq"]])
        p.op("dve", lambda e: e.reciprocal(out=A("sq"), in_=A("sq")), reads=[R_["sq"]], writes=[R_["sq"]])
        p.op("dve", lambda e: e.tensor_tensor(out=A("kap"), in0=A("kkv"), in1=A("sq"), op=ALU.mult), reads=[R_["kkv"], R_["sq"]], writes=[R_["kap"]])
        p.op("dve", lambda e: e.tensor_scalar(out=A("kd"), in0=A("a"), scalar1=-1.0, scalar2=prm(3), op0=ALU.add, op1=ALU.mult), reads=[R_["a"], rwp], writes=[R_["kd"]])
        p.op("dve", lambda e: e.scalar_tensor_tensor(out=A("kd"), in0=A("kd"), scalar=1.0, in1=k, op0=ALU.add, op1=ALU.mult), reads=[R_["kd"], czb[0]], writes=[R_["kd"]])
        p.op("dve", lambda e: e.tensor_tensor(out=A("b"), in0=A("a"), in1=A("kap"), op=ALU.mult), reads=[R_["a"], R_["kap"]], writes=[R_["b"]])
        p.op("dve", lambda e: e.scalar_tensor_tensor(out=A("kkv"), in0=r, scalar=prm(4), in1=A("kd"), op0=ALU.mult, op1=ALU.mult), reads=[czb[0], rwp, R_["kd"]], writes=[R_["kkv"]])
        p.op("pe", lambda e: e.matmul(ps2.ap()[:64, 128:256], lhsT=ones.ap()[0:64, 0:64], rhs=A("kkv"), start=True, stop=True), reads=[ones, R_["kkv"]], writes=[ps2])
        p.op("dve", lambda e: e.tensor_tensor(out=ybn.ap()[:, h, :], in0=ps2.ap()[:64, 128:256], in1=v, op=ALU.mult), reads=[ps2, czb[0]], writes=[ybn])
        decay_prep(64, A("sgw"), [R_["sgw"]], R_["cs"], RP[g]["G"], R_["Ginv"], R_["Gend"], R_["Gex"], A("sgw"))
        for (o_, x_, xr, g_) in (("rt", r, czb[0], RP[g]["G"]), ("kt", A("kap"), R_["kap"], R_["Gex"]), ("bt", A("b"), R_["b"], R_["Ginv"]),
                                 ("kkt", A("kd"), R_["kd"], R_["Ginv"]), ("Bh", A("b"), R_["b"], R_["Gend"]), ("Kh", A("kd"), R_["kd"], R_["Gend"])):
            p.op(EW(), lambda e, o_=o_, x_=x_, g_=g_: e.tensor_tensor(out=Q(o_), in0=x_, in1=g_.ap(), op=ALU.mult),
                 reads=[xr, g_], writes=[RP[g][o_]])

    def rw_chunk(h, g, c):
        v = czb[0].ap()[:64, 12 + h, :]
        Q = lambda n: RP[g][n].ap()
        RQ = lambda n: RP[g][n]
        tm4, tt5, Rr, Xx, nZ, Pb, QP = tm4s[g], tt5s[g], Rrs[g], Xxs[g], nZs[g], Pbs[g], QPs[g]
        cs_ = slice(64 * c, 64 * c + 64)
        pt = PS()
        for i, (src, rd) in enumerate(((v[:, cs_], czb[0]), (Q("kt")[:, cs_], RQ("kt")), (Q("Bh")[:, cs_], RQ("Bh")), (Q("Kh")[:, cs_], RQ("Kh")))):
            p.op("pe", lambda e, i=i, src=src: e.transpose(pt.ap()[:64, 64 * i:64 * i + 64], src, ident[0:64, 0:64]),
                 reads=[rd, cst], writes=[pt])
        p.op("act", lambda e: e.copy(out=tm4.ap().rearrange("p a b -> p (a b)"), in_=pt.ap()[:64, 0:256]), reads=[pt], writes=[tm4])
        Vtm = tm4.ap()[:, 0, :]; Ktm = tm4.ap()[:, 1, :]; Btm = tm4.ap()[:, 2, :]; Khtm = tm4.ap()[:, 3, :]
        p5 = PS()
        for i, (l_, r_) in enumerate((("bt", "kt"), ("kt", "bt"), ("kkt", "kt"), ("bt", "rt"), ("kkt", "rt"))):
            p.op("pe", lambda e, i=i, l_=l_, r_=r_: e.matmul(p5.ap()[:64, 64 * i:64 * i + 64], lhsT=Q(l_)[:, cs_], rhs=Q(r_)[:, cs_], start=True, stop=True),
                 reads=[RQ(l_), RQ(r_)], writes=[p5])
        p.op("dve", lambda e: e.tensor_tensor(out=tt5.ap().rearrange("p a b -> p (a b)"), in0=p5.ap()[:64, 0:320],
                                              in1=mask5.ap().rearrange("p a b -> p (a b)"), op=ALU.mult), reads=[p5, mask5], writes=[tt5])
        U = tt5.ap()[:, 0, :]; L = tt5.ap()[:, 1, :]; LkT = tt5.ap()[:, 2, :]; AbrT = tt5.ap()[:, 3, :]; AkrT = tt5.ap()[:, 4, :]
        yield
        p6 = PS()
        p.op("pe", lambda e: e.matmul(p6.ap()[:64, 0:64], lhsT=LkT, rhs=Vtm, start=True, stop=True), reads=[tt5, tm4], writes=[p6])
        p.op("act", lambda e: e.copy(out=Rr.ap()[:, 64:128], in_=p6.ap()[:64, 0:64]), reads=[p6], writes=[Rr])
        p.op("dve", lambda e: e.tensor_copy(out=Rr.ap()[:, 0:64], in_=Ktm), reads=[tm4], writes=[Rr])
        yield
        p7 = PS()
        p.op("pe", lambda e: e.matmul(p7.ap()[:64, 0:128], lhsT=U, rhs=Rr.ap(), start=True, stop=True), reads=[tt5, Rr], writes=[p7])
        p.op("dve", lambda e: e.tensor_tensor(out=Xx.ap(), in0=Rr.ap(), in1=p7.ap()[:64, 0:128], op=ALU.subtract), reads=[Rr, p7], writes=[Xx])
        Pc, PTc, Prd = L, U, [tt5]
        for lvl in range(5):
            pn = Pb[lvl % 2]
            pq = PS()
            last = lvl == 4
            if not last:
                p.op("pe", lambda e, Pc=Pc, PTc=PTc: e.matmul(pq.ap()[:64, 0:64], lhsT=PTc, rhs=Pc, start=True, stop=True), reads=Prd, writes=[pq])
            p.op("pe", lambda e, Pc=Pc, PTc=PTc: e.matmul(pq.ap()[:64, 64:128], lhsT=Pc, rhs=PTc, start=True, stop=True), reads=Prd, writes=[pq])
            if not last:
                p.op("act", lambda e, pn=pn: e.copy(out=pn.ap().rearrange("p a b -> p (a b)"), in_=pq.ap()[:64, 0:128]), reads=[pq], writes=[pn])
            else:
                p.op("act", lambda e, pn=pn: e.copy(out=pn.ap()[:, 1, :], in_=pq.ap()[:64, 64:128]), reads=[pq], writes=[pn])
            Pc, PTc, Prd = pn.ap()[:, 0, :], pn.ap()[:, 1, :], [pn]
            yield
            px = PS()
            p.op("pe", lambda e, PTc=PTc: e.matmul(px.ap()[:64, 0:128], lhsT=PTc, rhs=Xx.ap(), start=True, stop=True), reads=Prd + [Xx], writes=[px])
            p.op("dve", lambda e: e.tensor_tensor(out=Xx.ap(), in0=Xx.ap(), in1=px.ap()[:64, 0:128], op=ALU.add), reads=[Xx, px], writes=[Xx])
            yield
        Gm = Xx.ap()[:, 0:64]
        p.op("act", lambda e: e.mul(out=nZ.ap(), in_=Xx.ap()[:, 64:128], mul=-1.0), reads=[Xx], writes=[nZ])
        p8 = PS()
        p.op("pe", lambda e: e.matmul(p8.ap()[:64, 0:64], lhsT=Gm, rhs=AbrT, start=True, stop=True), reads=[Xx, tt5], writes=[p8])
        p.op("pe", lambda e: e.matmul(p8.ap()[:64, 64:128], lhsT=Gm, rhs=Btm, start=True, stop=True), reads=[Xx, tm4], writes=[p8])
        p.op("dve", lambda e: e.tensor_tensor(out=QP.ap()[:, 0, :], in0=Q("rt")[:, cs_], in1=p8.ap()[:64, 0:64], op=ALU.subtract), reads=[RQ("rt"), p8], writes=[QP])
        p.op("dve", lambda e: e.scalar_tensor_tensor(out=QP.ap()[:, 1, :], in0=ident[0:64, 0:64], scalar=Q("G")[:, 64 * c + 63:64 * c + 64],
                                                     in1=p8.ap()[:64, 64:128], op0=ALU.mult, op1=ALU.subtract), reads=[cst, RQ("G"), p8], writes=[QP])
        yield
        p9 = PS()
        p.op("pe", lambda e: e.matmul(p9.ap()[:64, 0:64], lhsT=rS.ap()[:, h, :], rhs=QP.ap()[:, 0, :], start=True, stop=False), reads=[rS, QP], writes=[p9])
        p.op("pe", lambda e: e.matmul(p9.ap()[:64, 0:64], lhsT=Vtm, rhs=AkrT, start=False, stop=False), reads=[tm4, tt5], writes=[p9])
        p.op("pe", lambda e: e.matmul(p9.ap()[:64, 0:64], lhsT=nZ.ap(), rhs=AbrT, start=False, stop=True), reads=[nZ, tt5], writes=[p9])
        p.op("act", lambda e: e.copy(out=yrw.ap()[:, h, cs_], in_=p9.ap()[:64, 0:64]), reads=[p9], writes=[yrw])
        p10 = PS()
        p.op("pe", lambda e: e.matmul(p10.ap()[:64, 0:64], lhsT=QP.ap()[:, 1, :], rhs=rS.ap()[:, h, :], start=True, stop=False), reads=[QP, rS], writes=[p10])
        p.op("pe", lambda e: e.matmul(p10.ap()[:64, 0:64], lhsT=Khtm, rhs=Vtm, start=False, stop=False), reads=[tm4], writes=[p10])
        p.op("pe", lambda e: e.matmul(p10.ap()[:64, 0:64], lhsT=Btm, rhs=nZ.ap(), start=False, stop=True), reads=[tm4, nZ], writes=[p10])
        p.op("dve", lambda e: e.tensor_copy(out=rS.ap()[:, h, :], in_=p10.ap()[:64, 0:64]), reads=[p10], writes=[rS])
        yield

    def run_interleaved(gens):
        alive = list(gens)
        while alive:
            nxt = []
            for gen in alive:
                try:
                    next(gen)
                    nxt.append(gen)
                except StopIteration:
                    pass
            alive = nxt

    def rwkv_block(t0, zo):
        wadn = zo.ap()[:, 7, :]
        p.op("act", lambda e: e.activation(out=tw.ap(), in_=wadn[0:64, :], func=AF.Tanh), reads=[zo], writes=[tw])
        for grp in range(6 // NG):
            heads = list(range(grp * NG, (grp + 1) * NG))
            for g, h in enumerate(heads):
                rw_prep(h, g, zo)
                yield
            for c in range(2):
                alive = [rw_chunk(h, g, c) for g, h in enumerate(heads)]
                while alive:
                    nxt = []
                    for gen in alive:
                        try:
                            next(gen)
                            nxt.append(gen)
                        except StopIteration:
                            pass
                    alive = nxt
                    yield
        store(o_rw, "h p t -> p h t", yrw, t0, 64, 6, "sp")
        store(o_bn, "h p t -> p h t", ybn, t0, 64, 6, "act")
        yield

    mS = T([96, 4, 192], "mlS"); p.op("pool", lambda e: e.memset(mS.ap(), 0.0), writes=[mS])
    gl1 = T([16, 128], "gl1"); gl2 = T([16, 128], "gl2")
    mn = ["q", "ks", "ei", "kp", "cs", "G", "Ginv", "Gend", "rt", "kkt", "Kh", "den"]
    M_ = {n: T([96, 128], "m_" + n) for n in mn}
    mtm = T([64, 2, 96], "mtm"); makr = T([64, 64], "makr")
    yml = T([96, 4, 128], "yml")
    KSC = float(96 ** -0.5)

    def mlstm_block(t0, zo):
        gl = zo.ap()[0:16, 4, :]
        p.op("dve", lambda e: e.tensor_scalar(out=gl1.ap(), in0=gl, scalar1=glb.ap()[:, 0:1], scalar2=None, op0=ALU.add), reads=[zo, glb], writes=[gl1])
        p.op("act", lambda e: e.activation(out=gl2.ap(), in_=gl1.ap(), func=AF.Sigmoid), reads=[gl1], writes=[gl2])
        p.op("act", lambda e: e.activation(out=gl2.ap(), in_=gl2.ap(), func=AF.Ln), reads=[gl2], writes=[gl2])
        for h in range(4):
            B = lambda n: M_[n].ap()
            q = czb[0].ap()[:96, 18 + h, :]; k = czb[0].ap()[:96, 22 + h, :]; v = zo.ap()[:96, h, :]
            ps = PS()
            p.op("pe", lambda e: e.matmul(ps.ap()[:96, 0:128], lhsT=sel.ap()[:, 4 + h, :], rhs=gl2.ap(), start=True, stop=True), reads=[sel, gl2], writes=[ps])
            p.op("pe", lambda e: e.matmul(ps.ap()[:96, 128:256], lhsT=sel.ap()[:, h, :], rhs=gl1.ap(), start=True, stop=True), reads=[sel, gl1], writes=[ps])
            p.op("act", lambda e: e.activation(out=B("ei"), in_=ps.ap()[:96, 128:256], func=AF.Exp), reads=[ps], writes=[M_["ei"]])
            p.op("act", lambda e: e.activation(out=B("q"), in_=q, func=AF.Silu), reads=[czb[0]], writes=[M_["q"]])
            p.op("act", lambda e: e.activation(out=B("ks"), in_=k, func=AF.Silu), reads=[czb[0]], writes=[M_["ks"]])
            p.op("dve", lambda e: e.scalar_tensor_tensor(out=B("kp"), in0=B("ks"), scalar=KSC, in1=B("ei"), op0=ALU.mult, op1=ALU.mult), reads=[M_["ks"], M_["ei"]], writes=[M_["kp"]])
            decay_prep(96, ps.ap()[:96, 0:128], [ps], M_["cs"], M_["G"], M_["Ginv"], M_["Gend"])
            for (o_, x_, g_) in (("rt", "q", "G"), ("kkt", "kp", "Ginv"), ("Kh", "kp", "Gend")):
                p.op(EW(), lambda e, o_=o_, x_=x_, g_=g_: e.tensor_tensor(out=B(o_), in0=B(x_), in1=B(g_), op=ALU.mult), reads=[M_[x_], M_[g_]], writes=[M_[o_]])
            yield
            for c in range(2):
                cs_ = slice(64 * c, 64 * c + 64)
                pt = PS()
                p.op("pe", lambda e: e.transpose(pt.ap()[:64, 0:96], v[:, cs_], ident[0:96, 0:96]), reads=[zo, cst], writes=[pt])
                p.op("pe", lambda e: e.transpose(pt.ap()[:64, 96:192], B("Kh")[:, cs_], ident[0:96, 0:96]), reads=[M_["Kh"], cst], writes=[pt])
                p.op("act", lambda e: e.copy(out=mtm.ap().rearrange("p a b -> p (a b)"), in_=pt.ap()[:64, 0:192]), reads=[pt], writes=[mtm])
                Vtm = mtm.ap()[:, 0, :]; Khtm = mtm.ap()[:, 1, :]
                yield
                pa = PS()
                p.op("pe", lambda e: e.matmul(pa.ap()[:64, 0:64], lhsT=B("kkt")[:, cs_], rhs=B("rt")[:, cs_], start=True, stop=True), reads=[M_["kkt"], M_["rt"]], writes=[pa])
                p.op("dve", lambda e: e.tensor_tensor(out=makr.ap(), in0=pa.ap()[:64, 0:64], in1=m_iu[0:64, 0:64], op=ALU.mult), reads=[pa, cst], writes=[makr])
                yield
                py = PS()
                p.op("pe", lambda e: e.matmul(py.ap()[:96, 0:64], lhsT=mS.ap()[:, h, 0:96], rhs=B("rt")[:, cs_], start=True, stop=False), reads=[mS, M_["rt"]], writes=[py])
                p.op("pe", lambda e: e.matmul(py.ap()[:96, 0:64], lhsT=Vtm, rhs=makr.ap(), start=False, stop=True), reads=[mtm, makr], writes=[py])
                p.op("pe", lambda e: e.matmul(py.ap()[:96, 64:128], lhsT=mS.ap()[:, h, 96:192], rhs=B("rt")[:, cs_], start=True, stop=False), reads=[mS, M_["rt"]], writes=[py])
                p.op("pe", lambda e: e.matmul(py.ap()[:96, 64:128], lhsT=ones.ap()[0:64, 0:96], rhs=makr.ap(), start=False, stop=True), reads=[ones, makr], writes=[py])
                p.op("act", lambda e: e.activation(out=B("den")[:, 0:64], in_=py.ap()[:96, 64:128], func=AF.Abs), reads=[py], writes=[M_["den"]])
                p.op("dve", lambda e: e.tensor_scalar(out=B("den")[:, 0:64], in0=B("den")[:, 0:64], scalar1=1.0, scalar2=None, op0=ALU.max), reads=[M_["den"]], writes=[M_["den"]])
                p.op("dve", lambda e: e.reciprocal(out=B("den")[:, 0:64], in_=B("den")[:, 0:64]), reads=[M_["den"]], writes=[M_["den"]])
                p.op("dve", lambda e: e.tensor_tensor(out=yml.ap()[:, h, cs_], in0=py.ap()[:96, 0:64], in1=B("den")[:, 0:64], op=ALU.mult), reads=[py, M_["den"]], writes=[yml])
                yield
                pu = PS()
                p.op("pe", lambda e: e.matmul(pu.ap()[:96, 0:96], lhsT=Khtm, rhs=Vtm, start=True, stop=True), reads=[mtm], writes=[pu])
                p.op("pe", lambda e: e.matmul(pu.ap()[:96, 96:192], lhsT=Khtm, rhs=ones.ap()[0:64, 0:96], start=True, stop=True), reads=[mtm, ones], writes=[pu])
                p.op("dve", lambda e: e.scalar_tensor_tensor(out=mS.ap()[:, h, :], in0=mS.ap()[:, h, :], scalar=B("G")[:, 64 * c + 63:64 * c + 64],
                                                             in1=pu.ap()[:96, 0:192], op0=ALU.mult, op1=ALU.add), reads=[mS, M_["G"], pu], writes=[mS])
        store(o_ml, "h p t -> p h t", yml, t0, 96, 4, "act")
        yield

    for _ in conv_block(0):
        pass
    for j in range(NB):
        czb[0] = cz2[j % 2]
        zo = zo_t[j % 2]
        th = [rwkv_block(j * 128, zo), mlstm_block(j * 128, zo), s5_block(j * 128, zo)]
        if j + 1 < NB:
            th.append(conv_block(j + 1))
        run_interleaved(th)


def mixer_consts():
    a = np.arange(128)
    ident = np.eye(128, dtype=np.float32)
    su = (a[:, None] < a[None, :]).astype(np.float32)
    sl = (a[:, None] > a[None, :]).astype(np.float32)
    iu = (a[:, None] <= a[None, :]).astype(np.float32)
    m01 = np.ones((128, 128), np.float32)
    m01[:, 0] = 0.0
    m01[:, 64] = 0.0
    cst = np.stack([ident, su, sl, iu, m01], 1).copy()
    ri = np.broadcast_to(np.arange(129, dtype=np.float32), (128, 129)).copy()
    return cst, ri


def mixer_params(P, i, d):
    m = {}
    cwf = P["conv_w"][i]
    if d == 1:
        cwf = cwf[::-1, ::-1]
    cw = np.zeros((128, NCONV, 9), np.float32)
    for ci in range(NCONV):
        _, c0, n = SCH[ci]
        cw[:n, ci, :] = cwf[:, :, c0:c0 + n].reshape(9, n).T
    m["cw"] = cw
    rwp = np.stack([P["rw_w0"][i, d].reshape(6, 64).T, P["rw_a0"][i, d].reshape(6, 64).T, P["rw_k_k"][i].reshape(6, 64).T,
                    P["rw_k_a"][i].reshape(6, 64).T, P["rw_r_k"][i].T], 1)
    m["rwp"] = np.ascontiguousarray(rwp, np.float32)
    m["wa_up"] = np.concatenate([P["rw_w_up"][i, d], P["rw_a_up"][i, d]], 0).astype(np.float32)
    m["glb"] = P["ml_gate_b"][i].reshape(16, 1).astype(np.float32)
    sel = np.zeros((16, 8, 96), np.float32)
    for j in range(8):
        sel[d * 8 + j, j, :] = 1.0
    m["sel"] = sel
    s5p = np.zeros((128, 3, 8), np.float32)
    bz = np.zeros((128, 16, 128), np.float32)
    cz = np.zeros((128, 16, 128), np.float32)
    for k in range(8):
        for gl2 in range(2):
            g = 2 * k + gl2
            js = slice(gl2 * 64, gl2 * 64 + 64)
            s5p[js, 0, k] = P["s5_a_re"][i, d, g]
            s5p[js, 1, k] = P["s5_a_im"][i, d, g]
            s5p[js, 2, k] = P["s5_log_dt"][i, d, g]
            r0 = (g % 8) * 16
            bz[r0:r0 + 16, k, js] = P["s5_b_re"][i, d, g].T
            bz[r0:r0 + 16, 8 + k, js] = P["s5_b_im"][i, d, g].T
            cz[js, k, r0:r0 + 16] = P["s5_c_re"][i, d, g].T
            cz[js, 8 + k, r0:r0 + 16] = P["s5_c_im"][i, d, g].T
    m["s5p"] = s5p
    m["bz"] = bz
    m["cz"] = cz
    cst, ri = mixer_consts()
    m["cst"] = cst
    m["ri"] = ri
    return m


def mixer_zs(zseq):
    NS = zseq.shape[0]
    zs = np.zeros((NSCH, 128, NS), np.float32)
    for ci, (_, c0, n) in enumerate(SCH):
        zs[ci, :n, :] = zseq[:, c0:c0 + n].T
    return zs


def emit_merge(p, x1T, oT, modT, lng_ap, lnb_ap, outs2, zall, zallb, mp):
    f32 = F32

    def T(shape, name, dt=f32):
        return p.tile(shape, dt, "g_" + name)

    mod = T([128, 72, 2], "mod"); g_sb = T([128, 8], "lng"); b_sb = T([128, 8], "lnb")
    gn = T([64, 2, 6], "gn"); gup = T([128, 384], "gup"); s5d = T([128, 2, 2], "s5d"); gluw = T([128, 2, 256], "gluw")
    mlg = T([96, 4], "mlg"); brb = T([128, 24], "brb")
    p.dma("sp", mod.ap(), modT.ap(), reads=[modT], writes=[mod])
    for i, (t, d) in enumerate(((g_sb, lng_ap), (b_sb, lnb_ap), (gn, mp["gn"]), (gup, mp["g_up"]), (s5d, mp["s5d"]),
                                (gluw, mp["glu_w"]), (mlg, mp["mlg"]), (brb, mp["brb"]))):
        p.dma(("sp", "act")[i % 2], t.ap(), d, writes=[t])
    ones = make_ones(p)
    uprw = T([64, 6, 1024], "uprw", BF16); ups5 = T([128, 2, 1024], "ups5", BF16); upml = T([96, 4, 1024], "upml", BF16)
    wout = T([128, 8, 1024], "wout", BF16)
    stage = [T([128, 1024], f"stage{i}") for i in range(2)]
    k = 0
    for h in range(6):
        st = stage[k % 2]
        p.dma(("sp", "act")[k % 2], st.ap()[:64, :], mp["up_rw"][:, h, :], writes=[st])
        p.op("dve", lambda e, h=h, st=st: e.tensor_copy(out=uprw.ap()[:, h, :], in_=st.ap()[:64, :]), reads=[st], writes=[uprw]); k += 1
    for h in range(2):
        st = stage[k % 2]
        p.dma(("sp", "act")[k % 2], st.ap(), mp["up_s5"][:, h, :], writes=[st])
        p.op("dve", lambda e, h=h, st=st: e.tensor_copy(out=ups5.ap()[:, h, :], in_=st.ap()), reads=[st], writes=[ups5]); k += 1
    for h in range(4):
        st = stage[k % 2]
        p.dma(("sp", "act")[k % 2], st.ap()[:96, :], mp["up_ml"][:, h, :], writes=[st])
        p.op("dve", lambda e, h=h, st=st: e.tensor_copy(out=upml.ap()[:, h, :], in_=st.ap()[:96, :]), reads=[st], writes=[upml]); k += 1
    for h in range(8):
        st = stage[k % 2]
        p.dma(("sp", "act")[k % 2], st.ap(), mp["w_out"][:, h, :], writes=[st])
        p.op("dve", lambda e, h=h, st=st: e.tensor_copy(out=wout.ap()[:, h, :], in_=st.ap()), reads=[st], writes=[wout]); k += 1

    x_sb = T([128, 8, TT], "x"); yrw = T([64, 2, 6, TT], "yrw"); ybn = T([64, 2, 6, TT], "ybn")
    ys5 = T([128, 2, 2, TT], "ys5"); yml = T([96, 2, 4, TT], "yml"); zg = T([128, 31, TT], "zg")
    rwy = T([64, 6, TT], "rwy", BF16); s5y = T([128, 2, TT], "s5y", BF16); mly = T([96, 4, TT], "mly", BF16)
    ym = T([128, 8, TT], "ym", BF16)
    yg = T([128, 2, TT], "yg")
    ta = T([128, TT], "ta"); tb = T([128, TT], "tb"); tc = T([128, TT], "tc"); td = T([128, TT], "td"); sgd = T([128, TT], "sgd")
    y_sb = T([128, 8, TT], "y"); ysq = T([128, 8, TT], "ysq"); mean = T([128, TT], "mean"); rstd = T([128, TT], "rstd")
    pss = [p.psum([128, 512], f32, f"pg{i}") for i in range(6)]
    ps1 = p.psum([128, 512], f32, "ps1"); ps2 = p.psum([128, 512], f32, "ps2")
    pc = [0]

    def PS():
        pc[0] += 1
        return pss[pc[0] % 6]

    def std_part(x, np_, n, eps, scale_ap, out_ap, out_buf):
        p.op("act", lambda e: e.activation(out=tb.ap()[:np_, :n], in_=x, func=AF.Square), reads=[ta], writes=[tb])
        q = PS()
        p.op("pe", lambda e: e.matmul(q.ap()[:np_, 0:n], lhsT=ones.ap()[0:np_, 0:np_], rhs=x, start=True, stop=True), reads=[ones, ta], writes=[q])
        p.op("pe", lambda e: e.matmul(q.ap()[:np_, 256:256 + n], lhsT=ones.ap()[0:np_, 0:np_], rhs=tb.ap()[:np_, :n], start=True, stop=True), reads=[ones, tb], writes=[q])
        p.op("dve", lambda e: e.tensor_scalar(out=tc.ap()[:np_, :n], in0=q.ap()[:np_, 0:n], scalar1=1.0 / np_, scalar2=None, op0=ALU.mult), reads=[q], writes=[tc])
        p.op("dve", lambda e: e.tensor_tensor(out=td.ap()[:np_, :n], in0=tc.ap()[:np_, :n], in1=tc.ap()[:np_, :n], op=ALU.mult), reads=[tc], writes=[td])
        p.op("dve", lambda e: e.scalar_tensor_tensor(out=td.ap()[:np_, :n], in0=q.ap()[:np_, 256:256 + n], scalar=1.0 / np_, in1=td.ap()[:np_, :n], op0=ALU.mult, op1=ALU.subtract), reads=[q, td], writes=[td])
        p.op("dve", lambda e: e.tensor_scalar(out=td.ap()[:np_, :n], in0=td.ap()[:np_, :n], scalar1=eps, scalar2=None, op0=ALU.add), reads=[td], writes=[td])
        p.op("act", lambda e: e.activation(out=td.ap()[:np_, :n], in_=td.ap()[:np_, :n], func=AF.Sqrt), reads=[td], writes=[td])
        p.op("dve", lambda e: e.reciprocal(out=td.ap()[:np_, :n], in_=td.ap()[:np_, :n]), reads=[td], writes=[td])
        p.op("dve", lambda e: e.tensor_tensor(out=x, in0=x, in1=tc.ap()[:np_, :n], op=ALU.subtract), reads=[ta, tc], writes=[ta])
        p.op("dve", lambda e: e.scalar_tensor_tensor(out=out_ap, in0=x, scalar=scale_ap, in1=td.ap()[:np_, :n], op0=ALU.mult, op1=ALU.mult), reads=[ta, td, gn, mlg], writes=[out_buf])

    xv = x1T.ap().rearrange("c p t -> p c t")
    ov = oT.ap().rearrange("c p t -> p c t")
    GC = 2.0 * float(np.sqrt(2.0 / np.pi))
    for ti, (t0, n, wh) in enumerate(TILES):
        sl = slice(t0, t0 + n)
        p.dma("sp", x_sb.ap()[:, :, :n], xv[:, :, sl], reads=[x1T], writes=[x_sb])
        for d in range(2):
            o_rw, o_bn, o_s5, o_ml = outs2[d]
            p.dma("act", yrw.ap()[:, d, :, :n], o_rw.ap()[:, :, sl].rearrange("h p t -> p h t"), reads=[o_rw], writes=[yrw])
            p.dma("sp", ybn.ap()[:, d, :, :n], o_bn.ap()[:, :, sl].rearrange("h p t -> p h t"), reads=[o_bn], writes=[ybn])
            p.dma("act", ys5.ap()[:, d, :, :n], o_s5.ap()[:, :, sl].rearrange("h p t -> p h t"), reads=[o_s5], writes=[ys5])
            p.dma("sp", yml.ap()[:, d, :, :n], o_ml.ap()[:, :, sl].rearrange("h p t -> p h t"), reads=[o_ml], writes=[yml])
        zav = zall.ap().rearrange("c p t -> p c t")
        zbv = zallb.ap().rearrange("c p t -> p c t")
        p.dma("act", zg.ap()[:, 0:29, :n], zbv[:, :, sl], reads=[zallb], writes=[zg])
        p.dma("sp", zg.ap()[:, 29:31, :n], zav[:, 31:33, sl], reads=[zall], writes=[zg])
        p.op("act", lambda e: e.activation(out=sgd.ap()[:, :n], in_=zg.ap()[:, 4, :n], func=AF.Sigmoid), reads=[zg], writes=[sgd])
        for h in range(6):
            x = ta.ap()[:64, :n]
            p.op("dve", lambda e, h=h: e.tensor_tensor(out=x, in0=yrw.ap()[:, 0, h, :n], in1=yrw.ap()[:, 1, h, :n], op=ALU.add), reads=[yrw], writes=[ta])
            std_part(x, 64, n, 64e-5, gn.ap()[:, 0, h:h + 1], x, ta)
            p.op("dve", lambda e, h=h: e.scalar_tensor_tensor(out=x, in0=x, scalar=gn.ap()[:, 1, h:h + 1], in1=ybn.ap()[:, 0, h, :n], op0=ALU.add, op1=ALU.add), reads=[ta, gn, ybn], writes=[ta])
            p.op("dve", lambda e, h=h: e.tensor_tensor(out=x, in0=x, in1=ybn.ap()[:, 1, h, :n], op=ALU.add), reads=[ta, ybn], writes=[ta])
            q = PS()
            p.op("pe", lambda e, h=h: e.matmul(q.ap()[:64, 0:n], lhsT=gup.ap()[:, h * 64:(h + 1) * 64], rhs=sgd.ap()[:, :n], start=True, stop=True), reads=[gup, sgd], writes=[q])
            p.op("dve", lambda e, h=h: e.tensor_tensor(out=rwy.ap()[:, h, :n], in0=x, in1=q.ap()[:64, 0:n], op=ALU.mult), reads=[ta, q], writes=[rwy])
        for mt in range(2):
            x = yg.ap()[:, mt, :n]
            p.op("dve", lambda e, mt=mt: e.scalar_tensor_tensor(out=x, in0=zg.ap()[:, 29 + mt, :n], scalar=s5d.ap()[:, 0, mt:mt + 1], in1=ys5.ap()[:, 0, mt, :n], op0=ALU.mult, op1=ALU.add), reads=[zg, s5d, ys5], writes=[yg])
            p.op("dve", lambda e, mt=mt: e.tensor_tensor(out=x, in0=x, in1=ys5.ap()[:, 1, mt, :n], op=ALU.add), reads=[yg, ys5], writes=[yg])
            p.op("act", lambda e: e.activation(out=tb.ap()[:, :n], in_=x, func=AF.Square), reads=[yg], writes=[tb])
            p.op("dve", lambda e: e.tensor_scalar(out=tb.ap()[:, :n], in0=tb.ap()[:, :n], scalar1=0.044715, scalar2=1.0, op0=ALU.mult, op1=ALU.add), reads=[tb], writes=[tb])
            p.op("dve", lambda e: e.tensor_tensor(out=tb.ap()[:, :n], in0=tb.ap()[:, :n], in1=x, op=ALU.mult), reads=[tb, yg], writes=[tb])
            p.op("act", lambda e: e.activation(out=tb.ap()[:, :n], in_=tb.ap()[:, :n], func=AF.Sigmoid, scale=GC), reads=[tb], writes=[tb])
            p.op("dve", lambda e: e.tensor_tensor(out=x, in0=x, in1=tb.ap()[:, :n], op=ALU.mult), reads=[yg, tb], writes=[yg])
        for mo in range(2):
            q = PS()
            for kc in range(2):
                p.op("pe", lambda e, mo=mo, kc=kc: e.matmul(q.ap()[:, 0:n], lhsT=gluw.ap()[:, kc, mo * 128:(mo + 1) * 128], rhs=yg.ap()[:, kc, :n], start=(kc == 0), stop=(kc == 1)), reads=[gluw, yg], writes=[q])
            p.op("act", lambda e, mo=mo: e.activation(out=tb.ap()[:, :n], in_=q.ap()[:, 0:n], func=AF.Sigmoid, bias=s5d.ap()[:, 1, mo:mo + 1]), reads=[q, s5d], writes=[tb])
            p.op("dve", lambda e, mo=mo: e.tensor_tensor(out=s5y.ap()[:, mo, :n], in0=yg.ap()[:, mo, :n], in1=tb.ap()[:, :n], op=ALU.mult), reads=[yg, tb], writes=[s5y])
        for h in range(4):
            x = ta.ap()[:96, :n]
            p.op("dve", lambda e, h=h: e.tensor_tensor(out=x, in0=yml.ap()[:, 0, h, :n], in1=yml.ap()[:, 1, h, :n], op=ALU.add), reads=[yml], writes=[ta])
            p.op("act", lambda e, h=h: e.activation(out=tb.ap()[:96, :n], in_=zg.ap()[:96, h, :n], func=AF.Sigmoid), reads=[zg], writes=[tb])
            p.op("dve", lambda e: e.tensor_tensor(out=x, in0=x, in1=tb.ap()[:96, :n], op=ALU.mult), reads=[ta, tb], writes=[ta])
            std_part(x, 96, n, 1e-5, mlg.ap()[:, h:h + 1], mly.ap()[:, h, :n], mly)
        for dc in range(8):
            q = PS()
            for h in range(6):
                p.op("pe", lambda e, h=h, dc=dc: e.matmul(q.ap()[:, 0:n], lhsT=uprw.ap()[:, h, dc * 128:(dc + 1) * 128], rhs=rwy.ap()[:, h, :n], start=(h == 0), stop=(h == 5)), reads=[uprw, rwy], writes=[q])
            p.op("act", lambda e, dc=dc: e.activation(out=tb.ap()[:, :n], in_=zg.ap()[:, 5 + dc, :n], func=AF.Sigmoid, bias=brb.ap()[:, dc:dc + 1]), reads=[zg, brb], writes=[tb])
            p.op("dve", lambda e: e.tensor_tensor(out=ta.ap()[:, :n], in0=q.ap()[:, 0:n], in1=tb.ap()[:, :n], op=ALU.mult), reads=[q, tb], writes=[ta])
            q = PS()
            for h in range(2):
                p.op("pe", lambda e, h=h, dc=dc: e.matmul(q.ap()[:, 0:n], lhsT=ups5.ap()[:, h, dc * 128:(dc + 1) * 128], rhs=s5y.ap()[:, h, :n], start=(h == 0), stop=(h == 1)), reads=[ups5, s5y], writes=[q])
            p.op("act", lambda e, dc=dc: e.activation(out=tb.ap()[:, :n], in_=zg.ap()[:, 13 + dc, :n], func=AF.Sigmoid, bias=brb.ap()[:, 8 + dc:9 + dc]), reads=[zg, brb], writes=[tb])
            p.op("dve", lambda e: e.tensor_tensor(out=tc.ap()[:, :n], in0=q.ap()[:, 0:n], in1=tb.ap()[:, :n], op=ALU.mult), reads=[q, tb], writes=[tc])
            p.op("dve", lambda e: e.tensor_tensor(out=ta.ap()[:, :n], in0=ta.ap()[:, :n], in1=tc.ap()[:, :n], op=ALU.add), reads=[ta, tc], writes=[ta])
            q = PS()
            for h in range(4):
                p.op("pe", lambda e, h=h, dc=dc: e.matmul(q.ap()[:, 0:n], lhsT=upml.ap()[:, h, dc * 128:(dc + 1) * 128], rhs=mly.ap()[:, h, :n], start=(h == 0), stop=(h == 3)), reads=[upml, mly], writes=[q])
            p.op("act", lambda e, dc=dc: e.activation(out=tb.ap()[:, :n], in_=zg.ap()[:, 21 + dc, :n], func=AF.Sigmoid, bias=brb.ap()[:, 16 + dc:17 + dc]), reads=[zg, brb], writes=[tb])
            p.op("dve", lambda e: e.tensor_tensor(out=tc.ap()[:, :n], in0=q.ap()[:, 0:n], in1=tb.ap()[:, :n], op=ALU.mult), reads=[q, tb], writes=[tc])
            p.op("dve", lambda e, dc=dc: e.tensor_tensor(out=ym.ap()[:, dc, :n], in0=ta.ap()[:, :n], in1=tc.ap()[:, :n], op=ALU.add), reads=[ta, tc], writes=[ym])
        p.op("act", lambda e: e.mul(out=x_sb.ap()[:, :, :n], in_=x_sb.ap()[:, :, :n], mul=ALPHA), reads=[x_sb], writes=[x_sb])
        for dc in range(8):
            q = PS()
            for kc in range(8):
                p.op("pe", lambda e, kc=kc, dc=dc: e.matmul(q.ap()[:, 0:n], lhsT=wout.ap()[:, kc, dc * 128:(dc + 1) * 128], rhs=ym.ap()[:, kc, :n], start=(kc == 0), stop=(kc == 7)), reads=[wout, ym], writes=[q])
            p.op("dve", lambda e, dc=dc: e.scalar_tensor_tensor(out=y_sb.ap()[:, dc, :n], in0=q.ap()[:, 0:n], scalar=mod.ap()[:, 40 + dc, wh:wh + 1], in1=x_sb.ap()[:, dc, :n], op0=ALU.mult, op1=ALU.add), reads=[q, mod, x_sb], writes=[y_sb])
        ln_feature_major(p, y_sb, ysq, n, ones, ps1, ps2, mean, rstd, g_sb, b_sb, ysq)
        p.dma("sp", ov[:, :, sl], ysq.ap()[:, :, :n], reads=[ysq], writes=[oT])


MIX_KEYS = (("cw", [128, NCONV, 9]), ("rwp", [64, 5, 6]), ("wa_up", [128, 384]), ("glb", [16, 1]), ("sel", [16, 8, 96]),
            ("s5p", [128, 3, 8]), ("bz", [128, 16, 128]), ("cz", [128, 16, 128]))
MRG_KEYS = (("gn", [64, 2, 6]), ("g_up", [128, 384]), ("s5d", [128, 2, 2]), ("glu_w", [128, 2, 256]), ("mlg", [96, 4]),
            ("up_rw", [64, 6, 1024]), ("up_s5", [128, 2, 1024]), ("up_ml", [96, 4, 1024]), ("brb", [128, 24]),
            ("w_out", [128, 8, 1024]))
NUSED = 4


def set_sizes(nctx, nlat):
    global NLAT, NCTX, NT, TILES
    NLAT, NCTX = nlat, nctx
    NT = NLAT + NCTX
    TILES = [(0, NCTX, 1)] + [(NCTX + i * TT, TT, 0) for i in range(NLAT // TT)]


def build_fused():
    p = Prog()
    NS = NT
    din = lambda n, sh: p.dram(n, sh, F32, "in")
    xT0 = din("xT0", [8, 128, NS])
    cvec = din("cvec", [128, 8, 2])
    ada_w = din("ada_w", [DEPTH, D, 9 * D])
    ada_b = din("ada_b_l", [DEPTH, 128, 72])
    ln_g = din("ln_g_l", [DEPTH, 3, 128, 8])
    ln_b = din("ln_b_l", [DEPTH, 3, 128, 8])
    wg = din("ffn_w_gate", [DEPTH, 2, D, DFF])
    wu = din("ffn_w_up", [DEPTH, 2, D, DFF])
    wd = din("ffn_w_down", [DEPTH, 2, DFF, D])
    w_in = din("w_in", [DEPTH, D, INW])
    cst = din("cst", [128, 5, 128])
    ri = din("ri", [128, 129])
    mixp = {}
    for i in range(DEPTH):
        for d in range(2):
            mixp[(i, d)] = {k: din(f"m{i}{d}_{k}", sh).ap() for k, sh in MIX_KEYS}
            mixp[(i, d)]["cst"] = cst.ap()
            mixp[(i, d)]["ri"] = ri.ap()
    mrgp = {i: {k: din(f"g{i}_{k}", sh).ap() for k, sh in MRG_KEYS} for i in range(DEPTH)}
    oT = p.dram("oT", [8, 128, NS], F32, "out")
    tmp = lambda n, sh: p.dram(n, sh, F32, "tmp")
    S1 = tmp("S1", [8, 128, NS])
    S2 = tmp("S2", [8, 128, NS])
    zall = tmp("zall", [NSCH, 128, NS])
    zallb = tmp("zallb", [NZALL - NSCH, 128, NS])
    modT = [tmp(f"modT{i}", [128, 72, 2]) for i in range(DEPTH)]
    outs = [(tmp(f"o_rw{d}", [6, 64, NS]), tmp(f"o_bn{d}", [6, 64, NS]), tmp(f"o_s5{d}", [2, 128, NS]),
             tmp(f"o_ml{d}", [4, 96, NS])) for d in range(2)]
    chain = [(xT0, S1, S1, S2, S1), (S1, S2, S2, S1, oT)]
    for i in range(DEPTH):
        a_src, a_dst, m_src, m_dst, f_dst = chain[i]
        with p.phase():
            emit_mod(p, cvec, ada_w.ap()[i], ada_b.ap()[i], modT[i])
        with p.phase():
            emit_ffn(p, 0, a_src, a_dst, modT[i], ln_g.ap()[i, 0], ln_b.ap()[i, 0], wg.ap()[i, 0], wu.ap()[i, 0], wd.ap()[i, 0])
        with p.phase():
            emit_inproj(p, a_dst, modT[i], w_in.ap()[i], zall, zallb)
        for d in range(2):
            with p.phase():
                emit_mixer(p, zall, mixp[(i, d)], outs[d], rev=(d == 1), nctx=NCTX, nlat=NLAT)
        with p.phase():
            emit_merge(p, m_src, m_dst, modT[i], ln_g.ap()[i, 1], ln_b.ap()[i, 1], outs, zall, zallb, mrgp[i])
        with p.phase():
            emit_ffn(p, 6, m_dst, f_dst, modT[i], ln_g.ap()[i, 2], ln_b.ap()[i, 2], wg.ap()[i, 1], wu.ap()[i, 1], wd.ap()[i, 1])
    p.finish()
    return p


def merge_params(P, i):
    m = {}
    m["gn"] = np.ascontiguousarray(np.stack([P["rw_gn_g"][i].reshape(6, 64).T, P["rw_gn_b"][i].reshape(6, 64).T], 1))
    m["g_up"] = P["rw_g_up"][i]
    m["s5d"] = np.ascontiguousarray(np.stack([P["s5_d"][i].reshape(2, 128).T, P["s5_glu_b"][i].reshape(2, 128).T], 1))
    m["glu_w"] = np.ascontiguousarray(P["s5_glu_w"][i].reshape(2, 128, 256).transpose(1, 0, 2))
    m["mlg"] = np.ascontiguousarray(P["ml_norm_g"][i].reshape(4, 96).T)
    m["up_rw"] = np.ascontiguousarray(P["up_rw"][i].reshape(6, 64, 1024).transpose(1, 0, 2))
    m["up_s5"] = np.ascontiguousarray(P["up_s5"][i].reshape(2, 128, 1024).transpose(1, 0, 2))
    m["up_ml"] = np.ascontiguousarray(P["up_ml"][i].reshape(4, 96, 1024).transpose(1, 0, 2))
    m["brb"] = np.ascontiguousarray(P["br_gate_b"][i].reshape(24, 128).T)
    m["w_out"] = np.ascontiguousarray(P["w_out"][i].reshape(8, 128, 1024).transpose(1, 0, 2))
    return m


_PROG = []


def kernel(**inputs):
    P = {k: np.asarray(v, dtype=np.float32) for k, v in inputs.items()}
    x, c, ctx, c_ctx = P["x"], P["c"], P["ctx"], P["c_ctx"]
    B, SEQ, _ = x.shape
    if not _PROG:
        set_sizes(ctx.shape[1], SEQ)
        _PROG.append(build_fused())
    prog = _PROG[0]
    NUSED = B
    shared = {
        "ada_w": P["ada_w"],
        "ada_b_l": np.ascontiguousarray(P["ada_b"].reshape(DEPTH, 72, 128).transpose(0, 2, 1)),
        "ln_g_l": np.ascontiguousarray(P["ln_g"].reshape(DEPTH, 3, 8, 128).transpose(0, 1, 3, 2)),
        "ln_b_l": np.ascontiguousarray(P["ln_b"].reshape(DEPTH, 3, 8, 128).transpose(0, 1, 3, 2)),
        "ffn_w_gate": P["ffn_w_gate"], "ffn_w_up": P["ffn_w_up"], "ffn_w_down": P["ffn_w_down"], "w_in": P["w_in"],
    }
    cst, ri = mixer_consts()
    shared["cst"] = cst
    shared["ri"] = ri
    for i in range(DEPTH):
        for d in range(2):
            mp = mixer_params(P, i, d)
            for k, _ in MIX_KEYS:
                shared[f"m{i}{d}_{k}"] = np.ascontiguousarray(mp[k], np.float32)
        gp = merge_params(P, i)
        for k, _ in MRG_KEYS:
            shared[f"g{i}_{k}"] = np.ascontiguousarray(gp[k], np.float32)
    ims = []
    for cid in range(NUSED):
        b = cid % B
        m = dict(shared)
        xx = np.concatenate([ctx[b], x[b]], 0)
        m["xT0"] = np.ascontiguousarray(xx.T).reshape(8, 128, NT)
        m["cvec"] = np.ascontiguousarray(np.stack([c[b], c_ctx], -1).reshape(8, 128, 2).transpose(1, 0, 2))
        ims.append(m)
    res = run_bass_kernel_spmd(prog.nc, ims, core_ids=list(range(NUSED)))
    out = np.empty((B, SEQ, D), np.float32)
    for b in range(B):
        out[b] = res.results[b]["oT"].reshape(D, NT).T[NCTX:]
    return out
```
